# Optimizing a Trainium2 kernel written in Bass

```python
import math
import jax, jax.numpy as jnp
from jax import lax
import numpy as np

D_MODEL = 1024
BATCH = 4
SEQ = 8192
DEPTH = 2

N_META = 16
CHUNK = 128
PAD = CHUNK - N_META
EPS = 1e-6
NEG_INF = -1e30
D_FF = 2816

FOX_HEADS = 8
FOX_DIM = 64
FOX_W = FOX_HEADS * FOX_DIM
GDN_HEADS = 4
GDN_DK = 128
GDN_DV = 128
GDN_CONV = 4
GDN_QKW = GDN_HEADS * GDN_DK
GDN_VW = GDN_HEADS * GDN_DV
GDN_QKV = 2 * GDN_QKW + GDN_VW
HYB_SIZES = (FOX_W, FOX_W, FOX_W, FOX_HEADS, GDN_QKV, GDN_HEADS, GDN_HEADS, GDN_VW)
HYB_IN = 3 * FOX_W + FOX_HEADS + GDN_QKV + 2 * GDN_HEADS + GDN_VW
MIX_W = FOX_W + GDN_VW
RWKV_HEAD = 64
RWKV_HEADS = D_MODEL // RWKV_HEAD
RWKV_DECAY_LORA = 64
RWKV_A_LORA = 64
RWKV_GATE_LORA = 160
RWKV_GN_EPS = 64e-5

kernel_name = 'hybrid_fox_gdn_rwkv7_macaron'


def rms_norm(x, gain):
    x32 = x.astype(jnp.float32)
    y = x32 * lax.rsqrt(jnp.mean(x32 * x32, axis=-1, keepdims=True) + EPS)
    return (y * gain.astype(jnp.float32)).astype(x.dtype)


def l2_normalize(x):
    x32 = x.astype(jnp.float32)
    return x32 * lax.rsqrt(jnp.sum(x32 * x32, axis=-1, keepdims=True) + EPS)


def swiglu_ffn(h, w_in, w_out):
    gate, up = jnp.split(h @ w_in, 2, axis=-1)
    return (jax.nn.silu(gate) * up) @ w_out


def split_cols(z, sizes):
    return jnp.split(z, [int(s) for s in np.cumsum(sizes)[:-1]], axis=-1)


def to_heads(t, n, d):
    B_, L, _ = t.shape
    return t.reshape(B_, L, n, d).transpose(0, 2, 1, 3).astype(jnp.float32)


def pad_seq(t):
    widths = [(0, 0)] * t.ndim
    widths[2] = (PAD, 0)
    return jnp.pad(t, widths)


def causal_depthwise_conv(x, w):
    K = w.shape[0]
    L = x.shape[1]
    xp = jnp.pad(x, ((0, 0), (K - 1, 0), (0, 0)))
    y = xp[:, 0:L] * w[0]
    for j in range(1, K):
        y = y + xp[:, j:j + L] * w[j]
    return y


def forgetting_attention(q, k, v, log_f):
    B_, H, LP, Dh = q.shape
    n_blocks = LP // CHUNK
    c = jnp.cumsum(log_f, axis=-1)
    kpos = jnp.arange(LP)
    scale = Dh ** -0.5

    def block(i):
        start = i * CHUNK
        qb = lax.dynamic_slice_in_dim(q, start, CHUNK, axis=2)
        cb = lax.dynamic_slice_in_dim(c, start, CHUNK, axis=2)
        qpos = start + jnp.arange(CHUNK)
        logits = jnp.einsum('bhqd,bhkd->bhqk', qb, k) * scale + cb[..., :, None] - c[..., None, :]
        valid = (kpos[None, :] <= qpos[:, None]) & (kpos[None, :] >= PAD)
        p = jax.nn.softmax(jnp.where(valid, logits, NEG_INF), axis=-1)
        return jnp.einsum('bhqk,bhkd->bhqd', p, v)

    out = lax.map(block, jnp.arange(n_blocks))
    return jnp.moveaxis(out, 0, 2).reshape(B_, H, LP, Dh)


def gated_delta_rule(q, k, v, log_g, beta):
    B_, H, LP, Dk = q.shape
    Dv = v.shape[-1]
    nc = LP // CHUNK
    q = (q * Dk ** -0.5).reshape(B_, H, nc, CHUNK, Dk)
    k = k.reshape(B_, H, nc, CHUNK, Dk)
    v = v.reshape(B_, H, nc, CHUNK, Dv)
    beta = beta.reshape(B_, H, nc, CHUNK)
    gc = jnp.cumsum(log_g.reshape(B_, H, nc, CHUNK), axis=-1)
    idx = jnp.arange(CHUNK)
    causal = idx[:, None] >= idx[None, :]
    strict = idx[:, None] > idx[None, :]
    decay = jnp.exp(jnp.where(causal, gc[..., :, None] - gc[..., None, :], -jnp.inf))
    kb = k * beta[..., None]
    lower = jnp.where(strict, jnp.einsum('bhnid,bhnjd->bhnij', kb, k) * decay, 0.0)
    egc = jnp.exp(gc)[..., None]
    rhs = jnp.concatenate([v * beta[..., None], kb * egc], axis=-1)
    sol = lax.linalg.triangular_solve(lower, rhs, left_side=True, lower=True, unit_diagonal=True)
    u_base, w = sol[..., :Dv], sol[..., Dv:]
    attn = jnp.where(causal, jnp.einsum('bhnid,bhnjd->bhnij', q, k) * decay, 0.0)
    q_dec = q * egc
    g_last = gc[..., -1]
    k_dec = k * jnp.exp(g_last[..., None] - gc)[..., None]
    xs = (jnp.moveaxis(u_base, 2, 0), jnp.moveaxis(w, 2, 0), jnp.moveaxis(attn, 2, 0),
          jnp.moveaxis(q_dec, 2, 0), jnp.moveaxis(k_dec, 2, 0), jnp.moveaxis(jnp.exp(g_last), 2, 0))

    def step(S, inp):
        u_b, w_c, a_c, qd, kd, dl = inp
        u = u_b - jnp.einsum('bhck,bhkv->bhcv', w_c, S)
        o = jnp.einsum('bhck,bhkv->bhcv', qd, S) + jnp.einsum('bhij,bhjv->bhiv', a_c, u)
        S = S * dl[..., None, None] + jnp.einsum('bhck,bhcv->bhkv', kd, u)
        return S, o

    S0 = jnp.zeros((B_, H, Dk, Dv), jnp.float32)
    _, o = lax.scan(step, S0, xs)
    return jnp.moveaxis(o, 0, 2).reshape(B_, H, LP, Dv)


def hybrid_attention_mixer(h, w_in, fox_bf, conv_w, a_log, dt_bias, o_gain, w_out):
    B_, L, _ = h.shape
    f32 = jnp.float32
    fq, fk, fv, ff, gqkv, ga, gb, gz = split_cols(h @ w_in, HYB_SIZES)
    q_f = pad_seq(to_heads(fq, FOX_HEADS, FOX_DIM))
    k_f = pad_seq(to_heads(fk, FOX_HEADS, FOX_DIM))
    v_f = pad_seq(to_heads(fv, FOX_HEADS, FOX_DIM))
    log_f = pad_seq(jax.nn.log_sigmoid((ff + fox_bf).astype(f32)).transpose(0, 2, 1))
    o_fox = forgetting_attention(q_f, k_f, v_f, log_f)[:, :, PAD:]
    o_fox = o_fox.transpose(0, 2, 1, 3).reshape(B_, L, FOX_W)
    gqkv = jax.nn.silu(causal_depthwise_conv(gqkv, conv_w))
    gq, gk, gv = split_cols(gqkv, (GDN_QKW, GDN_QKW, GDN_VW))
    q_g = pad_seq(l2_normalize(to_heads(gq, GDN_HEADS, GDN_DK)))
    k_g = pad_seq(l2_normalize(to_heads(gk, GDN_HEADS, GDN_DK)))
    v_g = pad_seq(to_heads(gv, GDN_HEADS, GDN_DV))
    log_g = -jnp.exp(a_log.astype(f32)) * jax.nn.softplus((ga + dt_bias).astype(f32))
    log_g = pad_seq(log_g.transpose(0, 2, 1))
    beta = pad_seq(jax.nn.sigmoid(gb.astype(f32)).transpose(0, 2, 1))
    o_gdn = gated_delta_rule(q_g, k_g, v_g, log_g, beta)[:, :, PAD:].transpose(0, 2, 1, 3)
    o_gdn = rms_norm(o_gdn, o_gain) * jax.nn.silu(gz.reshape(B_, L, GDN_HEADS, GDN_DV).astype(f32))
    o = jnp.concatenate([o_fox, o_gdn.reshape(B_, L, GDN_VW)], axis=-1)
    return o @ w_out


def rwkv7_recurrence(r, decay, k, v, a, b):
    B_, L, H, N = r.shape
    xs = (jnp.moveaxis(r, 1, 0), jnp.moveaxis(decay, 1, 0), jnp.moveaxis(k, 1, 0),
          jnp.moveaxis(v, 1, 0), jnp.moveaxis(a, 1, 0), jnp.moveaxis(b, 1, 0))

    def step(S, inp):
        r_t, w_t, k_t, v_t, a_t, b_t = inp
        sa = jnp.einsum('bhvk,bhk->bhv', S, a_t)
        S = S * w_t[:, :, None, :] + sa[..., None] * b_t[:, :, None, :] + v_t[..., None] * k_t[:, :, None, :]
        return S, jnp.einsum('bhvk,bhk->bhv', S, r_t)

    S0 = jnp.zeros((B_, H, N, N), jnp.float32)
    _, y = lax.scan(step, S0, xs)
    return jnp.moveaxis(y, 0, 1)


def rwkv7_time_mix(h, mu, w_r, w_k, w_v, w0, w1, w2, a0, a1, a2, g1, g2, k_k, k_a, r_k, ln_w, ln_b, w_o):
    B_, L, D = h.shape
    H, N = RWKV_HEADS, RWKV_HEAD
    f32 = jnp.float32
    xx = jnp.pad(h, ((0, 0), (1, 0), (0, 0)))[:, :L] - h
    xr, xw, xk, xv, xa, xg = [h + xx * mu[i] for i in range(6)]
    r = xr @ w_r
    k = xk @ w_k
    v = xv @ w_v
    w_log = -jax.nn.softplus(-(w0 + jnp.tanh(xw @ w1) @ w2)) - 0.5
    a = jax.nn.sigmoid(a0 + (xa @ a1) @ a2)
    g = jax.nn.sigmoid(xg @ g1) @ g2
    kk = l2_normalize((k * k_k).reshape(B_, L, H, N))
    k = k * (1 + (a - 1) * k_a)
    r_h = r.reshape(B_, L, H, N).astype(f32)
    k_h = k.reshape(B_, L, H, N).astype(f32)
    v_h = v.reshape(B_, L, H, N).astype(f32)
    a_h = a.reshape(B_, L, H, N).astype(f32)
    decay = jnp.exp(-jnp.exp(w_log.reshape(B_, L, H, N).astype(f32)))
    y = rwkv7_recurrence(r_h, decay, k_h, v_h, -kk, kk * a_h)
    mean = jnp.mean(y, axis=-1, keepdims=True)
    var = jnp.mean(jnp.square(y - mean), axis=-1, keepdims=True)
    y = ((y - mean) * lax.rsqrt(var + RWKV_GN_EPS)).reshape(B_, L, D) * ln_w + ln_b
    bonus = jnp.sum(r_h * k_h * r_k.astype(f32), axis=-1, keepdims=True) * v_h
    return ((y + bonus.reshape(B_, L, D)) * g) @ w_o


def setup_inputs(seed: int = 0) -> dict:
    key = jax.random.key(seed)
    ks = iter(jax.random.split(key, 40))
    f32 = jnp.float32
    NE = (DEPTH + 1) // 2
    NO = DEPTH // 2

    def nrm(shape, scale):
        return jax.random.normal(next(ks), shape, f32) * scale

    def unif(shape, lo, hi):
        return jax.random.uniform(next(ks), shape, f32, minval=lo, maxval=hi)

    dt = jnp.exp(unif((NE, GDN_HEADS), math.log(1e-3), math.log(1e-1)))
    return {
        'x': nrm((BATCH, SEQ, D_MODEL), 1.0),
        'meta': nrm((N_META, D_MODEL), 1.0),
        'ffn_norm': 1.0 + nrm((DEPTH, 2, D_MODEL), 0.02),
        'ffn_w_in': nrm((DEPTH, 2, D_MODEL, 2 * D_FF), D_MODEL ** -0.5),
        'ffn_w_out': nrm((DEPTH, 2, D_FF, D_MODEL), D_FF ** -0.5),
        'mix_norm': 1.0 + nrm((DEPTH, D_MODEL), 0.02),
        'hyb_w_in': nrm((NE, D_MODEL, HYB_IN), D_MODEL ** -0.5),
        'hyb_fox_bf': 3.0 + nrm((NE, FOX_HEADS), 0.5),
        'hyb_conv': nrm((NE, GDN_CONV, GDN_QKV), GDN_CONV ** -0.5),
        'hyb_a_log': jnp.log(unif((NE, GDN_HEADS), 1.0, 16.0)),
        'hyb_dt_bias': dt + jnp.log(-jnp.expm1(-dt)),
        'hyb_o_gain': 1.0 + nrm((NE, GDN_DV), 0.02),
        'hyb_w_out': nrm((NE, MIX_W, D_MODEL), MIX_W ** -0.5),
        'rwkv_mu': unif((NO, 6, D_MODEL), 0.0, 1.0),
        'rwkv_w_r': nrm((NO, D_MODEL, D_MODEL), D_MODEL ** -0.5),
        'rwkv_w_k': nrm((NO, D_MODEL, D_MODEL), D_MODEL ** -0.5),
        'rwkv_w_v': nrm((NO, D_MODEL, D_MODEL), D_MODEL ** -0.5),
        'rwkv_w0': unif((NO, D_MODEL), -6.0, -1.0),
        'rwkv_w1': nrm((NO, D_MODEL, RWKV_DECAY_LORA), D_MODEL ** -0.5),
        'rwkv_w2': nrm((NO, RWKV_DECAY_LORA, D_MODEL), 0.1 * RWKV_DECAY_LORA ** -0.5),
        'rwkv_a0': nrm((NO, D_MODEL), 0.1),
        'rwkv_a1': nrm((NO, D_MODEL, RWKV_A_LORA), D_MODEL ** -0.5),
        'rwkv_a2': nrm((NO, RWKV_A_LORA, D_MODEL), 0.1 * RWKV_A_LORA ** -0.5),
        'rwkv_g1': nrm((NO, D_MODEL, RWKV_GATE_LORA), D_MODEL ** -0.5),
        'rwkv_g2': nrm((NO, RWKV_GATE_LORA, D_MODEL), RWKV_GATE_LORA ** -0.5),
        'rwkv_k_k': 0.85 + nrm((NO, D_MODEL), 0.02),
        'rwkv_k_a': 1.0 + nrm((NO, D_MODEL), 0.02),
        'rwkv_r_k': nrm((NO, RWKV_HEADS, RWKV_HEAD), 0.1),
        'rwkv_ln_w': 1.0 + nrm((NO, D_MODEL), 0.02),
        'rwkv_ln_b': nrm((NO, D_MODEL), 0.02),
        'rwkv_w_o': nrm((NO, D_MODEL, D_MODEL), D_MODEL ** -0.5),
        'final_norm': 1.0 + nrm((D_MODEL,), 0.02),
    }


def reference(x, meta, ffn_norm, ffn_w_in, ffn_w_out, mix_norm, hyb_w_in, hyb_fox_bf, hyb_conv,
              hyb_a_log, hyb_dt_bias, hyb_o_gain, hyb_w_out, rwkv_mu, rwkv_w_r, rwkv_w_k, rwkv_w_v,
              rwkv_w0, rwkv_w1, rwkv_w2, rwkv_a0, rwkv_a1, rwkv_a2, rwkv_g1, rwkv_g2, rwkv_k_k,
              rwkv_k_a, rwkv_r_k, rwkv_ln_w, rwkv_ln_b, rwkv_w_o, final_norm):
    B_ = x.shape[0]
    h = jnp.concatenate([jnp.broadcast_to(meta[None], (B_, N_META, D_MODEL)).astype(x.dtype), x], axis=1)
    for layer in range(DEPTH):
        j = layer // 2
        h = h + 0.5 * swiglu_ffn(rms_norm(h, ffn_norm[layer, 0]), ffn_w_in[layer, 0], ffn_w_out[layer, 0])
        hn = rms_norm(h, mix_norm[layer])
        if layer % 2 == 0:
            h = h + hybrid_attention_mixer(hn, hyb_w_in[j], hyb_fox_bf[j], hyb_conv[j], hyb_a_log[j],
                                           hyb_dt_bias[j], hyb_o_gain[j], hyb_w_out[j])
        else:
            h = h + rwkv7_time_mix(hn, rwkv_mu[j], rwkv_w_r[j], rwkv_w_k[j], rwkv_w_v[j], rwkv_w0[j],
                                   rwkv_w1[j], rwkv_w2[j], rwkv_a0[j], rwkv_a1[j], rwkv_a2[j],
                                   rwkv_g1[j], rwkv_g2[j], rwkv_k_k[j], rwkv_k_a[j], rwkv_r_k[j],
                                   rwkv_ln_w[j], rwkv_ln_b[j], rwkv_w_o[j])
        h = h + 0.5 * swiglu_ffn(rms_norm(h, ffn_norm[layer, 1]), ffn_w_in[layer, 1], ffn_w_out[layer, 1])
    return rms_norm(h, final_norm)[:, N_META:]
```

```python
import contextlib
import numpy as np
import concourse.bass as bass
import concourse.mybir as mybir
from concourse.bass_utils import run_bass_kernel_spmd

F32 = mybir.dt.float32
BF16 = mybir.dt.bfloat16
ALU = mybir.AluOpType
AF = mybir.ActivationFunctionType

D = 1024
DFF = 2816
EPS = 1e-6
NMETA = 16
HYB_IN = 3600
GN_EPS = 64e-5


_NUN = [0]


NAMES = {}


def un(name):
    _NUN[0] += 1
    NAMES[name] = "t%d_%s" % (_NUN[0], name)
    return NAMES[name]


PSUM_PREFIXES = ("pt", "py", "ps_", "psx", "psy", "pstr", "gpps", "rpps")


class Res:
    __slots__ = ("name", "w", "r", "excl")

    def __init__(self, name):
        self.name = name
        self.w = None
        self.r = []
        base = name.split("_", 1)[1] if name.startswith(("gdn_", "rwkv_")) else name
        self.excl = base.startswith(PSUM_PREFIXES)


class _Rec:
    def __init__(self):
        self.calls = []

    def __getattr__(self, name):
        def f(*a, **k):
            self.calls.append((name, a, k))
            return self
        return f


class Prog:
    ENGS = ("tensor", "vector", "scalar", "gpsimd", "sync")

    def __init__(self, nc, stack, n_dma_sems=12):
        self.nc = nc
        self.lists = {e: [] for e in self.ENGS}
        self.stack = stack
        self.epoch = 0
        self.esem = {e: stack.enter_context(nc.semaphore("s_" + e)) for e in self.ENGS}
        self.ecount = {e: 0 for e in self.ENGS}
        self.LIMIT = 12000
        self.seen = {e: {} for e in self.ENGS}
        self.dsems, self.dcount, self.dnext = {}, {}, {}
        for q in ("sync", "gpsimd", "scalar"):
            self.dsems[q] = [stack.enter_context(nc.semaphore("d_%s%d" % (q, i)))
                             for i in range(n_dma_sems if q != "scalar" else 2)]
            self.dcount[q] = [0] * len(self.dsems[q])
            self.dnext[q] = 0
        self.semobj = {}
        for e in self.ENGS:
            self.semobj[("e", e, 0)] = self.esem[e]
        for q in self.dsems:
            for i, s in enumerate(self.dsems[q]):
                self.semobj[("d", q, i)] = s
        self.n_ins = 0

    def _need(self, eng, deps, key, val):
        if key[0] == "e":
            if key[2] < self.epoch:
                return
            if key[1] == eng and eng == "tensor":
                return
        if self.seen[eng].get(key, 0) >= val:
            return
        if deps.get(key, 0) < val:
            deps[key] = val

    def _collect(self, eng, reads, writes):
        deps = {}
        for r in reads:
            if r.w is not None:
                self._need(eng, deps, r.w[0], r.w[1])
        for w in writes:
            if w.w is not None:
                self._need(eng, deps, w.w[0], w.w[1])
            for (k, v) in w.r:
                self._need(eng, deps, k, v)
        for k, v in deps.items():
            self.seen[eng][k] = v
        return list(deps.items())

    def _mark(self, key, val, reads, writes):
        for r in reads:
            r.r = [(k, v) for (k, v) in r.r if k != key]
            r.r.append((key, val))
        for w in writes:
            w.w = (key, val)
            w.r = []

    def op(self, eng, fn, reads=(), writes=()):
        rec = _Rec()
        fn(rec)
        assert len(rec.calls) == 1, rec.calls
        name, a, k = rec.calls[0]
        fn = (lambda e, name=name, a=a, k=k: getattr(e, name)(*a, **k))
        ex = [r for r in reads if r.excl]
        if ex:
            writes = list(writes) + ex
        if self.ecount[eng] >= self.LIMIT:
            self.barrier(rotate=True)
        waits = self._collect(eng, reads, writes)
        self.ecount[eng] += 1
        key = ("e", eng, self.epoch)
        self.lists[eng].append((waits, fn, key, 1))
        self._mark(key, self.ecount[eng], reads, writes)
        self.n_ins += 1

    def V(self, fn, reads=(), writes=()):
        self.op("vector", fn, reads, writes)

    def S(self, fn, reads=(), writes=()):
        self.op("scalar", fn, reads, writes)

    def G(self, fn, reads=(), writes=()):
        self.op("gpsimd", fn, reads, writes)

    def T(self, fn, reads=(), writes=()):
        self.op("tensor", fn, reads, writes)

    def dma(self, q, out, in_, reads=(), writes=(), **kw):
        i = self.dnext[q]
        self.dnext[q] = (i + 1) % len(self.dsems[q])
        key = ("d", q, i)
        waits = self._collect(q, reads, writes)
        prev = self.dcount[q][i]
        if prev > 0 and self.seen[q].get(key, 0) < prev:
            waits.append((key, prev))
            self.seen[q][key] = prev
        self.dcount[q][i] += 16
        self.lists[q].append((waits, (lambda e: e.dma_start(out=out, in_=in_, **kw)), key, 16))
        self._mark(key, self.dcount[q][i], reads, writes)
        self.n_ins += 1

    def barrier(self, rotate=False):
        allw = {}
        for e in self.ENGS:
            if self.ecount[e] > 0:
                allw[("e", e, self.epoch)] = self.ecount[e]
        for q in self.dsems:
            for i, c in enumerate(self.dcount[q]):
                if c > 0:
                    allw[("d", q, i)] = c
        for e in self.ENGS:
            ws = []
            for k, v in allw.items():
                if k[0] == "e" and k[1] == e:
                    continue
                if self.seen[e].get(k, 0) < v:
                    ws.append((k, v))
                    self.seen[e][k] = v
            if ws:
                self.lists[e].append((ws, None, None, 0))
        if rotate:
            self.epoch += 1
            for e in self.ENGS:
                if e == "sync":
                    continue
                self.esem[e] = self.stack.enter_context(self.nc.semaphore("s_%s_%d" % (e, self.epoch)))
                self.semobj[("e", e, self.epoch)] = self.esem[e]
                self.ecount[e] = 0
                self.seen[e] = {k: v for k, v in self.seen[e].items() if k[0] != "e"}
            self.seen["sync"] = {k: v for k, v in self.seen["sync"].items() if k[0] != "e"}

    def finish(self):
        self.barrier()
        semobj, lists = self.semobj, self.lists

        def run(e, name):
            for (ws, fn, key, inc) in lists[name]:
                for (k, v) in ws:
                    e.wait_ge(semobj[k], v)
                if fn is not None:
                    fn(e).then_inc(semobj[key], inc)

        with self.nc.Block() as block:
            @block.tensor
            def _(e):
                run(e, "tensor")

            @block.vector
            def _(e):
                run(e, "vector")

            @block.scalar
            def _(e):
                run(e, "scalar")

            @block.gpsimd
            def _(e):
                run(e, "gpsimd")

            @block.sync
            def _(e):
                run(e, "sync")


PV_SLOTS = {}


def _pv_layout():
    off = 0
    def add(name, n):
        nonlocal off
        PV_SLOTS[name] = (off, n)
        off += n
    for i in range(4):
        add("ffn_norm%d" % i, 8)
    add("mix_norm0", 8); add("mix_norm1", 8); add("final_norm", 8)
    add("fox_bf", 8); add("conv", 48); add("a_log", 4); add("dt_bias", 4); add("o_gain", 1)
    add("mu", 48); add("w0", 8); add("a0", 8); add("k_k", 8); add("k_a", 8); add("ln_w", 8); add("ln_b", 8); add("r_k", 8)
    for nm in ("cmean", "ident", "m_su", "m_ui", "m_sl", "ones", "blk64", "triu", "bd32", "off64", "off128"):
        add(nm, 128)
    add("sel65", 64)
    return off


NPV = _pv_layout()


def fm(vec):
    return np.ascontiguousarray(np.asarray(vec, np.float32).reshape(8, 128).T)


def pack_params(inp):
    pv = np.zeros((128, NPV), np.float32)
    def put(name, arr):
        o, n = PV_SLOTS[name]
        pv[:, o:o + n] = np.asarray(arr, np.float32).reshape(128, n)
    k = 0
    for l in range(2):
        for h in range(2):
            put("ffn_norm%d" % k, fm(inp["ffn_norm"][l, h])); k += 1
    put("mix_norm0", fm(inp["mix_norm"][0])); put("mix_norm1", fm(inp["mix_norm"][1]))
    put("final_norm", fm(inp["final_norm"]))
    put("fox_bf", np.broadcast_to(inp["hyb_fox_bf"][0][None, :], (128, 8)))
    cw = inp["hyb_conv"][0]
    put("conv", cw.reshape(4, 12, 128).transpose(2, 1, 0).reshape(128, 48))
    put("a_log", np.broadcast_to(inp["hyb_a_log"][0][None, :], (128, 4)))
    put("dt_bias", np.broadcast_to(inp["hyb_dt_bias"][0][None, :], (128, 4)))
    put("o_gain", inp["hyb_o_gain"][0].reshape(128, 1))
    put("mu", inp["rwkv_mu"][0].reshape(6, 8, 128).transpose(2, 0, 1).reshape(128, 48))
    for nm in ("w0", "a0", "k_k", "k_a", "ln_w", "ln_b"):
        put(nm, fm(inp["rwkv_" + nm][0]))
    put("r_k", fm(inp["rwkv_r_k"][0].reshape(-1)))
    idx = np.arange(128)
    put("cmean", np.full((128, 128), 1.0 / 1024))
    put("ident", np.eye(128))
    put("m_su", (idx[:, None] < idx[None, :]).astype(np.float32))
    put("m_ui", (idx[:, None] <= idx[None, :]).astype(np.float32))
    put("m_sl", (idx[:, None] > idx[None, :]).astype(np.float32))
    put("ones", np.ones((128, 128)))
    put("blk64", ((idx[:, None] // 64) == (idx[None, :] // 64)).astype(np.float32))
    put("triu", (idx[:, None] <= idx[None, :]).astype(np.float32))
    bdm = lambda b: ((idx[:, None] // b) == (idx[None, :] // b)).astype(np.float32)
    put("bd32", bdm(32)); put("off64", bdm(64) - bdm(32)); put("off128", bdm(128) - bdm(64))
    s = np.zeros((128, 64), np.float32); s[64, :] = 1.0
    put("sel65", s)
    return pv


WNAMES = [("ffn_w_in0", D, 2 * DFF), ("ffn_w_in1", D, 2 * DFF), ("ffn_w_in2", D, 2 * DFF), ("ffn_w_in3", D, 2 * DFF),
          ("ffn_w_out0", DFF, D), ("ffn_w_out1", DFF, D), ("ffn_w_out2", DFF, D), ("ffn_w_out3", DFF, D),
          ("hyb_w_in", D, HYB_IN), ("hyb_w_out", D, D),
          ("rwkv_w_r", D, D), ("rwkv_w_k", D, D), ("rwkv_w_v", D, D), ("rwkv_w_o", D, D),
          ("rwkv_w1", D, 64), ("rwkv_w2", 64, D), ("rwkv_a1", D, 64), ("rwkv_a2", 64, D),
          ("rwkv_g1", D, 160), ("rwkv_g2", 160, D)]


DEBUG_SCR = False


def build(T, stop_after=None, dbg=(), only=None):
    NCH = T // 128
    tiles = []
    t0 = 0
    while t0 < T:
        n = min(512, T - t0)
        tiles.append((t0, n)); t0 += n
    nc = bass.Bass("TRN2", target_bir_lowering=False)
    hT0 = nc.dram_tensor("hT0", [D, T], F32, kind="ExternalInput").ap()
    pvec = nc.dram_tensor("pvec", [128, NPV], F32, kind="ExternalInput").ap()
    Wf = {nm: nc.dram_tensor(nm, [r, c], F32, kind="ExternalInput").ap() for (nm, r, c) in WNAMES}
    outT = nc.dram_tensor("outT", [D, T], F32, kind="ExternalOutput").ap()
    Wb = {nm: nc.dram_tensor(nm + "_b", [r, c], BF16, kind="Internal").ap() for (nm, r, c) in WNAMES}
    scr = {}
    def dscr(name, shape, dt):
        scr[name] = nc.dram_tensor("scr_" + name, shape, dt, kind=("ExternalOutput" if DEBUG_SCR else "Internal")).ap()
        return scr[name]
    hT = dscr("hT", [D, T], F32)
    fqT = dscr("fqT", [512, T], BF16); fkT = dscr("fkT", [512, T], BF16)
    fV = dscr("fV", [T, 520], BF16); flogf = dscr("flogf", [T, 8], F32)
    gS = {k: dscr("g_" + k, [512, T], BF16) for k in ("r", "k", "a", "b", "v", "z")}
    gS["lw"] = dscr("g_lw", [512, T], F32)
    oT = dscr("oT", [D, T], BF16)
    rS = {k: dscr("r_" + k, [D, T], BF16) for k in ("r", "k", "a", "b", "v", "bonus", "g")}
    rS["lw"] = dscr("r_lw", [D, T], F32)
    zT = dscr("zT", [D, T], BF16)
    dbg_out = {}
    for nm, shape in dbg:
        dbg_out[nm] = nc.dram_tensor("dbg_" + nm, shape, F32, kind="ExternalOutput").ap()

    with contextlib.ExitStack() as gst:
        P = Prog(nc, gst)
        R = {}
        def res(name):
            if name not in R:
                R[name] = Res(name)
            return R[name]

        for (nm, r, c) in WNAMES:
            c0 = 0
            while c0 < c:
                cw = min(2048, c - c0)
                P.dma("gpsimd", Wb[nm][:, c0:c0 + cw], Wf[nm][:, c0:c0 + cw], writes=[res("W_" + nm)])
                c0 += cw

        pv = gst.enter_context(nc.sbuf_tensor("pv", [128, NPV], F32)); r_pv = res("pv")
        P.dma("sync", pv[:], pvec, writes=[r_pv])
        def pvs(name, a=0, n=None):
            o, nn = PV_SLOTS[name]
            n = nn - a if n is None else n
            return pv[:, o + a:o + a + n]
        cb = {}
        for nm in ("cmean", "ident", "ones", "blk64", "bd32", "off64", "off128"):
            cb[nm] = gst.enter_context(nc.sbuf_tensor("cb_" + nm, [128, 128], BF16))
            P.V(lambda e, nm=nm: e.tensor_copy(out=cb[nm][:], in_=pvs(nm)), reads=[r_pv], writes=[res("cb")])
        m_ui_b = gst.enter_context(nc.sbuf_tensor("m_ui_b", [128, 128], BF16))
        P.V(lambda e: e.tensor_copy(out=m_ui_b[:], in_=pvs("m_ui")), reads=[r_pv], writes=[res("cb")])
        cvec = gst.enter_context(nc.sbuf_tensor("cvec", [128, 4], F32))
        P.V(lambda e: e.memset(cvec[:, 0:1], EPS), writes=[res("cb")])
        P.V(lambda e: e.memset(cvec[:, 1:2], 1.0), writes=[res("cb")])
        P.V(lambda e: e.memset(cvec[:, 2:3], GN_EPS), writes=[res("cb")])
        P.V(lambda e: e.memset(cvec[:, 3:4], 0.0), writes=[res("cb")])
        r_cb = res("cb")
        P.barrier()

        hview = lambda ap: ap.rearrange("(kc p) t -> p kc t", p=128)
        wview = lambda ap: ap.rearrange("(kc p) f -> p kc f", p=128)

        class TokCtx:
            pass

        def make_tok_ctx(st, nwi=3, nwo=2):
            c = TokCtx()
            sb = lambda name, shape, dt: st.enter_context(nc.sbuf_tensor(un(name), shape, dt))
            psm = lambda name, shape, dt: st.enter_context(nc.psum_tensor(un(name), shape, dt))
            c.hT = [sb("hT%d" % i, [128, 8, 512], F32) for i in range(2)]
            c.sq = sb("sq", [128, 8, 512], BF16)
            c.rstd = sb("rstd", [128, 512], F32)
            c.hn = sb("hn", [128, 8, 513], BF16)
            c.act = sb("act", [128, 22, 512], BF16)
            c.sg = [sb("sg%d" % i, [128, 512], F32) for i in range(2)]
            c.wi = [sb("wi%d" % i, [128, 8, 512], BF16) for i in range(nwi)]
            c.wo = [sb("wo%d" % i, [128, 4, 512], BF16) for i in range(nwo)]
            c.pt = [psm("pt%d" % i, [128, 512], F32) for i in range(4)]
            c.py = [psm("py%d" % i, [128, 512], F32) for i in range(4)]
            c.cnt = {"wi": 0, "wo": 0, "pt": 0, "sg": 0}
            return c

        def next_pt(c):
            i = c.cnt["pt"] % 4; c.cnt["pt"] += 1
            return i

        def sumsq_bc(c, src_ap_fn, nk, N, r_src, lhs_name="cmean"):
            pi = next_pt(c)
            for k in range(nk):
                P.T(lambda e, pi=pi, k=k: e.matmul(c.pt[pi][:, :N], lhsT=cb[lhs_name][:], rhs=src_ap_fn(k),
                                                    start=(k == 0), stop=(k == nk - 1)),
                    reads=[r_cb] + r_src, writes=[res("pt%d" % pi)])
            return pi

        def rsqrt_from_psum(c, pi, N, dst, r_dst, eps_col=0):
            P.S(lambda e: e.activation(out=dst, in_=c.pt[pi][:, :N], func=AF.Sqrt, bias=cvec[:, eps_col:eps_col + 1]),
                reads=[res("pt%d" % pi), r_cb], writes=[r_dst])
            P.V(lambda e: e.reciprocal(out=dst, in_=dst), reads=[r_dst], writes=[r_dst])

        def rmsnorm(c, h_t, r_h, N, gain_name):
            P.S(lambda e: e.activation(out=c.sq[:, :, :N], in_=h_t[:, :, :N], func=AF.Square), reads=[r_h], writes=[res("sq")])
            pi = sumsq_bc(c, lambda k: c.sq[:, k, :N], 8, N, [res("sq")])
            rsqrt_from_psum(c, pi, N, c.rstd[:, :N], res("rstd"))
            for kc in range(8):
                eng = "vector" if kc % 2 == 0 else "gpsimd"
                P.op("vector", lambda e, kc=kc: e.scalar_tensor_tensor(
                    out=c.hn[:, kc, 1:1 + N], in0=h_t[:, kc, :N], scalar=pvs(gain_name, kc, 1), in1=c.rstd[:, :N],
                    op0=ALU.mult, op1=ALU.mult), reads=[r_h, r_pv, res("rstd")], writes=[res("hn")])

        def lin_fm(c, x_fn, r_x, N, wname, col0, ncols, consume, kchunks=8):
            wv = wview(Wb[wname])
            c0 = 0
            while c0 < ncols:
                cw = min(512, ncols - c0)
                bi = c.cnt["wi"] % len(c.wi); c.cnt["wi"] += 1
                P.dma("sync", c.wi[bi][:, :kchunks, :cw], wv[:, :, col0 + c0:col0 + c0 + cw],
                      reads=[res("W_" + wname)], writes=[res("wi%d" % bi)])
                f0 = 0
                while f0 < cw:
                    m = min(128, cw - f0)
                    pi = next_pt(c)
                    for kc in range(kchunks):
                        P.T(lambda e, pi=pi, bi=bi, kc=kc, f0=f0, m=m: e.matmul(
                            c.pt[pi][:m, :N], lhsT=c.wi[bi][:, kc, f0:f0 + m], rhs=x_fn(kc),
                            start=(kc == 0), stop=(kc == kchunks - 1)),
                            reads=[res("wi%d" % bi)] + r_x, writes=[res("pt%d" % pi)])
                    consume((c0 + f0) // 128, pi, m)
                    f0 += m
                c0 += cw

        def ffn(c, h_t, r_h, N, idx):
            rmsnorm(c, h_t, r_h, N, "ffn_norm%d" % idx)
            wn_in, wn_out = "ffn_w_in%d" % idx, "ffn_w_out%d" % idx
            wv = wview(Wb[wn_in])
            for pc in range(6):
                cw = 512 if pc < 5 else 256
                bufs = []
                for which in range(2):
                    bi = c.cnt["wi"] % len(c.wi); c.cnt["wi"] += 1
                    cc0 = which * DFF + pc * 512
                    P.dma("sync", c.wi[bi][:, :, :cw], wv[:, :, cc0:cc0 + cw], reads=[res("W_" + wn_in)], writes=[res("wi%d" % bi)])
                    bufs.append(bi)
                for fl in range(cw // 128):
                    fc = pc * 4 + fl
                    pis = []
                    for which in range(2):
                        pi = next_pt(c); pis.append(pi)
                        bi = bufs[which]
                        for kc in range(8):
                            P.T(lambda e, pi=pi, bi=bi, kc=kc, fl=fl: e.matmul(
                                c.pt[pi][:, :N], lhsT=c.wi[bi][:, kc, fl * 128:(fl + 1) * 128], rhs=c.hn[:, kc, 1:1 + N],
                                start=(kc == 0), stop=(kc == 7)), reads=[res("wi%d" % bi), res("hn")], writes=[res("pt%d" % pi)])
                    si = c.cnt["sg"] % 2; c.cnt["sg"] += 1
                    P.S(lambda e, si=si, pi=pis[0]: e.activation(out=c.sg[si][:, :N], in_=c.pt[pi][:, :N], func=AF.Silu),
                        reads=[res("pt%d" % pis[0])], writes=[res("sg%d" % si)])
                    P.V(lambda e, si=si, pi=pis[1], fc=fc: e.tensor_tensor(
                        out=c.act[:, fc, :N], in0=c.sg[si][:, :N], in1=c.pt[pi][:, :N], op=ALU.mult),
                        reads=[res("sg%d" % si), res("pt%d" % pis[1])], writes=[res("act")])
            wvo = Wb[wn_out].rearrange("(fc p) d -> p fc d", p=128)
            for half in range(2):
                for g in range(6):
                    nf = 4 if g < 5 else 2
                    bi = c.cnt["wo"] % len(c.wo); c.cnt["wo"] += 1
                    P.dma("sync", c.wo[bi][:, :nf, :], wvo[:, g * 4:g * 4 + nf, half * 512:(half + 1) * 512],
                          reads=[res("W_" + wn_out)], writes=[res("wo%d" % bi)])
                    for fl in range(nf):
                        fc = g * 4 + fl
                        for dq in range(4):
                            P.T(lambda e, bi=bi, fl=fl, fc=fc, dq=dq: e.matmul(
                                c.py[dq][:, :N], lhsT=c.wo[bi][:, fl, dq * 128:(dq + 1) * 128], rhs=c.act[:, fc, :N],
                                start=(fc == 0), stop=(fc == 21)), reads=[res("wo%d" % bi), res("act")], writes=[res("py%d" % dq)])
                for dq in range(4):
                    kc = half * 4 + dq
                    P.V(lambda e, dq=dq, kc=kc: e.scalar_tensor_tensor(
                        out=h_t[:, kc, :N], in0=c.py[dq][:, :N], scalar=0.5, in1=h_t[:, kc, :N],
                        op0=ALU.mult, op1=ALU.add), reads=[res("py%d" % dq), r_h], writes=[r_h])

        def proj_residual(c, h_t, r_h, N, x_t, r_x, wname):
            def consume(fc, pi, m):
                P.V(lambda e: e.tensor_tensor(out=h_t[:, fc, :N], in0=c.pt[pi][:, :N], in1=h_t[:, fc, :N], op=ALU.add),
                    reads=[res("pt%d" % pi), r_h], writes=[r_h])
            lin_fm(c, lambda kc: x_t[:, kc, :N], [r_x], N, wname, 0, D, consume)

        with contextlib.ExitStack() as st:
          if only is None or "A" in only:
            sb = lambda name, shape, dt: st.enter_context(nc.sbuf_tensor(un(name), shape, dt))
            c = make_tok_ctx(st)
            wtm = sb("wtm", [128, 8, 520], BF16)
            wrep = sb("wrep", [128, 8, 8, 128], BF16)
            wsm = sb("wsm", [128, 8, 8], F32)
            P.dma("sync", wtm[:], wview(Wb["hyb_w_in"])[:, :, 1024:1544], reads=[res("W_hyb_w_in")], writes=[res("wtm")])
            P.dma("sync", wsm[:], wview(Wf["hyb_w_in"])[:, :, 3080:3088], writes=[res("wsm")])
            for kc in range(8):
                for j in range(8):
                    eng = "vector" if (kc + j) % 2 == 0 else "gpsimd"
                    P.op(eng, lambda e, kc=kc, j=j: e.tensor_scalar(out=wrep[:, kc, j, :], in0=cb["ones"][:], scalar1=wsm[:, kc, j:j + 1],
                                                                     scalar2=None, op0=ALU.mult), reads=[res("wsm"), r_cb], writes=[res("wrep")])
            negA = sb("negA", [128, 4], F32)
            P.S(lambda e: e.activation(out=negA[:], in_=pvs("a_log"), func=AF.Exp), reads=[r_pv], writes=[res("negA")])
            P.V(lambda e: e.tensor_scalar(out=negA[:], in0=negA[:], scalar1=-1.0, scalar2=None, op0=ALU.mult), reads=[res("negA")], writes=[res("negA")])
            xgs = [sb("xg%d" % i, [128, 515], F32) for i in range(2)]
            halo = sb("halo", [128, 12, 3], F32)
            P.V(lambda e: e.memset(halo[:], 0.0), writes=[res("halo")])
            xgc = [0]
            cacc = sb("cacc", [128, 512], F32)
            qk = [sb("qk%d" % i, [128, 512], F32) for i in range(2)]
            kn = sb("kn", [128, 512], F32)
            gb_ = sb("gbeta", [128, 4, 512], F32)
            gg_ = sb("gg", [128, 4, 512], F32)
            glw = sb("glw", [128, 4, 512], F32)
            ob = [sb("ob%d" % i, [128, 512], BF16) for i in range(4)]
            vtm = [sb("vtm%d" % i, [128, 8, 65], BF16) for i in range(2)]
            for i in range(2):
                P.V(lambda e, i=i: e.memset(vtm[i][:], 1.0), writes=[res("vtm%d" % i)])
            lft = [sb("lft%d" % i, [128, 8], F32) for i in range(2)]
            ocnt = [0]
            print("phase A sbuf remaining", nc.sbuf_bytes_remaining)

            def out_bf(dst_ap, make, reads):
                i = ocnt[0] % 4; ocnt[0] += 1
                make(ob[i], res("ob%d" % i))
                P.dma("sync", dst_ap, ob[i][:, :dst_ap.shape[-1]], reads=[res("ob%d" % i)], writes=[res("scrA")])

            for ti, (t0, N) in enumerate(tiles):
                h_t, r_h = c.hT[ti % 2], res("hT%d" % (ti % 2))
                P.dma("sync", h_t[:, :, :N], hview(hT0)[:, :, t0:t0 + N], writes=[r_h])
                ffn(c, h_t, r_h, N, 0)
                P.dma("sync", hview(hT)[:, :, t0:t0 + N], h_t[:, :, :N], reads=[r_h], writes=[res("hT_dram")])
                rmsnorm(c, h_t, r_h, N, "mix_norm0")
                xfn = lambda kc: c.hn[:, kc, 1:1 + N]
                rx = [res("hn")]
                def cons_q(fc, pi, m):
                    out_bf(fqT[fc * 128:(fc + 1) * 128, t0:t0 + N],
                           lambda o, ro: P.S(lambda e: e.activation(out=o[:, :N], in_=c.pt[pi][:, :N], func=AF.Copy, scale=0.125),
                                             reads=[res("pt%d" % pi)], writes=[ro]), None)
                lin_fm(c, xfn, rx, N, "hyb_w_in", 0, 512, cons_q)
                def cons_k(fc, pi, m):
                    out_bf(fkT[fc * 128:(fc + 1) * 128, t0:t0 + N],
                           lambda o, ro: P.V(lambda e: e.tensor_copy(out=o[:, :N], in_=c.pt[pi][:, :N]),
                                             reads=[res("pt%d" % pi)], writes=[ro]), None)
                lin_fm(c, xfn, rx, N, "hyb_w_in", 512, 512, cons_k)
                for tb in range(N // 128):
                    vi = (ti * 4 + tb) % 2
                    pi = next_pt(c)
                    for kc in range(8):
                        P.T(lambda e, pi=pi, kc=kc, tb=tb: e.matmul(c.pt[pi][:, :512], lhsT=c.hn[:, kc, 1 + tb * 128:1 + (tb + 1) * 128],
                                                                  rhs=wtm[:, kc, 0:512], start=(kc == 0), stop=(kc == 7)),
                            reads=[res("hn"), res("wtm")], writes=[res("pt%d" % pi)])
                    P.S(lambda e, pi=pi, vi=vi: e.activation(out=vtm[vi][:, :, 0:64], in_=c.pt[pi][:, :512].rearrange("p (h d) -> p h d", h=8), func=AF.Copy),
                        reads=[res("pt%d" % pi)], writes=[res("vtm%d" % vi)])
                    P.dma("sync", fV[t0 + tb * 128:t0 + (tb + 1) * 128, :], vtm[vi][:].rearrange("p h d -> p (h d)"),
                          reads=[res("vtm%d" % vi)], writes=[res("scrA")])
                    pi2 = next_pt(c)
                    for kc in range(8):
                        P.T(lambda e, pi2=pi2, kc=kc, tb=tb: e.matmul(c.pt[pi2][:, :8], lhsT=c.hn[:, kc, 1 + tb * 128:1 + (tb + 1) * 128],
                                                                    rhs=wtm[:, kc, 512:520], start=(kc == 0), stop=(kc == 7)),
                            reads=[res("hn"), res("wtm")], writes=[res("pt%d" % pi2)])
                    P.V(lambda e, pi2=pi2, vi=vi: e.tensor_tensor(out=lft[vi][:], in0=c.pt[pi2][:, :8], in1=pvs("fox_bf"), op=ALU.add),
                        reads=[res("pt%d" % pi2), r_pv], writes=[res("lft%d" % vi)])
                    P.S(lambda e, vi=vi: e.activation(out=lft[vi][:], in_=lft[vi][:], func=AF.Sigmoid), reads=[res("lft%d" % vi)], writes=[res("lft%d" % vi)])
                    P.S(lambda e, vi=vi: e.activation(out=lft[vi][:], in_=lft[vi][:], func=AF.Ln), reads=[res("lft%d" % vi)], writes=[res("lft%d" % vi)])
                    P.dma("sync", flogf[t0 + tb * 128:t0 + (tb + 1) * 128, :], lft[vi][:], reads=[res("lft%d" % vi)], writes=[res("scrA")])
                for j in range(8):
                    pi = next_pt(c)
                    for kc in range(8):
                        P.T(lambda e, pi=pi, kc=kc, j=j: e.matmul(c.pt[pi][:, :N], lhsT=wrep[:, kc, j, :], rhs=c.hn[:, kc, 1:1 + N],
                                                               start=(kc == 0), stop=(kc == 7)),
                            reads=[res("wrep"), res("hn")], writes=[res("pt%d" % pi)])
                    if j < 4:
                        P.S(lambda e, pi=pi, j=j: e.activation(out=glw[:, j, :N], in_=c.pt[pi][:, :N], func=AF.Exp, bias=pvs("dt_bias", j, 1)),
                            reads=[res("pt%d" % pi), r_pv], writes=[res("glw")])
                        P.S(lambda e, j=j: e.activation(out=glw[:, j, :N], in_=glw[:, j, :N], func=AF.Ln, bias=cvec[:, 1:2]),
                            reads=[res("glw"), r_cb], writes=[res("glw")])
                        P.V(lambda e, j=j: e.tensor_scalar(out=glw[:, j, :N], in0=glw[:, j, :N], scalar1=negA[:, j:j + 1], scalar2=None, op0=ALU.mult),
                            reads=[res("glw"), res("negA")], writes=[res("glw")])
                        P.S(lambda e, j=j: e.activation(out=gg_[:, j, :N], in_=glw[:, j, :N], func=AF.Exp), reads=[res("glw")], writes=[res("gg")])
                        P.dma("sync", gS["lw"][j * 128:(j + 1) * 128, t0:t0 + N], glw[:, j, :N], reads=[res("glw")], writes=[res("scrA")])
                    else:
                        P.S(lambda e, pi=pi, j=j: e.activation(out=gb_[:, j - 4, :N], in_=c.pt[pi][:, :N], func=AF.Sigmoid),
                            reads=[res("pt%d" % pi)], writes=[res("gbeta")])
                def cons_g(fc, pi, m):
                    xi = xgc[0] % 2; xgc[0] += 1
                    xg, rxg = xgs[xi], res("xg%d" % xi)
                    P.G(lambda e: e.tensor_copy(out=xg[:, 0:3], in_=halo[:, fc, :]), reads=[res("halo")], writes=[rxg])
                    P.S(lambda e: e.activation(out=xg[:, 3:3 + N], in_=c.pt[pi][:, :N], func=AF.Copy), reads=[res("pt%d" % pi)], writes=[rxg])
                    o_, _ = PV_SLOTS["conv"]
                    cwp = lambda j: pv[:, o_ + fc * 4 + j:o_ + fc * 4 + j + 1]
                    P.V(lambda e: e.tensor_scalar(out=cacc[:, :N], in0=xg[:, 0:N], scalar1=cwp(0), scalar2=None, op0=ALU.mult),
                        reads=[rxg, r_pv], writes=[res("cacc")])
                    for j in range(1, 4):
                        P.V(lambda e, j=j: e.scalar_tensor_tensor(out=cacc[:, :N], in0=xg[:, j:j + N], scalar=cwp(j), in1=cacc[:, :N],
                                                               op0=ALU.mult, op1=ALU.add), reads=[rxg, r_pv, res("cacc")], writes=[res("cacc")])
                    P.G(lambda e: e.tensor_copy(out=halo[:, fc, :], in_=xg[:, N:N + 3]), reads=[rxg], writes=[res("halo")])
                    kind, hh = fc // 4, fc % 4
                    if kind == 2:
                        out_bf(gS["v"][hh * 128:(hh + 1) * 128, t0:t0 + N],
                               lambda o, ro: P.S(lambda e: e.activation(out=o[:, :N], in_=cacc[:, :N], func=AF.Silu), reads=[res("cacc")], writes=[ro]), None)
                        return
                    qq = qk[kind]; rq = res("qk%d" % kind)
                    P.S(lambda e: e.activation(out=qq[:, :N], in_=cacc[:, :N], func=AF.Silu), reads=[res("cacc")], writes=[rq])
                    P.G(lambda e: e.tensor_tensor(out=c.sq[:, 0, :N], in0=qq[:, :N], in1=qq[:, :N], op=ALU.mult), reads=[rq], writes=[res("sq")])
                    pj = sumsq_bc(c, lambda k: c.sq[:, 0, :N], 1, N, [res("sq")], lhs_name="ones")
                    rsqrt_from_psum(c, pj, N, c.rstd[:, :N], res("rstd"))
                    if kind == 0:
                        out_bf(gS["r"][hh * 128:(hh + 1) * 128, t0:t0 + N],
                               lambda o, ro: P.V(lambda e: e.scalar_tensor_tensor(out=o[:, :N], in0=qq[:, :N], scalar=128.0 ** -0.5, in1=c.rstd[:, :N],
                                                                                 op0=ALU.mult, op1=ALU.mult), reads=[rq, res("rstd")], writes=[ro]), None)
                    else:
                        P.V(lambda e: e.tensor_tensor(out=kn[:, :N], in0=qq[:, :N], in1=c.rstd[:, :N], op=ALU.mult), reads=[rq, res("rstd")], writes=[res("kn")])
                        out_bf(gS["a"][hh * 128:(hh + 1) * 128, t0:t0 + N],
                               lambda o, ro: P.G(lambda e: e.tensor_copy(out=o[:, :N], in_=kn[:, :N]), reads=[res("kn")], writes=[ro]), None)
                        P.V(lambda e: e.tensor_tensor(out=kn[:, :N], in0=kn[:, :N], in1=gb_[:, hh, :N], op=ALU.mult), reads=[res("kn"), res("gbeta")], writes=[res("kn")])
                        out_bf(gS["k"][hh * 128:(hh + 1) * 128, t0:t0 + N],
                               lambda o, ro: P.G(lambda e: e.tensor_copy(out=o[:, :N], in_=kn[:, :N]), reads=[res("kn")], writes=[ro]), None)
                        out_bf(gS["b"][hh * 128:(hh + 1) * 128, t0:t0 + N],
                               lambda o, ro: P.V(lambda e: e.scalar_tensor_tensor(out=o[:, :N], in0=kn[:, :N], scalar=-1.0, in1=gg_[:, hh, :N],
                                                                                 op0=ALU.mult, op1=ALU.mult), reads=[res("kn"), res("gg")], writes=[ro]), None)
                lin_fm(c, xfn, rx, N, "hyb_w_in", 1544, 1536, cons_g)
                def cons_z(fc, pi, m):
                    out_bf(gS["z"][fc * 128:(fc + 1) * 128, t0:t0 + N],
                           lambda o, ro: P.S(lambda e: e.activation(out=o[:, :N], in_=c.pt[pi][:, :N], func=AF.Silu), reads=[res("pt%d" % pi)], writes=[ro]), None)
                lin_fm(c, xfn, rx, N, "hyb_w_in", 3088, 512, cons_z)
        P.barrier()
        if stop_after == "A":
            P.finish(); return nc

        with contextlib.ExitStack() as st:
          if only is None or "B" in only:
            sb = lambda name, shape, dt: st.enter_context(nc.sbuf_tensor(un(name), shape, dt))
            psm = lambda name, shape, dt: st.enter_context(nc.psum_tensor(un(name), shape, dt))
            Vall = sb("Vall", [128, NCH, 584], BF16)
            P.V(lambda e: e.memset(Vall[:, :, 520:584], 0.0), writes=[res("Vall")])
            P.dma("sync", Vall[:, :, 0:520], fV.rearrange("(c p) f -> p c f", p=128), reads=[res("scrA")], writes=[res("Vall")])
            lf = sb("lf", [128, NCH, 8], F32)
            P.dma("sync", lf[:], flogf.rearrange("(c p) h -> p c h", p=128), reads=[res("scrA")], writes=[res("lf")])
            negc = sb("negc", [128, 8, NCH], F32)
            pe = sb("pe", [128, 8, NCH + 1], F32)
            lff = lf[:].rearrange("p c h -> p (c h)")
            ps_s = [psm("ps_s%d" % i, [128, 512], F32) for i in range(2)]
            ps_o = [psm("ps_o%d" % i, [128, 512], F32) for i in range(2)]
            ps_b = psm("ps_b", [128, 512], F32)
            ncol = NCH * 8
            ctmp = sb("ctmp", [128, NCH, 8], F32)
            ctot = sb("ctot", [128, NCH, 8], F32)
            cf = ctmp[:].rearrange("p c h -> p (c h)")
            tf = ctot[:].rearrange("p c h -> p (c h)")
            c0 = 0
            while c0 < ncol:
                cw = min(512, ncol - c0)
                P.T(lambda e, c0=c0, cw=cw: e.matmul(ps_s[0][:, :cw], lhsT=pvs("triu"), rhs=lff[:, c0:c0 + cw], start=True, stop=True),
                    reads=[r_pv, res("lf")], writes=[res("ps_s0")])
                P.V(lambda e, c0=c0, cw=cw: e.tensor_copy(out=cf[:, c0:c0 + cw], in_=ps_s[0][:, :cw]), reads=[res("ps_s0")], writes=[res("ctmp")])
                P.T(lambda e, c0=c0, cw=cw: e.matmul(ps_s[1][:, :cw], lhsT=pvs("ones"), rhs=lff[:, c0:c0 + cw], start=True, stop=True),
                    reads=[r_pv, res("lf")], writes=[res("ps_s1")])
                P.V(lambda e, c0=c0, cw=cw: e.tensor_copy(out=tf[:, c0:c0 + cw], in_=ps_s[1][:, :cw]), reads=[res("ps_s1")], writes=[res("ctot")])
                c0 += cw
            P.V(lambda e: e.memset(pe[:, :, 0:1], 0.0), writes=[res("pe")])
            for h in range(8):
                P.V(lambda e, h=h: e.tensor_tensor_scan(out=pe[:, h, 1:NCH + 1], data0=pvs("ones")[:, 0:NCH], data1=ctot[:, :, h],
                                                         initial=0.0, op0=ALU.mult, op1=ALU.add),
                    reads=[r_pv, res("ctot")], writes=[res("pe")])
                P.V(lambda e, h=h: e.tensor_tensor(out=negc[:, h, :], in0=ctmp[:, :, h], in1=pe[:, h, 0:NCH], op=ALU.add),
                    reads=[res("ctmp"), res("pe")], writes=[res("negc")])
                P.V(lambda e, h=h: e.tensor_scalar(out=negc[:, h, :], in0=negc[:, h, :], scalar1=-1.0, scalar2=None, op0=ALU.mult),
                    reads=[res("negc")], writes=[res("negc")])
            kTs = [sb("kTs%d" % i, [64, T], BF16) for i in range(2)]
            qTs = [sb("qTs%d" % i, [64, T], BF16) for i in range(2)]
            biasg = [sb("biasg%d" % i, [128, NCH], F32) for i in range(2)]
            pT = [sb("pT%d" % i, [128, 512], BF16) for i in range(3)]
            osb = [sb("osb%d" % i, [128, 512], F32) for i in range(2)]
            for i in range(2):
                P.V(lambda e, i=i: e.memset(osb[i][:], 0.0), writes=[res("osb%d" % i)])
            rinv = sb("rinv", [64, 512], F32)
            oout = [sb("oout%d" % i, [64, 512], BF16) for i in range(2)]
            cntB = {"s": 0, "p": 0, "g": 0}
            for h in range(8):
                hb = h % 2
                P.dma("sync", kTs[hb][:], fkT[h * 64:(h + 1) * 64, :], reads=[res("scrA")], writes=[res("kTs%d" % hb)])
                P.dma("sync", qTs[hb][:], fqT[h * 64:(h + 1) * 64, :], reads=[res("scrA")], writes=[res("qTs%d" % hb)])
                for (t0, N) in tiles:
                    gi = cntB["g"] % 2; cntB["g"] += 1
                    i0 = t0 // 128; nb = N // 128
                    anc = min(i0 + 2, NCH)
                    J = i0 + nb
                    P.V(lambda e, gi=gi, anc=anc, J=J, h=h: e.tensor_scalar(out=biasg[gi][:, :J], in0=negc[:, h, :J], scalar1=pe[:, h, anc:anc + 1],
                                                                             scalar2=None, op0=ALU.add),
                        reads=[res("negc"), res("pe")], writes=[res("biasg%d" % gi)])
                    for j in range(J):
                        cs = 0 if j < i0 else (j - i0) * 128
                        ncols = N - cs
                        si = cntB["s"] % 2; cntB["s"] += 1
                        pi = cntB["p"] % 3; cntB["p"] += 1
                        P.T(lambda e, si=si, j=j, cs=cs, ncols=ncols, hb=hb, t0=t0: e.matmul(
                            ps_s[si][:, :ncols], lhsT=kTs[hb][:, j * 128:(j + 1) * 128], rhs=qTs[hb][:, t0 + cs:t0 + cs + ncols], start=True, stop=True),
                            reads=[res("kTs%d" % hb), res("qTs%d" % hb)], writes=[res("ps_s%d" % si)])
                        P.S(lambda e, si=si, pi=pi, j=j, ncols=ncols, gi=gi: e.activation(out=pT[pi][:, :ncols], in_=ps_s[si][:, :ncols], func=AF.Exp,
                                                                                         bias=biasg[gi][:, j:j + 1]),
                            reads=[res("ps_s%d" % si), res("biasg%d" % gi)], writes=[res("pT%d" % pi)])
                        if j >= i0:
                            P.G(lambda e, pi=pi: e.tensor_tensor(out=pT[pi][:, 0:128], in0=pT[pi][:, 0:128], in1=m_ui_b[:], op=ALU.mult),
                                reads=[res("pT%d" % pi), r_cb], writes=[res("pT%d" % pi)])
                        P.T(lambda e, pi=pi, j=j, cs=cs, ncols=ncols, gi=gi, h=h, J=J: e.matmul(
                            ps_o[gi][:, cs:cs + ncols], lhsT=Vall[:, j, h * 65:h * 65 + 128], rhs=pT[pi][:, :ncols],
                            start=(j == 0), stop=(j == J - 1)), reads=[res("Vall"), res("pT%d" % pi)], writes=[res("ps_o%d" % gi)])
                    P.S(lambda e, gi=gi, N=N: e.activation(out=osb[gi][0:65, :N], in_=ps_o[gi][0:65, :N], func=AF.Copy),
                        reads=[res("ps_o%d" % gi)], writes=[res("osb%d" % gi)])
                    o_, _ = PV_SLOTS["sel65"]
                    P.T(lambda e, gi=gi, N=N: e.matmul(ps_b[:64, :N], lhsT=pv[:, o_:o_ + 64], rhs=osb[gi][:, :N], start=True, stop=True),
                        reads=[r_pv, res("osb%d" % gi)], writes=[res("ps_b")])
                    P.V(lambda e, N=N: e.reciprocal(out=rinv[:, :N], in_=ps_b[:64, :N]), reads=[res("ps_b")], writes=[res("rinv")])
                    P.V(lambda e, gi=gi, N=N: e.tensor_tensor(out=oout[gi][:, :N], in0=osb[gi][0:64, :N], in1=rinv[:, :N], op=ALU.mult),
                        reads=[res("osb%d" % gi), res("rinv")], writes=[res("oout%d" % gi)])
                    P.dma("sync", oT[h * 64:(h + 1) * 64, t0:t0 + N], oout[gi][:, :N], reads=[res("oout%d" % gi)], writes=[res("oT_dram")])
        P.barrier()
        if stop_after == "B":
            P.finish(); return nc

        def dplr_phase(name, units, S_, post_setup, post, scalar_decay=False):
            with contextlib.ExitStack() as st:
                sb = lambda nm, shape, dt: st.enter_context(nc.sbuf_tensor(un(nm), shape, dt))
                psm = lambda nm, shape, dt: st.enter_context(nc.psum_tensor(un(nm), shape, dt))
                hd = units[0][0][2]
                ngs = len(set(cx[0] for cx in units[0]))
                SC = 4
                inb = {}
                for gs in range(ngs):
                    for k in ("r", "k", "a", "b", "v"):
                        inb[(gs, k)] = sb("in_%s%d" % (k, gs), [128, SC * 128], BF16)
                    inb[(gs, "lw")] = sb("in_lw%d" % gs, [128, SC * 128], F32)
                pobj = post_setup(st, ngs, SC)
                Lc = [sb("Lc%d" % g, [128, 129], F32) for g in range(ngs)]
                Lx = [sb("nLm%d" % g, [128, 1], F32) for g in range(ngs)]
                e1 = [sb("e1_%d" % g, [128, 129], F32) for g in range(ngs)]
                e2 = [sb("e2_%d" % g, [128, 129], F32) for g in range(ngs)]
                e5 = [sb("e5_%d" % g, [128, 128], F32) for g in range(ngs)]
                e6 = [sb("e6_%d" % g, [128, 128], F32) for g in range(ngs)]
                if scalar_decay:
                    Dm = {(g, k): sb("D%s_%d" % (k, g), [128, 128], F32) for g in range(ngs) for k in ("m_su", "m_sl", "m_ui")}
                    lcc = [sb("lcc%d" % g, [128, 2], F32) for g in range(ngs)]
                    dtmp = [sb("dtmp%d" % g, [128, 128], F32) for g in range(ngs)]
                opn = ("rh", "rt", "ah", "at", "bh", "kh", "btT", "ktT")
                ops_ = {(g, k): sb("%s%d" % (k, g), [128, 128], BF16) for g in range(ngs) for k in opn}
                tmj = {(g, k): sb("tm_%s%d" % (k, g), [128, 128], BF16) for g in range(ngs) for k in ("bt", "kt", "vt")}
                pstr = psm("pstr", [128, 4, 128], BF16)
                yT = [sb("yT%d" % g, [128, 128], F32) for g in range(ngs)]
                Sf = [sb("Sf%d" % g, [128, 128], F32) for g in range(ngs)]
                Sb = [sb("Sb%d" % g, [128, 128], BF16) for g in range(ngs)]
                hx = []
                for h in range(2):
                    o = TokCtx()
                    o.psx = [psm("psx%d_%d" % (h, i), [128, 512], F32) for i in range(2)]
                    o.psy = psm("psy%d" % h, [128, 512], F32)
                    o.xc = 0
                    for k in ("M0", "MT0", "Mb0", "Mb1", "MTb0", "MTb1", "P0", "P1", "Q0", "Q1", "Noff", "NoffT", "Z", "Z2", "AakT", "ArbT", "ArkT", "RHS", "U"):
                        setattr(o, k, sb("%s_%d" % (k, h), [128, 128], BF16))
                    hx.append(o)
                rn = lambda s: res(name + "_" + s)
                trc = [0]

                for unit in units:
                    gslots = []
                    for cx in unit:
                        if cx[0] not in gslots:
                            gslots.append(cx[0])
                    for g in range(ngs):
                        P.V(lambda e, g=g: e.memset(Sf[g][:], 0.0), writes=[rn("Sf%d" % g)])
                        P.V(lambda e, g=g: e.memset(Sb[g][:], 0.0), writes=[rn("Sb%d" % g)])
                        P.V(lambda e, g=g: e.memset(Lc[g][:, 0:1], 0.0), writes=[rn("Lc%d" % g)])
                    for c in range(NCH):
                        sc, cl = c // SC, c % SC
                        if cl == 0:
                            nsc = min(SC, NCH - c) * 128
                            for g, grow in enumerate(gslots):
                                for k in ("r", "k", "a", "b", "v", "lw"):
                                    P.dma("sync", inb[(g, k)][:, :nsc], S_[k][grow:grow + 128, c * 128:c * 128 + nsc],
                                          reads=[res("scr" + name)], writes=[rn("in_%s%d" % (k, g))])
                                post("load", g, grow, c, nsc, pobj)
                        cols = slice(cl * 128, (cl + 1) * 128)
                        for g in range(ngs):
                            rl = [rn("Lc%d" % g)]
                            P.V(lambda e, g=g: e.tensor_tensor_scan(out=Lc[g][:, 1:129], data0=pvs("ones"), data1=inb[(g, "lw")][:, cols],
                                                                     initial=0.0, op0=ALU.mult, op1=ALU.add),
                                reads=[r_pv, rn("in_lw%d" % g)], writes=rl)
                            P.S(lambda e, g=g: e.activation(out=e2[g][:], in_=Lc[g][:], func=AF.Exp), reads=rl, writes=[rn("e2_%d" % g)])
                            P.S(lambda e, g=g: e.activation(out=e6[g][:], in_=Lc[g][:, 1:129], func=AF.Exp, bias=Lc[g][:, 128:129], scale=-1.0),
                                reads=rl, writes=[rn("e6_%d" % g)])
                            if not scalar_decay:
                                P.V(lambda e, g=g: e.tensor_scalar(out=Lx[g][:], in0=Lc[g][:, 64:65], scalar1=-1.0, scalar2=None, op0=ALU.mult),
                                    reads=rl, writes=[rn("nLm%d" % g)])
                                P.S(lambda e, g=g: e.activation(out=e1[g][:], in_=Lc[g][:], func=AF.Exp, bias=Lx[g][:, 0:1]),
                                    reads=rl + [rn("nLm%d" % g)], writes=[rn("e1_%d" % g)])
                                P.S(lambda e, g=g: e.activation(out=e5[g][:], in_=Lc[g][:, 1:129], func=AF.Exp, bias=Lc[g][:, 64:65], scale=-1.0),
                                    reads=rl, writes=[rn("e5_%d" % g)])
                                specs = [("rh", "r", e1[g][:, 1:129], "e1_"), ("rt", "r", e2[g][:, 1:129], "e2_"),
                                         ("ah", "a", e1[g][:, 0:128], "e1_"), ("at", "a", e2[g][:, 0:128], "e2_"),
                                         ("bh", "b", e5[g][:], "e5_"), ("kh", "k", e5[g][:], "e5_"),
                                         ("btT", "b", e6[g][:], "e6_"), ("ktT", "k", e6[g][:], "e6_")]
                            else:
                                rd = [rn("dtmp%d" % g)]
                                P.V(lambda e, g=g: e.tensor_tensor(out=dtmp[g][:], in0=Lc[g][:, 1:129], in1=pvs("ident"), op=ALU.mult), reads=rl + [r_pv], writes=rd)
                                P.V(lambda e, g=g: e.reduce_sum(out=lcc[g][:, 0:1], in_=dtmp[g][:], axis=mybir.AxisListType.X), reads=rd, writes=[rn("lcc%d" % g)])
                                P.V(lambda e, g=g: e.tensor_tensor(out=dtmp[g][:], in0=Lc[g][:, 0:128], in1=pvs("ident"), op=ALU.mult), reads=rl + [r_pv], writes=rd)
                                P.V(lambda e, g=g: e.reduce_sum(out=lcc[g][:, 1:2], in_=dtmp[g][:], axis=mybir.AxisListType.X), reads=rd, writes=[rn("lcc%d" % g)])
                                rlc = [rn("lcc%d" % g)]
                                for (mk, src, colj, neg) in (("m_ui", Lc[g][:, 1:129], 0, False), ("m_su", Lc[g][:, 0:128], 0, False), ("m_sl", Lc[g][:, 1:129], 1, True)):
                                    D_ = Dm[(g, mk)]; rD = rn("D%s_%d" % (mk, g))
                                    if not neg:
                                        P.V(lambda e, g=g, src=src, colj=colj: e.tensor_scalar(out=dtmp[g][:], in0=src, scalar1=lcc[g][:, colj:colj + 1], scalar2=0.0,
                                                                                                op0=ALU.subtract, op1=ALU.min), reads=rl + rlc, writes=rd)
                                    else:
                                        P.V(lambda e, g=g, src=src, colj=colj: e.tensor_scalar(out=dtmp[g][:], in0=src, scalar1=-1.0, scalar2=lcc[g][:, colj:colj + 1],
                                                                                                op0=ALU.mult, op1=ALU.add), reads=rl + rlc, writes=rd)
                                        P.V(lambda e, g=g: e.tensor_scalar(out=dtmp[g][:], in0=dtmp[g][:], scalar1=0.0, scalar2=None, op0=ALU.min), reads=rd, writes=rd)
                                    P.S(lambda e, g=g, D_=D_: e.activation(out=D_[:], in_=dtmp[g][:], func=AF.Exp), reads=rd, writes=[rD])
                                    P.G(lambda e, D_=D_, mk=mk: e.tensor_tensor(out=D_[:], in0=D_[:], in1=pvs(mk), op=ALU.mult), reads=[rD, r_pv], writes=[rD])
                                specs = [("rt", "r", e2[g][:, 1:129], "e2_"), ("at", "a", e2[g][:, 0:128], "e2_"),
                                         ("btT", "b", e6[g][:], "e6_"), ("ktT", "k", e6[g][:], "e6_")]
                            for qi, (on, ik, eap, en) in enumerate(specs):
                                eng = "vector" if qi % 2 == 0 else "gpsimd"
                                P.op(eng, lambda e, g=g, on=on, ik=ik, eap=eap: e.tensor_tensor(out=ops_[(g, on)][:], in0=inb[(g, ik)][:, cols], in1=eap, op=ALU.mult),
                                     reads=[rn("in_%s%d" % (ik, g)), rn(en + "%d" % g)], writes=[rn("%s%d" % (on, g))])
                            for (tn, src, rsrc) in (("bt", ops_[(g, "btT")][:], rn("btT%d" % g)), ("kt", ops_[(g, "ktT")][:], rn("ktT%d" % g)),
                                                    ("vt", inb[(g, "v")][:, cols], rn("in_v%d" % g))):
                                ti_ = trc[0] % 4; trc[0] += 1
                                P.T(lambda e, ti_=ti_, src=src: e.transpose(out=pstr[:, ti_, :], in_=src, identity=cb["ident"][:]),
                                    reads=[rsrc, r_cb], writes=[rn("pstr")])
                                P.S(lambda e, ti_=ti_, g=g, tn=tn: e.activation(out=tmj[(g, tn)][:], in_=pstr[:, ti_, :], func=AF.Copy),
                                    reads=[rn("pstr")], writes=[rn("tm_%s%d" % (tn, g))])
                        H = []
                        for hi, cx in enumerate(unit):
                            g = gslots.index(cx[0])
                            H.append((hx[hi], g, slice(cx[1], cx[1] + hd), hi))
                        def xslot(o, hi):
                            i = o.xc % 2; o.xc += 1
                            return o.psx[i][:, 0:128], rn("psx%d_%d" % (hi, i))
                        def opr(g, k, ps_):
                            if scalar_decay and k in ("rh", "ah", "bh", "kh"):
                                return inb[(g, k[0])][ps_, cols], rn("in_%s%d" % (k[0], g))
                            return ops_[(g, k)][ps_, :], rn("%s%d" % (k, g))
                        for (o, g, ps_, hi) in H:
                            for (dst, l, r_, mask) in (("MT0", "bh", "ah", "m_su"), ("M0", "ah", "bh", "m_sl"), ("AakT", "kh", "ah", "m_su"),
                                                       ("ArbT", "bh", "rh", "m_ui"), ("ArkT", "kh", "rh", "m_ui")):
                                xa, rx_ = xslot(o, hi)
                                la, rl_ = opr(g, l, ps_); ra, rr_ = opr(g, r_, ps_)
                                P.T(lambda e, xa=xa, la=la, ra=ra: e.matmul(xa, lhsT=la, rhs=ra, start=True, stop=True), reads=[rl_, rr_], writes=[rx_])
                                mk_ap = Dm[(g, mask)][:] if scalar_decay else pvs(mask)
                                mk_r = rn("D%s_%d" % (mask, g)) if scalar_decay else r_pv
                                P.V(lambda e, o=o, dst=dst, xa=xa, mk_ap=mk_ap: e.tensor_tensor(out=getattr(o, dst)[:], in0=xa, in1=mk_ap, op=ALU.mult),
                                    reads=[rx_, mk_r], writes=[rn("%s_%d" % (dst, hi))])
                        def mm_evac(o, hi, lhs, rlhs, rhs, rrhs, dst, rdst, add=None, radd=None):
                            xa, rx_ = xslot(o, hi)
                            P.T(lambda e: e.matmul(xa, lhsT=lhs[:], rhs=rhs[:], start=True, stop=True), reads=[rlhs, rrhs], writes=[rx_])
                            if add is None:
                                P.S(lambda e: e.activation(out=dst[:], in_=xa, func=AF.Copy), reads=[rx_], writes=[rdst])
                            else:
                                P.V(lambda e: e.tensor_tensor(out=dst[:], in0=xa, in1=add[:], op=ALU.add), reads=[rx_, radd], writes=[rdst])
                        for (o, g, ps_, hi) in H:
                            R_ = lambda k: rn("%s_%d" % (k, hi))
                            P.G(lambda e, o=o: e.tensor_tensor(out=o.Mb0[:], in0=o.M0[:], in1=cb["bd32"][:], op=ALU.mult), reads=[R_("M0"), r_cb], writes=[R_("Mb0")])
                            P.G(lambda e, o=o: e.tensor_tensor(out=o.MTb0[:], in0=o.MT0[:], in1=cb["bd32"][:], op=ALU.mult), reads=[R_("MT0"), r_cb], writes=[R_("MTb0")])
                            P.G(lambda e, o=o: e.tensor_tensor(out=o.P0[:], in0=o.MTb0[:], in1=cb["ident"][:], op=ALU.add), reads=[R_("MTb0"), r_cb], writes=[R_("P0")])
                            P.G(lambda e, o=o: e.tensor_tensor(out=o.Q0[:], in0=o.Mb0[:], in1=cb["ident"][:], op=ALU.add), reads=[R_("Mb0"), r_cb], writes=[R_("Q0")])
                        cur = 0
                        for m in range(1, 5):
                            nxt = 1 - cur
                            for (o, g, ps_, hi) in H:
                                R_ = lambda k: rn("%s_%d" % (k, hi))
                                G_ = lambda k: getattr(o, k)
                                cn = lambda k, i: "%s%d" % (k, i)
                                mm_evac(o, hi, G_(cn("MTb", cur)), R_(cn("MTb", cur)), G_(cn("Mb", cur)), R_(cn("Mb", cur)), G_(cn("Mb", nxt)), R_(cn("Mb", nxt)))
                                mm_evac(o, hi, G_(cn("Mb", cur)), R_(cn("Mb", cur)), G_(cn("MTb", cur)), R_(cn("MTb", cur)), G_(cn("MTb", nxt)), R_(cn("MTb", nxt)))
                                mm_evac(o, hi, G_(cn("Mb", nxt)), R_(cn("Mb", nxt)), G_(cn("P", cur)), R_(cn("P", cur)), G_(cn("P", nxt)), R_(cn("P", nxt)),
                                        add=G_(cn("P", cur)), radd=R_(cn("P", cur)))
                                mm_evac(o, hi, G_(cn("MTb", nxt)), R_(cn("MTb", nxt)), G_(cn("Q", cur)), R_(cn("Q", cur)), G_(cn("Q", nxt)), R_(cn("Q", nxt)),
                                        add=G_(cn("Q", cur)), radd=R_(cn("Q", cur)))
                            cur = nxt
                        for (lev, offm) in ((64, "off64"), (128, "off128")):
                            nxt = 1 - cur
                            for (o, g, ps_, hi) in H:
                                R_ = lambda k: rn("%s_%d" % (k, hi))
                                G_ = lambda k: getattr(o, k)
                                cn = lambda k, i: "%s%d" % (k, i)
                                P.G(lambda e, o=o: e.tensor_tensor(out=o.Noff[:], in0=o.M0[:], in1=cb[offm][:], op=ALU.mult), reads=[R_("M0"), r_cb], writes=[R_("Noff")])
                                P.G(lambda e, o=o: e.tensor_tensor(out=o.NoffT[:], in0=o.MT0[:], in1=cb[offm][:], op=ALU.mult), reads=[R_("MT0"), r_cb], writes=[R_("NoffT")])
                                X_, rX = G_(cn("Q", cur)), R_(cn("Q", cur))
                                XT_, rXT = G_(cn("P", cur)), R_(cn("P", cur))
                                if lev != 128:
                                    mm_evac(o, hi, o.NoffT, R_("NoffT"), X_, rX, o.Z, R_("Z"))
                                    mm_evac(o, hi, XT_, rXT, o.Z, R_("Z"), G_(cn("Q", nxt)), R_(cn("Q", nxt)), add=X_, radd=rX)
                                mm_evac(o, hi, o.Noff, R_("Noff"), XT_, rXT, o.Z2, R_("Z2"))
                                mm_evac(o, hi, X_, rX, o.Z2, R_("Z2"), G_(cn("P", nxt)), R_(cn("P", nxt)), add=XT_, radd=rXT)
                            cur = nxt
                        for (o, g, ps_, hi) in H:
                            Pf, rPf = getattr(o, "P%d" % cur), rn("P%d_%d" % (cur, hi))
                            ry = rn("psy%d" % hi)
                            hs = slice(ps_.start, ps_.start + hd)
                            at_, rat = opr(g, "at", ps_)
                            rt_, rrt = opr(g, "rt", ps_)
                            P.T(lambda e, o=o, at_=at_, g=g, ps_=ps_: e.matmul(o.psy[:, 256:256 + hd], lhsT=at_, rhs=Sb[g][ps_, 0:hd], start=True, stop=False),
                                reads=[rat, rn("Sb%d" % g)], writes=[ry])
                            P.T(lambda e, o=o, g=g, hs=hs: e.matmul(o.psy[:, 256:256 + hd], lhsT=o.AakT[:], rhs=tmj[(g, "vt")][:, hs], start=False, stop=True),
                                reads=[rn("AakT_%d" % hi), rn("tm_vt%d" % g)], writes=[ry])
                            P.S(lambda e, o=o: e.activation(out=o.RHS[:, :hd], in_=o.psy[:, 256:256 + hd], func=AF.Copy), reads=[ry], writes=[rn("RHS_%d" % hi)])
                            P.T(lambda e, o=o, Pf=Pf: e.matmul(o.psy[:, 384:384 + hd], lhsT=Pf[:], rhs=o.RHS[:, :hd], start=True, stop=True),
                                reads=[rPf, rn("RHS_%d" % hi)], writes=[ry])
                            P.S(lambda e, o=o: e.activation(out=o.U[:, :hd], in_=o.psy[:, 384:384 + hd], func=AF.Copy), reads=[ry], writes=[rn("U_%d" % hi)])
                            P.T(lambda e, o=o, g=g, ps_=ps_, rt_=rt_: e.matmul(o.psy[ps_, 0:128], lhsT=Sb[g][ps_, 0:hd], rhs=rt_, start=True, stop=False),
                                reads=[rn("Sb%d" % g), rrt], writes=[ry])
                            P.T(lambda e, o=o, ps_=ps_: e.matmul(o.psy[ps_, 0:128], lhsT=o.U[:, :hd], rhs=o.ArbT[:], start=False, stop=False),
                                reads=[rn("U_%d" % hi), rn("ArbT_%d" % hi)], writes=[ry])
                            P.T(lambda e, o=o, g=g, ps_=ps_, hs=hs: e.matmul(o.psy[ps_, 0:128], lhsT=tmj[(g, "vt")][:, hs], rhs=o.ArkT[:], start=False, stop=True),
                                reads=[rn("tm_vt%d" % g), rn("ArkT_%d" % hi)], writes=[ry])
                            P.V(lambda e, o=o, g=g, ps_=ps_: e.tensor_copy(out=yT[g][ps_, :], in_=o.psy[ps_, 0:128]), reads=[ry], writes=[rn("yT%d" % g)])
                            P.T(lambda e, o=o, g=g, ps_=ps_, hs=hs: e.matmul(o.psy[ps_, 128:128 + hd], lhsT=tmj[(g, "bt")][:, hs], rhs=o.U[:, :hd], start=True, stop=False),
                                reads=[rn("tm_bt%d" % g), rn("U_%d" % hi)], writes=[ry])
                            P.T(lambda e, o=o, g=g, ps_=ps_, hs=hs: e.matmul(o.psy[ps_, 128:128 + hd], lhsT=tmj[(g, "kt")][:, hs], rhs=tmj[(g, "vt")][:, hs], start=False, stop=True),
                                reads=[rn("tm_kt%d" % g), rn("tm_vt%d" % g)], writes=[ry])
                            P.V(lambda e, o=o, g=g, ps_=ps_: e.scalar_tensor_tensor(out=Sf[g][ps_, 0:hd], in0=Sf[g][ps_, 0:hd], scalar=e2[g][ps_, 128:129],
                                                                                   in1=o.psy[ps_, 128:128 + hd], op0=ALU.mult, op1=ALU.add),
                                reads=[rn("Sf%d" % g), rn("e2_%d" % g), ry], writes=[rn("Sf%d" % g)])
                            P.G(lambda e, g=g, ps_=ps_: e.tensor_copy(out=Sb[g][ps_, 0:hd], in_=Sf[g][ps_, 0:hd]), reads=[rn("Sf%d" % g)], writes=[rn("Sb%d" % g)])
                        for g, grow in enumerate(gslots):
                            post("chunk", g, grow, c, (yT[g], rn("yT%d" % g), cl), pobj)
            P.barrier()

        def gdn_post_setup(st, ngs, SC):
            sb = lambda nm, shape, dt: st.enter_context(nc.sbuf_tensor(un(nm), shape, dt))
            o = TokCtx()
            o.z = [sb("gz%d" % g, [128, SC * 128], BF16) for g in range(ngs)]
            o.sq = sb("gpsq", [128, 128], BF16)
            o.rs = sb("gprs", [128, 128], F32)
            o.ob = [sb("gpob%d" % i, [128, 128], BF16) for i in range(2)]
            o.ps = st.enter_context(nc.psum_tensor(un("gpps"), [128, 128], F32))
            o.k = 0
            return o

        def gdn_post(kind, g, grow, c, arg, o):
            if kind == "load":
                P.dma("sync", o.z[g][:, :arg], gS["z"][grow:grow + 128, c * 128:c * 128 + arg], reads=[res("scrA")], writes=[res("gz%d" % g)])
                return
            yT_, ry, cl = arg
            P.G(lambda e: e.tensor_tensor(out=o.sq[:], in0=yT_[:], in1=yT_[:], op=ALU.mult), reads=[ry], writes=[res("gpsq")])
            P.T(lambda e: e.matmul(o.ps[:], lhsT=cb["ones"][:], rhs=o.sq[:], start=True, stop=True), reads=[r_cb, res("gpsq")], writes=[res("gpps")])
            P.S(lambda e: e.activation(out=o.rs[:], in_=o.ps[:], func=AF.Sqrt, bias=cvec[:, 0:1], scale=1.0 / 128), reads=[res("gpps"), r_cb], writes=[res("gprs")])
            P.V(lambda e: e.reciprocal(out=o.rs[:], in_=o.rs[:]), reads=[res("gprs")], writes=[res("gprs")])
            P.V(lambda e: e.scalar_tensor_tensor(out=o.rs[:], in0=yT_[:], scalar=pvs("o_gain"), in1=o.rs[:], op0=ALU.mult, op1=ALU.mult),
                reads=[ry, r_pv, res("gprs")], writes=[res("gprs")])
            i = o.k % 2; o.k += 1
            P.V(lambda e: e.tensor_tensor(out=o.ob[i][:], in0=o.rs[:], in1=o.z[g][:, cl * 128:(cl + 1) * 128], op=ALU.mult),
                reads=[res("gprs"), res("gz%d" % g)], writes=[res("gpob%d" % i)])
            P.dma("sync", oT[512 + grow:512 + grow + 128, c * 128:(c + 1) * 128], o.ob[i][:], reads=[res("gpob%d" % i)], writes=[res("oT_dram")])

        R["scrgdn"] = res("scrA")
        if only is None or "C" in only:
          dplr_phase("gdn", [[(0, 0, 128), (128, 0, 128)], [(256, 0, 128), (384, 0, 128)]], gS, gdn_post_setup, gdn_post, scalar_decay=True)
        if stop_after == "C":
            P.finish(); return nc

        with contextlib.ExitStack() as st:
          if only is None or "D" in only:
            sb = lambda name, shape, dt: st.enter_context(nc.sbuf_tensor(un(name), shape, dt))
            c = make_tok_ctx(st, nwi=2, nwo=2)
            oin = sb("oin", [128, 8, 512], BF16)
            xx = sb("xx", [128, 8, 512], BF16)
            xm = sb("xm", [128, 8, 512], BF16)
            w1s = sb("w1s", [128, 8, 64], BF16); a1s = sb("a1s", [128, 8, 64], BF16); g1s = sb("g1s", [128, 8, 160], BF16)
            w2s = sb("w2s", [64, D], BF16); a2s = sb("a2s", [64, D], BF16); g2s = sb("g2s", [128, 2, D], BF16)
            P.dma("sync", w1s[:], wview(Wb["rwkv_w1"]), reads=[res("W_rwkv_w1")], writes=[res("lora")])
            P.dma("sync", a1s[:], wview(Wb["rwkv_a1"]), reads=[res("W_rwkv_a1")], writes=[res("lora")])
            P.dma("sync", g1s[:], wview(Wb["rwkv_g1"]), reads=[res("W_rwkv_g1")], writes=[res("lora")])
            P.dma("sync", w2s[:], Wb["rwkv_w2"], reads=[res("W_rwkv_w2")], writes=[res("lora")])
            P.dma("sync", a2s[:], Wb["rwkv_a2"], reads=[res("W_rwkv_a2")], writes=[res("lora")])
            P.dma("sync", g2s[:, 0, :], Wb["rwkv_g2"][0:128, :], reads=[res("W_rwkv_g2")], writes=[res("lora")])
            P.dma("sync", g2s[0:32, 1, :], Wb["rwkv_g2"][128:160, :], reads=[res("W_rwkv_g2")], writes=[res("lora")])
            r_lora = res("lora")
            lmid = [sb("lmid%d" % i, [128, 512], BF16) for i in range(2)]
            rr = sb("rr", [128, 8, 512], BF16)
            kk_ = sb("kkk", [128, 8, 512], F32)
            aa = sb("aa", [128, 512], F32)
            t1 = sb("t1", [128, 512], F32)
            t2 = sb("t2", [128, 512], F32)
            ob = [sb("obD%d" % i, [128, 512], BF16) for i in range(4)]
            obf = [sb("obF%d" % i, [128, 512], F32) for i in range(2)]
            ocnt = [0, 0]
            P.V(lambda e: e.memset(c.hn[:, :, 0:1], 0.0), writes=[res("hn")])
            omk = sb("omk", [128, 8], F32)
            print("phase D sbuf remaining", nc.sbuf_bytes_remaining)
            P.V(lambda e: e.tensor_scalar(out=omk[:], in0=pvs("k_a"), scalar1=-1.0, scalar2=1.0, op0=ALU.mult, op1=ALU.add), reads=[r_pv], writes=[res("omk")])

            def out_bf(dst_ap, make):
                i = ocnt[0] % 4; ocnt[0] += 1
                make(ob[i], res("obD%d" % i))
                P.dma("sync", dst_ap, ob[i][:, :dst_ap.shape[-1]], reads=[res("obD%d" % i)], writes=[res("scrrwkv")])

            for ti, (t0, N) in enumerate(tiles):
                h_t, r_h = c.hT[ti % 2], res("hT%d" % (ti % 2))
                P.dma("sync", h_t[:, :, :N], hview(hT)[:, :, t0:t0 + N], reads=[res("hT_dram")], writes=[r_h])
                P.dma("sync", oin[:, :, :N], hview(oT)[:, :, t0:t0 + N], reads=[res("oT_dram")], writes=[res("oin")])
                proj_residual(c, h_t, r_h, N, oin, res("oin"), "hyb_w_out")
                ffn(c, h_t, r_h, N, 1)
                ffn(c, h_t, r_h, N, 2)
                P.dma("sync", hview(hT)[:, :, t0:t0 + N], h_t[:, :, :N], reads=[r_h], writes=[res("hT_dram")])
                rmsnorm(c, h_t, r_h, N, "mix_norm1")
                import os as _os
                DCUT = float(_os.environ.get("DCUT", "99"))
                if DCUT <= 0:
                    continue
                for kc in range(8):
                    P.op("vector", lambda e, kc=kc: e.tensor_tensor(out=xx[:, kc, :N], in0=c.hn[:, kc, 0:N], in1=c.hn[:, kc, 1:1 + N], op=ALU.subtract),
                         reads=[res("hn")], writes=[res("xx")])
                o_mu, _ = PV_SLOTS["mu"]
                def mix(i):
                    for kc in range(8):
                        eng = "vector" if kc % 2 == 0 else "gpsimd"
                        P.op("vector", lambda e, kc=kc: e.scalar_tensor_tensor(out=xm[:, kc, :N], in0=xx[:, kc, :N], scalar=pv[:, o_mu + i * 8 + kc:o_mu + i * 8 + kc + 1],
                                                                          in1=c.hn[:, kc, 1:1 + N], op0=ALU.mult, op1=ALU.add),
                             reads=[res("xx"), res("hn"), r_pv], writes=[res("xm")])
                xfn = lambda kc: xm[:, kc, :N]
                rxm = [res("xm")]
                if DCUT <= 0.3:
                    continue
                mix(0)
                if DCUT <= 0.6:
                    continue
                def cons_r(fc, pi, m):
                    if _os.environ.get("RVAR", "") != "a":
                        P.S(lambda e: e.activation(out=rr[:, fc, :N], in_=c.pt[pi][:, :N], func=AF.Copy), reads=[res("pt%d" % pi)], writes=[res("rr")])
                    if _os.environ.get("RVAR", "") == "b":
                        return
                    out_bf(rS["r"][fc * 128:(fc + 1) * 128, t0:t0 + N],
                           lambda o, ro: P.V(lambda e: e.tensor_copy(out=o[:, :N], in_=c.pt[pi][:, :N]), reads=[res("pt%d" % pi)], writes=[ro]))
                lin_fm(c, xfn, rxm, N, "rwkv_w_r", 0, D, cons_r)
                if DCUT <= 1:
                    continue
                mix(1)
                pi = next_pt(c)
                for kc in range(8):
                    P.T(lambda e, pi=pi, kc=kc: e.matmul(c.pt[pi][:64, :N], lhsT=w1s[:, kc, :], rhs=xm[:, kc, :N], start=(kc == 0), stop=(kc == 7)),
                        reads=[r_lora, res("xm")], writes=[res("pt%d" % pi)])
                P.S(lambda e, pi=pi: e.activation(out=lmid[0][:64, :N], in_=c.pt[pi][:64, :N], func=AF.Tanh), reads=[res("pt%d" % pi)], writes=[res("lmid0")])
                for fc in range(8):
                    pj = next_pt(c)
                    P.T(lambda e, pj=pj, fc=fc: e.matmul(c.pt[pj][:, :N], lhsT=w2s[:, fc * 128:(fc + 1) * 128], rhs=lmid[0][:64, :N], start=True, stop=True),
                        reads=[r_lora, res("lmid0")], writes=[res("pt%d" % pj)])
                    fi = ocnt[1] % 2; ocnt[1] += 1
                    P.S(lambda e, pj=pj, fc=fc, fi=fi: e.activation(out=obf[fi][:, :N], in_=c.pt[pj][:, :N], func=AF.Sigmoid, bias=pvs("w0", fc, 1)),
                        reads=[res("pt%d" % pj), r_pv], writes=[res("obF%d" % fi)])
                    P.V(lambda e, fi=fi: e.tensor_scalar(out=obf[fi][:, :N], in0=obf[fi][:, :N], scalar1=-float(np.exp(-0.5)), scalar2=None, op0=ALU.mult),
                        reads=[res("obF%d" % fi)], writes=[res("obF%d" % fi)])
                    P.dma("sync", rS["lw"][fc * 128:(fc + 1) * 128, t0:t0 + N], obf[fi][:, :N], reads=[res("obF%d" % fi)], writes=[res("scrrwkv")])
                if DCUT <= 2:
                    continue
                mix(2)
                def cons_k(fc, pi, m):
                    P.S(lambda e: e.activation(out=kk_[:, fc, :N], in_=c.pt[pi][:, :N], func=AF.Copy), reads=[res("pt%d" % pi)], writes=[res("kkk")])
                lin_fm(c, xfn, rxm, N, "rwkv_w_k", 0, D, cons_k)
                mix(3)
                def cons_v(fc, pi, m):
                    out_bf(rS["v"][fc * 128:(fc + 1) * 128, t0:t0 + N],
                           lambda o, ro: P.V(lambda e: e.tensor_copy(out=o[:, :N], in_=c.pt[pi][:, :N]), reads=[res("pt%d" % pi)], writes=[ro]))
                    P.S(lambda e: e.activation(out=c.act[:, fc, :N], in_=c.pt[pi][:, :N], func=AF.Copy), reads=[res("pt%d" % pi)], writes=[res("act")])
                lin_fm(c, xfn, rxm, N, "rwkv_w_v", 0, D, cons_v)
                if DCUT <= 3:
                    continue
                mix(4)
                pi = next_pt(c)
                for kc in range(8):
                    P.T(lambda e, pi=pi, kc=kc: e.matmul(c.pt[pi][:64, :N], lhsT=a1s[:, kc, :], rhs=xm[:, kc, :N], start=(kc == 0), stop=(kc == 7)),
                        reads=[r_lora, res("xm")], writes=[res("pt%d" % pi)])
                P.S(lambda e, pi=pi: e.activation(out=lmid[1][:64, :N], in_=c.pt[pi][:64, :N], func=AF.Copy), reads=[res("pt%d" % pi)], writes=[res("lmid1")])
                for fc in range(8):
                    pj = next_pt(c)
                    P.T(lambda e, pj=pj, fc=fc: e.matmul(c.pt[pj][:, :N], lhsT=a2s[:, fc * 128:(fc + 1) * 128], rhs=lmid[1][:64, :N], start=True, stop=True),
                        reads=[r_lora, res("lmid1")], writes=[res("pt%d" % pj)])
                    P.S(lambda e, pj=pj, fc=fc: e.activation(out=aa[:, :N], in_=c.pt[pj][:, :N], func=AF.Sigmoid, bias=pvs("a0", fc, 1)),
                        reads=[res("pt%d" % pj), r_pv], writes=[res("aa")])
                    P.V(lambda e, fc=fc: e.tensor_scalar(out=t1[:, :N], in0=kk_[:, fc, :N], scalar1=pvs("k_k", fc, 1), scalar2=None, op0=ALU.mult),
                        reads=[res("kkk"), r_pv], writes=[res("t1")])
                    P.G(lambda e: e.tensor_tensor(out=c.sq[:, 0, :N], in0=t1[:, :N], in1=t1[:, :N], op=ALU.mult), reads=[res("t1")], writes=[res("sq")])
                    pq = sumsq_bc(c, lambda k: c.sq[:, 0, :N], 1, N, [res("sq")], lhs_name="blk64")
                    rsqrt_from_psum(c, pq, N, c.rstd[:, :N], res("rstd"))
                    P.V(lambda e: e.tensor_tensor(out=t1[:, :N], in0=t1[:, :N], in1=c.rstd[:, :N], op=ALU.mult), reads=[res("t1"), res("rstd")], writes=[res("t1")])
                    out_bf(rS["a"][fc * 128:(fc + 1) * 128, t0:t0 + N],
                           lambda o, ro: P.G(lambda e: e.tensor_scalar(out=o[:, :N], in0=t1[:, :N], scalar1=-1.0, scalar2=None, op0=ALU.mult), reads=[res("t1")], writes=[ro]))
                    out_bf(rS["b"][fc * 128:(fc + 1) * 128, t0:t0 + N],
                           lambda o, ro: P.V(lambda e: e.tensor_tensor(out=o[:, :N], in0=t1[:, :N], in1=aa[:, :N], op=ALU.mult), reads=[res("t1"), res("aa")], writes=[ro]))
                    P.V(lambda e, fc=fc: e.tensor_scalar(out=t2[:, :N], in0=aa[:, :N], scalar1=pvs("k_a", fc, 1), scalar2=omk[:, fc:fc + 1], op0=ALU.mult, op1=ALU.add),
                        reads=[res("aa"), r_pv, res("omk")], writes=[res("t2")])
                    P.V(lambda e, fc=fc: e.tensor_tensor(out=t2[:, :N], in0=t2[:, :N], in1=kk_[:, fc, :N], op=ALU.mult), reads=[res("t2"), res("kkk")], writes=[res("t2")])
                    out_bf(rS["k"][fc * 128:(fc + 1) * 128, t0:t0 + N],
                           lambda o, ro: P.G(lambda e: e.tensor_copy(out=o[:, :N], in_=t2[:, :N]), reads=[res("t2")], writes=[ro]))
                    P.V(lambda e, fc=fc: e.scalar_tensor_tensor(out=c.sq[:, 1, :N], in0=t2[:, :N], scalar=pvs("r_k", fc, 1), in1=rr[:, fc, :N], op0=ALU.mult, op1=ALU.mult),
                        reads=[res("t2"), r_pv, res("rr")], writes=[res("sq")])
                    pb = sumsq_bc(c, lambda k: c.sq[:, 1, :N], 1, N, [res("sq")], lhs_name="blk64")
                    out_bf(rS["bonus"][fc * 128:(fc + 1) * 128, t0:t0 + N],
                           lambda o, ro: P.V(lambda e, fc=fc: e.tensor_tensor(out=o[:, :N], in0=c.pt[pb][:, :N], in1=c.act[:, fc, :N], op=ALU.mult),
                                             reads=[res("pt%d" % pb), res("act")], writes=[ro]))
                if DCUT <= 4:
                    continue
                mix(5)
                for (m0, mm, li) in ((0, 128, 0), (128, 32, 1)):
                    pi = next_pt(c)
                    for kc in range(8):
                        P.T(lambda e, pi=pi, kc=kc, m0=m0, mm=mm: e.matmul(c.pt[pi][:mm, :N], lhsT=g1s[:, kc, m0:m0 + mm], rhs=xm[:, kc, :N], start=(kc == 0), stop=(kc == 7)),
                            reads=[r_lora, res("xm")], writes=[res("pt%d" % pi)])
                    P.S(lambda e, pi=pi, mm=mm, li=li: e.activation(out=lmid[li][:mm, :N], in_=c.pt[pi][:mm, :N], func=AF.Sigmoid),
                        reads=[res("pt%d" % pi)], writes=[res("lmid%d" % li)])
                for fc in range(8):
                    pj = next_pt(c)
                    P.T(lambda e, pj=pj, fc=fc: e.matmul(c.pt[pj][:, :N], lhsT=g2s[:, 0, fc * 128:(fc + 1) * 128], rhs=lmid[0][:, :N], start=True, stop=False),
                        reads=[r_lora, res("lmid0")], writes=[res("pt%d" % pj)])
                    P.T(lambda e, pj=pj, fc=fc: e.matmul(c.pt[pj][:, :N], lhsT=g2s[0:32, 1, fc * 128:(fc + 1) * 128], rhs=lmid[1][0:32, :N], start=False, stop=True),
                        reads=[r_lora, res("lmid1")], writes=[res("pt%d" % pj)])
                    out_bf(rS["g"][fc * 128:(fc + 1) * 128, t0:t0 + N],
                           lambda o, ro: P.V(lambda e, pj=pj: e.tensor_copy(out=o[:, :N], in_=c.pt[pj][:, :N]), reads=[res("pt%d" % pj)], writes=[ro]))
                P.G(lambda e: e.tensor_copy(out=c.hn[:, :, 0:1], in_=c.hn[:, :, N:N + 1]), reads=[res("hn")], writes=[res("hn")])
        P.barrier()
        if stop_after == "D":
            P.finish(); return nc

        def rw_post_setup(st, ngs, SC):
            sb = lambda nm, shape, dt: st.enter_context(nc.sbuf_tensor(un(nm), shape, dt))
            o = TokCtx()
            o.bon = sb("rbon", [128, SC * 128], BF16)
            o.g = sb("rgate", [128, SC * 128], BF16)
            o.sq = sb("rpsq", [128, 128], BF16)
            o.yb = sb("rpyb", [128, 128], BF16)
            o.mean = sb("rpmean", [128, 128], F32)
            o.var = sb("rpvar", [128, 128], F32)
            o.yc = sb("rpyc", [128, 128], F32)
            o.ob = [sb("rpob%d" % i, [128, 128], BF16) for i in range(2)]
            _ps = st.enter_context(nc.psum_tensor(un("rpps"), [128, 128], F32))
            o.ps = [_ps, _ps]
            o.k = 0
            return o

        def rw_post(kind, g, grow, c, arg, o):
            if kind == "load":
                P.dma("sync", o.bon[:, :arg], rS["bonus"][grow:grow + 128, c * 128:c * 128 + arg], reads=[res("scrrwkv")], writes=[res("rbon")])
                P.dma("sync", o.g[:, :arg], rS["g"][grow:grow + 128, c * 128:c * 128 + arg], reads=[res("scrrwkv")], writes=[res("rgate")])
                return
            yT_, ry, cl = arg
            fc = grow // 128
            cols = slice(cl * 128, (cl + 1) * 128)
            P.G(lambda e: e.tensor_copy(out=o.yb[:], in_=yT_[:]), reads=[ry], writes=[res("rpyb")])
            P.T(lambda e: e.matmul(o.ps[0][:], lhsT=cb["blk64"][:], rhs=o.yb[:], start=True, stop=True), reads=[r_cb, res("rpyb")], writes=[res("rpps")])
            P.V(lambda e: e.scalar_tensor_tensor(out=o.yc[:], in0=o.ps[0][:], scalar=-1.0 / 64, in1=yT_[:], op0=ALU.mult, op1=ALU.add),
                reads=[res("rpps"), ry], writes=[res("rpyc")])
            P.G(lambda e: e.tensor_tensor(out=o.sq[:], in0=o.yc[:], in1=o.yc[:], op=ALU.mult), reads=[res("rpyc")], writes=[res("rpsq")])
            P.T(lambda e: e.matmul(o.ps[1][:], lhsT=cb["blk64"][:], rhs=o.sq[:], start=True, stop=True), reads=[r_cb, res("rpsq")], writes=[res("rpps")])
            P.S(lambda e: e.activation(out=o.var[:], in_=o.ps[1][:], func=AF.Sqrt, bias=cvec[:, 2:3], scale=1.0 / 64), reads=[res("rpps"), r_cb], writes=[res("rpvar")])
            P.V(lambda e: e.reciprocal(out=o.var[:], in_=o.var[:]), reads=[res("rpvar")], writes=[res("rpvar")])
            P.V(lambda e: e.scalar_tensor_tensor(out=o.yc[:], in0=o.yc[:], scalar=pvs("ln_w", fc, 1), in1=o.var[:], op0=ALU.mult, op1=ALU.mult),
                reads=[res("rpyc"), r_pv, res("rpvar")], writes=[res("rpyc")])
            P.V(lambda e: e.scalar_tensor_tensor(out=o.yc[:], in0=o.yc[:], scalar=pvs("ln_b", fc, 1), in1=o.bon[:, cols], op0=ALU.add, op1=ALU.add),
                reads=[res("rpyc"), r_pv, res("rbon")], writes=[res("rpyc")])
            i = o.k % 2; o.k += 1
            P.V(lambda e: e.tensor_tensor(out=o.ob[i][:], in0=o.yc[:], in1=o.g[:, cols], op=ALU.mult), reads=[res("rpyc"), res("rgate")], writes=[res("rpob%d" % i)])
            P.dma("sync", zT[grow:grow + 128, c * 128:(c + 1) * 128], o.ob[i][:], reads=[res("rpob%d" % i)], writes=[res("zT_dram")])

        if only is None or "E" in only:
          dplr_phase("rwkv", [[(gq * 128, 0, 64), (gq * 128, 64, 64)] for gq in range(8)], rS, rw_post_setup, rw_post)
        if stop_after == "E":
            P.finish(); return nc

        with contextlib.ExitStack() as st:
            sb = lambda name, shape, dt: st.enter_context(nc.sbuf_tensor(un(name), shape, dt))
            c = make_tok_ctx(st, nwi=3, nwo=2)
            oin = sb("oinF", [128, 8, 512], BF16)
            for ti, (t0, N) in enumerate(tiles):
                h_t, r_h = c.hT[ti % 2], res("hT%d" % (ti % 2))
                P.dma("sync", h_t[:, :, :N], hview(hT)[:, :, t0:t0 + N], reads=[res("hT_dram")], writes=[r_h])
                P.dma("sync", oin[:, :, :N], hview(zT)[:, :, t0:t0 + N], reads=[res("zT_dram")], writes=[res("oinF")])
                proj_residual(c, h_t, r_h, N, oin, res("oinF"), "rwkv_w_o")
                ffn(c, h_t, r_h, N, 3)
                P.S(lambda e: e.activation(out=c.sq[:, :, :N], in_=h_t[:, :, :N], func=AF.Square), reads=[r_h], writes=[res("sq")])
                pi = sumsq_bc(c, lambda k: c.sq[:, k, :N], 8, N, [res("sq")])
                rsqrt_from_psum(c, pi, N, c.rstd[:, :N], res("rstd"))
                for kc in range(8):
                    eng = "vector" if kc % 2 == 0 else "gpsimd"
                    P.op("vector", lambda e, kc=kc: e.scalar_tensor_tensor(out=h_t[:, kc, :N], in0=h_t[:, kc, :N], scalar=pvs("final_norm", kc, 1), in1=c.rstd[:, :N],
                                                                       op0=ALU.mult, op1=ALU.mult), reads=[r_h, r_pv, res("rstd")], writes=[r_h])
                P.dma("sync", hview(outT)[:, :, t0:t0 + N], h_t[:, :, :N], reads=[r_h], writes=[res("out_dram")])
        P.finish()
    return nc


_NC_CACHE = {}


def make_in_maps(inp, T):
    x = np.asarray(inp["x"], np.float32)
    B, S, _ = x.shape
    meta = np.asarray(inp["meta"], np.float32)
    pv = pack_params(inp)
    base = {"pvec": pv}
    k = 0
    for l in range(2):
        for h in range(2):
            base["ffn_w_in%d" % k] = np.ascontiguousarray(inp["ffn_w_in"][l, h], np.float32)
            base["ffn_w_out%d" % k] = np.ascontiguousarray(inp["ffn_w_out"][l, h], np.float32)
            k += 1
    base["hyb_w_in"] = np.ascontiguousarray(inp["hyb_w_in"][0], np.float32)
    base["hyb_w_out"] = np.ascontiguousarray(inp["hyb_w_out"][0], np.float32)
    for nm in ("w_r", "w_k", "w_v", "w_o", "w1", "w2", "a1", "a2", "g1", "g2"):
        base["rwkv_" + nm] = np.ascontiguousarray(inp["rwkv_" + nm][0], np.float32)
    maps = []
    for core in range(8):
        b = core % B
        hT0 = np.zeros((D, T), np.float32)
        hT0[:, :NMETA] = meta.T
        hT0[:, NMETA:NMETA + S] = x[b].T
        m = dict(base)
        m["hT0"] = hT0
        maps.append(m)
    return maps


def kernel(**inp):
    x = np.asarray(inp["x"])
    B, S, _ = x.shape
    T = ((NMETA + S + 127) // 128) * 128
    if T not in _NC_CACHE:
        _NC_CACHE[T] = build(T)
    nc = _NC_CACHE[T]
    maps = make_in_maps(inp, T)
    res = run_bass_kernel_spmd(nc, maps, core_ids=list(range(8)))
    out = np.empty((B, S, D), np.float32)
    for b in range(B):
        out[b] = res.results[b]["outT"][:, NMETA:NMETA + S].T
    return out
```

```python
import contextlib
import numpy as np
import concourse.bass as bass
import concourse.mybir as mybir
from concourse.bass_utils import run_bass_kernel_spmd

F32 = mybir.dt.float32
BF16 = mybir.dt.bfloat16
ALU = mybir.AluOpType
AF = mybir.ActivationFunctionType

D = 1024
DFF = 2816
EPS = 1e-6
NMETA = 16
HYB_IN = 3600
GN_EPS = 64e-5


_NUN = [0]


NAMES = {}


def un(name):
    _NUN[0] += 1
    NAMES[name] = "t%d_%s" % (_NUN[0], name)
    return NAMES[name]


SAME_ENGINE_SYNC = True
PSUM_PREFIXES = ("pt", "py", "ps_", "psx", "psy", "pstr", "gpps", "rpps")


class Res:
    __slots__ = ("name", "w", "r", "excl")

    def __init__(self, name):
        self.name = name
        self.w = None
        self.r = []
        base = name.split("_", 1)[1] if name.startswith(("gdn_", "rwkv_")) else name
        self.excl = base.startswith(PSUM_PREFIXES)


class _Rec:
    def __init__(self):
        self.calls = []

    def __getattr__(self, name):
        def f(*a, **k):
            self.calls.append((name, a, k))
            return self
        return f


class Prog:
    ENGS = ("tensor", "vector", "scalar", "gpsimd", "sync")

    def __init__(self, nc, stack, n_dma_sems=12):
        self.nc = nc
        self.lists = {e: [] for e in self.ENGS}
        self.stack = stack
        self.epoch = 0
        self.esem = {e: stack.enter_context(nc.semaphore("s_" + e)) for e in self.ENGS}
        self.ecount = {e: 0 for e in self.ENGS}
        self.LIMIT = 12000
        self.seen = {e: {} for e in self.ENGS}
        self.dsems, self.dcount, self.dnext = {}, {}, {}
        for q in ("sync", "gpsimd", "scalar"):
            self.dsems[q] = [stack.enter_context(nc.semaphore("d_%s%d" % (q, i)))
                             for i in range(n_dma_sems if q != "scalar" else 2)]
            self.dcount[q] = [0] * len(self.dsems[q])
            self.dnext[q] = 0
        self.semobj = {}
        for e in self.ENGS:
            self.semobj[("e", e, 0)] = self.esem[e]
        for q in self.dsems:
            for i, s in enumerate(self.dsems[q]):
                self.semobj[("d", q, i)] = s
        self.n_ins = 0

    def _need(self, eng, deps, key, val):
        if key[0] == "e":
            if key[2] < self.epoch:
                return
            if key[1] == eng and (eng == "tensor" or not SAME_ENGINE_SYNC):
                return
        if self.seen[eng].get(key, 0) >= val:
            return
        if deps.get(key, 0) < val:
            deps[key] = val

    def _collect(self, eng, reads, writes):
        deps = {}
        for r in reads:
            if r.w is not None:
                self._need(eng, deps, r.w[0], r.w[1])
        for w in writes:
            if w.w is not None:
                self._need(eng, deps, w.w[0], w.w[1])
            for (k, v) in w.r:
                self._need(eng, deps, k, v)
        for k, v in deps.items():
            self.seen[eng][k] = v
        return list(deps.items())

    def _mark(self, key, val, reads, writes):
        for r in reads:
            r.r = [(k, v) for (k, v) in r.r if k != key]
            r.r.append((key, val))
        for w in writes:
            w.w = (key, val)
            w.r = []

    def op(self, eng, fn, reads=(), writes=()):
        rec = _Rec()
        fn(rec)
        assert len(rec.calls) == 1, rec.calls
        name, a, k = rec.calls[0]
        fn = (lambda e, name=name, a=a, k=k: getattr(e, name)(*a, **k))
        ex = [r for r in reads if r.excl]
        if ex:
            writes = list(writes) + ex
        if self.ecount[eng] >= self.LIMIT:
            self.barrier(rotate=True)
        waits = self._collect(eng, reads, writes)
        self.ecount[eng] += 1
        key = ("e", eng, self.epoch)
        self.lists[eng].append((waits, fn, key, 1))
        self._mark(key, self.ecount[eng], reads, writes)
        self.n_ins += 1

    def V(self, fn, reads=(), writes=()):
        self.op("vector", fn, reads, writes)

    def S(self, fn, reads=(), writes=()):
        self.op("scalar", fn, reads, writes)

    def G(self, fn, reads=(), writes=()):
        self.op("gpsimd", fn, reads, writes)

    def T(self, fn, reads=(), writes=()):
        self.op("tensor", fn, reads, writes)

    def dma(self, q, out, in_, reads=(), writes=(), **kw):
        i = self.dnext[q]
        self.dnext[q] = (i + 1) % len(self.dsems[q])
        key = ("d", q, i)
        waits = self._collect(q, reads, writes)
        prev = self.dcount[q][i]
        if prev > 0 and self.seen[q].get(key, 0) < prev:
            waits.append((key, prev))
            self.seen[q][key] = prev
        self.dcount[q][i] += 16
        self.lists[q].append((waits, (lambda e: e.dma_start(out=out, in_=in_, **kw)), key, 16))
        self._mark(key, self.dcount[q][i], reads, writes)
        self.n_ins += 1

    def barrier(self, rotate=False):
        allw = {}
        for e in self.ENGS:
            if self.ecount[e] > 0:
                allw[("e", e, self.epoch)] = self.ecount[e]
        for q in self.dsems:
            for i, c in enumerate(self.dcount[q]):
                if c > 0:
                    allw[("d", q, i)] = c
        for e in self.ENGS:
            ws = []
            for k, v in allw.items():
                if k[0] == "e" and k[1] == e:
                    continue
                if self.seen[e].get(k, 0) < v:
                    ws.append((k, v))
                    self.seen[e][k] = v
            if ws:
                self.lists[e].append((ws, None, None, 0))
        if rotate:
            self.epoch += 1
            for e in self.ENGS:
                if e == "sync":
                    continue
                self.esem[e] = self.stack.enter_context(self.nc.semaphore("s_%s_%d" % (e, self.epoch)))
                self.semobj[("e", e, self.epoch)] = self.esem[e]
                self.ecount[e] = 0
                self.seen[e] = {k: v for k, v in self.seen[e].items() if k[0] != "e"}
            self.seen["sync"] = {k: v for k, v in self.seen["sync"].items() if k[0] != "e"}

    def finish(self):
        self.barrier()
        semobj, lists = self.semobj, self.lists

        def run(e, name):
            for (ws, fn, key, inc) in lists[name]:
                for (k, v) in ws:
                    e.wait_ge(semobj[k], v)
                if fn is not None:
                    fn(e).then_inc(semobj[key], inc)

        with self.nc.Block() as block:
            @block.tensor
            def _(e):
                run(e, "tensor")

            @block.vector
            def _(e):
                run(e, "vector")

            @block.scalar
            def _(e):
                run(e, "scalar")

            @block.gpsimd
            def _(e):
                run(e, "gpsimd")

            @block.sync
            def _(e):
                run(e, "sync")


PV_SLOTS = {}


def _pv_layout():
    off = 0
    def add(name, n):
        nonlocal off
        PV_SLOTS[name] = (off, n)
        off += n
    for i in range(4):
        add("ffn_norm%d" % i, 8)
    add("mix_norm0", 8); add("mix_norm1", 8); add("final_norm", 8)
    add("fox_bf", 8); add("conv", 48); add("a_log", 4); add("dt_bias", 4); add("o_gain", 1)
    add("mu", 48); add("w0", 8); add("a0", 8); add("k_k", 8); add("k_a", 8); add("ln_w", 8); add("ln_b", 8); add("r_k", 8)
    for nm in ("cmean", "ident", "m_su", "m_ui", "m_sl", "ones", "blk64", "triu", "bd32", "off64", "off128"):
        add(nm, 128)
    add("sel65", 64)
    return off


NPV = _pv_layout()


def fm(vec):
    return np.ascontiguousarray(np.asarray(vec, np.float32).reshape(8, 128).T)


def pack_params(inp):
    pv = np.zeros((128, NPV), np.float32)
    def put(name, arr):
        o, n = PV_SLOTS[name]
        pv[:, o:o + n] = np.asarray(arr, np.float32).reshape(128, n)
    k = 0
    for l in range(2):
        for h in range(2):
            put("ffn_norm%d" % k, fm(inp["ffn_norm"][l, h])); k += 1
    put("mix_norm0", fm(inp["mix_norm"][0])); put("mix_norm1", fm(inp["mix_norm"][1]))
    put("final_norm", fm(inp["final_norm"]))
    put("fox_bf", np.broadcast_to(inp["hyb_fox_bf"][0][None, :], (128, 8)))
    cw = inp["hyb_conv"][0]
    put("conv", cw.reshape(4, 12, 128).transpose(2, 1, 0).reshape(128, 48))
    put("a_log", np.broadcast_to(inp["hyb_a_log"][0][None, :], (128, 4)))
    put("dt_bias", np.broadcast_to(inp["hyb_dt_bias"][0][None, :], (128, 4)))
    put("o_gain", inp["hyb_o_gain"][0].reshape(128, 1))
    put("mu", inp["rwkv_mu"][0].reshape(6, 8, 128).transpose(2, 0, 1).reshape(128, 48))
    for nm in ("w0", "a0", "k_k", "k_a", "ln_w", "ln_b"):
        put(nm, fm(inp["rwkv_" + nm][0]))
    put("r_k", fm(inp["rwkv_r_k"][0].reshape(-1)))
    idx = np.arange(128)
    put("cmean", np.full((128, 128), 1.0 / 1024))
    put("ident", np.eye(128))
    put("m_su", (idx[:, None] < idx[None, :]).astype(np.float32))
    put("m_ui", (idx[:, None] <= idx[None, :]).astype(np.float32))
    put("m_sl", (idx[:, None] > idx[None, :]).astype(np.float32))
    put("ones", np.ones((128, 128)))
    put("blk64", ((idx[:, None] // 64) == (idx[None, :] // 64)).astype(np.float32))
    put("triu", (idx[:, None] <= idx[None, :]).astype(np.float32))
    bdm = lambda b: ((idx[:, None] // b) == (idx[None, :] // b)).astype(np.float32)
    put("bd32", bdm(32)); put("off64", bdm(64) - bdm(32)); put("off128", bdm(128) - bdm(64))
    s = np.zeros((128, 64), np.float32); s[64, :] = 1.0
    put("sel65", s)
    return pv


WNAMES = [("ffn_w_in0", D, 2 * DFF), ("ffn_w_in1", D, 2 * DFF), ("ffn_w_in2", D, 2 * DFF), ("ffn_w_in3", D, 2 * DFF),
          ("ffn_w_out0", DFF, D), ("ffn_w_out1", DFF, D), ("ffn_w_out2", DFF, D), ("ffn_w_out3", DFF, D),
          ("hyb_w_in", D, HYB_IN), ("hyb_w_out", D, D),
          ("rwkv_w_r", D, D), ("rwkv_w_k", D, D), ("rwkv_w_v", D, D), ("rwkv_w_o", D, D),
          ("rwkv_w1", D, 64), ("rwkv_w2", 64, D), ("rwkv_a1", D, 64), ("rwkv_a2", 64, D),
          ("rwkv_g1", D, 160), ("rwkv_g2", 160, D)]


DEBUG_SCR = False


def build(T, stop_after=None, dbg=(), only=None):
    NCH = T // 128
    tiles = []
    t0 = 0
    while t0 < T:
        n = min(512, T - t0)
        tiles.append((t0, n)); t0 += n
    nc = bass.Bass("TRN2", target_bir_lowering=False)
    hT0 = nc.dram_tensor("hT0", [D, T], F32, kind="ExternalInput").ap()
    pvec = nc.dram_tensor("pvec", [128, NPV], F32, kind="ExternalInput").ap()
    Wf = {nm: nc.dram_tensor(nm, [r, c], F32, kind="ExternalInput").ap() for (nm, r, c) in WNAMES}
    outT = nc.dram_tensor("outT", [D, T], F32, kind="ExternalOutput").ap()
    Wb = {nm: nc.dram_tensor(nm + "_b", [r, c], BF16, kind="Internal").ap() for (nm, r, c) in WNAMES}
    scr = {}
    def dscr(name, shape, dt):
        scr[name] = nc.dram_tensor("scr_" + name, shape, dt, kind=("ExternalOutput" if DEBUG_SCR else "Internal")).ap()
        return scr[name]
    hT = dscr("hT", [D, T], F32)
    fqT = dscr("fqT", [512, T], BF16); fkT = dscr("fkT", [512, T], BF16)
    fV = dscr("fV", [T, 520], BF16); flogf = dscr("flogf", [T, 8], F32)
    gS = {k: dscr("g_" + k, [512, T], BF16) for k in ("r", "k", "a", "b", "v", "z")}
    gS["lw"] = dscr("g_lw", [512, T], F32)
    oT = dscr("oT", [D, T], BF16)
    rS = {k: dscr("r_" + k, [D, T], BF16) for k in ("r", "k", "a", "b", "v", "bonus", "g")}
    rS["lw"] = dscr("r_lw", [D, T], F32)
    zT = dscr("zT", [D, T], BF16)
    dbg_out = {}
    for nm, shape in dbg:
        dbg_out[nm] = nc.dram_tensor("dbg_" + nm, shape, F32, kind="ExternalOutput").ap()

    with contextlib.ExitStack() as gst:
        P = Prog(nc, gst)
        R = {}
        def res(name):
            if name not in R:
                R[name] = Res(name)
            return R[name]

        for (nm, r, c) in WNAMES:
            c0 = 0
            while c0 < c:
                cw = min(2048, c - c0)
                P.dma("gpsimd", Wb[nm][:, c0:c0 + cw], Wf[nm][:, c0:c0 + cw], writes=[res("W_" + nm)])
                c0 += cw

        pv = gst.enter_context(nc.sbuf_tensor("pv", [128, NPV], F32)); r_pv = res("pv")
        P.dma("sync", pv[:], pvec, writes=[r_pv])
        def pvs(name, a=0, n=None):
            o, nn = PV_SLOTS[name]
            n = nn - a if n is None else n
            return pv[:, o + a:o + a + n]
        cb = {}
        for nm in ("cmean", "ident", "ones", "blk64", "bd32", "off64", "off128"):
            cb[nm] = gst.enter_context(nc.sbuf_tensor("cb_" + nm, [128, 128], BF16))
            P.V(lambda e, nm=nm: e.tensor_copy(out=cb[nm][:], in_=pvs(nm)), reads=[r_pv], writes=[res("cb")])
        m_ui_b = gst.enter_context(nc.sbuf_tensor("m_ui_b", [128, 128], BF16))
        P.V(lambda e: e.tensor_copy(out=m_ui_b[:], in_=pvs("m_ui")), reads=[r_pv], writes=[res("cb")])
        cvec = gst.enter_context(nc.sbuf_tensor("cvec", [128, 4], F32))
        P.V(lambda e: e.memset(cvec[:, 0:1], EPS), writes=[res("cb")])
        P.V(lambda e: e.memset(cvec[:, 1:2], 1.0), writes=[res("cb")])
        P.V(lambda e: e.memset(cvec[:, 2:3], GN_EPS), writes=[res("cb")])
        P.V(lambda e: e.memset(cvec[:, 3:4], 0.0), writes=[res("cb")])
        r_cb = res("cb")
        P.barrier()

        hview = lambda ap: ap.rearrange("(kc p) t -> p kc t", p=128)
        wview = lambda ap: ap.rearrange("(kc p) f -> p kc f", p=128)

        class TokCtx:
            pass

        def make_tok_ctx(st, nwi=4, nwo=4):
            c = TokCtx()
            sb = lambda name, shape, dt: st.enter_context(nc.sbuf_tensor(un(name), shape, dt))
            psm = lambda name, shape, dt: st.enter_context(nc.psum_tensor(un(name), shape, dt))
            c.hT = [sb("hT%d" % i, [128, 8, 512], F32) for i in range(2)]
            c.sq = sb("sq", [128, 8, 512], BF16)
            c.rstd = sb("rstd", [128, 512], F32)
            c.hn = sb("hn", [128, 8, 513], BF16)
            c.act = sb("act", [128, 22, 512], BF16)
            c.sg = [sb("sg%d" % i, [128, 512], F32) for i in range(2)]
            c.wi = [sb("wi%d" % i, [128, 8, 512], BF16) for i in range(nwi)]
            c.wo = [sb("wo%d" % i, [128, 4, 512], BF16) for i in range(nwo)]
            c.pt = [psm("pt%d" % i, [128, 512], F32) for i in range(4)]
            c.py = [psm("py%d" % i, [128, 512], F32) for i in range(4)]
            c.cnt = {"wi": 0, "wo": 0, "pt": 0, "sg": 0}
            return c

        def next_pt(c):
            i = c.cnt["pt"] % 4; c.cnt["pt"] += 1
            return i

        def sumsq_bc(c, src_ap_fn, nk, N, r_src, lhs_name="cmean"):
            pi = next_pt(c)
            for k in range(nk):
                P.T(lambda e, pi=pi, k=k: e.matmul(c.pt[pi][:, :N], lhsT=cb[lhs_name][:], rhs=src_ap_fn(k),
                                                    start=(k == 0), stop=(k == nk - 1)),
                    reads=[r_cb] + r_src, writes=[res("pt%d" % pi)])
            return pi

        def rsqrt_from_psum(c, pi, N, dst, r_dst, eps_col=0):
            P.S(lambda e: e.activation(out=dst, in_=c.pt[pi][:, :N], func=AF.Sqrt, bias=cvec[:, eps_col:eps_col + 1]),
                reads=[res("pt%d" % pi), r_cb], writes=[r_dst])
            P.V(lambda e: e.reciprocal(out=dst, in_=dst), reads=[r_dst], writes=[r_dst])

        def rmsnorm(c, h_t, r_h, N, gain_name):
            P.S(lambda e: e.activation(out=c.sq[:, :, :N], in_=h_t[:, :, :N], func=AF.Square), reads=[r_h], writes=[res("sq")])
            pi = sumsq_bc(c, lambda k: c.sq[:, k, :N], 8, N, [res("sq")])
            rsqrt_from_psum(c, pi, N, c.rstd[:, :N], res("rstd"))
            for kc in range(8):
                eng = "vector" if kc % 2 == 0 else "gpsimd"
                P.op("vector", lambda e, kc=kc: e.scalar_tensor_tensor(
                    out=c.hn[:, kc, 1:1 + N], in0=h_t[:, kc, :N], scalar=pvs(gain_name, kc, 1), in1=c.rstd[:, :N],
                    op0=ALU.mult, op1=ALU.mult), reads=[r_h, r_pv, res("rstd")], writes=[res("hn")])

        def lin_fm(c, x_fn, r_x, N, wname, col0, ncols, consume, kchunks=8):
            wv = wview(Wb[wname])
            c0 = 0
            while c0 < ncols:
                cw = min(512, ncols - c0)
                bi = c.cnt["wi"] % len(c.wi); c.cnt["wi"] += 1
                P.dma("sync", c.wi[bi][:, :kchunks, :cw], wv[:, :, col0 + c0:col0 + c0 + cw],
                      reads=[res("W_" + wname)], writes=[res("wi%d" % bi)])
                f0 = 0
                while f0 < cw:
                    m = min(128, cw - f0)
                    pi = next_pt(c)
                    for kc in range(kchunks):
                        P.T(lambda e, pi=pi, bi=bi, kc=kc, f0=f0, m=m: e.matmul(
                            c.pt[pi][:m, :N], lhsT=c.wi[bi][:, kc, f0:f0 + m], rhs=x_fn(kc),
                            start=(kc == 0), stop=(kc == kchunks - 1)),
                            reads=[res("wi%d" % bi)] + r_x, writes=[res("pt%d" % pi)])
                    consume((c0 + f0) // 128, pi, m)
                    f0 += m
                c0 += cw

        def ffn(c, h_t, r_h, N, idx):
            rmsnorm(c, h_t, r_h, N, "ffn_norm%d" % idx)
            wn_in, wn_out = "ffn_w_in%d" % idx, "ffn_w_out%d" % idx
            wv = wview(Wb[wn_in])
            for pc in range(6):
                cw = 512 if pc < 5 else 256
                bufs = []
                for which in range(2):
                    bi = c.cnt["wi"] % len(c.wi); c.cnt["wi"] += 1
                    cc0 = which * DFF + pc * 512
                    P.dma("sync", c.wi[bi][:, :, :cw], wv[:, :, cc0:cc0 + cw], reads=[res("W_" + wn_in)], writes=[res("wi%d" % bi)])
                    bufs.append(bi)
                for fl in range(cw // 128):
                    fc = pc * 4 + fl
                    pis = []
                    for which in range(2):
                        pi = next_pt(c); pis.append(pi)
                        bi = bufs[which]
                        for kc in range(8):
                            P.T(lambda e, pi=pi, bi=bi, kc=kc, fl=fl: e.matmul(
                                c.pt[pi][:, :N], lhsT=c.wi[bi][:, kc, fl * 128:(fl + 1) * 128], rhs=c.hn[:, kc, 1:1 + N],
                                start=(kc == 0), stop=(kc == 7)), reads=[res("wi%d" % bi), res("hn")], writes=[res("pt%d" % pi)])
                    si = c.cnt["sg"] % 2; c.cnt["sg"] += 1
                    P.S(lambda e, si=si, pi=pis[0]: e.activation(out=c.sg[si][:, :N], in_=c.pt[pi][:, :N], func=AF.Silu),
                        reads=[res("pt%d" % pis[0])], writes=[res("sg%d" % si)])
                    P.V(lambda e, si=si, pi=pis[1], fc=fc: e.tensor_tensor(
                        out=c.act[:, fc, :N], in0=c.sg[si][:, :N], in1=c.pt[pi][:, :N], op=ALU.mult),
                        reads=[res("sg%d" % si), res("pt%d" % pis[1])], writes=[res("act")])
            wvo = Wb[wn_out].rearrange("(fc p) d -> p fc d", p=128)
            for half in range(2):
                for g in range(6):
                    nf = 4 if g < 5 else 2
                    bi = c.cnt["wo"] % len(c.wo); c.cnt["wo"] += 1
                    P.dma("sync", c.wo[bi][:, :nf, :], wvo[:, g * 4:g * 4 + nf, half * 512:(half + 1) * 512],
                          reads=[res("W_" + wn_out)], writes=[res("wo%d" % bi)])
                    for fl in range(nf):
                        fc = g * 4 + fl
                        for dq in range(4):
                            P.T(lambda e, bi=bi, fl=fl, fc=fc, dq=dq: e.matmul(
                                c.py[dq][:, :N], lhsT=c.wo[bi][:, fl, dq * 128:(dq + 1) * 128], rhs=c.act[:, fc, :N],
                                start=(fc == 0), stop=(fc == 21)), reads=[res("wo%d" % bi), res("act")], writes=[res("py%d" % dq)])
                for dq in range(4):
                    kc = half * 4 + dq
                    P.V(lambda e, dq=dq, kc=kc: e.scalar_tensor_tensor(
                        out=h_t[:, kc, :N], in0=c.py[dq][:, :N], scalar=0.5, in1=h_t[:, kc, :N],
                        op0=ALU.mult, op1=ALU.add), reads=[res("py%d" % dq), r_h], writes=[r_h])

        def proj_residual(c, h_t, r_h, N, x_t, r_x, wname):
            def consume(fc, pi, m):
                P.V(lambda e: e.tensor_tensor(out=h_t[:, fc, :N], in0=c.pt[pi][:, :N], in1=h_t[:, fc, :N], op=ALU.add),
                    reads=[res("pt%d" % pi), r_h], writes=[r_h])
            lin_fm(c, lambda kc: x_t[:, kc, :N], [r_x], N, wname, 0, D, consume)

        with contextlib.ExitStack() as st:
          if only is None or "A" in only:
            sb = lambda name, shape, dt: st.enter_context(nc.sbuf_tensor(un(name), shape, dt))
            c = make_tok_ctx(st)
            wtm = sb("wtm", [128, 8, 520], BF16)
            wrep = sb("wrep", [128, 8, 8, 128], BF16)
            wsm = sb("wsm", [128, 8, 8], F32)
            P.dma("sync", wtm[:], wview(Wb["hyb_w_in"])[:, :, 1024:1544], reads=[res("W_hyb_w_in")], writes=[res("wtm")])
            P.dma("sync", wsm[:], wview(Wf["hyb_w_in"])[:, :, 3080:3088], writes=[res("wsm")])
            for kc in range(8):
                for j in range(8):
                    eng = "vector" if (kc + j) % 2 == 0 else "gpsimd"
                    P.op(eng, lambda e, kc=kc, j=j: e.tensor_scalar(out=wrep[:, kc, j, :], in0=cb["ones"][:], scalar1=wsm[:, kc, j:j + 1],
                                                                     scalar2=None, op0=ALU.mult), reads=[res("wsm"), r_cb], writes=[res("wrep")])
            negA = sb("negA", [128, 4], F32)
            P.S(lambda e: e.activation(out=negA[:], in_=pvs("a_log"), func=AF.Exp), reads=[r_pv], writes=[res("negA")])
            P.V(lambda e: e.tensor_scalar(out=negA[:], in0=negA[:], scalar1=-1.0, scalar2=None, op0=ALU.mult), reads=[res("negA")], writes=[res("negA")])
            xgs = [sb("xg%d" % i, [128, 515], F32) for i in range(2)]
            halo = sb("halo", [128, 12, 3], F32)
            P.V(lambda e: e.memset(halo[:], 0.0), writes=[res("halo")])
            xgc = [0]
            cacc = sb("cacc", [128, 512], F32)
            qk = [sb("qk%d" % i, [128, 512], F32) for i in range(2)]
            kn = sb("kn", [128, 512], F32)
            gb_ = sb("gbeta", [128, 4, 512], F32)
            gg_ = sb("gg", [128, 4, 512], F32)
            glw = sb("glw", [128, 4, 512], F32)
            ob = [sb("ob%d" % i, [128, 512], BF16) for i in range(4)]
            vtm = [sb("vtm%d" % i, [128, 8, 65], BF16) for i in range(2)]
            for i in range(2):
                P.V(lambda e, i=i: e.memset(vtm[i][:], 1.0), writes=[res("vtm%d" % i)])
            lft = [sb("lft%d" % i, [128, 8], F32) for i in range(2)]
            ocnt = [0]
            print("phase A sbuf remaining", nc.sbuf_bytes_remaining)

            def out_bf(dst_ap, make, reads):
                i = ocnt[0] % 4; ocnt[0] += 1
                make(ob[i], res("ob%d" % i))
                P.dma("sync", dst_ap, ob[i][:, :dst_ap.shape[-1]], reads=[res("ob%d" % i)], writes=[res("scrA")])

            for ti, (t0, N) in enumerate(tiles):
                h_t, r_h = c.hT[ti % 2], res("hT%d" % (ti % 2))
                P.dma("sync", h_t[:, :, :N], hview(hT0)[:, :, t0:t0 + N], writes=[r_h])
                ffn(c, h_t, r_h, N, 0)
                P.dma("sync", hview(hT)[:, :, t0:t0 + N], h_t[:, :, :N], reads=[r_h], writes=[res("hT_dram")])
                rmsnorm(c, h_t, r_h, N, "mix_norm0")
                xfn = lambda kc: c.hn[:, kc, 1:1 + N]
                rx = [res("hn")]
                def cons_q(fc, pi, m):
                    out_bf(fqT[fc * 128:(fc + 1) * 128, t0:t0 + N],
                           lambda o, ro: P.S(lambda e: e.activation(out=o[:, :N], in_=c.pt[pi][:, :N], func=AF.Copy, scale=0.125),
                                             reads=[res("pt%d" % pi)], writes=[ro]), None)
                lin_fm(c, xfn, rx, N, "hyb_w_in", 0, 512, cons_q)
                def cons_k(fc, pi, m):
                    out_bf(fkT[fc * 128:(fc + 1) * 128, t0:t0 + N],
                           lambda o, ro: P.V(lambda e: e.tensor_copy(out=o[:, :N], in_=c.pt[pi][:, :N]),
                                             reads=[res("pt%d" % pi)], writes=[ro]), None)
                lin_fm(c, xfn, rx, N, "hyb_w_in", 512, 512, cons_k)
                for tb in range(N // 128):
                    vi = (ti * 4 + tb) % 2
                    pi = next_pt(c)
                    for kc in range(8):
                        P.T(lambda e, pi=pi, kc=kc, tb=tb: e.matmul(c.pt[pi][:, :512], lhsT=c.hn[:, kc, 1 + tb * 128:1 + (tb + 1) * 128],
                                                                  rhs=wtm[:, kc, 0:512], start=(kc == 0), stop=(kc == 7)),
                            reads=[res("hn"), res("wtm")], writes=[res("pt%d" % pi)])
                    P.S(lambda e, pi=pi, vi=vi: e.activation(out=vtm[vi][:, :, 0:64], in_=c.pt[pi][:, :512].rearrange("p (h d) -> p h d", h=8), func=AF.Copy),
                        reads=[res("pt%d" % pi)], writes=[res("vtm%d" % vi)])
                    P.dma("sync", fV[t0 + tb * 128:t0 + (tb + 1) * 128, :], vtm[vi][:].rearrange("p h d -> p (h d)"),
                          reads=[res("vtm%d" % vi)], writes=[res("scrA")])
                    pi2 = next_pt(c)
                    for kc in range(8):
                        P.T(lambda e, pi2=pi2, kc=kc, tb=tb: e.matmul(c.pt[pi2][:, :8], lhsT=c.hn[:, kc, 1 + tb * 128:1 + (tb + 1) * 128],
                                                                    rhs=wtm[:, kc, 512:520], start=(kc == 0), stop=(kc == 7)),
                            reads=[res("hn"), res("wtm")], writes=[res("pt%d" % pi2)])
                    P.V(lambda e, pi2=pi2, vi=vi: e.tensor_tensor(out=lft[vi][:], in0=c.pt[pi2][:, :8], in1=pvs("fox_bf"), op=ALU.add),
                        reads=[res("pt%d" % pi2), r_pv], writes=[res("lft%d" % vi)])
                    P.S(lambda e, vi=vi: e.activation(out=lft[vi][:], in_=lft[vi][:], func=AF.Sigmoid), reads=[res("lft%d" % vi)], writes=[res("lft%d" % vi)])
                    P.S(lambda e, vi=vi: e.activation(out=lft[vi][:], in_=lft[vi][:], func=AF.Ln), reads=[res("lft%d" % vi)], writes=[res("lft%d" % vi)])
                    P.dma("sync", flogf[t0 + tb * 128:t0 + (tb + 1) * 128, :], lft[vi][:], reads=[res("lft%d" % vi)], writes=[res("scrA")])
                for j in range(8):
                    pi = next_pt(c)
                    for kc in range(8):
                        P.T(lambda e, pi=pi, kc=kc, j=j: e.matmul(c.pt[pi][:, :N], lhsT=wrep[:, kc, j, :], rhs=c.hn[:, kc, 1:1 + N],
                                                               start=(kc == 0), stop=(kc == 7)),
                            reads=[res("wrep"), res("hn")], writes=[res("pt%d" % pi)])
                    if j < 4:
                        P.S(lambda e, pi=pi, j=j: e.activation(out=glw[:, j, :N], in_=c.pt[pi][:, :N], func=AF.Exp, bias=pvs("dt_bias", j, 1)),
                            reads=[res("pt%d" % pi), r_pv], writes=[res("glw")])
                        P.S(lambda e, j=j: e.activation(out=glw[:, j, :N], in_=glw[:, j, :N], func=AF.Ln, bias=cvec[:, 1:2]),
                            reads=[res("glw"), r_cb], writes=[res("glw")])
                        P.V(lambda e, j=j: e.tensor_scalar(out=glw[:, j, :N], in0=glw[:, j, :N], scalar1=negA[:, j:j + 1], scalar2=None, op0=ALU.mult),
                            reads=[res("glw"), res("negA")], writes=[res("glw")])
                        P.S(lambda e, j=j: e.activation(out=gg_[:, j, :N], in_=glw[:, j, :N], func=AF.Exp), reads=[res("glw")], writes=[res("gg")])
                        P.dma("sync", gS["lw"][j * 128:(j + 1) * 128, t0:t0 + N], glw[:, j, :N], reads=[res("glw")], writes=[res("scrA")])
                    else:
                        P.S(lambda e, pi=pi, j=j: e.activation(out=gb_[:, j - 4, :N], in_=c.pt[pi][:, :N], func=AF.Sigmoid),
                            reads=[res("pt%d" % pi)], writes=[res("gbeta")])
                def cons_g(fc, pi, m):
                    xi = xgc[0] % 2; xgc[0] += 1
                    xg, rxg = xgs[xi], res("xg%d" % xi)
                    P.G(lambda e: e.tensor_copy(out=xg[:, 0:3], in_=halo[:, fc, :]), reads=[res("halo")], writes=[rxg])
                    P.S(lambda e: e.activation(out=xg[:, 3:3 + N], in_=c.pt[pi][:, :N], func=AF.Copy), reads=[res("pt%d" % pi)], writes=[rxg])
                    o_, _ = PV_SLOTS["conv"]
                    cwp = lambda j: pv[:, o_ + fc * 4 + j:o_ + fc * 4 + j + 1]
                    P.V(lambda e: e.tensor_scalar(out=cacc[:, :N], in0=xg[:, 0:N], scalar1=cwp(0), scalar2=None, op0=ALU.mult),
                        reads=[rxg, r_pv], writes=[res("cacc")])
                    for j in range(1, 4):
                        P.V(lambda e, j=j: e.scalar_tensor_tensor(out=cacc[:, :N], in0=xg[:, j:j + N], scalar=cwp(j), in1=cacc[:, :N],
                                                               op0=ALU.mult, op1=ALU.add), reads=[rxg, r_pv, res("cacc")], writes=[res("cacc")])
                    P.G(lambda e: e.tensor_copy(out=halo[:, fc, :], in_=xg[:, N:N + 3]), reads=[rxg], writes=[res("halo")])
                    kind, hh = fc // 4, fc % 4
                    if kind == 2:
                        out_bf(gS["v"][hh * 128:(hh + 1) * 128, t0:t0 + N],
                               lambda o, ro: P.S(lambda e: e.activation(out=o[:, :N], in_=cacc[:, :N], func=AF.Silu), reads=[res("cacc")], writes=[ro]), None)
                        return
                    qq = qk[kind]; rq = res("qk%d" % kind)
                    P.S(lambda e: e.activation(out=qq[:, :N], in_=cacc[:, :N], func=AF.Silu), reads=[res("cacc")], writes=[rq])
                    P.G(lambda e: e.tensor_tensor(out=c.sq[:, 0, :N], in0=qq[:, :N], in1=qq[:, :N], op=ALU.mult), reads=[rq], writes=[res("sq")])
                    pj = sumsq_bc(c, lambda k: c.sq[:, 0, :N], 1, N, [res("sq")], lhs_name="ones")
                    rsqrt_from_psum(c, pj, N, c.rstd[:, :N], res("rstd"))
                    if kind == 0:
                        out_bf(gS["r"][hh * 128:(hh + 1) * 128, t0:t0 + N],
                               lambda o, ro: P.V(lambda e: e.scalar_tensor_tensor(out=o[:, :N], in0=qq[:, :N], scalar=128.0 ** -0.5, in1=c.rstd[:, :N],
                                                                                 op0=ALU.mult, op1=ALU.mult), reads=[rq, res("rstd")], writes=[ro]), None)
                    else:
                        P.V(lambda e: e.tensor_tensor(out=kn[:, :N], in0=qq[:, :N], in1=c.rstd[:, :N], op=ALU.mult), reads=[rq, res("rstd")], writes=[res("kn")])
                        out_bf(gS["a"][hh * 128:(hh + 1) * 128, t0:t0 + N],
                               lambda o, ro: P.G(lambda e: e.tensor_copy(out=o[:, :N], in_=kn[:, :N]), reads=[res("kn")], writes=[ro]), None)
                        P.V(lambda e: e.tensor_tensor(out=kn[:, :N], in0=kn[:, :N], in1=gb_[:, hh, :N], op=ALU.mult), reads=[res("kn"), res("gbeta")], writes=[res("kn")])
                        out_bf(gS["k"][hh * 128:(hh + 1) * 128, t0:t0 + N],
                               lambda o, ro: P.G(lambda e: e.tensor_copy(out=o[:, :N], in_=kn[:, :N]), reads=[res("kn")], writes=[ro]), None)
                        out_bf(gS["b"][hh * 128:(hh + 1) * 128, t0:t0 + N],
                               lambda o, ro: P.V(lambda e: e.scalar_tensor_tensor(out=o[:, :N], in0=kn[:, :N], scalar=-1.0, in1=gg_[:, hh, :N],
                                                                                 op0=ALU.mult, op1=ALU.mult), reads=[res("kn"), res("gg")], writes=[ro]), None)
                lin_fm(c, xfn, rx, N, "hyb_w_in", 1544, 1536, cons_g)
                def cons_z(fc, pi, m):
                    out_bf(gS["z"][fc * 128:(fc + 1) * 128, t0:t0 + N],
                           lambda o, ro: P.S(lambda e: e.activation(out=o[:, :N], in_=c.pt[pi][:, :N], func=AF.Silu), reads=[res("pt%d" % pi)], writes=[ro]), None)
                lin_fm(c, xfn, rx, N, "hyb_w_in", 3088, 512, cons_z)
        P.barrier()
        if stop_after == "A":
            P.finish(); return nc

        with contextlib.ExitStack() as st:
          if only is None or "B" in only:
            sb = lambda name, shape, dt: st.enter_context(nc.sbuf_tensor(un(name), shape, dt))
            psm = lambda name, shape, dt: st.enter_context(nc.psum_tensor(un(name), shape, dt))
            Vall = sb("Vall", [128, NCH, 584], BF16)
            P.V(lambda e: e.memset(Vall[:, :, 520:584], 0.0), writes=[res("Vall")])
            P.dma("sync", Vall[:, :, 0:520], fV.rearrange("(c p) f -> p c f", p=128), reads=[res("scrA")], writes=[res("Vall")])
            lf = sb("lf", [128, NCH, 8], F32)
            P.dma("sync", lf[:], flogf.rearrange("(c p) h -> p c h", p=128), reads=[res("scrA")], writes=[res("lf")])
            negc = sb("negc", [128, 8, NCH], F32)
            pe = sb("pe", [128, 8, NCH + 1], F32)
            lff = lf[:].rearrange("p c h -> p (c h)")
            ps_s = [psm("ps_s%d" % i, [128, 512], F32) for i in range(2)]
            ps_o = [psm("ps_o%d" % i, [128, 512], F32) for i in range(2)]
            ps_b = psm("ps_b", [128, 512], F32)
            ncol = NCH * 8
            ctmp = sb("ctmp", [128, NCH, 8], F32)
            ctot = sb("ctot", [128, NCH, 8], F32)
            cf = ctmp[:].rearrange("p c h -> p (c h)")
            tf = ctot[:].rearrange("p c h -> p (c h)")
            c0 = 0
            while c0 < ncol:
                cw = min(512, ncol - c0)
                P.T(lambda e, c0=c0, cw=cw: e.matmul(ps_s[0][:, :cw], lhsT=pvs("triu"), rhs=lff[:, c0:c0 + cw], start=True, stop=True),
                    reads=[r_pv, res("lf")], writes=[res("ps_s0")])
                P.V(lambda e, c0=c0, cw=cw: e.tensor_copy(out=cf[:, c0:c0 + cw], in_=ps_s[0][:, :cw]), reads=[res("ps_s0")], writes=[res("ctmp")])
                P.T(lambda e, c0=c0, cw=cw: e.matmul(ps_s[1][:, :cw], lhsT=pvs("ones"), rhs=lff[:, c0:c0 + cw], start=True, stop=True),
                    reads=[r_pv, res("lf")], writes=[res("ps_s1")])
                P.V(lambda e, c0=c0, cw=cw: e.tensor_copy(out=tf[:, c0:c0 + cw], in_=ps_s[1][:, :cw]), reads=[res("ps_s1")], writes=[res("ctot")])
                c0 += cw
            P.V(lambda e: e.memset(pe[:, :, 0:1], 0.0), writes=[res("pe")])
            for h in range(8):
                P.V(lambda e, h=h: e.tensor_tensor_scan(out=pe[:, h, 1:NCH + 1], data0=pvs("ones")[:, 0:NCH], data1=ctot[:, :, h],
                                                         initial=0.0, op0=ALU.mult, op1=ALU.add),
                    reads=[r_pv, res("ctot")], writes=[res("pe")])
                P.V(lambda e, h=h: e.tensor_tensor(out=negc[:, h, :], in0=ctmp[:, :, h], in1=pe[:, h, 0:NCH], op=ALU.add),
                    reads=[res("ctmp"), res("pe")], writes=[res("negc")])
                P.V(lambda e, h=h: e.tensor_scalar(out=negc[:, h, :], in0=negc[:, h, :], scalar1=-1.0, scalar2=None, op0=ALU.mult),
                    reads=[res("negc")], writes=[res("negc")])
            kTs = [sb("kTs%d" % i, [64, T], BF16) for i in range(2)]
            qTs = [sb("qTs%d" % i, [64, T], BF16) for i in range(2)]
            biasg = [sb("biasg%d" % i, [128, NCH], F32) for i in range(2)]
            pT = [sb("pT%d" % i, [128, 512], BF16) for i in range(3)]
            osb = [sb("osb%d" % i, [128, 512], F32) for i in range(2)]
            for i in range(2):
                P.V(lambda e, i=i: e.memset(osb[i][:], 0.0), writes=[res("osb%d" % i)])
            rinv = sb("rinv", [64, 512], F32)
            oout = [sb("oout%d" % i, [64, 512], BF16) for i in range(2)]
            cntB = {"s": 0, "p": 0, "g": 0}
            for h in range(8):
                hb = h % 2
                P.dma("sync", kTs[hb][:], fkT[h * 64:(h + 1) * 64, :], reads=[res("scrA")], writes=[res("kTs%d" % hb)])
                P.dma("sync", qTs[hb][:], fqT[h * 64:(h + 1) * 64, :], reads=[res("scrA")], writes=[res("qTs%d" % hb)])
                for (t0, N) in tiles:
                    gi = cntB["g"] % 2; cntB["g"] += 1
                    i0 = t0 // 128; nb = N // 128
                    anc = min(i0 + 2, NCH)
                    J = i0 + nb
                    P.V(lambda e, gi=gi, anc=anc, J=J, h=h: e.tensor_scalar(out=biasg[gi][:, :J], in0=negc[:, h, :J], scalar1=pe[:, h, anc:anc + 1],
                                                                             scalar2=None, op0=ALU.add),
                        reads=[res("negc"), res("pe")], writes=[res("biasg%d" % gi)])
                    pend = None
                    for j in range(J + 1):
                        if j < J:
                            cs = 0 if j < i0 else (j - i0) * 128
                            ncols = N - cs
                            si = cntB["s"] % 2; cntB["s"] += 1
                            pi = cntB["p"] % 3; cntB["p"] += 1
                            P.T(lambda e: e.matmul(ps_s[si][:, :ncols], lhsT=kTs[hb][:, j * 128:(j + 1) * 128], rhs=qTs[hb][:, t0 + cs:t0 + cs + ncols], start=True, stop=True),
                                reads=[res("kTs%d" % hb), res("qTs%d" % hb)], writes=[res("ps_s%d" % si)])
                            P.S(lambda e: e.activation(out=pT[pi][:, :ncols], in_=ps_s[si][:, :ncols], func=AF.Exp, bias=biasg[gi][:, j:j + 1]),
                                reads=[res("ps_s%d" % si), res("biasg%d" % gi)], writes=[res("pT%d" % pi)])
                            if j >= i0:
                                P.G(lambda e: e.tensor_tensor(out=pT[pi][:, 0:128], in0=pT[pi][:, 0:128], in1=m_ui_b[:], op=ALU.mult),
                                    reads=[res("pT%d" % pi), r_cb], writes=[res("pT%d" % pi)])
                        if pend is not None:
                            (pj, pcs, pncols, ppi) = pend
                            P.T(lambda e: e.matmul(ps_o[gi][:, pcs:pcs + pncols], lhsT=Vall[:, pj, h * 65:h * 65 + 128], rhs=pT[ppi][:, :pncols],
                                                   start=(pj == 0), stop=(pj == J - 1)), reads=[res("Vall"), res("pT%d" % ppi)], writes=[res("ps_o%d" % gi)])
                        pend = (j, cs, ncols, pi) if j < J else None
                    P.S(lambda e, gi=gi, N=N: e.activation(out=osb[gi][0:65, :N], in_=ps_o[gi][0:65, :N], func=AF.Copy),
                        reads=[res("ps_o%d" % gi)], writes=[res("osb%d" % gi)])
                    o_, _ = PV_SLOTS["sel65"]
                    P.T(lambda e, gi=gi, N=N: e.matmul(ps_b[:64, :N], lhsT=pv[:, o_:o_ + 64], rhs=osb[gi][:, :N], start=True, stop=True),
                        reads=[r_pv, res("osb%d" % gi)], writes=[res("ps_b")])
                    P.V(lambda e, N=N: e.reciprocal(out=rinv[:, :N], in_=ps_b[:64, :N]), reads=[res("ps_b")], writes=[res("rinv")])
                    P.V(lambda e, gi=gi, N=N: e.tensor_tensor(out=oout[gi][:, :N], in0=osb[gi][0:64, :N], in1=rinv[:, :N], op=ALU.mult),
                        reads=[res("osb%d" % gi), res("rinv")], writes=[res("oout%d" % gi)])
                    P.dma("sync", oT[h * 64:(h + 1) * 64, t0:t0 + N], oout[gi][:, :N], reads=[res("oout%d" % gi)], writes=[res("oT_dram")])
        P.barrier()
        if stop_after == "B":
            P.finish(); return nc

        def dplr_phase(name, units, S_, post_setup, post, scalar_decay=False):
            with contextlib.ExitStack() as st:
                sb = lambda nm, shape, dt: st.enter_context(nc.sbuf_tensor(un(nm), shape, dt))
                psm = lambda nm, shape, dt: st.enter_context(nc.psum_tensor(un(nm), shape, dt))
                hd = units[0][0][2]
                ngs = len(set(cx[0] for cx in units[0]))
                SC = 4
                inb = {}
                for gs in range(ngs):
                    for k in ("r", "k", "a", "b", "v"):
                        inb[(gs, k)] = sb("in_%s%d" % (k, gs), [128, SC * 128], BF16)
                    inb[(gs, "lw")] = sb("in_lw%d" % gs, [128, SC * 128], F32)
                pobj = post_setup(st, ngs, SC)
                Lc = [sb("Lc%d" % g, [128, 129], F32) for g in range(ngs)]
                Lx = [sb("nLm%d" % g, [128, 1], F32) for g in range(ngs)]
                e1 = [sb("e1_%d" % g, [128, 129], F32) for g in range(ngs)]
                e2 = [sb("e2_%d" % g, [128, 129], F32) for g in range(ngs)]
                e5 = [sb("e5_%d" % g, [128, 128], F32) for g in range(ngs)]
                e6 = [sb("e6_%d" % g, [128, 128], F32) for g in range(ngs)]
                if scalar_decay:
                    Dm = {(g, k): sb("D%s_%d" % (k, g), [128, 128], F32) for g in range(ngs) for k in ("m_su", "m_sl", "m_ui")}
                    lcc = [sb("lcc%d" % g, [128, 2], F32) for g in range(ngs)]
                    dtmp = [sb("dtmp%d" % g, [128, 128], F32) for g in range(ngs)]
                opn = ("rh", "rt", "ah", "at", "bh", "kh", "btT", "ktT")
                ops_ = {(g, k): sb("%s%d" % (k, g), [128, 128], BF16) for g in range(ngs) for k in opn}
                tmj = {(g, k): sb("tm_%s%d" % (k, g), [128, 128], BF16) for g in range(ngs) for k in ("bt", "kt", "vt")}
                pstr = psm("pstr", [128, 4, 128], BF16)
                yT = [sb("yT%d" % g, [128, 128], F32) for g in range(ngs)]
                Sf = [sb("Sf%d" % g, [128, 128], F32) for g in range(ngs)]
                Sb = [sb("Sb%d" % g, [128, 128], BF16) for g in range(ngs)]
                hx = []
                for h in range(2):
                    o = TokCtx()
                    o.psx = [psm("psx%d_%d" % (h, i), [128, 512], F32) for i in range(2)]
                    o.psy = psm("psy%d" % h, [128, 512], F32)
                    o.xc = 0
                    for k in ("M0", "MT0", "Mb0", "Mb1", "MTb0", "MTb1", "P0", "P1", "Q0", "Q1", "Noff", "NoffT", "Z", "Z2", "AakT", "ArbT", "ArkT", "RHS", "U"):
                        setattr(o, k, sb("%s_%d" % (k, h), [128, 128], BF16))
                    hx.append(o)
                rn = lambda s: res(name + "_" + s)
                trc = [0]

                for unit in units:
                    gslots = []
                    for cx in unit:
                        if cx[0] not in gslots:
                            gslots.append(cx[0])
                    for g in range(ngs):
                        P.V(lambda e, g=g: e.memset(Sf[g][:], 0.0), writes=[rn("Sf%d" % g)])
                        P.V(lambda e, g=g: e.memset(Sb[g][:], 0.0), writes=[rn("Sb%d" % g)])
                        P.V(lambda e, g=g: e.memset(Lc[g][:, 0:1], 0.0), writes=[rn("Lc%d" % g)])
                    for c in range(NCH):
                        sc, cl = c // SC, c % SC
                        if cl == 0:
                            nsc = min(SC, NCH - c) * 128
                            for g, grow in enumerate(gslots):
                                for k in ("r", "k", "a", "b", "v", "lw"):
                                    P.dma("sync", inb[(g, k)][:, :nsc], S_[k][grow:grow + 128, c * 128:c * 128 + nsc],
                                          reads=[res("scr" + name)], writes=[rn("in_%s%d" % (k, g))])
                                post("load", g, grow, c, nsc, pobj)
                        cols = slice(cl * 128, (cl + 1) * 128)
                        for g in range(ngs):
                            rl = [rn("Lc%d" % g)]
                            P.V(lambda e, g=g: e.tensor_tensor_scan(out=Lc[g][:, 1:129], data0=pvs("ones"), data1=inb[(g, "lw")][:, cols],
                                                                     initial=0.0, op0=ALU.mult, op1=ALU.add),
                                reads=[r_pv, rn("in_lw%d" % g)], writes=rl)
                            P.S(lambda e, g=g: e.activation(out=e2[g][:], in_=Lc[g][:], func=AF.Exp), reads=rl, writes=[rn("e2_%d" % g)])
                            P.S(lambda e, g=g: e.activation(out=e6[g][:], in_=Lc[g][:, 1:129], func=AF.Exp, bias=Lc[g][:, 128:129], scale=-1.0),
                                reads=rl, writes=[rn("e6_%d" % g)])
                            if not scalar_decay:
                                P.V(lambda e, g=g: e.tensor_scalar(out=Lx[g][:], in0=Lc[g][:, 64:65], scalar1=-1.0, scalar2=None, op0=ALU.mult),
                                    reads=rl, writes=[rn("nLm%d" % g)])
                                P.S(lambda e, g=g: e.activation(out=e1[g][:], in_=Lc[g][:], func=AF.Exp, bias=Lx[g][:, 0:1]),
                                    reads=rl + [rn("nLm%d" % g)], writes=[rn("e1_%d" % g)])
                                P.S(lambda e, g=g: e.activation(out=e5[g][:], in_=Lc[g][:, 1:129], func=AF.Exp, bias=Lc[g][:, 64:65], scale=-1.0),
                                    reads=rl, writes=[rn("e5_%d" % g)])
                                specs = [("rh", "r", e1[g][:, 1:129], "e1_"), ("rt", "r", e2[g][:, 1:129], "e2_"),
                                         ("ah", "a", e1[g][:, 0:128], "e1_"), ("at", "a", e2[g][:, 0:128], "e2_"),
                                         ("bh", "b", e5[g][:], "e5_"), ("kh", "k", e5[g][:], "e5_"),
                                         ("btT", "b", e6[g][:], "e6_"), ("ktT", "k", e6[g][:], "e6_")]
                            else:
                                rd = [rn("dtmp%d" % g)]
                                P.V(lambda e, g=g: e.tensor_tensor(out=dtmp[g][:], in0=Lc[g][:, 1:129], in1=pvs("ident"), op=ALU.mult), reads=rl + [r_pv], writes=rd)
                                P.V(lambda e, g=g: e.reduce_sum(out=lcc[g][:, 0:1], in_=dtmp[g][:], axis=mybir.AxisListType.X), reads=rd, writes=[rn("lcc%d" % g)])
                                P.V(lambda e, g=g: e.tensor_tensor(out=dtmp[g][:], in0=Lc[g][:, 0:128], in1=pvs("ident"), op=ALU.mult), reads=rl + [r_pv], writes=rd)
                                P.V(lambda e, g=g: e.reduce_sum(out=lcc[g][:, 1:2], in_=dtmp[g][:], axis=mybir.AxisListType.X), reads=rd, writes=[rn("lcc%d" % g)])
                                rlc = [rn("lcc%d" % g)]
                                for (mk, src, colj, neg) in (("m_ui", Lc[g][:, 1:129], 0, False), ("m_su", Lc[g][:, 0:128], 0, False), ("m_sl", Lc[g][:, 1:129], 1, True)):
                                    D_ = Dm[(g, mk)]; rD = rn("D%s_%d" % (mk, g))
                                    if not neg:
                                        P.V(lambda e, g=g, src=src, colj=colj: e.tensor_scalar(out=dtmp[g][:], in0=src, scalar1=lcc[g][:, colj:colj + 1], scalar2=0.0,
                                                                                                op0=ALU.subtract, op1=ALU.min), reads=rl + rlc, writes=rd)
                                    else:
                                        P.V(lambda e, g=g, src=src, colj=colj: e.tensor_scalar(out=dtmp[g][:], in0=src, scalar1=-1.0, scalar2=lcc[g][:, colj:colj + 1],
                                                                                                op0=ALU.mult, op1=ALU.add), reads=rl + rlc, writes=rd)
                                        P.V(lambda e, g=g: e.tensor_scalar(out=dtmp[g][:], in0=dtmp[g][:], scalar1=0.0, scalar2=None, op0=ALU.min), reads=rd, writes=rd)
                                    P.S(lambda e, g=g, D_=D_: e.activation(out=D_[:], in_=dtmp[g][:], func=AF.Exp), reads=rd, writes=[rD])
                                    P.G(lambda e, D_=D_, mk=mk: e.tensor_tensor(out=D_[:], in0=D_[:], in1=pvs(mk), op=ALU.mult), reads=[rD, r_pv], writes=[rD])
                                specs = [("rt", "r", e2[g][:, 1:129], "e2_"), ("at", "a", e2[g][:, 0:128], "e2_"),
                                         ("btT", "b", e6[g][:], "e6_"), ("ktT", "k", e6[g][:], "e6_")]
                            for qi, (on, ik, eap, en) in enumerate(specs):
                                eng = "vector" if qi % 2 == 0 else "gpsimd"
                                P.op(eng, lambda e, g=g, on=on, ik=ik, eap=eap: e.tensor_tensor(out=ops_[(g, on)][:], in0=inb[(g, ik)][:, cols], in1=eap, op=ALU.mult),
                                     reads=[rn("in_%s%d" % (ik, g)), rn(en + "%d" % g)], writes=[rn("%s%d" % (on, g))])
                            for (tn, src, rsrc) in (("bt", ops_[(g, "btT")][:], rn("btT%d" % g)), ("kt", ops_[(g, "ktT")][:], rn("ktT%d" % g)),
                                                    ("vt", inb[(g, "v")][:, cols], rn("in_v%d" % g))):
                                ti_ = trc[0] % 4; trc[0] += 1
                                P.T(lambda e, ti_=ti_, src=src: e.transpose(out=pstr[:, ti_, :], in_=src, identity=cb["ident"][:]),
                                    reads=[rsrc, r_cb], writes=[rn("pstr")])
                                P.S(lambda e, ti_=ti_, g=g, tn=tn: e.activation(out=tmj[(g, tn)][:], in_=pstr[:, ti_, :], func=AF.Copy),
                                    reads=[rn("pstr")], writes=[rn("tm_%s%d" % (tn, g))])
                        H = []
                        for hi, cx in enumerate(unit):
                            g = gslots.index(cx[0])
                            H.append((hx[hi], g, slice(cx[1], cx[1] + hd), hi))
                        def xslot(o, hi):
                            i = o.xc % 2; o.xc += 1
                            return o.psx[i][:, 0:128], rn("psx%d_%d" % (hi, i))
                        def opr(g, k, ps_):
                            if scalar_decay and k in ("rh", "ah", "bh", "kh"):
                                return inb[(g, k[0])][ps_, cols], rn("in_%s%d" % (k[0], g))
                            return ops_[(g, k)][ps_, :], rn("%s%d" % (k, g))
                        for (o, g, ps_, hi) in H:
                            for (dst, l, r_, mask) in (("MT0", "bh", "ah", "m_su"), ("M0", "ah", "bh", "m_sl"), ("AakT", "kh", "ah", "m_su"),
                                                       ("ArbT", "bh", "rh", "m_ui"), ("ArkT", "kh", "rh", "m_ui")):
                                xa, rx_ = xslot(o, hi)
                                la, rl_ = opr(g, l, ps_); ra, rr_ = opr(g, r_, ps_)
                                P.T(lambda e, xa=xa, la=la, ra=ra: e.matmul(xa, lhsT=la, rhs=ra, start=True, stop=True), reads=[rl_, rr_], writes=[rx_])
                                mk_ap = Dm[(g, mask)][:] if scalar_decay else pvs(mask)
                                mk_r = rn("D%s_%d" % (mask, g)) if scalar_decay else r_pv
                                P.V(lambda e, o=o, dst=dst, xa=xa, mk_ap=mk_ap: e.tensor_tensor(out=getattr(o, dst)[:], in0=xa, in1=mk_ap, op=ALU.mult),
                                    reads=[rx_, mk_r], writes=[rn("%s_%d" % (dst, hi))])
                        def mm_evac(o, hi, lhs, rlhs, rhs, rrhs, dst, rdst, add=None, radd=None):
                            xa, rx_ = xslot(o, hi)
                            P.T(lambda e: e.matmul(xa, lhsT=lhs[:], rhs=rhs[:], start=True, stop=True), reads=[rlhs, rrhs], writes=[rx_])
                            if add is None:
                                P.S(lambda e: e.activation(out=dst[:], in_=xa, func=AF.Copy), reads=[rx_], writes=[rdst])
                            else:
                                P.V(lambda e: e.tensor_tensor(out=dst[:], in0=xa, in1=add[:], op=ALU.add), reads=[rx_, radd], writes=[rdst])
                        for (o, g, ps_, hi) in H:
                            R_ = lambda k: rn("%s_%d" % (k, hi))
                            P.G(lambda e, o=o: e.tensor_tensor(out=o.Mb0[:], in0=o.M0[:], in1=cb["bd32"][:], op=ALU.mult), reads=[R_("M0"), r_cb], writes=[R_("Mb0")])
                            P.G(lambda e, o=o: e.tensor_tensor(out=o.MTb0[:], in0=o.MT0[:], in1=cb["bd32"][:], op=ALU.mult), reads=[R_("MT0"), r_cb], writes=[R_("MTb0")])
                            P.G(lambda e, o=o: e.tensor_tensor(out=o.P0[:], in0=o.MTb0[:], in1=cb["ident"][:], op=ALU.add), reads=[R_("MTb0"), r_cb], writes=[R_("P0")])
                            P.G(lambda e, o=o: e.tensor_tensor(out=o.Q0[:], in0=o.Mb0[:], in1=cb["ident"][:], op=ALU.add), reads=[R_("Mb0"), r_cb], writes=[R_("Q0")])
                        cur = 0
                        for m in range(1, 5):
                            nxt = 1 - cur
                            for (o, g, ps_, hi) in H:
                                R_ = lambda k: rn("%s_%d" % (k, hi))
                                G_ = lambda k: getattr(o, k)
                                cn = lambda k, i: "%s%d" % (k, i)
                                mm_evac(o, hi, G_(cn("MTb", cur)), R_(cn("MTb", cur)), G_(cn("Mb", cur)), R_(cn("Mb", cur)), G_(cn("Mb", nxt)), R_(cn("Mb", nxt)))
                                mm_evac(o, hi, G_(cn("Mb", cur)), R_(cn("Mb", cur)), G_(cn("MTb", cur)), R_(cn("MTb", cur)), G_(cn("MTb", nxt)), R_(cn("MTb", nxt)))
                                mm_evac(o, hi, G_(cn("Mb", nxt)), R_(cn("Mb", nxt)), G_(cn("P", cur)), R_(cn("P", cur)), G_(cn("P", nxt)), R_(cn("P", nxt)),
                                        add=G_(cn("P", cur)), radd=R_(cn("P", cur)))
                                mm_evac(o, hi, G_(cn("MTb", nxt)), R_(cn("MTb", nxt)), G_(cn("Q", cur)), R_(cn("Q", cur)), G_(cn("Q", nxt)), R_(cn("Q", nxt)),
                                        add=G_(cn("Q", cur)), radd=R_(cn("Q", cur)))
                            cur = nxt
                        for (lev, offm) in ((64, "off64"), (128, "off128")):
                            nxt = 1 - cur
                            for (o, g, ps_, hi) in H:
                                R_ = lambda k: rn("%s_%d" % (k, hi))
                                G_ = lambda k: getattr(o, k)
                                cn = lambda k, i: "%s%d" % (k, i)
                                P.G(lambda e, o=o: e.tensor_tensor(out=o.Noff[:], in0=o.M0[:], in1=cb[offm][:], op=ALU.mult), reads=[R_("M0"), r_cb], writes=[R_("Noff")])
                                P.G(lambda e, o=o: e.tensor_tensor(out=o.NoffT[:], in0=o.MT0[:], in1=cb[offm][:], op=ALU.mult), reads=[R_("MT0"), r_cb], writes=[R_("NoffT")])
                                X_, rX = G_(cn("Q", cur)), R_(cn("Q", cur))
                                XT_, rXT = G_(cn("P", cur)), R_(cn("P", cur))
                                if lev != 128:
                                    mm_evac(o, hi, o.NoffT, R_("NoffT"), X_, rX, o.Z, R_("Z"))
                                    mm_evac(o, hi, XT_, rXT, o.Z, R_("Z"), G_(cn("Q", nxt)), R_(cn("Q", nxt)), add=X_, radd=rX)
                                mm_evac(o, hi, o.Noff, R_("Noff"), XT_, rXT, o.Z2, R_("Z2"))
                                mm_evac(o, hi, X_, rX, o.Z2, R_("Z2"), G_(cn("P", nxt)), R_(cn("P", nxt)), add=XT_, radd=rXT)
                            cur = nxt
                        for (o, g, ps_, hi) in H:
                            Pf, rPf = getattr(o, "P%d" % cur), rn("P%d_%d" % (cur, hi))
                            ry = rn("psy%d" % hi)
                            hs = slice(ps_.start, ps_.start + hd)
                            at_, rat = opr(g, "at", ps_)
                            rt_, rrt = opr(g, "rt", ps_)
                            P.T(lambda e, o=o, at_=at_, g=g, ps_=ps_: e.matmul(o.psy[:, 256:256 + hd], lhsT=at_, rhs=Sb[g][ps_, 0:hd], start=True, stop=False),
                                reads=[rat, rn("Sb%d" % g)], writes=[ry])
                            P.T(lambda e, o=o, g=g, hs=hs: e.matmul(o.psy[:, 256:256 + hd], lhsT=o.AakT[:], rhs=tmj[(g, "vt")][:, hs], start=False, stop=True),
                                reads=[rn("AakT_%d" % hi), rn("tm_vt%d" % g)], writes=[ry])
                            P.S(lambda e, o=o: e.activation(out=o.RHS[:, :hd], in_=o.psy[:, 256:256 + hd], func=AF.Copy), reads=[ry], writes=[rn("RHS_%d" % hi)])
                            P.T(lambda e, o=o, Pf=Pf: e.matmul(o.psy[:, 384:384 + hd], lhsT=Pf[:], rhs=o.RHS[:, :hd], start=True, stop=True),
                                reads=[rPf, rn("RHS_%d" % hi)], writes=[ry])
                            P.S(lambda e, o=o: e.activation(out=o.U[:, :hd], in_=o.psy[:, 384:384 + hd], func=AF.Copy), reads=[ry], writes=[rn("U_%d" % hi)])
                            P.T(lambda e, o=o, g=g, ps_=ps_, rt_=rt_: e.matmul(o.psy[ps_, 0:128], lhsT=Sb[g][ps_, 0:hd], rhs=rt_, start=True, stop=False),
                                reads=[rn("Sb%d" % g), rrt], writes=[ry])
                            P.T(lambda e, o=o, ps_=ps_: e.matmul(o.psy[ps_, 0:128], lhsT=o.U[:, :hd], rhs=o.ArbT[:], start=False, stop=False),
                                reads=[rn("U_%d" % hi), rn("ArbT_%d" % hi)], writes=[ry])
                            P.T(lambda e, o=o, g=g, ps_=ps_, hs=hs: e.matmul(o.psy[ps_, 0:128], lhsT=tmj[(g, "vt")][:, hs], rhs=o.ArkT[:], start=False, stop=True),
                                reads=[rn("tm_vt%d" % g), rn("ArkT_%d" % hi)], writes=[ry])
                            P.V(lambda e, o=o, g=g, ps_=ps_: e.tensor_copy(out=yT[g][ps_, :], in_=o.psy[ps_, 0:128]), reads=[ry], writes=[rn("yT%d" % g)])
                            P.T(lambda e, o=o, g=g, ps_=ps_, hs=hs: e.matmul(o.psy[ps_, 128:128 + hd], lhsT=tmj[(g, "bt")][:, hs], rhs=o.U[:, :hd], start=True, stop=False),
                                reads=[rn("tm_bt%d" % g), rn("U_%d" % hi)], writes=[ry])
                            P.T(lambda e, o=o, g=g, ps_=ps_, hs=hs: e.matmul(o.psy[ps_, 128:128 + hd], lhsT=tmj[(g, "kt")][:, hs], rhs=tmj[(g, "vt")][:, hs], start=False, stop=True),
                                reads=[rn("tm_kt%d" % g), rn("tm_vt%d" % g)], writes=[ry])
                            P.V(lambda e, o=o, g=g, ps_=ps_: e.scalar_tensor_tensor(out=Sf[g][ps_, 0:hd], in0=Sf[g][ps_, 0:hd], scalar=e2[g][ps_, 128:129],
                                                                                   in1=o.psy[ps_, 128:128 + hd], op0=ALU.mult, op1=ALU.add),
                                reads=[rn("Sf%d" % g), rn("e2_%d" % g), ry], writes=[rn("Sf%d" % g)])
                            P.G(lambda e, g=g, ps_=ps_: e.tensor_copy(out=Sb[g][ps_, 0:hd], in_=Sf[g][ps_, 0:hd]), reads=[rn("Sf%d" % g)], writes=[rn("Sb%d" % g)])
                        for g, grow in enumerate(gslots):
                            post("chunk", g, grow, c, (yT[g], rn("yT%d" % g), cl), pobj)
            P.barrier()

        def gdn_post_setup(st, ngs, SC):
            sb = lambda nm, shape, dt: st.enter_context(nc.sbuf_tensor(un(nm), shape, dt))
            o = TokCtx()
            o.z = [sb("gz%d" % g, [128, SC * 128], BF16) for g in range(ngs)]
            o.sq = sb("gpsq", [128, 128], BF16)
            o.rs = sb("gprs", [128, 128], F32)
            o.ob = [sb("gpob%d" % i, [128, 128], BF16) for i in range(2)]
            o.ps = st.enter_context(nc.psum_tensor(un("gpps"), [128, 128], F32))
            o.k = 0
            return o

        def gdn_post(kind, g, grow, c, arg, o):
            if kind == "load":
                P.dma("sync", o.z[g][:, :arg], gS["z"][grow:grow + 128, c * 128:c * 128 + arg], reads=[res("scrA")], writes=[res("gz%d" % g)])
                return
            yT_, ry, cl = arg
            P.G(lambda e: e.tensor_tensor(out=o.sq[:], in0=yT_[:], in1=yT_[:], op=ALU.mult), reads=[ry], writes=[res("gpsq")])
            P.T(lambda e: e.matmul(o.ps[:], lhsT=cb["ones"][:], rhs=o.sq[:], start=True, stop=True), reads=[r_cb, res("gpsq")], writes=[res("gpps")])
            P.S(lambda e: e.activation(out=o.rs[:], in_=o.ps[:], func=AF.Sqrt, bias=cvec[:, 0:1], scale=1.0 / 128), reads=[res("gpps"), r_cb], writes=[res("gprs")])
            P.V(lambda e: e.reciprocal(out=o.rs[:], in_=o.rs[:]), reads=[res("gprs")], writes=[res("gprs")])
            P.V(lambda e: e.scalar_tensor_tensor(out=o.rs[:], in0=yT_[:], scalar=pvs("o_gain"), in1=o.rs[:], op0=ALU.mult, op1=ALU.mult),
                reads=[ry, r_pv, res("gprs")], writes=[res("gprs")])
            i = o.k % 2; o.k += 1
            P.V(lambda e: e.tensor_tensor(out=o.ob[i][:], in0=o.rs[:], in1=o.z[g][:, cl * 128:(cl + 1) * 128], op=ALU.mult),
                reads=[res("gprs"), res("gz%d" % g)], writes=[res("gpob%d" % i)])
            P.dma("sync", oT[512 + grow:512 + grow + 128, c * 128:(c + 1) * 128], o.ob[i][:], reads=[res("gpob%d" % i)], writes=[res("oT_dram")])

        R["scrgdn"] = res("scrA")
        if only is None or "C" in only:
          dplr_phase("gdn", [[(0, 0, 128), (128, 0, 128)], [(256, 0, 128), (384, 0, 128)]], gS, gdn_post_setup, gdn_post, scalar_decay=True)
        if stop_after == "C":
            P.finish(); return nc

        with contextlib.ExitStack() as st:
          if only is None or "D" in only:
            sb = lambda name, shape, dt: st.enter_context(nc.sbuf_tensor(un(name), shape, dt))
            c = make_tok_ctx(st, nwi=3, nwo=3)
            oin = sb("oin", [128, 8, 512], BF16)
            xx = sb("xx", [128, 8, 512], BF16)
            xm = sb("xm", [128, 8, 512], BF16)
            w1s = sb("w1s", [128, 8, 64], BF16); a1s = sb("a1s", [128, 8, 64], BF16); g1s = sb("g1s", [128, 8, 160], BF16)
            w2s = sb("w2s", [64, D], BF16); a2s = sb("a2s", [64, D], BF16); g2s = sb("g2s", [128, 2, D], BF16)
            P.dma("sync", w1s[:], wview(Wb["rwkv_w1"]), reads=[res("W_rwkv_w1")], writes=[res("lora")])
            P.dma("sync", a1s[:], wview(Wb["rwkv_a1"]), reads=[res("W_rwkv_a1")], writes=[res("lora")])
            P.dma("sync", g1s[:], wview(Wb["rwkv_g1"]), reads=[res("W_rwkv_g1")], writes=[res("lora")])
            P.dma("sync", w2s[:], Wb["rwkv_w2"], reads=[res("W_rwkv_w2")], writes=[res("lora")])
            P.dma("sync", a2s[:], Wb["rwkv_a2"], reads=[res("W_rwkv_a2")], writes=[res("lora")])
            P.dma("sync", g2s[:, 0, :], Wb["rwkv_g2"][0:128, :], reads=[res("W_rwkv_g2")], writes=[res("lora")])
            P.dma("sync", g2s[0:32, 1, :], Wb["rwkv_g2"][128:160, :], reads=[res("W_rwkv_g2")], writes=[res("lora")])
            r_lora = res("lora")
            lmid = [sb("lmid%d" % i, [128, 512], BF16) for i in range(2)]
            rr = sb("rr", [128, 8, 512], BF16)
            kk_ = sb("kkk", [128, 8, 512], F32)
            aa = sb("aa", [128, 512], F32)
            t1 = sb("t1", [128, 512], F32)
            t2 = sb("t2", [128, 512], F32)
            ob = [sb("obD%d" % i, [128, 512], BF16) for i in range(4)]
            obf = [sb("obF%d" % i, [128, 512], F32) for i in range(2)]
            ocnt = [0, 0]
            P.V(lambda e: e.memset(c.hn[:, :, 0:1], 0.0), writes=[res("hn")])
            omk = sb("omk", [128, 8], F32)
            print("phase D sbuf remaining", nc.sbuf_bytes_remaining)
            P.V(lambda e: e.tensor_scalar(out=omk[:], in0=pvs("k_a"), scalar1=-1.0, scalar2=1.0, op0=ALU.mult, op1=ALU.add), reads=[r_pv], writes=[res("omk")])

            def out_bf(dst_ap, make):
                i = ocnt[0] % 4; ocnt[0] += 1
                make(ob[i], res("obD%d" % i))
                P.dma("sync", dst_ap, ob[i][:, :dst_ap.shape[-1]], reads=[res("obD%d" % i)], writes=[res("scrrwkv")])

            for ti, (t0, N) in enumerate(tiles):
                h_t, r_h = c.hT[ti % 2], res("hT%d" % (ti % 2))
                P.dma("sync", h_t[:, :, :N], hview(hT)[:, :, t0:t0 + N], reads=[res("hT_dram")], writes=[r_h])
                P.dma("sync", oin[:, :, :N], hview(oT)[:, :, t0:t0 + N], reads=[res("oT_dram")], writes=[res("oin")])
                proj_residual(c, h_t, r_h, N, oin, res("oin"), "hyb_w_out")
                ffn(c, h_t, r_h, N, 1)
                ffn(c, h_t, r_h, N, 2)
                P.dma("sync", hview(hT)[:, :, t0:t0 + N], h_t[:, :, :N], reads=[r_h], writes=[res("hT_dram")])
                rmsnorm(c, h_t, r_h, N, "mix_norm1")
                import os as _os
                DCUT = float(_os.environ.get("DCUT", "99"))
                if DCUT <= 0:
                    continue
                for kc in range(8):
                    P.op("vector", lambda e, kc=kc: e.tensor_tensor(out=xx[:, kc, :N], in0=c.hn[:, kc, 0:N], in1=c.hn[:, kc, 1:1 + N], op=ALU.subtract),
                         reads=[res("hn")], writes=[res("xx")])
                o_mu, _ = PV_SLOTS["mu"]
                def mix(i):
                    for kc in range(8):
                        eng = "vector" if kc % 2 == 0 else "gpsimd"
                        P.op("vector", lambda e, kc=kc: e.scalar_tensor_tensor(out=xm[:, kc, :N], in0=xx[:, kc, :N], scalar=pv[:, o_mu + i * 8 + kc:o_mu + i * 8 + kc + 1],
                                                                          in1=c.hn[:, kc, 1:1 + N], op0=ALU.mult, op1=ALU.add),
                             reads=[res("xx"), res("hn"), r_pv], writes=[res("xm")])
                xfn = lambda kc: xm[:, kc, :N]
                rxm = [res("xm")]
                if DCUT <= 0.3:
                    continue
                mix(0)
                if DCUT <= 0.6:
                    continue
                def cons_r(fc, pi, m):
                    if _os.environ.get("RVAR", "") != "a":
                        P.S(lambda e: e.activation(out=rr[:, fc, :N], in_=c.pt[pi][:, :N], func=AF.Copy), reads=[res("pt%d" % pi)], writes=[res("rr")])
                    if _os.environ.get("RVAR", "") == "b":
                        return
                    out_bf(rS["r"][fc * 128:(fc + 1) * 128, t0:t0 + N],
                           lambda o, ro: P.V(lambda e: e.tensor_copy(out=o[:, :N], in_=c.pt[pi][:, :N]), reads=[res("pt%d" % pi)], writes=[ro]))
                lin_fm(c, xfn, rxm, N, "rwkv_w_r", 0, D, cons_r)
                if DCUT <= 1:
                    continue
                mix(1)
                pi = next_pt(c)
                for kc in range(8):
                    P.T(lambda e, pi=pi, kc=kc: e.matmul(c.pt[pi][:64, :N], lhsT=w1s[:, kc, :], rhs=xm[:, kc, :N], start=(kc == 0), stop=(kc == 7)),
                        reads=[r_lora, res("xm")], writes=[res("pt%d" % pi)])
                P.S(lambda e, pi=pi: e.activation(out=lmid[0][:64, :N], in_=c.pt[pi][:64, :N], func=AF.Tanh), reads=[res("pt%d" % pi)], writes=[res("lmid0")])
                for fc in range(8):
                    pj = next_pt(c)
                    P.T(lambda e, pj=pj, fc=fc: e.matmul(c.pt[pj][:, :N], lhsT=w2s[:, fc * 128:(fc + 1) * 128], rhs=lmid[0][:64, :N], start=True, stop=True),
                        reads=[r_lora, res("lmid0")], writes=[res("pt%d" % pj)])
                    fi = ocnt[1] % 2; ocnt[1] += 1
                    P.S(lambda e, pj=pj, fc=fc, fi=fi: e.activation(out=obf[fi][:, :N], in_=c.pt[pj][:, :N], func=AF.Sigmoid, bias=pvs("w0", fc, 1)),
                        reads=[res("pt%d" % pj), r_pv], writes=[res("obF%d" % fi)])
                    P.V(lambda e, fi=fi: e.tensor_scalar(out=obf[fi][:, :N], in0=obf[fi][:, :N], scalar1=-float(np.exp(-0.5)), scalar2=None, op0=ALU.mult),
                        reads=[res("obF%d" % fi)], writes=[res("obF%d" % fi)])
                    P.dma("sync", rS["lw"][fc * 128:(fc + 1) * 128, t0:t0 + N], obf[fi][:, :N], reads=[res("obF%d" % fi)], writes=[res("scrrwkv")])
                if DCUT <= 2:
                    continue
                mix(2)
                def cons_k(fc, pi, m):
                    P.S(lambda e: e.activation(out=kk_[:, fc, :N], in_=c.pt[pi][:, :N], func=AF.Copy), reads=[res("pt%d" % pi)], writes=[res("kkk")])
                lin_fm(c, xfn, rxm, N, "rwkv_w_k", 0, D, cons_k)
                mix(3)
                def cons_v(fc, pi, m):
                    out_bf(rS["v"][fc * 128:(fc + 1) * 128, t0:t0 + N],
                           lambda o, ro: P.V(lambda e: e.tensor_copy(out=o[:, :N], in_=c.pt[pi][:, :N]), reads=[res("pt%d" % pi)], writes=[ro]))
                    P.S(lambda e: e.activation(out=c.act[:, fc, :N], in_=c.pt[pi][:, :N], func=AF.Copy), reads=[res("pt%d" % pi)], writes=[res("act")])
                lin_fm(c, xfn, rxm, N, "rwkv_w_v", 0, D, cons_v)
                if DCUT <= 3:
                    continue
                mix(4)
                pi = next_pt(c)
                for kc in range(8):
                    P.T(lambda e, pi=pi, kc=kc: e.matmul(c.pt[pi][:64, :N], lhsT=a1s[:, kc, :], rhs=xm[:, kc, :N], start=(kc == 0), stop=(kc == 7)),
                        reads=[r_lora, res("xm")], writes=[res("pt%d" % pi)])
                P.S(lambda e, pi=pi: e.activation(out=lmid[1][:64, :N], in_=c.pt[pi][:64, :N], func=AF.Copy), reads=[res("pt%d" % pi)], writes=[res("lmid1")])
                for fc in range(8):
                    pj = next_pt(c)
                    P.T(lambda e, pj=pj, fc=fc: e.matmul(c.pt[pj][:, :N], lhsT=a2s[:, fc * 128:(fc + 1) * 128], rhs=lmid[1][:64, :N], start=True, stop=True),
                        reads=[r_lora, res("lmid1")], writes=[res("pt%d" % pj)])
                    P.S(lambda e, pj=pj, fc=fc: e.activation(out=aa[:, :N], in_=c.pt[pj][:, :N], func=AF.Sigmoid, bias=pvs("a0", fc, 1)),
                        reads=[res("pt%d" % pj), r_pv], writes=[res("aa")])
                    P.V(lambda e, fc=fc: e.tensor_scalar(out=t1[:, :N], in0=kk_[:, fc, :N], scalar1=pvs("k_k", fc, 1), scalar2=None, op0=ALU.mult),
                        reads=[res("kkk"), r_pv], writes=[res("t1")])
                    P.G(lambda e: e.tensor_tensor(out=c.sq[:, 0, :N], in0=t1[:, :N], in1=t1[:, :N], op=ALU.mult), reads=[res("t1")], writes=[res("sq")])
                    pq = sumsq_bc(c, lambda k: c.sq[:, 0, :N], 1, N, [res("sq")], lhs_name="blk64")
                    rsqrt_from_psum(c, pq, N, c.rstd[:, :N], res("rstd"))
                    P.V(lambda e: e.tensor_tensor(out=t1[:, :N], in0=t1[:, :N], in1=c.rstd[:, :N], op=ALU.mult), reads=[res("t1"), res("rstd")], writes=[res("t1")])
                    out_bf(rS["a"][fc * 128:(fc + 1) * 128, t0:t0 + N],
                           lambda o, ro: P.G(lambda e: e.tensor_scalar(out=o[:, :N], in0=t1[:, :N], scalar1=-1.0, scalar2=None, op0=ALU.mult), reads=[res("t1")], writes=[ro]))
                    out_bf(rS["b"][fc * 128:(fc + 1) * 128, t0:t0 + N],
                           lambda o, ro: P.V(lambda e: e.tensor_tensor(out=o[:, :N], in0=t1[:, :N], in1=aa[:, :N], op=ALU.mult), reads=[res("t1"), res("aa")], writes=[ro]))
                    P.V(lambda e, fc=fc: e.tensor_scalar(out=t2[:, :N], in0=aa[:, :N], scalar1=pvs("k_a", fc, 1), scalar2=omk[:, fc:fc + 1], op0=ALU.mult, op1=ALU.add),
                        reads=[res("aa"), r_pv, res("omk")], writes=[res("t2")])
                    P.V(lambda e, fc=fc: e.tensor_tensor(out=t2[:, :N], in0=t2[:, :N], in1=kk_[:, fc, :N], op=ALU.mult), reads=[res("t2"), res("kkk")], writes=[res("t2")])
                    out_bf(rS["k"][fc * 128:(fc + 1) * 128, t0:t0 + N],
                           lambda o, ro: P.G(lambda e: e.tensor_copy(out=o[:, :N], in_=t2[:, :N]), reads=[res("t2")], writes=[ro]))
                    P.V(lambda e, fc=fc: e.scalar_tensor_tensor(out=c.sq[:, 1, :N], in0=t2[:, :N], scalar=pvs("r_k", fc, 1), in1=rr[:, fc, :N], op0=ALU.mult, op1=ALU.mult),
                        reads=[res("t2"), r_pv, res("rr")], writes=[res("sq")])
                    pb = sumsq_bc(c, lambda k: c.sq[:, 1, :N], 1, N, [res("sq")], lhs_name="blk64")
                    out_bf(rS["bonus"][fc * 128:(fc + 1) * 128, t0:t0 + N],
                           lambda o, ro: P.V(lambda e, fc=fc: e.tensor_tensor(out=o[:, :N], in0=c.pt[pb][:, :N], in1=c.act[:, fc, :N], op=ALU.mult),
                                             reads=[res("pt%d" % pb), res("act")], writes=[ro]))
                if DCUT <= 4:
                    continue
                mix(5)
                for (m0, mm, li) in ((0, 128, 0), (128, 32, 1)):
                    pi = next_pt(c)
                    for kc in range(8):
                        P.T(lambda e, pi=pi, kc=kc, m0=m0, mm=mm: e.matmul(c.pt[pi][:mm, :N], lhsT=g1s[:, kc, m0:m0 + mm], rhs=xm[:, kc, :N], start=(kc == 0), stop=(kc == 7)),
                            reads=[r_lora, res("xm")], writes=[res("pt%d" % pi)])
                    P.S(lambda e, pi=pi, mm=mm, li=li: e.activation(out=lmid[li][:mm, :N], in_=c.pt[pi][:mm, :N], func=AF.Sigmoid),
                        reads=[res("pt%d" % pi)], writes=[res("lmid%d" % li)])
                for fc in range(8):
                    pj = next_pt(c)
                    P.T(lambda e, pj=pj, fc=fc: e.matmul(c.pt[pj][:, :N], lhsT=g2s[:, 0, fc * 128:(fc + 1) * 128], rhs=lmid[0][:, :N], start=True, stop=False),
                        reads=[r_lora, res("lmid0")], writes=[res("pt%d" % pj)])
                    P.T(lambda e, pj=pj, fc=fc: e.matmul(c.pt[pj][:, :N], lhsT=g2s[0:32, 1, fc * 128:(fc + 1) * 128], rhs=lmid[1][0:32, :N], start=False, stop=True),
                        reads=[r_lora, res("lmid1")], writes=[res("pt%d" % pj)])
                    out_bf(rS["g"][fc * 128:(fc + 1) * 128, t0:t0 + N],
                           lambda o, ro: P.V(lambda e, pj=pj: e.tensor_copy(out=o[:, :N], in_=c.pt[pj][:, :N]), reads=[res("pt%d" % pj)], writes=[ro]))
                P.G(lambda e: e.tensor_copy(out=c.hn[:, :, 0:1], in_=c.hn[:, :, N:N + 1]), reads=[res("hn")], writes=[res("hn")])
        P.barrier()
        if stop_after == "D":
            P.finish(); return nc

        def rw_post_setup(st, ngs, SC):
            sb = lambda nm, shape, dt: st.enter_context(nc.sbuf_tensor(un(nm), shape, dt))
            o = TokCtx()
            o.bon = sb("rbon", [128, SC * 128], BF16)
            o.g = sb("rgate", [128, SC * 128], BF16)
            o.sq = sb("rpsq", [128, 128], BF16)
            o.yb = sb("rpyb", [128, 128], BF16)
            o.mean = sb("rpmean", [128, 128], F32)
            o.var = sb("rpvar", [128, 128], F32)
            o.yc = sb("rpyc", [128, 128], F32)
            o.ob = [sb("rpob%d" % i, [128, 128], BF16) for i in range(2)]
            _ps = st.enter_context(nc.psum_tensor(un("rpps"), [128, 128], F32))
            o.ps = [_ps, _ps]
            o.k = 0
            return o

        def rw_post(kind, g, grow, c, arg, o):
            if kind == "load":
                P.dma("sync", o.bon[:, :arg], rS["bonus"][grow:grow + 128, c * 128:c * 128 + arg], reads=[res("scrrwkv")], writes=[res("rbon")])
                P.dma("sync", o.g[:, :arg], rS["g"][grow:grow + 128, c * 128:c * 128 + arg], reads=[res("scrrwkv")], writes=[res("rgate")])
                return
            yT_, ry, cl = arg
            fc = grow // 128
            cols = slice(cl * 128, (cl + 1) * 128)
            P.G(lambda e: e.tensor_copy(out=o.yb[:], in_=yT_[:]), reads=[ry], writes=[res("rpyb")])
            P.T(lambda e: e.matmul(o.ps[0][:], lhsT=cb["blk64"][:], rhs=o.yb[:], start=True, stop=True), reads=[r_cb, res("rpyb")], writes=[res("rpps")])
            P.V(lambda e: e.scalar_tensor_tensor(out=o.yc[:], in0=o.ps[0][:], scalar=-1.0 / 64, in1=yT_[:], op0=ALU.mult, op1=ALU.add),
                reads=[res("rpps"), ry], writes=[res("rpyc")])
            P.G(lambda e: e.tensor_tensor(out=o.sq[:], in0=o.yc[:], in1=o.yc[:], op=ALU.mult), reads=[res("rpyc")], writes=[res("rpsq")])
            P.T(lambda e: e.matmul(o.ps[1][:], lhsT=cb["blk64"][:], rhs=o.sq[:], start=True, stop=True), reads=[r_cb, res("rpsq")], writes=[res("rpps")])
            P.S(lambda e: e.activation(out=o.var[:], in_=o.ps[1][:], func=AF.Sqrt, bias=cvec[:, 2:3], scale=1.0 / 64), reads=[res("rpps"), r_cb], writes=[res("rpvar")])
            P.V(lambda e: e.reciprocal(out=o.var[:], in_=o.var[:]), reads=[res("rpvar")], writes=[res("rpvar")])
            P.V(lambda e: e.scalar_tensor_tensor(out=o.yc[:], in0=o.yc[:], scalar=pvs("ln_w", fc, 1), in1=o.var[:], op0=ALU.mult, op1=ALU.mult),
                reads=[res("rpyc"), r_pv, res("rpvar")], writes=[res("rpyc")])
            P.V(lambda e: e.scalar_tensor_tensor(out=o.yc[:], in0=o.yc[:], scalar=pvs("ln_b", fc, 1), in1=o.bon[:, cols], op0=ALU.add, op1=ALU.add),
                reads=[res("rpyc"), r_pv, res("rbon")], writes=[res("rpyc")])
            i = o.k % 2; o.k += 1
            P.V(lambda e: e.tensor_tensor(out=o.ob[i][:], in0=o.yc[:], in1=o.g[:, cols], op=ALU.mult), reads=[res("rpyc"), res("rgate")], writes=[res("rpob%d" % i)])
            P.dma("sync", zT[grow:grow + 128, c * 128:(c + 1) * 128], o.ob[i][:], reads=[res("rpob%d" % i)], writes=[res("zT_dram")])

        if only is None or "E" in only:
          dplr_phase("rwkv", [[(gq * 128, 0, 64), (gq * 128, 64, 64)] for gq in range(8)], rS, rw_post_setup, rw_post)
        if stop_after == "E":
            P.finish(); return nc

        with contextlib.ExitStack() as st:
            sb = lambda name, shape, dt: st.enter_context(nc.sbuf_tensor(un(name), shape, dt))
            c = make_tok_ctx(st, nwi=4, nwo=4)
            oin = sb("oinF", [128, 8, 512], BF16)
            for ti, (t0, N) in enumerate(tiles):
                h_t, r_h = c.hT[ti % 2], res("hT%d" % (ti % 2))
                P.dma("sync", h_t[:, :, :N], hview(hT)[:, :, t0:t0 + N], reads=[res("hT_dram")], writes=[r_h])
                P.dma("sync", oin[:, :, :N], hview(zT)[:, :, t0:t0 + N], reads=[res("zT_dram")], writes=[res("oinF")])
                proj_residual(c, h_t, r_h, N, oin, res("oinF"), "rwkv_w_o")
                ffn(c, h_t, r_h, N, 3)
                P.S(lambda e: e.activation(out=c.sq[:, :, :N], in_=h_t[:, :, :N], func=AF.Square), reads=[r_h], writes=[res("sq")])
                pi = sumsq_bc(c, lambda k: c.sq[:, k, :N], 8, N, [res("sq")])
                rsqrt_from_psum(c, pi, N, c.rstd[:, :N], res("rstd"))
                for kc in range(8):
                    eng = "vector" if kc % 2 == 0 else "gpsimd"
                    P.op("vector", lambda e, kc=kc: e.scalar_tensor_tensor(out=h_t[:, kc, :N], in0=h_t[:, kc, :N], scalar=pvs("final_norm", kc, 1), in1=c.rstd[:, :N],
                                                                       op0=ALU.mult, op1=ALU.mult), reads=[r_h, r_pv, res("rstd")], writes=[r_h])
                P.dma("sync", hview(outT)[:, :, t0:t0 + N], h_t[:, :, :N], reads=[r_h], writes=[res("out_dram")])
        P.finish()
    return nc


_NC_CACHE = {}


def make_in_maps(inp, T):
    x = np.asarray(inp["x"], np.float32)
    B, S, _ = x.shape
    meta = np.asarray(inp["meta"], np.float32)
    pv = pack_params(inp)
    base = {"pvec": pv}
    k = 0
    for l in range(2):
        for h in range(2):
            base["ffn_w_in%d" % k] = np.ascontiguousarray(inp["ffn_w_in"][l, h], np.float32)
            base["ffn_w_out%d" % k] = np.ascontiguousarray(inp["ffn_w_out"][l, h], np.float32)
            k += 1
    base["hyb_w_in"] = np.ascontiguousarray(inp["hyb_w_in"][0], np.float32)
    base["hyb_w_out"] = np.ascontiguousarray(inp["hyb_w_out"][0], np.float32)
    for nm in ("w_r", "w_k", "w_v", "w_o", "w1", "w2", "a1", "a2", "g1", "g2"):
        base["rwkv_" + nm] = np.ascontiguousarray(inp["rwkv_" + nm][0], np.float32)
    maps = []
    for core in range(8):
        b = core % B
        hT0 = np.zeros((D, T), np.float32)
        hT0[:, :NMETA] = meta.T
        hT0[:, NMETA:NMETA + S] = x[b].T
        m = dict(base)
        m["hT0"] = hT0
        maps.append(m)
    return maps


def kernel(**inp):
    x = np.asarray(inp["x"])
    B, S, _ = x.shape
    T = ((NMETA + S + 127) // 128) * 128
    if T not in _NC_CACHE:
        _NC_CACHE[T] = build(T)
    nc = _NC_CACHE[T]
    maps = make_in_maps(inp, T)
    res = run_bass_kernel_spmd(nc, maps, core_ids=list(range(8)))
    out = np.empty((B, S, D), np.float32)
    for b in range(B):
        out[b] = res.results[b]["outT"][:, NMETA:NMETA + S].T
    return out
```

```python
import contextlib
import numpy as np
import concourse.bass as bass
import concourse.mybir as mybir
from concourse.bass_utils import run_bass_kernel_spmd

F32 = mybir.dt.float32
BF16 = mybir.dt.bfloat16
ALU = mybir.AluOpType
AF = mybir.ActivationFunctionType

D = 1024
DFF = 2816
EPS = 1e-6
NMETA = 16
HYB_IN = 3600
GN_EPS = 64e-5


_NUN = [0]


NAMES = {}


def un(name):
    _NUN[0] += 1
    NAMES[name] = "t%d_%s" % (_NUN[0], name)
    return NAMES[name]


SAME_ENGINE_SYNC = True
PSUM_PREFIXES = ("pt", "py", "ps_", "psx", "psy", "pstr", "gpps", "rpps")


class Res:
    __slots__ = ("name", "w", "r", "excl")

    def __init__(self, name):
        self.name = name
        self.w = None
        self.r = []
        base = name.split("_", 1)[1] if name.startswith(("gdn_", "rwkv_")) else name
        self.excl = base.startswith(PSUM_PREFIXES)


class _Rec:
    def __init__(self):
        self.calls = []

    def __getattr__(self, name):
        def f(*a, **k):
            self.calls.append((name, a, k))
            return self
        return f


class Prog:
    ENGS = ("tensor", "vector", "scalar", "gpsimd", "sync")

    def __init__(self, nc, stack, n_dma_sems=12):
        self.nc = nc
        self.lists = {e: [] for e in self.ENGS}
        self.stack = stack
        self.epoch = 0
        self.esem = {e: stack.enter_context(nc.semaphore("s_" + e)) for e in self.ENGS}
        self.ecount = {e: 0 for e in self.ENGS}
        self.LIMIT = 12000
        self.seen = {e: {} for e in self.ENGS}
        self.dsems, self.dcount, self.dnext = {}, {}, {}
        for q in ("sync", "gpsimd", "scalar"):
            self.dsems[q] = [stack.enter_context(nc.semaphore("d_%s%d" % (q, i)))
                             for i in range(n_dma_sems if q != "scalar" else 2)]
            self.dcount[q] = [0] * len(self.dsems[q])
            self.dnext[q] = 0
        self.semobj = {}
        for e in self.ENGS:
            self.semobj[("e", e, 0)] = self.esem[e]
        for q in self.dsems:
            for i, s in enumerate(self.dsems[q]):
                self.semobj[("d", q, i)] = s
        self.n_ins = 0

    def _need(self, eng, deps, key, val):
        if key[0] == "e":
            if key[2] < self.epoch:
                return
            if key[1] == eng and (eng == "tensor" or not SAME_ENGINE_SYNC):
                return
        if self.seen[eng].get(key, 0) >= val:
            return
        if deps.get(key, 0) < val:
            deps[key] = val

    def _collect(self, eng, reads, writes):
        deps = {}
        for r in reads:
            if r.w is not None:
                self._need(eng, deps, r.w[0], r.w[1])
        for w in writes:
            if w.w is not None:
                self._need(eng, deps, w.w[0], w.w[1])
            for (k, v) in w.r:
                self._need(eng, deps, k, v)
        for k, v in deps.items():
            self.seen[eng][k] = v
        return list(deps.items())

    def _mark(self, key, val, reads, writes):
        for r in reads:
            r.r = [(k, v) for (k, v) in r.r if k != key]
            r.r.append((key, val))
        for w in writes:
            w.w = (key, val)
            w.r = []

    def op(self, eng, fn, reads=(), writes=()):
        rec = _Rec()
        fn(rec)
        assert len(rec.calls) == 1, rec.calls
        name, a, k = rec.calls[0]
        fn = (lambda e, name=name, a=a, k=k: getattr(e, name)(*a, **k))
        ex = [r for r in reads if r.excl]
        if ex:
            writes = list(writes) + ex
        if self.ecount[eng] >= self.LIMIT:
            self.barrier(rotate=True)
        waits = self._collect(eng, reads, writes)
        self.ecount[eng] += 1
        key = ("e", eng, self.epoch)
        self.lists[eng].append((waits, fn, key, 1))
        self._mark(key, self.ecount[eng], reads, writes)
        self.n_ins += 1

    def V(self, fn, reads=(), writes=()):
        self.op("vector", fn, reads, writes)

    def S(self, fn, reads=(), writes=()):
        self.op("scalar", fn, reads, writes)

    def G(self, fn, reads=(), writes=()):
        self.op("gpsimd", fn, reads, writes)

    def T(self, fn, reads=(), writes=()):
        self.op("tensor", fn, reads, writes)

    def dma(self, q, out, in_, reads=(), writes=(), **kw):
        i = self.dnext[q]
        self.dnext[q] = (i + 1) % len(self.dsems[q])
        key = ("d", q, i)
        waits = self._collect(q, reads, writes)
        prev = self.dcount[q][i]
        if prev > 0 and self.seen[q].get(key, 0) < prev:
            waits.append((key, prev))
            self.seen[q][key] = prev
        self.dcount[q][i] += 16
        self.lists[q].append((waits, (lambda e: e.dma_start(out=out, in_=in_, **kw)), key, 16))
        self._mark(key, self.dcount[q][i], reads, writes)
        self.n_ins += 1

    def barrier(self, rotate=False):
        allw = {}
        for e in self.ENGS:
            if self.ecount[e] > 0:
                allw[("e", e, self.epoch)] = self.ecount[e]
        for q in self.dsems:
            for i, c in enumerate(self.dcount[q]):
                if c > 0:
                    allw[("d", q, i)] = c
        for e in self.ENGS:
            ws = []
            for k, v in allw.items():
                if k[0] == "e" and k[1] == e:
                    continue
                if self.seen[e].get(k, 0) < v:
                    ws.append((k, v))
                    self.seen[e][k] = v
            if ws:
                self.lists[e].append((ws, None, None, 0))
        if rotate:
            self.epoch += 1
            for e in self.ENGS:
                if e == "sync":
                    continue
                self.esem[e] = self.stack.enter_context(self.nc.semaphore("s_%s_%d" % (e, self.epoch)))
                self.semobj[("e", e, self.epoch)] = self.esem[e]
                self.ecount[e] = 0
                self.seen[e] = {k: v for k, v in self.seen[e].items() if k[0] != "e"}
            self.seen["sync"] = {k: v for k, v in self.seen["sync"].items() if k[0] != "e"}

    def finish(self):
        self.barrier()
        semobj, lists = self.semobj, self.lists

        def run(e, name):
            for (ws, fn, key, inc) in lists[name]:
                for (k, v) in ws:
                    e.wait_ge(semobj[k], v)
                if fn is not None:
                    fn(e).then_inc(semobj[key], inc)

        with self.nc.Block() as block:
            @block.tensor
            def _(e):
                run(e, "tensor")

            @block.vector
            def _(e):
                run(e, "vector")

            @block.scalar
            def _(e):
                run(e, "scalar")

            @block.gpsimd
            def _(e):
                run(e, "gpsimd")

            @block.sync
            def _(e):
                run(e, "sync")


PV_SLOTS = {}


def _pv_layout():
    off = 0
    def add(name, n):
        nonlocal off
        PV_SLOTS[name] = (off, n)
        off += n
    for i in range(4):
        add("ffn_norm%d" % i, 8)
    add("mix_norm0", 8); add("mix_norm1", 8); add("final_norm", 8)
    add("fox_bf", 8); add("conv", 48); add("a_log", 4); add("dt_bias", 4); add("o_gain", 1)
    add("mu", 48); add("w0", 8); add("a0", 8); add("k_k", 8); add("k_a", 8); add("ln_w", 8); add("ln_b", 8); add("r_k", 8)
    for nm in ("cmean", "ident", "m_su", "m_ui", "m_sl", "ones", "blk64", "triu", "bd32", "off64", "off128"):
        add(nm, 128)
    add("sel65", 64)
    return off


NPV = _pv_layout()


def fm(vec):
    return np.ascontiguousarray(np.asarray(vec, np.float32).reshape(8, 128).T)


def pack_params(inp):
    pv = np.zeros((128, NPV), np.float32)
    def put(name, arr):
        o, n = PV_SLOTS[name]
        pv[:, o:o + n] = np.asarray(arr, np.float32).reshape(128, n)
    k = 0
    for l in range(2):
        for h in range(2):
            put("ffn_norm%d" % k, fm(inp["ffn_norm"][l, h])); k += 1
    put("mix_norm0", fm(inp["mix_norm"][0])); put("mix_norm1", fm(inp["mix_norm"][1]))
    put("final_norm", fm(inp["final_norm"]))
    put("fox_bf", np.broadcast_to(inp["hyb_fox_bf"][0][None, :], (128, 8)))
    cw = inp["hyb_conv"][0]
    put("conv", cw.reshape(4, 12, 128).transpose(2, 1, 0).reshape(128, 48))
    put("a_log", np.broadcast_to(inp["hyb_a_log"][0][None, :], (128, 4)))
    put("dt_bias", np.broadcast_to(inp["hyb_dt_bias"][0][None, :], (128, 4)))
    put("o_gain", inp["hyb_o_gain"][0].reshape(128, 1))
    put("mu", inp["rwkv_mu"][0].reshape(6, 8, 128).transpose(2, 0, 1).reshape(128, 48))
    for nm in ("w0", "a0", "k_k", "k_a", "ln_w", "ln_b"):
        put(nm, fm(inp["rwkv_" + nm][0]))
    put("r_k", fm(inp["rwkv_r_k"][0].reshape(-1)))
    idx = np.arange(128)
    put("cmean", np.full((128, 128), 1.0 / 1024))
    put("ident", np.eye(128))
    put("m_su", (idx[:, None] < idx[None, :]).astype(np.float32))
    put("m_ui", (idx[:, None] <= idx[None, :]).astype(np.float32))
    put("m_sl", (idx[:, None] > idx[None, :]).astype(np.float32))
    put("ones", np.ones((128, 128)))
    put("blk64", ((idx[:, None] // 64) == (idx[None, :] // 64)).astype(np.float32))
    put("triu", (idx[:, None] <= idx[None, :]).astype(np.float32))
    bdm = lambda b: ((idx[:, None] // b) == (idx[None, :] // b)).astype(np.float32)
    put("bd32", bdm(32)); put("off64", bdm(64) - bdm(32)); put("off128", bdm(128) - bdm(64))
    s = np.zeros((128, 64), np.float32); s[64, :] = 1.0
    put("sel65", s)
    return pv


WNAMES = [("ffn_w_in0", D, 2 * DFF), ("ffn_w_in1", D, 2 * DFF), ("ffn_w_in2", D, 2 * DFF), ("ffn_w_in3", D, 2 * DFF),
          ("ffn_w_out0", DFF, D), ("ffn_w_out1", DFF, D), ("ffn_w_out2", DFF, D), ("ffn_w_out3", DFF, D),
          ("hyb_w_in", D, HYB_IN), ("hyb_w_out", D, D),
          ("rwkv_w_r", D, D), ("rwkv_w_k", D, D), ("rwkv_w_v", D, D), ("rwkv_w_o", D, D),
          ("rwkv_w1", D, 64), ("rwkv_w2", 64, D), ("rwkv_a1", D, 64), ("rwkv_a2", 64, D),
          ("rwkv_g1", D, 160), ("rwkv_g2", 160, D)]


DEBUG_SCR = False


def build(T, stop_after=None, dbg=(), only=None):
    NCH = T // 128
    tiles = []
    t0 = 0
    while t0 < T:
        n = min(512, T - t0)
        tiles.append((t0, n)); t0 += n
    nc = bass.Bass("TRN2", target_bir_lowering=False)
    hT0 = nc.dram_tensor("hT0", [D, T], F32, kind="ExternalInput").ap()
    pvec = nc.dram_tensor("pvec", [128, NPV], F32, kind="ExternalInput").ap()
    Wf = {nm: nc.dram_tensor(nm, [r, c], F32, kind="ExternalInput").ap() for (nm, r, c) in WNAMES}
    outT = nc.dram_tensor("outT", [D, T], F32, kind="ExternalOutput").ap()
    Wb = {nm: nc.dram_tensor(nm + "_b", [r, c], BF16, kind="Internal").ap() for (nm, r, c) in WNAMES}
    scr = {}
    def dscr(name, shape, dt):
        scr[name] = nc.dram_tensor("scr_" + name, shape, dt, kind=("ExternalOutput" if DEBUG_SCR else "Internal")).ap()
        return scr[name]
    hT = dscr("hT", [D, T], F32)
    fqT = dscr("fqT", [512, T], BF16); fkT = dscr("fkT", [512, T], BF16)
    fV = dscr("fV", [T, 520], BF16); flogf = dscr("flogf", [T, 8], F32)
    gS = {k: dscr("g_" + k, [512, T], BF16) for k in ("r", "k", "a", "b", "v", "z")}
    gS["lw"] = dscr("g_lw", [512, T], F32)
    oT = dscr("oT", [D, T], BF16)
    rS = {k: dscr("r_" + k, [D, T], BF16) for k in ("r", "k", "a", "b", "v", "bonus", "g")}
    rS["lw"] = dscr("r_lw", [D, T], F32)
    zT = dscr("zT", [D, T], BF16)
    dbg_out = {}
    for nm, shape in dbg:
        dbg_out[nm] = nc.dram_tensor("dbg_" + nm, shape, F32, kind="ExternalOutput").ap()

    with contextlib.ExitStack() as gst:
        P = Prog(nc, gst)
        R = {}
        def res(name):
            if name not in R:
                R[name] = Res(name)
            return R[name]

        for (nm, r, c) in WNAMES:
            c0 = 0
            while c0 < c:
                cw = min(2048, c - c0)
                P.dma("gpsimd", Wb[nm][:, c0:c0 + cw], Wf[nm][:, c0:c0 + cw], writes=[res("W_" + nm)])
                c0 += cw

        pv = gst.enter_context(nc.sbuf_tensor("pv", [128, NPV], F32)); r_pv = res("pv")
        P.dma("sync", pv[:], pvec, writes=[r_pv])
        def pvs(name, a=0, n=None):
            o, nn = PV_SLOTS[name]
            n = nn - a if n is None else n
            return pv[:, o + a:o + a + n]
        cb = {}
        for nm in ("cmean", "ident", "ones", "blk64", "bd32", "off64", "off128"):
            cb[nm] = gst.enter_context(nc.sbuf_tensor("cb_" + nm, [128, 128], BF16))
            P.V(lambda e, nm=nm: e.tensor_copy(out=cb[nm][:], in_=pvs(nm)), reads=[r_pv], writes=[res("cb")])
        m_ui_b = gst.enter_context(nc.sbuf_tensor("m_ui_b", [128, 128], BF16))
        P.V(lambda e: e.tensor_copy(out=m_ui_b[:], in_=pvs("m_ui")), reads=[r_pv], writes=[res("cb")])
        cvec = gst.enter_context(nc.sbuf_tensor("cvec", [128, 4], F32))
        P.V(lambda e: e.memset(cvec[:, 0:1], EPS), writes=[res("cb")])
        P.V(lambda e: e.memset(cvec[:, 1:2], 1.0), writes=[res("cb")])
        P.V(lambda e: e.memset(cvec[:, 2:3], GN_EPS), writes=[res("cb")])
        P.V(lambda e: e.memset(cvec[:, 3:4], 0.0), writes=[res("cb")])
        r_cb = res("cb")
        P.barrier()

        hview = lambda ap: ap.rearrange("(kc p) t -> p kc t", p=128)
        wview = lambda ap: ap.rearrange("(kc p) f -> p kc f", p=128)

        class TokCtx:
            pass

        def make_tok_ctx(st, nwi=4, nwo=4):
            c = TokCtx()
            sb = lambda name, shape, dt: st.enter_context(nc.sbuf_tensor(un(name), shape, dt))
            psm = lambda name, shape, dt: st.enter_context(nc.psum_tensor(un(name), shape, dt))
            c.hT = [sb("hT%d" % i, [128, 8, 512], F32) for i in range(2)]
            c.sq = sb("sq", [128, 8, 512], BF16)
            c.rstd = sb("rstd", [128, 512], F32)
            c.hn = sb("hn", [128, 8, 513], BF16)
            c.act = sb("act", [128, 22, 512], BF16)
            c.sg = [sb("sg%d" % i, [128, 512], F32) for i in range(2)]
            c.wi = [sb("wi%d" % i, [128, 8, 512], BF16) for i in range(nwi)]
            c.wo = [sb("wo%d" % i, [128, 4, 512], BF16) for i in range(nwo)]
            c.pt = [psm("pt%d" % i, [128, 512], F32) for i in range(4)]
            c.py = [psm("py%d" % i, [128, 512], F32) for i in range(4)]
            c.cnt = {"wi": 0, "wo": 0, "pt": 0, "sg": 0}
            return c

        def next_pt(c):
            i = c.cnt["pt"] % 4; c.cnt["pt"] += 1
            return i

        def sumsq_bc(c, src_ap_fn, nk, N, r_src, lhs_name="cmean"):
            pi = next_pt(c)
            for k in range(nk):
                P.T(lambda e, pi=pi, k=k: e.matmul(c.pt[pi][:, :N], lhsT=cb[lhs_name][:], rhs=src_ap_fn(k),
                                                    start=(k == 0), stop=(k == nk - 1)),
                    reads=[r_cb] + r_src, writes=[res("pt%d" % pi)])
            return pi

        def rsqrt_from_psum(c, pi, N, dst, r_dst, eps_col=0):
            P.S(lambda e: e.activation(out=dst, in_=c.pt[pi][:, :N], func=AF.Sqrt, bias=cvec[:, eps_col:eps_col + 1]),
                reads=[res("pt%d" % pi), r_cb], writes=[r_dst])
            P.V(lambda e: e.reciprocal(out=dst, in_=dst), reads=[r_dst], writes=[r_dst])

        def rmsnorm(c, h_t, r_h, N, gain_name):
            P.S(lambda e: e.activation(out=c.sq[:, :, :N], in_=h_t[:, :, :N], func=AF.Square), reads=[r_h], writes=[res("sq")])
            pi = sumsq_bc(c, lambda k: c.sq[:, k, :N], 8, N, [res("sq")])
            rsqrt_from_psum(c, pi, N, c.rstd[:, :N], res("rstd"))
            for kc in range(8):
                eng = "vector" if kc % 2 == 0 else "gpsimd"
                P.op("vector", lambda e, kc=kc: e.scalar_tensor_tensor(
                    out=c.hn[:, kc, 1:1 + N], in0=h_t[:, kc, :N], scalar=pvs(gain_name, kc, 1), in1=c.rstd[:, :N],
                    op0=ALU.mult, op1=ALU.mult), reads=[r_h, r_pv, res("rstd")], writes=[res("hn")])

        def lin_fm(c, x_fn, r_x, N, wname, col0, ncols, consume, kchunks=8):
            wv = wview(Wb[wname])
            c0 = 0
            while c0 < ncols:
                cw = min(512, ncols - c0)
                bi = c.cnt["wi"] % len(c.wi); c.cnt["wi"] += 1
                P.dma("sync", c.wi[bi][:, :kchunks, :cw], wv[:, :, col0 + c0:col0 + c0 + cw],
                      reads=[res("W_" + wname)], writes=[res("wi%d" % bi)])
                f0 = 0
                while f0 < cw:
                    m = min(128, cw - f0)
                    pi = next_pt(c)
                    for kc in range(kchunks):
                        P.T(lambda e, pi=pi, bi=bi, kc=kc, f0=f0, m=m: e.matmul(
                            c.pt[pi][:m, :N], lhsT=c.wi[bi][:, kc, f0:f0 + m], rhs=x_fn(kc),
                            start=(kc == 0), stop=(kc == kchunks - 1)),
                            reads=[res("wi%d" % bi)] + r_x, writes=[res("pt%d" % pi)])
                    consume((c0 + f0) // 128, pi, m)
                    f0 += m
                c0 += cw

        def ffn(c, h_t, r_h, N, idx):
            rmsnorm(c, h_t, r_h, N, "ffn_norm%d" % idx)
            wn_in, wn_out = "ffn_w_in%d" % idx, "ffn_w_out%d" % idx
            wv = wview(Wb[wn_in])
            for pc in range(6):
                cw = 512 if pc < 5 else 256
                bufs = []
                for which in range(2):
                    bi = c.cnt["wi"] % len(c.wi); c.cnt["wi"] += 1
                    cc0 = which * DFF + pc * 512
                    P.dma("sync", c.wi[bi][:, :, :cw], wv[:, :, cc0:cc0 + cw], reads=[res("W_" + wn_in)], writes=[res("wi%d" % bi)])
                    bufs.append(bi)
                for fl in range(cw // 128):
                    fc = pc * 4 + fl
                    pis = []
                    for which in range(2):
                        pi = next_pt(c); pis.append(pi)
                        bi = bufs[which]
                        for kc in range(8):
                            P.T(lambda e, pi=pi, bi=bi, kc=kc, fl=fl: e.matmul(
                                c.pt[pi][:, :N], lhsT=c.wi[bi][:, kc, fl * 128:(fl + 1) * 128], rhs=c.hn[:, kc, 1:1 + N],
                                start=(kc == 0), stop=(kc == 7)), reads=[res("wi%d" % bi), res("hn")], writes=[res("pt%d" % pi)])
                    si = c.cnt["sg"] % 2; c.cnt["sg"] += 1
                    P.S(lambda e, si=si, pi=pis[0]: e.activation(out=c.sg[si][:, :N], in_=c.pt[pi][:, :N], func=AF.Silu),
                        reads=[res("pt%d" % pis[0])], writes=[res("sg%d" % si)])
                    P.V(lambda e, si=si, pi=pis[1], fc=fc: e.tensor_tensor(
                        out=c.act[:, fc, :N], in0=c.sg[si][:, :N], in1=c.pt[pi][:, :N], op=ALU.mult),
                        reads=[res("sg%d" % si), res("pt%d" % pis[1])], writes=[res("act")])
            wvo = Wb[wn_out].rearrange("(fc p) d -> p fc d", p=128)
            for half in range(2):
                for g in range(6):
                    nf = 4 if g < 5 else 2
                    bi = c.cnt["wo"] % len(c.wo); c.cnt["wo"] += 1
                    P.dma("sync", c.wo[bi][:, :nf, :], wvo[:, g * 4:g * 4 + nf, half * 512:(half + 1) * 512],
                          reads=[res("W_" + wn_out)], writes=[res("wo%d" % bi)])
                    for fl in range(nf):
                        fc = g * 4 + fl
                        for dq in range(4):
                            P.T(lambda e, bi=bi, fl=fl, fc=fc, dq=dq: e.matmul(
                                c.py[dq][:, :N], lhsT=c.wo[bi][:, fl, dq * 128:(dq + 1) * 128], rhs=c.act[:, fc, :N],
                                start=(fc == 0), stop=(fc == 21)), reads=[res("wo%d" % bi), res("act")], writes=[res("py%d" % dq)])
                for dq in range(4):
                    kc = half * 4 + dq
                    P.V(lambda e, dq=dq, kc=kc: e.scalar_tensor_tensor(
                        out=h_t[:, kc, :N], in0=c.py[dq][:, :N], scalar=0.5, in1=h_t[:, kc, :N],
                        op0=ALU.mult, op1=ALU.add), reads=[res("py%d" % dq), r_h], writes=[r_h])

        def proj_residual(c, h_t, r_h, N, x_t, r_x, wname):
            def consume(fc, pi, m):
                P.V(lambda e: e.tensor_tensor(out=h_t[:, fc, :N], in0=c.pt[pi][:, :N], in1=h_t[:, fc, :N], op=ALU.add),
                    reads=[res("pt%d" % pi), r_h], writes=[r_h])
            lin_fm(c, lambda kc: x_t[:, kc, :N], [r_x], N, wname, 0, D, consume)

        with contextlib.ExitStack() as st:
          if only is None or "A" in only:
            sb = lambda name, shape, dt: st.enter_context(nc.sbuf_tensor(un(name), shape, dt))
            c = make_tok_ctx(st)
            wtm = sb("wtm", [128, 8, 520], BF16)
            wrep = sb("wrep", [128, 8, 8, 128], BF16)
            wsm = sb("wsm", [128, 8, 8], F32)
            P.dma("sync", wtm[:], wview(Wb["hyb_w_in"])[:, :, 1024:1544], reads=[res("W_hyb_w_in")], writes=[res("wtm")])
            P.dma("sync", wsm[:], wview(Wf["hyb_w_in"])[:, :, 3080:3088], writes=[res("wsm")])
            for kc in range(8):
                for j in range(8):
                    eng = "vector" if (kc + j) % 2 == 0 else "gpsimd"
                    P.op(eng, lambda e, kc=kc, j=j: e.tensor_scalar(out=wrep[:, kc, j, :], in0=cb["ones"][:], scalar1=wsm[:, kc, j:j + 1],
                                                                     scalar2=None, op0=ALU.mult), reads=[res("wsm"), r_cb], writes=[res("wrep")])
            negA = sb("negA", [128, 4], F32)
            P.S(lambda e: e.activation(out=negA[:], in_=pvs("a_log"), func=AF.Exp), reads=[r_pv], writes=[res("negA")])
            P.V(lambda e: e.tensor_scalar(out=negA[:], in0=negA[:], scalar1=-1.0, scalar2=None, op0=ALU.mult), reads=[res("negA")], writes=[res("negA")])
            xgs = [sb("xg%d" % i, [128, 515], F32) for i in range(2)]
            halo = sb("halo", [128, 12, 3], F32)
            P.V(lambda e: e.memset(halo[:], 0.0), writes=[res("halo")])
            xgc = [0]
            cacc = sb("cacc", [128, 512], F32)
            qk = [sb("qk%d" % i, [128, 512], F32) for i in range(2)]
            kn = sb("kn", [128, 512], F32)
            gb_ = sb("gbeta", [128, 4, 512], F32)
            gg_ = sb("gg", [128, 4, 512], F32)
            glw = sb("glw", [128, 4, 512], F32)
            ob = [sb("ob%d" % i, [128, 512], BF16) for i in range(4)]
            vtm = [sb("vtm%d" % i, [128, 8, 65], BF16) for i in range(2)]
            for i in range(2):
                P.V(lambda e, i=i: e.memset(vtm[i][:], 1.0), writes=[res("vtm%d" % i)])
            lft = [sb("lft%d" % i, [128, 8], F32) for i in range(2)]
            ocnt = [0]
            print("phase A sbuf remaining", nc.sbuf_bytes_remaining)

            def out_bf(dst_ap, make, reads):
                i = ocnt[0] % 4; ocnt[0] += 1
                make(ob[i], res("ob%d" % i))
                P.dma("sync", dst_ap, ob[i][:, :dst_ap.shape[-1]], reads=[res("ob%d" % i)], writes=[res("scrA")])

            for ti, (t0, N) in enumerate(tiles):
                h_t, r_h = c.hT[ti % 2], res("hT%d" % (ti % 2))
                P.dma("sync", h_t[:, :, :N], hview(hT0)[:, :, t0:t0 + N], writes=[r_h])
                ffn(c, h_t, r_h, N, 0)
                P.dma("sync", hview(hT)[:, :, t0:t0 + N], h_t[:, :, :N], reads=[r_h], writes=[res("hT_dram")])
                rmsnorm(c, h_t, r_h, N, "mix_norm0")
                xfn = lambda kc: c.hn[:, kc, 1:1 + N]
                rx = [res("hn")]
                def cons_q(fc, pi, m):
                    out_bf(fqT[fc * 128:(fc + 1) * 128, t0:t0 + N],
                           lambda o, ro: P.S(lambda e: e.activation(out=o[:, :N], in_=c.pt[pi][:, :N], func=AF.Copy, scale=0.125),
                                             reads=[res("pt%d" % pi)], writes=[ro]), None)
                lin_fm(c, xfn, rx, N, "hyb_w_in", 0, 512, cons_q)
                def cons_k(fc, pi, m):
                    out_bf(fkT[fc * 128:(fc + 1) * 128, t0:t0 + N],
                           lambda o, ro: P.V(lambda e: e.tensor_copy(out=o[:, :N], in_=c.pt[pi][:, :N]),
                                             reads=[res("pt%d" % pi)], writes=[ro]), None)
                lin_fm(c, xfn, rx, N, "hyb_w_in", 512, 512, cons_k)
                for tb in range(N // 128):
                    vi = (ti * 4 + tb) % 2
                    pi = next_pt(c)
                    for kc in range(8):
                        P.T(lambda e, pi=pi, kc=kc, tb=tb: e.matmul(c.pt[pi][:, :512], lhsT=c.hn[:, kc, 1 + tb * 128:1 + (tb + 1) * 128],
                                                                  rhs=wtm[:, kc, 0:512], start=(kc == 0), stop=(kc == 7)),
                            reads=[res("hn"), res("wtm")], writes=[res("pt%d" % pi)])
                    P.S(lambda e, pi=pi, vi=vi: e.activation(out=vtm[vi][:, :, 0:64], in_=c.pt[pi][:, :512].rearrange("p (h d) -> p h d", h=8), func=AF.Copy),
                        reads=[res("pt%d" % pi)], writes=[res("vtm%d" % vi)])
                    P.dma("sync", fV[t0 + tb * 128:t0 + (tb + 1) * 128, :], vtm[vi][:].rearrange("p h d -> p (h d)"),
                          reads=[res("vtm%d" % vi)], writes=[res("scrA")])
                    pi2 = next_pt(c)
                    for kc in range(8):
                        P.T(lambda e, pi2=pi2, kc=kc, tb=tb: e.matmul(c.pt[pi2][:, :8], lhsT=c.hn[:, kc, 1 + tb * 128:1 + (tb + 1) * 128],
                                                                    rhs=wtm[:, kc, 512:520], start=(kc == 0), stop=(kc == 7)),
                            reads=[res("hn"), res("wtm")], writes=[res("pt%d" % pi2)])
                    P.V(lambda e, pi2=pi2, vi=vi: e.tensor_tensor(out=lft[vi][:], in0=c.pt[pi2][:, :8], in1=pvs("fox_bf"), op=ALU.add),
                        reads=[res("pt%d" % pi2), r_pv], writes=[res("lft%d" % vi)])
                    P.S(lambda e, vi=vi: e.activation(out=lft[vi][:], in_=lft[vi][:], func=AF.Sigmoid), reads=[res("lft%d" % vi)], writes=[res("lft%d" % vi)])
                    P.S(lambda e, vi=vi: e.activation(out=lft[vi][:], in_=lft[vi][:], func=AF.Ln), reads=[res("lft%d" % vi)], writes=[res("lft%d" % vi)])
                    P.dma("sync", flogf[t0 + tb * 128:t0 + (tb + 1) * 128, :], lft[vi][:], reads=[res("lft%d" % vi)], writes=[res("scrA")])
                for j in range(8):
                    pi = next_pt(c)
                    for kc in range(8):
                        P.T(lambda e, pi=pi, kc=kc, j=j: e.matmul(c.pt[pi][:, :N], lhsT=wrep[:, kc, j, :], rhs=c.hn[:, kc, 1:1 + N],
                                                               start=(kc == 0), stop=(kc == 7)),
                            reads=[res("wrep"), res("hn")], writes=[res("pt%d" % pi)])
                    if j < 4:
                        P.S(lambda e, pi=pi, j=j: e.activation(out=glw[:, j, :N], in_=c.pt[pi][:, :N], func=AF.Exp, bias=pvs("dt_bias", j, 1)),
                            reads=[res("pt%d" % pi), r_pv], writes=[res("glw")])
                        P.S(lambda e, j=j: e.activation(out=glw[:, j, :N], in_=glw[:, j, :N], func=AF.Ln, bias=cvec[:, 1:2]),
                            reads=[res("glw"), r_cb], writes=[res("glw")])
                        P.V(lambda e, j=j: e.tensor_scalar(out=glw[:, j, :N], in0=glw[:, j, :N], scalar1=negA[:, j:j + 1], scalar2=None, op0=ALU.mult),
                            reads=[res("glw"), res("negA")], writes=[res("glw")])
                        P.S(lambda e, j=j: e.activation(out=gg_[:, j, :N], in_=glw[:, j, :N], func=AF.Exp), reads=[res("glw")], writes=[res("gg")])
                        P.dma("sync", gS["lw"][j * 128:(j + 1) * 128, t0:t0 + N], glw[:, j, :N], reads=[res("glw")], writes=[res("scrA")])
                    else:
                        P.S(lambda e, pi=pi, j=j: e.activation(out=gb_[:, j - 4, :N], in_=c.pt[pi][:, :N], func=AF.Sigmoid),
                            reads=[res("pt%d" % pi)], writes=[res("gbeta")])
                def cons_g(fc, pi, m):
                    xi = xgc[0] % 2; xgc[0] += 1
                    xg, rxg = xgs[xi], res("xg%d" % xi)
                    P.G(lambda e: e.tensor_copy(out=xg[:, 0:3], in_=halo[:, fc, :]), reads=[res("halo")], writes=[rxg])
                    P.S(lambda e: e.activation(out=xg[:, 3:3 + N], in_=c.pt[pi][:, :N], func=AF.Copy), reads=[res("pt%d" % pi)], writes=[rxg])
                    o_, _ = PV_SLOTS["conv"]
                    cwp = lambda j: pv[:, o_ + fc * 4 + j:o_ + fc * 4 + j + 1]
                    P.V(lambda e: e.tensor_scalar(out=cacc[:, :N], in0=xg[:, 0:N], scalar1=cwp(0), scalar2=None, op0=ALU.mult),
                        reads=[rxg, r_pv], writes=[res("cacc")])
                    for j in range(1, 4):
                        P.V(lambda e, j=j: e.scalar_tensor_tensor(out=cacc[:, :N], in0=xg[:, j:j + N], scalar=cwp(j), in1=cacc[:, :N],
                                                               op0=ALU.mult, op1=ALU.add), reads=[rxg, r_pv, res("cacc")], writes=[res("cacc")])
                    P.G(lambda e: e.tensor_copy(out=halo[:, fc, :], in_=xg[:, N:N + 3]), reads=[rxg], writes=[res("halo")])
                    kind, hh = fc // 4, fc % 4
                    if kind == 2:
                        out_bf(gS["v"][hh * 128:(hh + 1) * 128, t0:t0 + N],
                               lambda o, ro: P.S(lambda e: e.activation(out=o[:, :N], in_=cacc[:, :N], func=AF.Silu), reads=[res("cacc")], writes=[ro]), None)
                        return
                    qq = qk[kind]; rq = res("qk%d" % kind)
                    P.S(lambda e: e.activation(out=qq[:, :N], in_=cacc[:, :N], func=AF.Silu), reads=[res("cacc")], writes=[rq])
                    P.G(lambda e: e.tensor_tensor(out=c.sq[:, 0, :N], in0=qq[:, :N], in1=qq[:, :N], op=ALU.mult), reads=[rq], writes=[res("sq")])
                    pj = sumsq_bc(c, lambda k: c.sq[:, 0, :N], 1, N, [res("sq")], lhs_name="ones")
                    rsqrt_from_psum(c, pj, N, c.rstd[:, :N], res("rstd"))
                    if kind == 0:
                        out_bf(gS["r"][hh * 128:(hh + 1) * 128, t0:t0 + N],
                               lambda o, ro: P.V(lambda e: e.scalar_tensor_tensor(out=o[:, :N], in0=qq[:, :N], scalar=128.0 ** -0.5, in1=c.rstd[:, :N],
                                                                                 op0=ALU.mult, op1=ALU.mult), reads=[rq, res("rstd")], writes=[ro]), None)
                    else:
                        P.V(lambda e: e.tensor_tensor(out=kn[:, :N], in0=qq[:, :N], in1=c.rstd[:, :N], op=ALU.mult), reads=[rq, res("rstd")], writes=[res("kn")])
                        out_bf(gS["a"][hh * 128:(hh + 1) * 128, t0:t0 + N],
                               lambda o, ro: P.G(lambda e: e.tensor_copy(out=o[:, :N], in_=kn[:, :N]), reads=[res("kn")], writes=[ro]), None)
                        P.V(lambda e: e.tensor_tensor(out=kn[:, :N], in0=kn[:, :N], in1=gb_[:, hh, :N], op=ALU.mult), reads=[res("kn"), res("gbeta")], writes=[res("kn")])
                        out_bf(gS["k"][hh * 128:(hh + 1) * 128, t0:t0 + N],
                               lambda o, ro: P.G(lambda e: e.tensor_copy(out=o[:, :N], in_=kn[:, :N]), reads=[res("kn")], writes=[ro]), None)
                        out_bf(gS["b"][hh * 128:(hh + 1) * 128, t0:t0 + N],
                               lambda o, ro: P.V(lambda e: e.scalar_tensor_tensor(out=o[:, :N], in0=kn[:, :N], scalar=-1.0, in1=gg_[:, hh, :N],
                                                                                 op0=ALU.mult, op1=ALU.mult), reads=[res("kn"), res("gg")], writes=[ro]), None)
                lin_fm(c, xfn, rx, N, "hyb_w_in", 1544, 1536, cons_g)
                def cons_z(fc, pi, m):
                    out_bf(gS["z"][fc * 128:(fc + 1) * 128, t0:t0 + N],
                           lambda o, ro: P.S(lambda e: e.activation(out=o[:, :N], in_=c.pt[pi][:, :N], func=AF.Silu), reads=[res("pt%d" % pi)], writes=[ro]), None)
                lin_fm(c, xfn, rx, N, "hyb_w_in", 3088, 512, cons_z)
        P.barrier()
        if stop_after == "A":
            P.finish(); return nc

        with contextlib.ExitStack() as st:
          if only is None or "B" in only:
            sb = lambda name, shape, dt: st.enter_context(nc.sbuf_tensor(un(name), shape, dt))
            psm = lambda name, shape, dt: st.enter_context(nc.psum_tensor(un(name), shape, dt))
            Vall = sb("Vall", [128, NCH, 584], BF16)
            P.V(lambda e: e.memset(Vall[:, :, 520:584], 0.0), writes=[res("Vall")])
            P.dma("sync", Vall[:, :, 0:520], fV.rearrange("(c p) f -> p c f", p=128), reads=[res("scrA")], writes=[res("Vall")])
            lf = sb("lf", [128, NCH, 8], F32)
            P.dma("sync", lf[:], flogf.rearrange("(c p) h -> p c h", p=128), reads=[res("scrA")], writes=[res("lf")])
            negc = sb("negc", [128, 8, NCH], F32)
            pe = sb("pe", [128, 8, NCH + 1], F32)
            lff = lf[:].rearrange("p c h -> p (c h)")
            ps_s = [psm("ps_s%d" % i, [128, 512], F32) for i in range(2)]
            ps_o = [psm("ps_o%d" % i, [128, 512], F32) for i in range(2)]
            ps_b = psm("ps_b", [128, 512], F32)
            ncol = NCH * 8
            ctmp = sb("ctmp", [128, NCH, 8], F32)
            ctot = sb("ctot", [128, NCH, 8], F32)
            cf = ctmp[:].rearrange("p c h -> p (c h)")
            tf = ctot[:].rearrange("p c h -> p (c h)")
            c0 = 0
            while c0 < ncol:
                cw = min(512, ncol - c0)
                P.T(lambda e, c0=c0, cw=cw: e.matmul(ps_s[0][:, :cw], lhsT=pvs("triu"), rhs=lff[:, c0:c0 + cw], start=True, stop=True),
                    reads=[r_pv, res("lf")], writes=[res("ps_s0")])
                P.V(lambda e, c0=c0, cw=cw: e.tensor_copy(out=cf[:, c0:c0 + cw], in_=ps_s[0][:, :cw]), reads=[res("ps_s0")], writes=[res("ctmp")])
                P.T(lambda e, c0=c0, cw=cw: e.matmul(ps_s[1][:, :cw], lhsT=pvs("ones"), rhs=lff[:, c0:c0 + cw], start=True, stop=True),
                    reads=[r_pv, res("lf")], writes=[res("ps_s1")])
                P.V(lambda e, c0=c0, cw=cw: e.tensor_copy(out=tf[:, c0:c0 + cw], in_=ps_s[1][:, :cw]), reads=[res("ps_s1")], writes=[res("ctot")])
                c0 += cw
            P.V(lambda e: e.memset(pe[:, :, 0:1], 0.0), writes=[res("pe")])
            for h in range(8):
                P.V(lambda e, h=h: e.tensor_tensor_scan(out=pe[:, h, 1:NCH + 1], data0=pvs("ones")[:, 0:NCH], data1=ctot[:, :, h],
                                                         initial=0.0, op0=ALU.mult, op1=ALU.add),
                    reads=[r_pv, res("ctot")], writes=[res("pe")])
                P.V(lambda e, h=h: e.tensor_tensor(out=negc[:, h, :], in0=ctmp[:, :, h], in1=pe[:, h, 0:NCH], op=ALU.add),
                    reads=[res("ctmp"), res("pe")], writes=[res("negc")])
                P.V(lambda e, h=h: e.tensor_scalar(out=negc[:, h, :], in0=negc[:, h, :], scalar1=-1.0, scalar2=None, op0=ALU.mult),
                    reads=[res("negc")], writes=[res("negc")])
            kTs = [sb("kTs%d" % i, [64, T], BF16) for i in range(2)]
            qTs = [sb("qTs%d" % i, [64, T], BF16) for i in range(2)]
            biasg = [sb("biasg%d" % i, [128, NCH], F32) for i in range(2)]
            pT = [sb("pT%d" % i, [128, 512], BF16) for i in range(3)]
            osb = [sb("osb%d" % i, [128, 512], F32) for i in range(2)]
            for i in range(2):
                P.V(lambda e, i=i: e.memset(osb[i][:], 0.0), writes=[res("osb%d" % i)])
            rinv = sb("rinv", [64, 512], F32)
            oout = [sb("oout%d" % i, [64, 512], BF16) for i in range(2)]
            cntB = {"s": 0, "p": 0, "g": 0}
            for h in range(8):
                hb = h % 2
                P.dma("sync", kTs[hb][:], fkT[h * 64:(h + 1) * 64, :], reads=[res("scrA")], writes=[res("kTs%d" % hb)])
                P.dma("sync", qTs[hb][:], fqT[h * 64:(h + 1) * 64, :], reads=[res("scrA")], writes=[res("qTs%d" % hb)])
                for (t0, N) in tiles:
                    gi = cntB["g"] % 2; cntB["g"] += 1
                    i0 = t0 // 128; nb = N // 128
                    anc = min(i0 + 2, NCH)
                    J = i0 + nb
                    P.V(lambda e, gi=gi, anc=anc, J=J, h=h: e.tensor_scalar(out=biasg[gi][:, :J], in0=negc[:, h, :J], scalar1=pe[:, h, anc:anc + 1],
                                                                             scalar2=None, op0=ALU.add),
                        reads=[res("negc"), res("pe")], writes=[res("biasg%d" % gi)])
                    pend = None
                    for j in range(J + 1):
                        if j < J:
                            cs = 0 if j < i0 else (j - i0) * 128
                            ncols = N - cs
                            si = cntB["s"] % 2; cntB["s"] += 1
                            pi = cntB["p"] % 3; cntB["p"] += 1
                            P.T(lambda e: e.matmul(ps_s[si][:, :ncols], lhsT=kTs[hb][:, j * 128:(j + 1) * 128], rhs=qTs[hb][:, t0 + cs:t0 + cs + ncols], start=True, stop=True),
                                reads=[res("kTs%d" % hb), res("qTs%d" % hb)], writes=[res("ps_s%d" % si)])
                            P.S(lambda e: e.activation(out=pT[pi][:, :ncols], in_=ps_s[si][:, :ncols], func=AF.Exp, bias=biasg[gi][:, j:j + 1]),
                                reads=[res("ps_s%d" % si), res("biasg%d" % gi)], writes=[res("pT%d" % pi)])
                            if j >= i0:
                                P.G(lambda e: e.tensor_tensor(out=pT[pi][:, 0:128], in0=pT[pi][:, 0:128], in1=m_ui_b[:], op=ALU.mult),
                                    reads=[res("pT%d" % pi), r_cb], writes=[res("pT%d" % pi)])
                        if pend is not None:
                            (pj, pcs, pncols, ppi) = pend
                            P.T(lambda e: e.matmul(ps_o[gi][:, pcs:pcs + pncols], lhsT=Vall[:, pj, h * 65:h * 65 + 128], rhs=pT[ppi][:, :pncols],
                                                   start=(pj == 0), stop=(pj == J - 1)), reads=[res("Vall"), res("pT%d" % ppi)], writes=[res("ps_o%d" % gi)])
                        pend = (j, cs, ncols, pi) if j < J else None
                    P.S(lambda e, gi=gi, N=N: e.activation(out=osb[gi][0:65, :N], in_=ps_o[gi][0:65, :N], func=AF.Copy),
                        reads=[res("ps_o%d" % gi)], writes=[res("osb%d" % gi)])
                    o_, _ = PV_SLOTS["sel65"]
                    P.T(lambda e, gi=gi, N=N: e.matmul(ps_b[:64, :N], lhsT=pv[:, o_:o_ + 64], rhs=osb[gi][:, :N], start=True, stop=True),
                        reads=[r_pv, res("osb%d" % gi)], writes=[res("ps_b")])
                    P.V(lambda e, N=N: e.reciprocal(out=rinv[:, :N], in_=ps_b[:64, :N]), reads=[res("ps_b")], writes=[res("rinv")])
                    P.V(lambda e, gi=gi, N=N: e.tensor_tensor(out=oout[gi][:, :N], in0=osb[gi][0:64, :N], in1=rinv[:, :N], op=ALU.mult),
                        reads=[res("osb%d" % gi), res("rinv")], writes=[res("oout%d" % gi)])
                    P.dma("sync", oT[h * 64:(h + 1) * 64, t0:t0 + N], oout[gi][:, :N], reads=[res("oout%d" % gi)], writes=[res("oT_dram")])
        P.barrier()
        if stop_after == "B":
            P.finish(); return nc

        def dplr_phase(name, units, S_, post_setup, post, scalar_decay=False):
            with contextlib.ExitStack() as st:
                sb = lambda nm, shape, dt: st.enter_context(nc.sbuf_tensor(un(nm), shape, dt))
                psm = lambda nm, shape, dt: st.enter_context(nc.psum_tensor(un(nm), shape, dt))
                hd = units[0][0][2]
                ngs = len(set(cx[0] for cx in units[0]))
                SC = 4
                NG2 = 2 * ngs
                inb = {}
                for gs in range(NG2):
                    for k in ("r", "k", "a", "b", "v"):
                        inb[(gs, k)] = sb("in_%s%d" % (k, gs), [128, SC * 128], BF16)
                    inb[(gs, "lw")] = sb("in_lw%d" % gs, [128, SC * 128], F32)
                pobj = post_setup(st, NG2, SC)
                Lc = [sb("Lc%d" % g, [128, 129], F32) for g in range(NG2)]
                Lx = [sb("nLm%d" % g, [128, 1], F32) for g in range(NG2)]
                e1 = [sb("e1_%d" % g, [128, 129], F32) for g in range(NG2)]
                e2 = [sb("e2_%d" % g, [128, 129], F32) for g in range(NG2)]
                e5 = [sb("e5_%d" % g, [128, 128], F32) for g in range(NG2)]
                e6 = [sb("e6_%d" % g, [128, 128], F32) for g in range(NG2)]
                if scalar_decay:
                    Dm = {(g, k): sb("D%s_%d" % (k, g), [128, 128], F32) for g in range(NG2) for k in ("m_su", "m_sl", "m_ui")}
                    lcc = [sb("lcc%d" % g, [128, 2], F32) for g in range(NG2)]
                    dtmp = [sb("dtmp%d" % g, [128, 128], F32) for g in range(NG2)]
                opn = ("rh", "rt", "ah", "at", "bh", "kh", "btT", "ktT")
                ops_ = {(g, k): sb("%s%d" % (k, g), [128, 128], BF16) for g in range(NG2) for k in opn}
                tmj = {(g, k): sb("tm_%s%d" % (k, g), [128, 128], BF16) for g in range(NG2) for k in ("bt", "kt", "vt")}
                pstr = psm("pstr", [128, 4, 128], BF16)
                yT = [sb("yT%d" % g, [128, 128], F32) for g in range(ngs)]
                Sf = [sb("Sf%d" % g, [128, 128], F32) for g in range(ngs)]
                Sb = [sb("Sb%d" % g, [128, 128], BF16) for g in range(ngs)]
                hx = []
                for h in range(2):
                    o = TokCtx()
                    o.psx = [psm("psx%d_%d" % (h, i), [128, 512], F32) for i in range(2)]
                    o.psy = psm("psy%d" % h, [128, 512], F32)
                    o.xc = 0
                    for k in ("M0", "MT0", "Mb0", "Mb1", "MTb0", "MTb1", "P0", "P1", "Q0", "Q1", "Noff", "NoffT", "Z", "Z2", "AakT", "ArbT", "ArkT", "RHS", "U"):
                        setattr(o, k, sb("%s_%d" % (k, h), [128, 128], BF16))
                    hx.append(o)
                rn = lambda s: res(name + "_" + s)
                trc = [0]

                for unit in units:
                    gslots = []
                    for cx in unit:
                        if cx[0] not in gslots:
                            gslots.append(cx[0])
                    for g in range(ngs):
                        P.V(lambda e, g=g: e.memset(Sf[g][:], 0.0), writes=[rn("Sf%d" % g)])
                        P.V(lambda e, g=g: e.memset(Sb[g][:], 0.0), writes=[rn("Sb%d" % g)])
                        P.V(lambda e, g=g: e.memset(Lc[g][:, 0:1], 0.0), writes=[rn("Lc%d" % g)])
                        P.V(lambda e, g=g: e.memset(Lc[ngs + g][:, 0:1], 0.0), writes=[rn("Lc%d" % (ngs + g))])
                    def prologue(c):
                      sc, cl = c // SC, c % SC
                      ipar = (sc % 2) * ngs
                      cpar = (c % 2) * ngs
                      if cl == 0:
                            nsc = min(SC, NCH - c) * 128
                            for g0, grow in enumerate(gslots):
                                g = ipar + g0
                                for k in ("r", "k", "a", "b", "v", "lw"):
                                    P.dma("sync", inb[(g, k)][:, :nsc], S_[k][grow:grow + 128, c * 128:c * 128 + nsc],
                                          reads=[res("scr" + name)], writes=[rn("in_%s%d" % (k, g))])
                                post("load", g, grow, c, nsc, pobj)
                      cols = slice(cl * 128, (cl + 1) * 128)
                      for g0 in range(ngs):
                        g = cpar + g0
                        gi_ = ipar + g0
                        if True:
                            rl = [rn("Lc%d" % g)]
                            P.V(lambda e, g=g, gi_=gi_: e.tensor_tensor_scan(out=Lc[g][:, 1:129], data0=pvs("ones"), data1=inb[(gi_, "lw")][:, cols],
                                                                     initial=0.0, op0=ALU.mult, op1=ALU.add),
                                reads=[r_pv, rn("in_lw%d" % gi_)], writes=rl)
                            P.S(lambda e, g=g: e.activation(out=e2[g][:], in_=Lc[g][:], func=AF.Exp), reads=rl, writes=[rn("e2_%d" % g)])
                            P.S(lambda e, g=g: e.activation(out=e6[g][:], in_=Lc[g][:, 1:129], func=AF.Exp, bias=Lc[g][:, 128:129], scale=-1.0),
                                reads=rl, writes=[rn("e6_%d" % g)])
                            if not scalar_decay:
                                P.V(lambda e, g=g: e.tensor_scalar(out=Lx[g][:], in0=Lc[g][:, 64:65], scalar1=-1.0, scalar2=None, op0=ALU.mult),
                                    reads=rl, writes=[rn("nLm%d" % g)])
                                P.S(lambda e, g=g: e.activation(out=e1[g][:], in_=Lc[g][:], func=AF.Exp, bias=Lx[g][:, 0:1]),
                                    reads=rl + [rn("nLm%d" % g)], writes=[rn("e1_%d" % g)])
                                P.S(lambda e, g=g: e.activation(out=e5[g][:], in_=Lc[g][:, 1:129], func=AF.Exp, bias=Lc[g][:, 64:65], scale=-1.0),
                                    reads=rl, writes=[rn("e5_%d" % g)])
                                specs = [("rh", "r", e1[g][:, 1:129], "e1_"), ("rt", "r", e2[g][:, 1:129], "e2_"),
                                         ("ah", "a", e1[g][:, 0:128], "e1_"), ("at", "a", e2[g][:, 0:128], "e2_"),
                                         ("bh", "b", e5[g][:], "e5_"), ("kh", "k", e5[g][:], "e5_"),
                                         ("btT", "b", e6[g][:], "e6_"), ("ktT", "k", e6[g][:], "e6_")]
                            else:
                                rd = [rn("dtmp%d" % g)]
                                P.V(lambda e, g=g: e.tensor_tensor(out=dtmp[g][:], in0=Lc[g][:, 1:129], in1=pvs("ident"), op=ALU.mult), reads=rl + [r_pv], writes=rd)
                                P.V(lambda e, g=g: e.reduce_sum(out=lcc[g][:, 0:1], in_=dtmp[g][:], axis=mybir.AxisListType.X), reads=rd, writes=[rn("lcc%d" % g)])
                                P.V(lambda e, g=g: e.tensor_tensor(out=dtmp[g][:], in0=Lc[g][:, 0:128], in1=pvs("ident"), op=ALU.mult), reads=rl + [r_pv], writes=rd)
                                P.V(lambda e, g=g: e.reduce_sum(out=lcc[g][:, 1:2], in_=dtmp[g][:], axis=mybir.AxisListType.X), reads=rd, writes=[rn("lcc%d" % g)])
                                rlc = [rn("lcc%d" % g)]
                                for (mk, src, colj, neg) in (("m_ui", Lc[g][:, 1:129], 0, False), ("m_su", Lc[g][:, 0:128], 0, False), ("m_sl", Lc[g][:, 1:129], 1, True)):
                                    D_ = Dm[(g, mk)]; rD = rn("D%s_%d" % (mk, g))
                                    if not neg:
                                        P.V(lambda e, g=g, src=src, colj=colj: e.tensor_scalar(out=dtmp[g][:], in0=src, scalar1=lcc[g][:, colj:colj + 1], scalar2=0.0,
                                                                                                op0=ALU.subtract, op1=ALU.min), reads=rl + rlc, writes=rd)
                                    else:
                                        P.V(lambda e, g=g, src=src, colj=colj: e.tensor_scalar(out=dtmp[g][:], in0=src, scalar1=-1.0, scalar2=lcc[g][:, colj:colj + 1],
                                                                                                op0=ALU.mult, op1=ALU.add), reads=rl + rlc, writes=rd)
                                        P.V(lambda e, g=g: e.tensor_scalar(out=dtmp[g][:], in0=dtmp[g][:], scalar1=0.0, scalar2=None, op0=ALU.min), reads=rd, writes=rd)
                                    P.S(lambda e, g=g, D_=D_: e.activation(out=D_[:], in_=dtmp[g][:], func=AF.Exp), reads=rd, writes=[rD])
                                    P.G(lambda e, D_=D_, mk=mk: e.tensor_tensor(out=D_[:], in0=D_[:], in1=pvs(mk), op=ALU.mult), reads=[rD, r_pv], writes=[rD])
                                specs = [("rt", "r", e2[g][:, 1:129], "e2_"), ("at", "a", e2[g][:, 0:128], "e2_"),
                                         ("btT", "b", e6[g][:], "e6_"), ("ktT", "k", e6[g][:], "e6_")]
                            for qi, (on, ik, eap, en) in enumerate(specs):
                                eng = "vector" if qi % 2 == 0 else "gpsimd"
                                P.op(eng, lambda e, g=g, gi_=gi_, on=on, ik=ik, eap=eap: e.tensor_tensor(out=ops_[(g, on)][:], in0=inb[(gi_, ik)][:, cols], in1=eap, op=ALU.mult),
                                     reads=[rn("in_%s%d" % (ik, gi_)), rn(en + "%d" % g)], writes=[rn("%s%d" % (on, g))])
                            for (tn, src, rsrc) in (("bt", ops_[(g, "btT")][:], rn("btT%d" % g)), ("kt", ops_[(g, "ktT")][:], rn("ktT%d" % g)),
                                                    ("vt", inb[(gi_, "v")][:, cols], rn("in_v%d" % gi_))):
                                ti_ = trc[0] % 4; trc[0] += 1
                                P.T(lambda e, ti_=ti_, src=src: e.transpose(out=pstr[:, ti_, :], in_=src, identity=cb["ident"][:]),
                                    reads=[rsrc, r_cb], writes=[rn("pstr")])
                                P.S(lambda e, ti_=ti_, g=g, tn=tn: e.activation(out=tmj[(g, tn)][:], in_=pstr[:, ti_, :], func=AF.Copy),
                                    reads=[rn("pstr")], writes=[rn("tm_%s%d" % (tn, g))])
                    prologue(0)
                    for c in range(NCH):
                        sc, cl = c // SC, c % SC
                        ipar = (sc % 2) * ngs
                        cpar = (c % 2) * ngs
                        cols = slice(cl * 128, (cl + 1) * 128)
                        H = []
                        for hi, cx in enumerate(unit):
                            g = cpar + gslots.index(cx[0])
                            H.append((hx[hi], g, slice(cx[1], cx[1] + hd), hi))
                        def xslot(o, hi):
                            i = o.xc % 2; o.xc += 1
                            return o.psx[i][:, 0:128], rn("psx%d_%d" % (hi, i))
                        def opr(g, k, ps_):
                            if scalar_decay and k in ("rh", "ah", "bh", "kh"):
                                gi_ = ipar + (g - cpar)
                                return inb[(gi_, k[0])][ps_, cols], rn("in_%s%d" % (k[0], gi_))
                            return ops_[(g, k)][ps_, :], rn("%s%d" % (k, g))
                        for (o, g, ps_, hi) in H:
                            for (dst, l, r_, mask) in (("MT0", "bh", "ah", "m_su"), ("M0", "ah", "bh", "m_sl"), ("AakT", "kh", "ah", "m_su"),
                                                       ("ArbT", "bh", "rh", "m_ui"), ("ArkT", "kh", "rh", "m_ui")):
                                xa, rx_ = xslot(o, hi)
                                la, rl_ = opr(g, l, ps_); ra, rr_ = opr(g, r_, ps_)
                                P.T(lambda e, xa=xa, la=la, ra=ra: e.matmul(xa, lhsT=la, rhs=ra, start=True, stop=True), reads=[rl_, rr_], writes=[rx_])
                                mk_ap = Dm[(g, mask)][:] if scalar_decay else pvs(mask)
                                mk_r = rn("D%s_%d" % (mask, g)) if scalar_decay else r_pv
                                P.V(lambda e, o=o, dst=dst, xa=xa, mk_ap=mk_ap: e.tensor_tensor(out=getattr(o, dst)[:], in0=xa, in1=mk_ap, op=ALU.mult),
                                    reads=[rx_, mk_r], writes=[rn("%s_%d" % (dst, hi))])
                        def mm_evac(o, hi, lhs, rlhs, rhs, rrhs, dst, rdst, add=None, radd=None):
                            xa, rx_ = xslot(o, hi)
                            P.T(lambda e: e.matmul(xa, lhsT=lhs[:], rhs=rhs[:], start=True, stop=True), reads=[rlhs, rrhs], writes=[rx_])
                            if add is None:
                                P.S(lambda e: e.activation(out=dst[:], in_=xa, func=AF.Copy), reads=[rx_], writes=[rdst])
                            else:
                                P.V(lambda e: e.tensor_tensor(out=dst[:], in0=xa, in1=add[:], op=ALU.add), reads=[rx_, radd], writes=[rdst])
                        for (o, g, ps_, hi) in H:
                            R_ = lambda k: rn("%s_%d" % (k, hi))
                            P.G(lambda e, o=o: e.tensor_tensor(out=o.Mb0[:], in0=o.M0[:], in1=cb["bd32"][:], op=ALU.mult), reads=[R_("M0"), r_cb], writes=[R_("Mb0")])
                            P.G(lambda e, o=o: e.tensor_tensor(out=o.MTb0[:], in0=o.MT0[:], in1=cb["bd32"][:], op=ALU.mult), reads=[R_("MT0"), r_cb], writes=[R_("MTb0")])
                            P.G(lambda e, o=o: e.tensor_tensor(out=o.P0[:], in0=o.MTb0[:], in1=cb["ident"][:], op=ALU.add), reads=[R_("MTb0"), r_cb], writes=[R_("P0")])
                            P.G(lambda e, o=o: e.tensor_tensor(out=o.Q0[:], in0=o.Mb0[:], in1=cb["ident"][:], op=ALU.add), reads=[R_("Mb0"), r_cb], writes=[R_("Q0")])
                        cur = 0
                        for m in range(1, 5):
                            nxt = 1 - cur
                            for (o, g, ps_, hi) in H:
                                R_ = lambda k: rn("%s_%d" % (k, hi))
                                G_ = lambda k: getattr(o, k)
                                cn = lambda k, i: "%s%d" % (k, i)
                                mm_evac(o, hi, G_(cn("MTb", cur)), R_(cn("MTb", cur)), G_(cn("Mb", cur)), R_(cn("Mb", cur)), G_(cn("Mb", nxt)), R_(cn("Mb", nxt)))
                                mm_evac(o, hi, G_(cn("Mb", cur)), R_(cn("Mb", cur)), G_(cn("MTb", cur)), R_(cn("MTb", cur)), G_(cn("MTb", nxt)), R_(cn("MTb", nxt)))
                                mm_evac(o, hi, G_(cn("Mb", nxt)), R_(cn("Mb", nxt)), G_(cn("P", cur)), R_(cn("P", cur)), G_(cn("P", nxt)), R_(cn("P", nxt)),
                                        add=G_(cn("P", cur)), radd=R_(cn("P", cur)))
                                mm_evac(o, hi, G_(cn("MTb", nxt)), R_(cn("MTb", nxt)), G_(cn("Q", cur)), R_(cn("Q", cur)), G_(cn("Q", nxt)), R_(cn("Q", nxt)),
                                        add=G_(cn("Q", cur)), radd=R_(cn("Q", cur)))
                            cur = nxt
                        for (lev, offm) in ((64, "off64"), (128, "off128")):
                            nxt = 1 - cur
                            for (o, g, ps_, hi) in H:
                                R_ = lambda k: rn("%s_%d" % (k, hi))
                                G_ = lambda k: getattr(o, k)
                                cn = lambda k, i: "%s%d" % (k, i)
                                P.G(lambda e, o=o: e.tensor_tensor(out=o.Noff[:], in0=o.M0[:], in1=cb[offm][:], op=ALU.mult), reads=[R_("M0"), r_cb], writes=[R_("Noff")])
                                P.G(lambda e, o=o: e.tensor_tensor(out=o.NoffT[:], in0=o.MT0[:], in1=cb[offm][:], op=ALU.mult), reads=[R_("MT0"), r_cb], writes=[R_("NoffT")])
                                X_, rX = G_(cn("Q", cur)), R_(cn("Q", cur))
                                XT_, rXT = G_(cn("P", cur)), R_(cn("P", cur))
                                if lev != 128:
                                    mm_evac(o, hi, o.NoffT, R_("NoffT"), X_, rX, o.Z, R_("Z"))
                                    mm_evac(o, hi, XT_, rXT, o.Z, R_("Z"), G_(cn("Q", nxt)), R_(cn("Q", nxt)), add=X_, radd=rX)
                                mm_evac(o, hi, o.Noff, R_("Noff"), XT_, rXT, o.Z2, R_("Z2"))
                                mm_evac(o, hi, X_, rX, o.Z2, R_("Z2"), G_(cn("P", nxt)), R_(cn("P", nxt)), add=XT_, radd=rXT)
                            cur = nxt
                        if c + 1 < NCH:
                            prologue(c + 1)
                        for (o, g, ps_, hi) in H:
                            Pf, rPf = getattr(o, "P%d" % cur), rn("P%d_%d" % (cur, hi))
                            ry = rn("psy%d" % hi)
                            hs = slice(ps_.start, ps_.start + hd)
                            at_, rat = opr(g, "at", ps_)
                            rt_, rrt = opr(g, "rt", ps_)
                            P.T(lambda e, o=o, at_=at_, g=g, ps_=ps_: e.matmul(o.psy[:, 256:256 + hd], lhsT=at_, rhs=Sb[g - cpar][ps_, 0:hd], start=True, stop=False),
                                reads=[rat, rn("Sb%d" % (g - cpar))], writes=[ry])
                            P.T(lambda e, o=o, g=g, hs=hs: e.matmul(o.psy[:, 256:256 + hd], lhsT=o.AakT[:], rhs=tmj[(g, "vt")][:, hs], start=False, stop=True),
                                reads=[rn("AakT_%d" % hi), rn("tm_vt%d" % g)], writes=[ry])
                            P.S(lambda e, o=o: e.activation(out=o.RHS[:, :hd], in_=o.psy[:, 256:256 + hd], func=AF.Copy), reads=[ry], writes=[rn("RHS_%d" % hi)])
                            P.T(lambda e, o=o, Pf=Pf: e.matmul(o.psy[:, 384:384 + hd], lhsT=Pf[:], rhs=o.RHS[:, :hd], start=True, stop=True),
                                reads=[rPf, rn("RHS_%d" % hi)], writes=[ry])
                            P.S(lambda e, o=o: e.activation(out=o.U[:, :hd], in_=o.psy[:, 384:384 + hd], func=AF.Copy), reads=[ry], writes=[rn("U_%d" % hi)])
                            P.T(lambda e, o=o, g=g, ps_=ps_, rt_=rt_: e.matmul(o.psy[ps_, 0:128], lhsT=Sb[g - cpar][ps_, 0:hd], rhs=rt_, start=True, stop=False),
                                reads=[rn("Sb%d" % (g - cpar)), rrt], writes=[ry])
                            P.T(lambda e, o=o, ps_=ps_: e.matmul(o.psy[ps_, 0:128], lhsT=o.U[:, :hd], rhs=o.ArbT[:], start=False, stop=False),
                                reads=[rn("U_%d" % hi), rn("ArbT_%d" % hi)], writes=[ry])
                            P.T(lambda e, o=o, g=g, ps_=ps_, hs=hs: e.matmul(o.psy[ps_, 0:128], lhsT=tmj[(g, "vt")][:, hs], rhs=o.ArkT[:], start=False, stop=True),
                                reads=[rn("tm_vt%d" % g), rn("ArkT_%d" % hi)], writes=[ry])
                            P.V(lambda e, o=o, g=g, ps_=ps_: e.tensor_copy(out=yT[g - cpar][ps_, :], in_=o.psy[ps_, 0:128]), reads=[ry], writes=[rn("yT%d" % (g - cpar))])
                            P.T(lambda e, o=o, g=g, ps_=ps_, hs=hs: e.matmul(o.psy[ps_, 128:128 + hd], lhsT=tmj[(g, "bt")][:, hs], rhs=o.U[:, :hd], start=True, stop=False),
                                reads=[rn("tm_bt%d" % g), rn("U_%d" % hi)], writes=[ry])
                            P.T(lambda e, o=o, g=g, ps_=ps_, hs=hs: e.matmul(o.psy[ps_, 128:128 + hd], lhsT=tmj[(g, "kt")][:, hs], rhs=tmj[(g, "vt")][:, hs], start=False, stop=True),
                                reads=[rn("tm_kt%d" % g), rn("tm_vt%d" % g)], writes=[ry])
                            P.V(lambda e, o=o, g=g, ps_=ps_: e.scalar_tensor_tensor(out=Sf[g - cpar][ps_, 0:hd], in0=Sf[g - cpar][ps_, 0:hd], scalar=e2[g][ps_, 128:129],
                                                                                   in1=o.psy[ps_, 128:128 + hd], op0=ALU.mult, op1=ALU.add),
                                reads=[rn("Sf%d" % (g - cpar)), rn("e2_%d" % g), ry], writes=[rn("Sf%d" % (g - cpar))])
                            P.G(lambda e, g=g, ps_=ps_: e.tensor_copy(out=Sb[g - cpar][ps_, 0:hd], in_=Sf[g - cpar][ps_, 0:hd]), reads=[rn("Sf%d" % (g - cpar))], writes=[rn("Sb%d" % (g - cpar))])
                        for g0, grow in enumerate(gslots):
                            post("chunk", ipar + g0, grow, c, (yT[g0], rn("yT%d" % g0), cl), pobj)
            P.barrier()

        def gdn_post_setup(st, ngs, SC):
            sb = lambda nm, shape, dt: st.enter_context(nc.sbuf_tensor(un(nm), shape, dt))
            o = TokCtx()
            o.z = [sb("gz%d" % g, [128, SC * 128], BF16) for g in range(ngs)]
            o.sq = sb("gpsq", [128, 128], BF16)
            o.rs = sb("gprs", [128, 128], F32)
            o.ob = [sb("gpob%d" % i, [128, 128], BF16) for i in range(2)]
            o.ps = st.enter_context(nc.psum_tensor(un("gpps"), [128, 128], F32))
            o.k = 0
            return o

        def gdn_post(kind, g, grow, c, arg, o):
            if kind == "load":
                P.dma("sync", o.z[g][:, :arg], gS["z"][grow:grow + 128, c * 128:c * 128 + arg], reads=[res("scrA")], writes=[res("gz%d" % g)])
                return
            yT_, ry, cl = arg
            P.G(lambda e: e.tensor_tensor(out=o.sq[:], in0=yT_[:], in1=yT_[:], op=ALU.mult), reads=[ry], writes=[res("gpsq")])
            P.T(lambda e: e.matmul(o.ps[:], lhsT=cb["ones"][:], rhs=o.sq[:], start=True, stop=True), reads=[r_cb, res("gpsq")], writes=[res("gpps")])
            P.S(lambda e: e.activation(out=o.rs[:], in_=o.ps[:], func=AF.Sqrt, bias=cvec[:, 0:1], scale=1.0 / 128), reads=[res("gpps"), r_cb], writes=[res("gprs")])
            P.V(lambda e: e.reciprocal(out=o.rs[:], in_=o.rs[:]), reads=[res("gprs")], writes=[res("gprs")])
            P.V(lambda e: e.scalar_tensor_tensor(out=o.rs[:], in0=yT_[:], scalar=pvs("o_gain"), in1=o.rs[:], op0=ALU.mult, op1=ALU.mult),
                reads=[ry, r_pv, res("gprs")], writes=[res("gprs")])
            i = o.k % 2; o.k += 1
            P.V(lambda e: e.tensor_tensor(out=o.ob[i][:], in0=o.rs[:], in1=o.z[g][:, cl * 128:(cl + 1) * 128], op=ALU.mult),
                reads=[res("gprs"), res("gz%d" % g)], writes=[res("gpob%d" % i)])
            P.dma("sync", oT[512 + grow:512 + grow + 128, c * 128:(c + 1) * 128], o.ob[i][:], reads=[res("gpob%d" % i)], writes=[res("oT_dram")])

        R["scrgdn"] = res("scrA")
        if only is None or "C" in only:
          dplr_phase("gdn", [[(0, 0, 128), (128, 0, 128)], [(256, 0, 128), (384, 0, 128)]], gS, gdn_post_setup, gdn_post, scalar_decay=True)
        if stop_after == "C":
            P.finish(); return nc

        with contextlib.ExitStack() as st:
          if only is None or "D" in only:
            sb = lambda name, shape, dt: st.enter_context(nc.sbuf_tensor(un(name), shape, dt))
            c = make_tok_ctx(st, nwi=3, nwo=3)
            oin = sb("oin", [128, 8, 512], BF16)
            xx = sb("xx", [128, 8, 512], BF16)
            xm = sb("xm", [128, 8, 512], BF16)
            w1s = sb("w1s", [128, 8, 64], BF16); a1s = sb("a1s", [128, 8, 64], BF16); g1s = sb("g1s", [128, 8, 160], BF16)
            w2s = sb("w2s", [64, D], BF16); a2s = sb("a2s", [64, D], BF16); g2s = sb("g2s", [128, 2, D], BF16)
            P.dma("sync", w1s[:], wview(Wb["rwkv_w1"]), reads=[res("W_rwkv_w1")], writes=[res("lora")])
            P.dma("sync", a1s[:], wview(Wb["rwkv_a1"]), reads=[res("W_rwkv_a1")], writes=[res("lora")])
            P.dma("sync", g1s[:], wview(Wb["rwkv_g1"]), reads=[res("W_rwkv_g1")], writes=[res("lora")])
            P.dma("sync", w2s[:], Wb["rwkv_w2"], reads=[res("W_rwkv_w2")], writes=[res("lora")])
            P.dma("sync", a2s[:], Wb["rwkv_a2"], reads=[res("W_rwkv_a2")], writes=[res("lora")])
            P.dma("sync", g2s[:, 0, :], Wb["rwkv_g2"][0:128, :], reads=[res("W_rwkv_g2")], writes=[res("lora")])
            P.dma("sync", g2s[0:32, 1, :], Wb["rwkv_g2"][128:160, :], reads=[res("W_rwkv_g2")], writes=[res("lora")])
            r_lora = res("lora")
            lmid = [sb("lmid%d" % i, [128, 512], BF16) for i in range(2)]
            rr = sb("rr", [128, 8, 512], BF16)
            kk_ = sb("kkk", [128, 8, 512], F32)
            aa = sb("aa", [128, 512], F32)
            t1 = sb("t1", [128, 512], F32)
            t2 = sb("t2", [128, 512], F32)
            ob = [sb("obD%d" % i, [128, 512], BF16) for i in range(4)]
            obf = [sb("obF%d" % i, [128, 512], F32) for i in range(2)]
            ocnt = [0, 0]
            P.V(lambda e: e.memset(c.hn[:, :, 0:1], 0.0), writes=[res("hn")])
            omk = sb("omk", [128, 8], F32)
            print("phase D sbuf remaining", nc.sbuf_bytes_remaining)
            P.V(lambda e: e.tensor_scalar(out=omk[:], in0=pvs("k_a"), scalar1=-1.0, scalar2=1.0, op0=ALU.mult, op1=ALU.add), reads=[r_pv], writes=[res("omk")])

            def out_bf(dst_ap, make):
                i = ocnt[0] % 4; ocnt[0] += 1
                make(ob[i], res("obD%d" % i))
                P.dma("sync", dst_ap, ob[i][:, :dst_ap.shape[-1]], reads=[res("obD%d" % i)], writes=[res("scrrwkv")])

            for ti, (t0, N) in enumerate(tiles):
                h_t, r_h = c.hT[ti % 2], res("hT%d" % (ti % 2))
                P.dma("sync", h_t[:, :, :N], hview(hT)[:, :, t0:t0 + N], reads=[res("hT_dram")], writes=[r_h])
                P.dma("sync", oin[:, :, :N], hview(oT)[:, :, t0:t0 + N], reads=[res("oT_dram")], writes=[res("oin")])
                proj_residual(c, h_t, r_h, N, oin, res("oin"), "hyb_w_out")
                ffn(c, h_t, r_h, N, 1)
                ffn(c, h_t, r_h, N, 2)
                P.dma("sync", hview(hT)[:, :, t0:t0 + N], h_t[:, :, :N], reads=[r_h], writes=[res("hT_dram")])
                rmsnorm(c, h_t, r_h, N, "mix_norm1")
                import os as _os
                DCUT = float(_os.environ.get("DCUT", "99"))
                if DCUT <= 0:
                    continue
                for kc in range(8):
                    P.op("vector", lambda e, kc=kc: e.tensor_tensor(out=xx[:, kc, :N], in0=c.hn[:, kc, 0:N], in1=c.hn[:, kc, 1:1 + N], op=ALU.subtract),
                         reads=[res("hn")], writes=[res("xx")])
                o_mu, _ = PV_SLOTS["mu"]
                def mix(i):
                    for kc in range(8):
                        eng = "vector" if kc % 2 == 0 else "gpsimd"
                        P.op("vector", lambda e, kc=kc: e.scalar_tensor_tensor(out=xm[:, kc, :N], in0=xx[:, kc, :N], scalar=pv[:, o_mu + i * 8 + kc:o_mu + i * 8 + kc + 1],
                                                                          in1=c.hn[:, kc, 1:1 + N], op0=ALU.mult, op1=ALU.add),
                             reads=[res("xx"), res("hn"), r_pv], writes=[res("xm")])
                xfn = lambda kc: xm[:, kc, :N]
                rxm = [res("xm")]
                if DCUT <= 0.3:
                    continue
                mix(0)
                if DCUT <= 0.6:
                    continue
                def cons_r(fc, pi, m):
                    if _os.environ.get("RVAR", "") != "a":
                        P.S(lambda e: e.activation(out=rr[:, fc, :N], in_=c.pt[pi][:, :N], func=AF.Copy), reads=[res("pt%d" % pi)], writes=[res("rr")])
                    if _os.environ.get("RVAR", "") == "b":
                        return
                    out_bf(rS["r"][fc * 128:(fc + 1) * 128, t0:t0 + N],
                           lambda o, ro: P.V(lambda e: e.tensor_copy(out=o[:, :N], in_=c.pt[pi][:, :N]), reads=[res("pt%d" % pi)], writes=[ro]))
                lin_fm(c, xfn, rxm, N, "rwkv_w_r", 0, D, cons_r)
                if DCUT <= 1:
                    continue
                mix(1)
                pi = next_pt(c)
                for kc in range(8):
                    P.T(lambda e, pi=pi, kc=kc: e.matmul(c.pt[pi][:64, :N], lhsT=w1s[:, kc, :], rhs=xm[:, kc, :N], start=(kc == 0), stop=(kc == 7)),
                        reads=[r_lora, res("xm")], writes=[res("pt%d" % pi)])
                P.S(lambda e, pi=pi: e.activation(out=lmid[0][:64, :N], in_=c.pt[pi][:64, :N], func=AF.Tanh), reads=[res("pt%d" % pi)], writes=[res("lmid0")])
                for fc in range(8):
                    pj = next_pt(c)
                    P.T(lambda e, pj=pj, fc=fc: e.matmul(c.pt[pj][:, :N], lhsT=w2s[:, fc * 128:(fc + 1) * 128], rhs=lmid[0][:64, :N], start=True, stop=True),
                        reads=[r_lora, res("lmid0")], writes=[res("pt%d" % pj)])
                    fi = ocnt[1] % 2; ocnt[1] += 1
                    P.S(lambda e, pj=pj, fc=fc, fi=fi: e.activation(out=obf[fi][:, :N], in_=c.pt[pj][:, :N], func=AF.Sigmoid, bias=pvs("w0", fc, 1)),
                        reads=[res("pt%d" % pj), r_pv], writes=[res("obF%d" % fi)])
                    P.V(lambda e, fi=fi: e.tensor_scalar(out=obf[fi][:, :N], in0=obf[fi][:, :N], scalar1=-float(np.exp(-0.5)), scalar2=None, op0=ALU.mult),
                        reads=[res("obF%d" % fi)], writes=[res("obF%d" % fi)])
                    P.dma("sync", rS["lw"][fc * 128:(fc + 1) * 128, t0:t0 + N], obf[fi][:, :N], reads=[res("obF%d" % fi)], writes=[res("scrrwkv")])
                if DCUT <= 2:
                    continue
                mix(2)
                def cons_k(fc, pi, m):
                    P.S(lambda e: e.activation(out=kk_[:, fc, :N], in_=c.pt[pi][:, :N], func=AF.Copy), reads=[res("pt%d" % pi)], writes=[res("kkk")])
                lin_fm(c, xfn, rxm, N, "rwkv_w_k", 0, D, cons_k)
                mix(3)
                def cons_v(fc, pi, m):
                    out_bf(rS["v"][fc * 128:(fc + 1) * 128, t0:t0 + N],
                           lambda o, ro: P.V(lambda e: e.tensor_copy(out=o[:, :N], in_=c.pt[pi][:, :N]), reads=[res("pt%d" % pi)], writes=[ro]))
                    P.S(lambda e: e.activation(out=c.act[:, fc, :N], in_=c.pt[pi][:, :N], func=AF.Copy), reads=[res("pt%d" % pi)], writes=[res("act")])
                lin_fm(c, xfn, rxm, N, "rwkv_w_v", 0, D, cons_v)
                if DCUT <= 3:
                    continue
                mix(4)
                pi = next_pt(c)
                for kc in range(8):
                    P.T(lambda e, pi=pi, kc=kc: e.matmul(c.pt[pi][:64, :N], lhsT=a1s[:, kc, :], rhs=xm[:, kc, :N], start=(kc == 0), stop=(kc == 7)),
                        reads=[r_lora, res("xm")], writes=[res("pt%d" % pi)])
                P.S(lambda e, pi=pi: e.activation(out=lmid[1][:64, :N], in_=c.pt[pi][:64, :N], func=AF.Copy), reads=[res("pt%d" % pi)], writes=[res("lmid1")])
                for fc in range(8):
                    pj = next_pt(c)
                    P.T(lambda e, pj=pj, fc=fc: e.matmul(c.pt[pj][:, :N], lhsT=a2s[:, fc * 128:(fc + 1) * 128], rhs=lmid[1][:64, :N], start=True, stop=True),
                        reads=[r_lora, res("lmid1")], writes=[res("pt%d" % pj)])
                    P.S(lambda e, pj=pj, fc=fc: e.activation(out=aa[:, :N], in_=c.pt[pj][:, :N], func=AF.Sigmoid, bias=pvs("a0", fc, 1)),
                        reads=[res("pt%d" % pj), r_pv], writes=[res("aa")])
                    P.V(lambda e, fc=fc: e.tensor_scalar(out=t1[:, :N], in0=kk_[:, fc, :N], scalar1=pvs("k_k", fc, 1), scalar2=None, op0=ALU.mult),
                        reads=[res("kkk"), r_pv], writes=[res("t1")])
                    P.G(lambda e: e.tensor_tensor(out=c.sq[:, 0, :N], in0=t1[:, :N], in1=t1[:, :N], op=ALU.mult), reads=[res("t1")], writes=[res("sq")])
                    pq = sumsq_bc(c, lambda k: c.sq[:, 0, :N], 1, N, [res("sq")], lhs_name="blk64")
                    rsqrt_from_psum(c, pq, N, c.rstd[:, :N], res("rstd"))
                    P.V(lambda e: e.tensor_tensor(out=t1[:, :N], in0=t1[:, :N], in1=c.rstd[:, :N], op=ALU.mult), reads=[res("t1"), res("rstd")], writes=[res("t1")])
                    out_bf(rS["a"][fc * 128:(fc + 1) * 128, t0:t0 + N],
                           lambda o, ro: P.G(lambda e: e.tensor_scalar(out=o[:, :N], in0=t1[:, :N], scalar1=-1.0, scalar2=None, op0=ALU.mult), reads=[res("t1")], writes=[ro]))
                    out_bf(rS["b"][fc * 128:(fc + 1) * 128, t0:t0 + N],
                           lambda o, ro: P.V(lambda e: e.tensor_tensor(out=o[:, :N], in0=t1[:, :N], in1=aa[:, :N], op=ALU.mult), reads=[res("t1"), res("aa")], writes=[ro]))
                    P.V(lambda e, fc=fc: e.tensor_scalar(out=t2[:, :N], in0=aa[:, :N], scalar1=pvs("k_a", fc, 1), scalar2=omk[:, fc:fc + 1], op0=ALU.mult, op1=ALU.add),
                        reads=[res("aa"), r_pv, res("omk")], writes=[res("t2")])
                    P.V(lambda e, fc=fc: e.tensor_tensor(out=t2[:, :N], in0=t2[:, :N], in1=kk_[:, fc, :N], op=ALU.mult), reads=[res("t2"), res("kkk")], writes=[res("t2")])
                    out_bf(rS["k"][fc * 128:(fc + 1) * 128, t0:t0 + N],
                           lambda o, ro: P.G(lambda e: e.tensor_copy(out=o[:, :N], in_=t2[:, :N]), reads=[res("t2")], writes=[ro]))
                    P.V(lambda e, fc=fc: e.scalar_tensor_tensor(out=c.sq[:, 1, :N], in0=t2[:, :N], scalar=pvs("r_k", fc, 1), in1=rr[:, fc, :N], op0=ALU.mult, op1=ALU.mult),
                        reads=[res("t2"), r_pv, res("rr")], writes=[res("sq")])
                    pb = sumsq_bc(c, lambda k: c.sq[:, 1, :N], 1, N, [res("sq")], lhs_name="blk64")
                    out_bf(rS["bonus"][fc * 128:(fc + 1) * 128, t0:t0 + N],
                           lambda o, ro: P.V(lambda e, fc=fc: e.tensor_tensor(out=o[:, :N], in0=c.pt[pb][:, :N], in1=c.act[:, fc, :N], op=ALU.mult),
                                             reads=[res("pt%d" % pb), res("act")], writes=[ro]))
                if DCUT <= 4:
                    continue
                mix(5)
                for (m0, mm, li) in ((0, 128, 0), (128, 32, 1)):
                    pi = next_pt(c)
                    for kc in range(8):
                        P.T(lambda e, pi=pi, kc=kc, m0=m0, mm=mm: e.matmul(c.pt[pi][:mm, :N], lhsT=g1s[:, kc, m0:m0 + mm], rhs=xm[:, kc, :N], start=(kc == 0), stop=(kc == 7)),
                            reads=[r_lora, res("xm")], writes=[res("pt%d" % pi)])
                    P.S(lambda e, pi=pi, mm=mm, li=li: e.activation(out=lmid[li][:mm, :N], in_=c.pt[pi][:mm, :N], func=AF.Sigmoid),
                        reads=[res("pt%d" % pi)], writes=[res("lmid%d" % li)])
                for fc in range(8):
                    pj = next_pt(c)
                    P.T(lambda e, pj=pj, fc=fc: e.matmul(c.pt[pj][:, :N], lhsT=g2s[:, 0, fc * 128:(fc + 1) * 128], rhs=lmid[0][:, :N], start=True, stop=False),
                        reads=[r_lora, res("lmid0")], writes=[res("pt%d" % pj)])
                    P.T(lambda e, pj=pj, fc=fc: e.matmul(c.pt[pj][:, :N], lhsT=g2s[0:32, 1, fc * 128:(fc + 1) * 128], rhs=lmid[1][0:32, :N], start=False, stop=True),
                        reads=[r_lora, res("lmid1")], writes=[res("pt%d" % pj)])
                    out_bf(rS["g"][fc * 128:(fc + 1) * 128, t0:t0 + N],
                           lambda o, ro: P.V(lambda e, pj=pj: e.tensor_copy(out=o[:, :N], in_=c.pt[pj][:, :N]), reads=[res("pt%d" % pj)], writes=[ro]))
                P.G(lambda e: e.tensor_copy(out=c.hn[:, :, 0:1], in_=c.hn[:, :, N:N + 1]), reads=[res("hn")], writes=[res("hn")])
        P.barrier()
        if stop_after == "D":
            P.finish(); return nc

        def rw_post_setup(st, ngs, SC):
            sb = lambda nm, shape, dt: st.enter_context(nc.sbuf_tensor(un(nm), shape, dt))
            o = TokCtx()
            o.bon = [sb("rbon%d" % i, [128, SC * 128], BF16) for i in range(ngs)]
            o.g = [sb("rgate%d" % i, [128, SC * 128], BF16) for i in range(ngs)]
            o.sq = sb("rpsq", [128, 128], BF16)
            o.yb = sb("rpyb", [128, 128], BF16)
            o.mean = sb("rpmean", [128, 128], F32)
            o.var = sb("rpvar", [128, 128], F32)
            o.yc = sb("rpyc", [128, 128], F32)
            o.ob = [sb("rpob%d" % i, [128, 128], BF16) for i in range(2)]
            _ps = st.enter_context(nc.psum_tensor(un("rpps"), [128, 128], F32))
            o.ps = [_ps, _ps]
            o.k = 0
            return o

        def rw_post(kind, g, grow, c, arg, o):
            if kind == "load":
                P.dma("sync", o.bon[g][:, :arg], rS["bonus"][grow:grow + 128, c * 128:c * 128 + arg], reads=[res("scrrwkv")], writes=[res("rbon%d" % g)])
                P.dma("sync", o.g[g][:, :arg], rS["g"][grow:grow + 128, c * 128:c * 128 + arg], reads=[res("scrrwkv")], writes=[res("rgate%d" % g)])
                return
            yT_, ry, cl = arg
            fc = grow // 128
            cols = slice(cl * 128, (cl + 1) * 128)
            P.G(lambda e: e.tensor_copy(out=o.yb[:], in_=yT_[:]), reads=[ry], writes=[res("rpyb")])
            P.T(lambda e: e.matmul(o.ps[0][:], lhsT=cb["blk64"][:], rhs=o.yb[:], start=True, stop=True), reads=[r_cb, res("rpyb")], writes=[res("rpps")])
            P.V(lambda e: e.scalar_tensor_tensor(out=o.yc[:], in0=o.ps[0][:], scalar=-1.0 / 64, in1=yT_[:], op0=ALU.mult, op1=ALU.add),
                reads=[res("rpps"), ry], writes=[res("rpyc")])
            P.G(lambda e: e.tensor_tensor(out=o.sq[:], in0=o.yc[:], in1=o.yc[:], op=ALU.mult), reads=[res("rpyc")], writes=[res("rpsq")])
            P.T(lambda e: e.matmul(o.ps[1][:], lhsT=cb["blk64"][:], rhs=o.sq[:], start=True, stop=True), reads=[r_cb, res("rpsq")], writes=[res("rpps")])
            P.S(lambda e: e.activation(out=o.var[:], in_=o.ps[1][:], func=AF.Sqrt, bias=cvec[:, 2:3], scale=1.0 / 64), reads=[res("rpps"), r_cb], writes=[res("rpvar")])
            P.V(lambda e: e.reciprocal(out=o.var[:], in_=o.var[:]), reads=[res("rpvar")], writes=[res("rpvar")])
            P.V(lambda e: e.scalar_tensor_tensor(out=o.yc[:], in0=o.yc[:], scalar=pvs("ln_w", fc, 1), in1=o.var[:], op0=ALU.mult, op1=ALU.mult),
                reads=[res("rpyc"), r_pv, res("rpvar")], writes=[res("rpyc")])
            P.V(lambda e: e.scalar_tensor_tensor(out=o.yc[:], in0=o.yc[:], scalar=pvs("ln_b", fc, 1), in1=o.bon[g][:, cols], op0=ALU.add, op1=ALU.add),
                reads=[res("rpyc"), r_pv, res("rbon%d" % g)], writes=[res("rpyc")])
            i = o.k % 2; o.k += 1
            P.V(lambda e: e.tensor_tensor(out=o.ob[i][:], in0=o.yc[:], in1=o.g[g][:, cols], op=ALU.mult), reads=[res("rpyc"), res("rgate%d" % g)], writes=[res("rpob%d" % i)])
            P.dma("sync", zT[grow:grow + 128, c * 128:(c + 1) * 128], o.ob[i][:], reads=[res("rpob%d" % i)], writes=[res("zT_dram")])

        if only is None or "E" in only:
          dplr_phase("rwkv", [[(gq * 128, 0, 64), (gq * 128, 64, 64)] for gq in range(8)], rS, rw_post_setup, rw_post)
        if stop_after == "E":
            P.finish(); return nc

        with contextlib.ExitStack() as st:
            sb = lambda name, shape, dt: st.enter_context(nc.sbuf_tensor(un(name), shape, dt))
            c = make_tok_ctx(st, nwi=4, nwo=4)
            oin = sb("oinF", [128, 8, 512], BF16)
            for ti, (t0, N) in enumerate(tiles):
                h_t, r_h = c.hT[ti % 2], res("hT%d" % (ti % 2))
                P.dma("sync", h_t[:, :, :N], hview(hT)[:, :, t0:t0 + N], reads=[res("hT_dram")], writes=[r_h])
                P.dma("sync", oin[:, :, :N], hview(zT)[:, :, t0:t0 + N], reads=[res("zT_dram")], writes=[res("oinF")])
                proj_residual(c, h_t, r_h, N, oin, res("oinF"), "rwkv_w_o")
                ffn(c, h_t, r_h, N, 3)
                P.S(lambda e: e.activation(out=c.sq[:, :, :N], in_=h_t[:, :, :N], func=AF.Square), reads=[r_h], writes=[res("sq")])
                pi = sumsq_bc(c, lambda k: c.sq[:, k, :N], 8, N, [res("sq")])
                rsqrt_from_psum(c, pi, N, c.rstd[:, :N], res("rstd"))
                for kc in range(8):
                    eng = "vector" if kc % 2 == 0 else "gpsimd"
                    P.op("vector", lambda e, kc=kc: e.scalar_tensor_tensor(out=h_t[:, kc, :N], in0=h_t[:, kc, :N], scalar=pvs("final_norm", kc, 1), in1=c.rstd[:, :N],
                                                                       op0=ALU.mult, op1=ALU.mult), reads=[r_h, r_pv, res("rstd")], writes=[r_h])
                P.dma("sync", hview(outT)[:, :, t0:t0 + N], h_t[:, :, :N], reads=[r_h], writes=[res("out_dram")])
        P.finish()
    return nc


_NC_CACHE = {}


def make_in_maps(inp, T):
    x = np.asarray(inp["x"], np.float32)
    B, S, _ = x.shape
    meta = np.asarray(inp["meta"], np.float32)
    pv = pack_params(inp)
    base = {"pvec": pv}
    k = 0
    for l in range(2):
        for h in range(2):
            base["ffn_w_in%d" % k] = np.ascontiguousarray(inp["ffn_w_in"][l, h], np.float32)
            base["ffn_w_out%d" % k] = np.ascontiguousarray(inp["ffn_w_out"][l, h], np.float32)
            k += 1
    base["hyb_w_in"] = np.ascontiguousarray(inp["hyb_w_in"][0], np.float32)
    base["hyb_w_out"] = np.ascontiguousarray(inp["hyb_w_out"][0], np.float32)
    for nm in ("w_r", "w_k", "w_v", "w_o", "w1", "w2", "a1", "a2", "g1", "g2"):
        base["rwkv_" + nm] = np.ascontiguousarray(inp["rwkv_" + nm][0], np.float32)
    maps = []
    for core in range(8):
        b = core % B
        hT0 = np.zeros((D, T), np.float32)
        hT0[:, :NMETA] = meta.T
        hT0[:, NMETA:NMETA + S] = x[b].T
        m = dict(base)
        m["hT0"] = hT0
        maps.append(m)
    return maps


def kernel(**inp):
    x = np.asarray(inp["x"])
    B, S, _ = x.shape
    T = ((NMETA + S + 127) // 128) * 128
    if T not in _NC_CACHE:
        _NC_CACHE[T] = build(T)
    nc = _NC_CACHE[T]
    maps = make_in_maps(inp, T)
    res = run_bass_kernel_spmd(nc, maps, core_ids=list(range(8)))
    out = np.empty((B, S, D), np.float32)
    for b in range(B):
        out[b] = res.results[b]["outT"][:, NMETA:NMETA + S].T
    return out
```

```python
import contextlib
import numpy as np
import concourse.bass as bass
import concourse.mybir as mybir
from concourse.bass_utils import run_bass_kernel_spmd

F32 = mybir.dt.float32
BF16 = mybir.dt.bfloat16
ALU = mybir.AluOpType
AF = mybir.ActivationFunctionType

D = 1024
DFF = 2816
EPS = 1e-6
NMETA = 16
HYB_IN = 3600
GN_EPS = 64e-5


_NUN = [0]


NAMES = {}


def un(name):
    _NUN[0] += 1
    NAMES[name] = "t%d_%s" % (_NUN[0], name)
    return NAMES[name]


SAME_ENGINE_SYNC = True
PSUM_PREFIXES = ("pt", "py", "ps_", "psx", "psy", "pstr", "gpps", "rpps")


class Res:
    __slots__ = ("name", "w", "r", "excl")

    def __init__(self, name):
        self.name = name
        self.w = None
        self.r = []
        base = name.split("_", 1)[1] if name.startswith(("gdn_", "rwkv_")) else name
        self.excl = base.startswith(PSUM_PREFIXES)


class _Rec:
    def __init__(self):
        self.calls = []

    def __getattr__(self, name):
        def f(*a, **k):
            self.calls.append((name, a, k))
            return self
        return f


class Prog:
    ENGS = ("tensor", "vector", "scalar", "gpsimd", "sync")

    def __init__(self, nc, stack, n_dma_sems=12):
        self.nc = nc
        self.lists = {e: [] for e in self.ENGS}
        self.stack = stack
        self.epoch = 0
        self.esem = {e: stack.enter_context(nc.semaphore("s_" + e)) for e in self.ENGS}
        self.ecount = {e: 0 for e in self.ENGS}
        self.LIMIT = 12000
        self.seen = {e: {} for e in self.ENGS}
        self.dsems, self.dcount, self.dnext = {}, {}, {}
        for q in ("sync", "gpsimd", "scalar"):
            self.dsems[q] = [stack.enter_context(nc.semaphore("d_%s%d" % (q, i)))
                             for i in range(n_dma_sems if q != "scalar" else 2)]
            self.dcount[q] = [0] * len(self.dsems[q])
            self.dnext[q] = 0
        self.semobj = {}
        for e in self.ENGS:
            self.semobj[("e", e, 0)] = self.esem[e]
        for q in self.dsems:
            for i, s in enumerate(self.dsems[q]):
                self.semobj[("d", q, i)] = s
        self.n_ins = 0

    def _need(self, eng, deps, key, val):
        if key[0] == "e":
            if key[2] < self.epoch:
                return
            if key[1] == eng and (eng == "tensor" or not SAME_ENGINE_SYNC):
                return
        if self.seen[eng].get(key, 0) >= val:
            return
        if deps.get(key, 0) < val:
            deps[key] = val

    def _collect(self, eng, reads, writes):
        deps = {}
        for r in reads:
            if r.w is not None:
                self._need(eng, deps, r.w[0], r.w[1])
        for w in writes:
            if w.w is not None:
                self._need(eng, deps, w.w[0], w.w[1])
            for (k, v) in w.r:
                self._need(eng, deps, k, v)
        for k, v in deps.items():
            self.seen[eng][k] = v
        return list(deps.items())

    def _mark(self, key, val, reads, writes):
        for r in reads:
            r.r = [(k, v) for (k, v) in r.r if k != key]
            r.r.append((key, val))
        for w in writes:
            w.w = (key, val)
            w.r = []

    def op(self, eng, fn, reads=(), writes=()):
        rec = _Rec()
        fn(rec)
        assert len(rec.calls) == 1, rec.calls
        name, a, k = rec.calls[0]
        fn = (lambda e, name=name, a=a, k=k: getattr(e, name)(*a, **k))
        ex = [r for r in reads if r.excl]
        if ex:
            writes = list(writes) + ex
        if self.ecount[eng] >= self.LIMIT:
            self.barrier(rotate=True)
        waits = self._collect(eng, reads, writes)
        self.ecount[eng] += 1
        key = ("e", eng, self.epoch)
        self.lists[eng].append((waits, fn, key, 1))
        self._mark(key, self.ecount[eng], reads, writes)
        self.n_ins += 1

    def V(self, fn, reads=(), writes=()):
        self.op("vector", fn, reads, writes)

    def S(self, fn, reads=(), writes=()):
        self.op("scalar", fn, reads, writes)

    def G(self, fn, reads=(), writes=()):
        self.op("gpsimd", fn, reads, writes)

    def T(self, fn, reads=(), writes=()):
        self.op("tensor", fn, reads, writes)

    def dma(self, q, out, in_, reads=(), writes=(), **kw):
        i = self.dnext[q]
        self.dnext[q] = (i + 1) % len(self.dsems[q])
        key = ("d", q, i)
        waits = self._collect(q, reads, writes)
        prev = self.dcount[q][i]
        if prev > 0 and self.seen[q].get(key, 0) < prev:
            waits.append((key, prev))
            self.seen[q][key] = prev
        self.dcount[q][i] += 16
        self.lists[q].append((waits, (lambda e: e.dma_start(out=out, in_=in_, **kw)), key, 16))
        self._mark(key, self.dcount[q][i], reads, writes)
        self.n_ins += 1

    def barrier(self, rotate=False):
        allw = {}
        for e in self.ENGS:
            if self.ecount[e] > 0:
                allw[("e", e, self.epoch)] = self.ecount[e]
        for q in self.dsems:
            for i, c in enumerate(self.dcount[q]):
                if c > 0:
                    allw[("d", q, i)] = c
        for e in self.ENGS:
            ws = []
            for k, v in allw.items():
                if k[0] == "e" and k[1] == e:
                    continue
                if self.seen[e].get(k, 0) < v:
                    ws.append((k, v))
                    self.seen[e][k] = v
            if ws:
                self.lists[e].append((ws, None, None, 0))
        if rotate:
            self.epoch += 1
            for e in self.ENGS:
                if e == "sync":
                    continue
                self.esem[e] = self.stack.enter_context(self.nc.semaphore("s_%s_%d" % (e, self.epoch)))
                self.semobj[("e", e, self.epoch)] = self.esem[e]
                self.ecount[e] = 0
                self.seen[e] = {k: v for k, v in self.seen[e].items() if k[0] != "e"}
            self.seen["sync"] = {k: v for k, v in self.seen["sync"].items() if k[0] != "e"}

    def finish(self):
        self.barrier()
        semobj, lists = self.semobj, self.lists

        def run(e, name):
            for (ws, fn, key, inc) in lists[name]:
                for (k, v) in ws:
                    e.wait_ge(semobj[k], v)
                if fn is not None:
                    fn(e).then_inc(semobj[key], inc)

        with self.nc.Block() as block:
            @block.tensor
            def _(e):
                run(e, "tensor")

            @block.vector
            def _(e):
                run(e, "vector")

            @block.scalar
            def _(e):
                run(e, "scalar")

            @block.gpsimd
            def _(e):
                run(e, "gpsimd")

            @block.sync
            def _(e):
                run(e, "sync")


PV_SLOTS = {}


def _pv_layout():
    off = 0
    def add(name, n):
        nonlocal off
        PV_SLOTS[name] = (off, n)
        off += n
    for i in range(4):
        add("ffn_norm%d" % i, 8)
    add("mix_norm0", 8); add("mix_norm1", 8); add("final_norm", 8)
    add("fox_bf", 8); add("conv", 48); add("a_log", 4); add("dt_bias", 4); add("o_gain", 1)
    add("mu", 48); add("w0", 8); add("a0", 8); add("k_k", 8); add("k_a", 8); add("ln_w", 8); add("ln_b", 8); add("r_k", 8)
    for nm in ("cmean", "ident", "m_su", "m_ui", "m_sl", "ones", "blk64", "triu", "bd32", "off64", "off128"):
        add(nm, 128)
    add("sel65", 64)
    return off


NPV = _pv_layout()


def fm(vec):
    return np.ascontiguousarray(np.asarray(vec, np.float32).reshape(8, 128).T)


def pack_params(inp):
    pv = np.zeros((128, NPV), np.float32)
    def put(name, arr):
        o, n = PV_SLOTS[name]
        pv[:, o:o + n] = np.asarray(arr, np.float32).reshape(128, n)
    k = 0
    for l in range(2):
        for h in range(2):
            put("ffn_norm%d" % k, fm(inp["ffn_norm"][l, h])); k += 1
    put("mix_norm0", fm(inp["mix_norm"][0])); put("mix_norm1", fm(inp["mix_norm"][1]))
    put("final_norm", fm(inp["final_norm"]))
    put("fox_bf", np.broadcast_to(inp["hyb_fox_bf"][0][None, :], (128, 8)))
    cw = inp["hyb_conv"][0]
    put("conv", cw.reshape(4, 12, 128).transpose(2, 1, 0).reshape(128, 48))
    put("a_log", np.broadcast_to(inp["hyb_a_log"][0][None, :], (128, 4)))
    put("dt_bias", np.broadcast_to(inp["hyb_dt_bias"][0][None, :], (128, 4)))
    put("o_gain", inp["hyb_o_gain"][0].reshape(128, 1))
    put("mu", inp["rwkv_mu"][0].reshape(6, 8, 128).transpose(2, 0, 1).reshape(128, 48))
    for nm in ("w0", "a0", "k_k", "k_a", "ln_w", "ln_b"):
        put(nm, fm(inp["rwkv_" + nm][0]))
    put("r_k", fm(inp["rwkv_r_k"][0].reshape(-1)))
    idx = np.arange(128)
    put("cmean", np.full((128, 128), 1.0 / 1024))
    put("ident", np.eye(128))
    put("m_su", (idx[:, None] < idx[None, :]).astype(np.float32))
    put("m_ui", (idx[:, None] <= idx[None, :]).astype(np.float32))
    put("m_sl", (idx[:, None] > idx[None, :]).astype(np.float32))
    put("ones", np.ones((128, 128)))
    put("blk64", ((idx[:, None] // 64) == (idx[None, :] // 64)).astype(np.float32))
    put("triu", (idx[:, None] <= idx[None, :]).astype(np.float32))
    bdm = lambda b: ((idx[:, None] // b) == (idx[None, :] // b)).astype(np.float32)
    put("bd32", bdm(32)); put("off64", bdm(64) - bdm(32)); put("off128", bdm(128) - bdm(64))
    s = np.zeros((128, 64), np.float32); s[64, :] = 1.0
    put("sel65", s)
    return pv


WNAMES = [("ffn_w_in0", D, 2 * DFF), ("ffn_w_in1", D, 2 * DFF), ("ffn_w_in2", D, 2 * DFF), ("ffn_w_in3", D, 2 * DFF),
          ("ffn_w_out0", DFF, D), ("ffn_w_out1", DFF, D), ("ffn_w_out2", DFF, D), ("ffn_w_out3", DFF, D),
          ("hyb_w_in", D, HYB_IN), ("hyb_w_out", D, D),
          ("rwkv_w_r", D, D), ("rwkv_w_k", D, D), ("rwkv_w_v", D, D), ("rwkv_w_o", D, D),
          ("rwkv_w1", D, 64), ("rwkv_w2", 64, D), ("rwkv_a1", D, 64), ("rwkv_a2", 64, D),
          ("rwkv_g1", D, 160), ("rwkv_g2", 160, D)]


DEBUG_SCR = False


def build(T, stop_after=None, dbg=(), only=None):
    NCH = T // 128
    tiles = []
    t0 = 0
    while t0 < T:
        n = min(512, T - t0)
        tiles.append((t0, n)); t0 += n
    nc = bass.Bass("TRN2", target_bir_lowering=False)
    hT0 = nc.dram_tensor("hT0", [D, T], F32, kind="ExternalInput").ap()
    pvec = nc.dram_tensor("pvec", [128, NPV], F32, kind="ExternalInput").ap()
    Wf = {nm: nc.dram_tensor(nm, [r, c], F32, kind="ExternalInput").ap() for (nm, r, c) in WNAMES}
    outT = nc.dram_tensor("outT", [D, T], F32, kind="ExternalOutput").ap()
    Wb = {nm: nc.dram_tensor(nm + "_b", [r, c], BF16, kind="Internal").ap() for (nm, r, c) in WNAMES}
    scr = {}
    def dscr(name, shape, dt):
        scr[name] = nc.dram_tensor("scr_" + name, shape, dt, kind=("ExternalOutput" if DEBUG_SCR else "Internal")).ap()
        return scr[name]
    hT = dscr("hT", [D, T], F32)
    fqT = dscr("fqT", [512, T], BF16); fkT = dscr("fkT", [512, T], BF16)
    fV = dscr("fV", [T, 520], BF16); flogf = dscr("flogf", [T, 8], F32)
    gS = {k: dscr("g_" + k, [512, T], BF16) for k in ("r", "k", "a", "b", "v", "z")}
    gS["lw"] = dscr("g_lw", [512, T], F32)
    oT = dscr("oT", [D, T], BF16)
    rS = {k: dscr("r_" + k, [D, T], BF16) for k in ("r", "k", "a", "b", "v", "bonus", "g")}
    rS["lw"] = dscr("r_lw", [D, T], F32)
    zT = dscr("zT", [D, T], BF16)
    dbg_out = {}
    for nm, shape in dbg:
        dbg_out[nm] = nc.dram_tensor("dbg_" + nm, shape, F32, kind="ExternalOutput").ap()

    with contextlib.ExitStack() as gst:
        P = Prog(nc, gst)
        R = {}
        def res(name):
            if name not in R:
                R[name] = Res(name)
            return R[name]

        for (nm, r, c) in WNAMES:
            c0 = 0
            while c0 < c:
                cw = min(2048, c - c0)
                P.dma("gpsimd", Wb[nm][:, c0:c0 + cw], Wf[nm][:, c0:c0 + cw], writes=[res("W_" + nm)])
                c0 += cw

        pv = gst.enter_context(nc.sbuf_tensor("pv", [128, NPV], F32)); r_pv = res("pv")
        P.dma("sync", pv[:], pvec, writes=[r_pv])
        def pvs(name, a=0, n=None):
            o, nn = PV_SLOTS[name]
            n = nn - a if n is None else n
            return pv[:, o + a:o + a + n]
        cb = {}
        for nm in ("cmean", "ident", "ones", "blk64", "bd32", "off64", "off128"):
            cb[nm] = gst.enter_context(nc.sbuf_tensor("cb_" + nm, [128, 128], BF16))
            P.V(lambda e, nm=nm: e.tensor_copy(out=cb[nm][:], in_=pvs(nm)), reads=[r_pv], writes=[res("cb")])
        m_ui_b = gst.enter_context(nc.sbuf_tensor("m_ui_b", [128, 128], BF16))
        P.V(lambda e: e.tensor_copy(out=m_ui_b[:], in_=pvs("m_ui")), reads=[r_pv], writes=[res("cb")])
        cvec = gst.enter_context(nc.sbuf_tensor("cvec", [128, 4], F32))
        P.V(lambda e: e.memset(cvec[:, 0:1], EPS), writes=[res("cb")])
        P.V(lambda e: e.memset(cvec[:, 1:2], 1.0), writes=[res("cb")])
        P.V(lambda e: e.memset(cvec[:, 2:3], GN_EPS), writes=[res("cb")])
        P.V(lambda e: e.memset(cvec[:, 3:4], 0.0), writes=[res("cb")])
        r_cb = res("cb")
        P.barrier()

        hview = lambda ap: ap.rearrange("(kc p) t -> p kc t", p=128)
        wview = lambda ap: ap.rearrange("(kc p) f -> p kc f", p=128)

        class TokCtx:
            pass

        def make_tok_ctx(st, nwi=4, nwo=4):
            c = TokCtx()
            sb = lambda name, shape, dt: st.enter_context(nc.sbuf_tensor(un(name), shape, dt))
            psm = lambda name, shape, dt: st.enter_context(nc.psum_tensor(un(name), shape, dt))
            c.hT = [sb("hT%d" % i, [128, 8, 512], F32) for i in range(2)]
            c.sq = sb("sq", [128, 8, 512], BF16)
            c.rstd = sb("rstd", [128, 512], F32)
            c.hn = sb("hn", [128, 8, 513], BF16)
            c.act = sb("act", [128, 22, 512], BF16)
            c.sg = [sb("sg%d" % i, [128, 512], F32) for i in range(2)]
            c.wi = [sb("wi%d" % i, [128, 8, 512], BF16) for i in range(nwi)]
            c.wo = [sb("wo%d" % i, [128, 4, 512], BF16) for i in range(nwo)]
            c.pt = [psm("pt%d" % i, [128, 512], F32) for i in range(4)]
            c.py = [psm("py%d" % i, [128, 512], F32) for i in range(4)]
            c.cnt = {"wi": 0, "wo": 0, "pt": 0, "sg": 0}
            return c

        def next_pt(c):
            i = c.cnt["pt"] % 4; c.cnt["pt"] += 1
            return i

        def sumsq_bc(c, src_ap_fn, nk, N, r_src, lhs_name="cmean"):
            pi = next_pt(c)
            for k in range(nk):
                P.T(lambda e, pi=pi, k=k: e.matmul(c.pt[pi][:, :N], lhsT=cb[lhs_name][:], rhs=src_ap_fn(k),
                                                    start=(k == 0), stop=(k == nk - 1)),
                    reads=[r_cb] + r_src, writes=[res("pt%d" % pi)])
            return pi

        def rsqrt_from_psum(c, pi, N, dst, r_dst, eps_col=0):
            P.S(lambda e: e.activation(out=dst, in_=c.pt[pi][:, :N], func=AF.Sqrt, bias=cvec[:, eps_col:eps_col + 1]),
                reads=[res("pt%d" % pi), r_cb], writes=[r_dst])
            P.V(lambda e: e.reciprocal(out=dst, in_=dst), reads=[r_dst], writes=[r_dst])

        def rmsnorm(c, h_t, r_h, N, gain_name):
            P.S(lambda e: e.activation(out=c.sq[:, :, :N], in_=h_t[:, :, :N], func=AF.Square), reads=[r_h], writes=[res("sq")])
            pi = sumsq_bc(c, lambda k: c.sq[:, k, :N], 8, N, [res("sq")])
            rsqrt_from_psum(c, pi, N, c.rstd[:, :N], res("rstd"))
            for kc in range(8):
                eng = "vector" if kc % 2 == 0 else "gpsimd"
                P.op("vector", lambda e, kc=kc: e.scalar_tensor_tensor(
                    out=c.hn[:, kc, 1:1 + N], in0=h_t[:, kc, :N], scalar=pvs(gain_name, kc, 1), in1=c.rstd[:, :N],
                    op0=ALU.mult, op1=ALU.mult), reads=[r_h, r_pv, res("rstd")], writes=[res("hn")])

        def lin_fm(c, x_fn, r_x, N, wname, col0, ncols, consume, kchunks=8):
            wv = wview(Wb[wname])
            c0 = 0
            while c0 < ncols:
                cw = min(512, ncols - c0)
                bi = c.cnt["wi"] % len(c.wi); c.cnt["wi"] += 1
                P.dma("sync", c.wi[bi][:, :kchunks, :cw], wv[:, :, col0 + c0:col0 + c0 + cw],
                      reads=[res("W_" + wname)], writes=[res("wi%d" % bi)])
                f0 = 0
                while f0 < cw:
                    m = min(128, cw - f0)
                    pi = next_pt(c)
                    for kc in range(kchunks):
                        P.T(lambda e, pi=pi, bi=bi, kc=kc, f0=f0, m=m: e.matmul(
                            c.pt[pi][:m, :N], lhsT=c.wi[bi][:, kc, f0:f0 + m], rhs=x_fn(kc),
                            start=(kc == 0), stop=(kc == kchunks - 1)),
                            reads=[res("wi%d" % bi)] + r_x, writes=[res("pt%d" % pi)])
                    consume((c0 + f0) // 128, pi, m)
                    f0 += m
                c0 += cw

        def ffn(c, h_t, r_h, N, idx):
            rmsnorm(c, h_t, r_h, N, "ffn_norm%d" % idx)
            wn_in, wn_out = "ffn_w_in%d" % idx, "ffn_w_out%d" % idx
            wv = wview(Wb[wn_in])
            for pc in range(6):
                cw = 512 if pc < 5 else 256
                bufs = []
                for which in range(2):
                    bi = c.cnt["wi"] % len(c.wi); c.cnt["wi"] += 1
                    cc0 = which * DFF + pc * 512
                    P.dma("sync", c.wi[bi][:, :, :cw], wv[:, :, cc0:cc0 + cw], reads=[res("W_" + wn_in)], writes=[res("wi%d" % bi)])
                    bufs.append(bi)
                for fl in range(cw // 128):
                    fc = pc * 4 + fl
                    pis = []
                    for which in range(2):
                        pi = next_pt(c); pis.append(pi)
                        bi = bufs[which]
                        for kc in range(8):
                            P.T(lambda e, pi=pi, bi=bi, kc=kc, fl=fl: e.matmul(
                                c.pt[pi][:, :N], lhsT=c.wi[bi][:, kc, fl * 128:(fl + 1) * 128], rhs=c.hn[:, kc, 1:1 + N],
                                start=(kc == 0), stop=(kc == 7)), reads=[res("wi%d" % bi), res("hn")], writes=[res("pt%d" % pi)])
                    si = c.cnt["sg"] % 2; c.cnt["sg"] += 1
                    P.S(lambda e, si=si, pi=pis[0]: e.activation(out=c.sg[si][:, :N], in_=c.pt[pi][:, :N], func=AF.Silu),
                        reads=[res("pt%d" % pis[0])], writes=[res("sg%d" % si)])
                    P.V(lambda e, si=si, pi=pis[1], fc=fc: e.tensor_tensor(
                        out=c.act[:, fc, :N], in0=c.sg[si][:, :N], in1=c.pt[pi][:, :N], op=ALU.mult),
                        reads=[res("sg%d" % si), res("pt%d" % pis[1])], writes=[res("act")])
            wvo = Wb[wn_out].rearrange("(fc p) d -> p fc d", p=128)
            for half in range(2):
                for g in range(6):
                    nf = 4 if g < 5 else 2
                    bi = c.cnt["wo"] % len(c.wo); c.cnt["wo"] += 1
                    P.dma("sync", c.wo[bi][:, :nf, :], wvo[:, g * 4:g * 4 + nf, half * 512:(half + 1) * 512],
                          reads=[res("W_" + wn_out)], writes=[res("wo%d" % bi)])
                    for fl in range(nf):
                        fc = g * 4 + fl
                        for dq in range(4):
                            P.T(lambda e, bi=bi, fl=fl, fc=fc, dq=dq: e.matmul(
                                c.py[dq][:, :N], lhsT=c.wo[bi][:, fl, dq * 128:(dq + 1) * 128], rhs=c.act[:, fc, :N],
                                start=(fc == 0), stop=(fc == 21)), reads=[res("wo%d" % bi), res("act")], writes=[res("py%d" % dq)])
                for dq in range(4):
                    kc = half * 4 + dq
                    P.V(lambda e, dq=dq, kc=kc: e.scalar_tensor_tensor(
                        out=h_t[:, kc, :N], in0=c.py[dq][:, :N], scalar=0.5, in1=h_t[:, kc, :N],
                        op0=ALU.mult, op1=ALU.add), reads=[res("py%d" % dq), r_h], writes=[r_h])

        def proj_residual(c, h_t, r_h, N, x_t, r_x, wname):
            def consume(fc, pi, m):
                P.V(lambda e: e.tensor_tensor(out=h_t[:, fc, :N], in0=c.pt[pi][:, :N], in1=h_t[:, fc, :N], op=ALU.add),
                    reads=[res("pt%d" % pi), r_h], writes=[r_h])
            lin_fm(c, lambda kc: x_t[:, kc, :N], [r_x], N, wname, 0, D, consume)

        with contextlib.ExitStack() as st:
          if only is None or "A" in only:
            sb = lambda name, shape, dt: st.enter_context(nc.sbuf_tensor(un(name), shape, dt))
            c = make_tok_ctx(st)
            wtm = sb("wtm", [128, 8, 520], BF16)
            wrep = sb("wrep", [128, 8, 8, 128], BF16)
            wsm = sb("wsm", [128, 8, 8], F32)
            P.dma("sync", wtm[:], wview(Wb["hyb_w_in"])[:, :, 1024:1544], reads=[res("W_hyb_w_in")], writes=[res("wtm")])
            P.dma("sync", wsm[:], wview(Wf["hyb_w_in"])[:, :, 3080:3088], writes=[res("wsm")])
            for kc in range(8):
                for j in range(8):
                    eng = "vector" if (kc + j) % 2 == 0 else "gpsimd"
                    P.op(eng, lambda e, kc=kc, j=j: e.tensor_scalar(out=wrep[:, kc, j, :], in0=cb["ones"][:], scalar1=wsm[:, kc, j:j + 1],
                                                                     scalar2=None, op0=ALU.mult), reads=[res("wsm"), r_cb], writes=[res("wrep")])
            negA = sb("negA", [128, 4], F32)
            P.S(lambda e: e.activation(out=negA[:], in_=pvs("a_log"), func=AF.Exp), reads=[r_pv], writes=[res("negA")])
            P.V(lambda e: e.tensor_scalar(out=negA[:], in0=negA[:], scalar1=-1.0, scalar2=None, op0=ALU.mult), reads=[res("negA")], writes=[res("negA")])
            xgs = [sb("xg%d" % i, [128, 515], F32) for i in range(2)]
            halo = sb("halo", [128, 12, 3], F32)
            P.V(lambda e: e.memset(halo[:], 0.0), writes=[res("halo")])
            xgc = [0]
            cacc = sb("cacc", [128, 512], F32)
            qk = [sb("qk%d" % i, [128, 512], F32) for i in range(2)]
            kn = sb("kn", [128, 512], F32)
            gb_ = sb("gbeta", [128, 4, 512], F32)
            gg_ = sb("gg", [128, 4, 512], F32)
            glw = sb("glw", [128, 4, 512], F32)
            ob = [sb("ob%d" % i, [128, 512], BF16) for i in range(4)]
            vtm = [sb("vtm%d" % i, [128, 8, 65], BF16) for i in range(2)]
            for i in range(2):
                P.V(lambda e, i=i: e.memset(vtm[i][:], 1.0), writes=[res("vtm%d" % i)])
            lft = [sb("lft%d" % i, [128, 8], F32) for i in range(2)]
            ocnt = [0]
            print("phase A sbuf remaining", nc.sbuf_bytes_remaining)

            def out_bf(dst_ap, make, reads):
                i = ocnt[0] % 4; ocnt[0] += 1
                make(ob[i], res("ob%d" % i))
                P.dma("sync", dst_ap, ob[i][:, :dst_ap.shape[-1]], reads=[res("ob%d" % i)], writes=[res("scrA")])

            for ti, (t0, N) in enumerate(tiles):
                h_t, r_h = c.hT[ti % 2], res("hT%d" % (ti % 2))
                P.dma("sync", h_t[:, :, :N], hview(hT0)[:, :, t0:t0 + N], writes=[r_h])
                ffn(c, h_t, r_h, N, 0)
                P.dma("sync", hview(hT)[:, :, t0:t0 + N], h_t[:, :, :N], reads=[r_h], writes=[res("hT_dram")])
                rmsnorm(c, h_t, r_h, N, "mix_norm0")
                xfn = lambda kc: c.hn[:, kc, 1:1 + N]
                rx = [res("hn")]
                def cons_q(fc, pi, m):
                    out_bf(fqT[fc * 128:(fc + 1) * 128, t0:t0 + N],
                           lambda o, ro: P.S(lambda e: e.activation(out=o[:, :N], in_=c.pt[pi][:, :N], func=AF.Copy, scale=0.125),
                                             reads=[res("pt%d" % pi)], writes=[ro]), None)
                lin_fm(c, xfn, rx, N, "hyb_w_in", 0, 512, cons_q)
                def cons_k(fc, pi, m):
                    out_bf(fkT[fc * 128:(fc + 1) * 128, t0:t0 + N],
                           lambda o, ro: P.V(lambda e: e.tensor_copy(out=o[:, :N], in_=c.pt[pi][:, :N]),
                                             reads=[res("pt%d" % pi)], writes=[ro]), None)
                lin_fm(c, xfn, rx, N, "hyb_w_in", 512, 512, cons_k)
                for tb in range(N // 128):
                    vi = (ti * 4 + tb) % 2
                    pi = next_pt(c)
                    for kc in range(8):
                        P.T(lambda e, pi=pi, kc=kc, tb=tb: e.matmul(c.pt[pi][:, :512], lhsT=c.hn[:, kc, 1 + tb * 128:1 + (tb + 1) * 128],
                                                                  rhs=wtm[:, kc, 0:512], start=(kc == 0), stop=(kc == 7)),
                            reads=[res("hn"), res("wtm")], writes=[res("pt%d" % pi)])
                    P.S(lambda e, pi=pi, vi=vi: e.activation(out=vtm[vi][:, :, 0:64], in_=c.pt[pi][:, :512].rearrange("p (h d) -> p h d", h=8), func=AF.Copy),
                        reads=[res("pt%d" % pi)], writes=[res("vtm%d" % vi)])
                    P.dma("sync", fV[t0 + tb * 128:t0 + (tb + 1) * 128, :], vtm[vi][:].rearrange("p h d -> p (h d)"),
                          reads=[res("vtm%d" % vi)], writes=[res("scrA")])
                    pi2 = next_pt(c)
                    for kc in range(8):
                        P.T(lambda e, pi2=pi2, kc=kc, tb=tb: e.matmul(c.pt[pi2][:, :8], lhsT=c.hn[:, kc, 1 + tb * 128:1 + (tb + 1) * 128],
                                                                    rhs=wtm[:, kc, 512:520], start=(kc == 0), stop=(kc == 7)),
                            reads=[res("hn"), res("wtm")], writes=[res("pt%d" % pi2)])
                    P.V(lambda e, pi2=pi2, vi=vi: e.tensor_tensor(out=lft[vi][:], in0=c.pt[pi2][:, :8], in1=pvs("fox_bf"), op=ALU.add),
                        reads=[res("pt%d" % pi2), r_pv], writes=[res("lft%d" % vi)])
                    P.S(lambda e, vi=vi: e.activation(out=lft[vi][:], in_=lft[vi][:], func=AF.Sigmoid), reads=[res("lft%d" % vi)], writes=[res("lft%d" % vi)])
                    P.S(lambda e, vi=vi: e.activation(out=lft[vi][:], in_=lft[vi][:], func=AF.Ln), reads=[res("lft%d" % vi)], writes=[res("lft%d" % vi)])
                    P.dma("sync", flogf[t0 + tb * 128:t0 + (tb + 1) * 128, :], lft[vi][:], reads=[res("lft%d" % vi)], writes=[res("scrA")])
                for j in range(8):
                    pi = next_pt(c)
                    for kc in range(8):
                        P.T(lambda e, pi=pi, kc=kc, j=j: e.matmul(c.pt[pi][:, :N], lhsT=wrep[:, kc, j, :], rhs=c.hn[:, kc, 1:1 + N],
                                                               start=(kc == 0), stop=(kc == 7)),
                            reads=[res("wrep"), res("hn")], writes=[res("pt%d" % pi)])
                    if j < 4:
                        P.S(lambda e, pi=pi, j=j: e.activation(out=glw[:, j, :N], in_=c.pt[pi][:, :N], func=AF.Exp, bias=pvs("dt_bias", j, 1)),
                            reads=[res("pt%d" % pi), r_pv], writes=[res("glw")])
                        P.S(lambda e, j=j: e.activation(out=glw[:, j, :N], in_=glw[:, j, :N], func=AF.Ln, bias=cvec[:, 1:2]),
                            reads=[res("glw"), r_cb], writes=[res("glw")])
                        P.V(lambda e, j=j: e.tensor_scalar(out=glw[:, j, :N], in0=glw[:, j, :N], scalar1=negA[:, j:j + 1], scalar2=None, op0=ALU.mult),
                            reads=[res("glw"), res("negA")], writes=[res("glw")])
                        P.S(lambda e, j=j: e.activation(out=gg_[:, j, :N], in_=glw[:, j, :N], func=AF.Exp), reads=[res("glw")], writes=[res("gg")])
                        P.dma("sync", gS["lw"][j * 128:(j + 1) * 128, t0:t0 + N], glw[:, j, :N], reads=[res("glw")], writes=[res("scrA")])
                    else:
                        P.S(lambda e, pi=pi, j=j: e.activation(out=gb_[:, j - 4, :N], in_=c.pt[pi][:, :N], func=AF.Sigmoid),
                            reads=[res("pt%d" % pi)], writes=[res("gbeta")])
                def cons_g(fc, pi, m):
                    xi = xgc[0] % 2; xgc[0] += 1
                    xg, rxg = xgs[xi], res("xg%d" % xi)
                    P.G(lambda e: e.tensor_copy(out=xg[:, 0:3], in_=halo[:, fc, :]), reads=[res("halo")], writes=[rxg])
                    P.S(lambda e: e.activation(out=xg[:, 3:3 + N], in_=c.pt[pi][:, :N], func=AF.Copy), reads=[res("pt%d" % pi)], writes=[rxg])
                    o_, _ = PV_SLOTS["conv"]
                    cwp = lambda j: pv[:, o_ + fc * 4 + j:o_ + fc * 4 + j + 1]
                    P.V(lambda e: e.tensor_scalar(out=cacc[:, :N], in0=xg[:, 0:N], scalar1=cwp(0), scalar2=None, op0=ALU.mult),
                        reads=[rxg, r_pv], writes=[res("cacc")])
                    for j in range(1, 4):
                        P.V(lambda e, j=j: e.scalar_tensor_tensor(out=cacc[:, :N], in0=xg[:, j:j + N], scalar=cwp(j), in1=cacc[:, :N],
                                                               op0=ALU.mult, op1=ALU.add), reads=[rxg, r_pv, res("cacc")], writes=[res("cacc")])
                    P.G(lambda e: e.tensor_copy(out=halo[:, fc, :], in_=xg[:, N:N + 3]), reads=[rxg], writes=[res("halo")])
                    kind, hh = fc // 4, fc % 4
                    if kind == 2:
                        out_bf(gS["v"][hh * 128:(hh + 1) * 128, t0:t0 + N],
                               lambda o, ro: P.S(lambda e: e.activation(out=o[:, :N], in_=cacc[:, :N], func=AF.Silu), reads=[res("cacc")], writes=[ro]), None)
                        return
                    qq = qk[kind]; rq = res("qk%d" % kind)
                    P.S(lambda e: e.activation(out=qq[:, :N], in_=cacc[:, :N], func=AF.Silu), reads=[res("cacc")], writes=[rq])
                    P.G(lambda e: e.tensor_tensor(out=c.sq[:, 0, :N], in0=qq[:, :N], in1=qq[:, :N], op=ALU.mult), reads=[rq], writes=[res("sq")])
                    pj = sumsq_bc(c, lambda k: c.sq[:, 0, :N], 1, N, [res("sq")], lhs_name="ones")
                    rsqrt_from_psum(c, pj, N, c.rstd[:, :N], res("rstd"))
                    if kind == 0:
                        out_bf(gS["r"][hh * 128:(hh + 1) * 128, t0:t0 + N],
                               lambda o, ro: P.V(lambda e: e.scalar_tensor_tensor(out=o[:, :N], in0=qq[:, :N], scalar=128.0 ** -0.5, in1=c.rstd[:, :N],
                                                                                 op0=ALU.mult, op1=ALU.mult), reads=[rq, res("rstd")], writes=[ro]), None)
                    else:
                        P.V(lambda e: e.tensor_tensor(out=kn[:, :N], in0=qq[:, :N], in1=c.rstd[:, :N], op=ALU.mult), reads=[rq, res("rstd")], writes=[res("kn")])
                        out_bf(gS["a"][hh * 128:(hh + 1) * 128, t0:t0 + N],
                               lambda o, ro: P.G(lambda e: e.tensor_copy(out=o[:, :N], in_=kn[:, :N]), reads=[res("kn")], writes=[ro]), None)
                        P.V(lambda e: e.tensor_tensor(out=kn[:, :N], in0=kn[:, :N], in1=gb_[:, hh, :N], op=ALU.mult), reads=[res("kn"), res("gbeta")], writes=[res("kn")])
                        out_bf(gS["k"][hh * 128:(hh + 1) * 128, t0:t0 + N],
                               lambda o, ro: P.G(lambda e: e.tensor_copy(out=o[:, :N], in_=kn[:, :N]), reads=[res("kn")], writes=[ro]), None)
                        out_bf(gS["b"][hh * 128:(hh + 1) * 128, t0:t0 + N],
                               lambda o, ro: P.V(lambda e: e.scalar_tensor_tensor(out=o[:, :N], in0=kn[:, :N], scalar=-1.0, in1=gg_[:, hh, :N],
                                                                                 op0=ALU.mult, op1=ALU.mult), reads=[res("kn"), res("gg")], writes=[ro]), None)
                lin_fm(c, xfn, rx, N, "hyb_w_in", 1544, 1536, cons_g)
                def cons_z(fc, pi, m):
                    out_bf(gS["z"][fc * 128:(fc + 1) * 128, t0:t0 + N],
                           lambda o, ro: P.S(lambda e: e.activation(out=o[:, :N], in_=c.pt[pi][:, :N], func=AF.Silu), reads=[res("pt%d" % pi)], writes=[ro]), None)
                lin_fm(c, xfn, rx, N, "hyb_w_in", 3088, 512, cons_z)
        P.barrier()
        if stop_after == "A":
            P.finish(); return nc

        with contextlib.ExitStack() as st:
          if only is None or "B" in only:
            sb = lambda name, shape, dt: st.enter_context(nc.sbuf_tensor(un(name), shape, dt))
            psm = lambda name, shape, dt: st.enter_context(nc.psum_tensor(un(name), shape, dt))
            Vall = sb("Vall", [128, NCH, 584], BF16)
            P.V(lambda e: e.memset(Vall[:, :, 520:584], 0.0), writes=[res("Vall")])
            P.dma("sync", Vall[:, :, 0:520], fV.rearrange("(c p) f -> p c f", p=128), reads=[res("scrA")], writes=[res("Vall")])
            lf = sb("lf", [128, NCH, 8], F32)
            P.dma("sync", lf[:], flogf.rearrange("(c p) h -> p c h", p=128), reads=[res("scrA")], writes=[res("lf")])
            negc = sb("negc", [128, 8, NCH], F32)
            pe = sb("pe", [128, 8, NCH + 1], F32)
            lff = lf[:].rearrange("p c h -> p (c h)")
            ps_s = [psm("ps_s%d" % i, [128, 512], F32) for i in range(2)]
            ps_o = [psm("ps_o%d" % i, [128, 512], F32) for i in range(2)]
            ps_b = psm("ps_b", [128, 512], F32)
            ncol = NCH * 8
            ctmp = sb("ctmp", [128, NCH, 8], F32)
            ctot = sb("ctot", [128, NCH, 8], F32)
            cf = ctmp[:].rearrange("p c h -> p (c h)")
            tf = ctot[:].rearrange("p c h -> p (c h)")
            c0 = 0
            while c0 < ncol:
                cw = min(512, ncol - c0)
                P.T(lambda e, c0=c0, cw=cw: e.matmul(ps_s[0][:, :cw], lhsT=pvs("triu"), rhs=lff[:, c0:c0 + cw], start=True, stop=True),
                    reads=[r_pv, res("lf")], writes=[res("ps_s0")])
                P.V(lambda e, c0=c0, cw=cw: e.tensor_copy(out=cf[:, c0:c0 + cw], in_=ps_s[0][:, :cw]), reads=[res("ps_s0")], writes=[res("ctmp")])
                P.T(lambda e, c0=c0, cw=cw: e.matmul(ps_s[1][:, :cw], lhsT=pvs("ones"), rhs=lff[:, c0:c0 + cw], start=True, stop=True),
                    reads=[r_pv, res("lf")], writes=[res("ps_s1")])
                P.V(lambda e, c0=c0, cw=cw: e.tensor_copy(out=tf[:, c0:c0 + cw], in_=ps_s[1][:, :cw]), reads=[res("ps_s1")], writes=[res("ctot")])
                c0 += cw
            P.V(lambda e: e.memset(pe[:, :, 0:1], 0.0), writes=[res("pe")])
            for h in range(8):
                P.V(lambda e, h=h: e.tensor_tensor_scan(out=pe[:, h, 1:NCH + 1], data0=pvs("ones")[:, 0:NCH], data1=ctot[:, :, h],
                                                         initial=0.0, op0=ALU.mult, op1=ALU.add),
                    reads=[r_pv, res("ctot")], writes=[res("pe")])
                P.V(lambda e, h=h: e.tensor_tensor(out=negc[:, h, :], in0=ctmp[:, :, h], in1=pe[:, h, 0:NCH], op=ALU.add),
                    reads=[res("ctmp"), res("pe")], writes=[res("negc")])
                P.V(lambda e, h=h: e.tensor_scalar(out=negc[:, h, :], in0=negc[:, h, :], scalar1=-1.0, scalar2=None, op0=ALU.mult),
                    reads=[res("negc")], writes=[res("negc")])
            kTs = [sb("kTs%d" % i, [64, T], BF16) for i in range(2)]
            qTs = [sb("qTs%d" % i, [64, T], BF16) for i in range(2)]
            biasg = [sb("biasg%d" % i, [128, NCH], F32) for i in range(2)]
            pT = [sb("pT%d" % i, [128, 512], BF16) for i in range(3)]
            osb = [sb("osb%d" % i, [128, 512], F32) for i in range(2)]
            for i in range(2):
                P.V(lambda e, i=i: e.memset(osb[i][:], 0.0), writes=[res("osb%d" % i)])
            rinv = sb("rinv", [64, 512], F32)
            oout = [sb("oout%d" % i, [64, 512], BF16) for i in range(2)]
            cntB = {"s": 0, "p": 0, "g": 0}
            for h in range(8):
                hb = h % 2
                P.dma("sync", kTs[hb][:], fkT[h * 64:(h + 1) * 64, :], reads=[res("scrA")], writes=[res("kTs%d" % hb)])
                P.dma("sync", qTs[hb][:], fqT[h * 64:(h + 1) * 64, :], reads=[res("scrA")], writes=[res("qTs%d" % hb)])
                for (t0, N) in tiles:
                    gi = cntB["g"] % 2; cntB["g"] += 1
                    i0 = t0 // 128; nb = N // 128
                    anc = min(i0 + 2, NCH)
                    J = i0 + nb
                    P.V(lambda e, gi=gi, anc=anc, J=J, h=h: e.tensor_scalar(out=biasg[gi][:, :J], in0=negc[:, h, :J], scalar1=pe[:, h, anc:anc + 1],
                                                                             scalar2=None, op0=ALU.add),
                        reads=[res("negc"), res("pe")], writes=[res("biasg%d" % gi)])
                    pend = None
                    for j in range(J + 1):
                        if j < J:
                            cs = 0 if j < i0 else (j - i0) * 128
                            ncols = N - cs
                            si = cntB["s"] % 2; cntB["s"] += 1
                            pi = cntB["p"] % 3; cntB["p"] += 1
                            P.T(lambda e: e.matmul(ps_s[si][:, :ncols], lhsT=kTs[hb][:, j * 128:(j + 1) * 128], rhs=qTs[hb][:, t0 + cs:t0 + cs + ncols], start=True, stop=True),
                                reads=[res("kTs%d" % hb), res("qTs%d" % hb)], writes=[res("ps_s%d" % si)])
                            P.S(lambda e: e.activation(out=pT[pi][:, :ncols], in_=ps_s[si][:, :ncols], func=AF.Exp, bias=biasg[gi][:, j:j + 1]),
                                reads=[res("ps_s%d" % si), res("biasg%d" % gi)], writes=[res("pT%d" % pi)])
                            if j >= i0:
                                P.G(lambda e: e.tensor_tensor(out=pT[pi][:, 0:128], in0=pT[pi][:, 0:128], in1=m_ui_b[:], op=ALU.mult),
                                    reads=[res("pT%d" % pi), r_cb], writes=[res("pT%d" % pi)])
                        if pend is not None:
                            (pj, pcs, pncols, ppi) = pend
                            P.T(lambda e: e.matmul(ps_o[gi][:, pcs:pcs + pncols], lhsT=Vall[:, pj, h * 65:h * 65 + 128], rhs=pT[ppi][:, :pncols],
                                                   start=(pj == 0), stop=(pj == J - 1)), reads=[res("Vall"), res("pT%d" % ppi)], writes=[res("ps_o%d" % gi)])
                        pend = (j, cs, ncols, pi) if j < J else None
                    P.S(lambda e, gi=gi, N=N: e.activation(out=osb[gi][0:65, :N], in_=ps_o[gi][0:65, :N], func=AF.Copy),
                        reads=[res("ps_o%d" % gi)], writes=[res("osb%d" % gi)])
                    o_, _ = PV_SLOTS["sel65"]
                    P.T(lambda e, gi=gi, N=N: e.matmul(ps_b[:64, :N], lhsT=pv[:, o_:o_ + 64], rhs=osb[gi][:, :N], start=True, stop=True),
                        reads=[r_pv, res("osb%d" % gi)], writes=[res("ps_b")])
                    P.V(lambda e, N=N: e.reciprocal(out=rinv[:, :N], in_=ps_b[:64, :N]), reads=[res("ps_b")], writes=[res("rinv")])
                    P.V(lambda e, gi=gi, N=N: e.tensor_tensor(out=oout[gi][:, :N], in0=osb[gi][0:64, :N], in1=rinv[:, :N], op=ALU.mult),
                        reads=[res("osb%d" % gi), res("rinv")], writes=[res("oout%d" % gi)])
                    P.dma("sync", oT[h * 64:(h + 1) * 64, t0:t0 + N], oout[gi][:, :N], reads=[res("oout%d" % gi)], writes=[res("oT_dram")])
        P.barrier()
        if stop_after == "B":
            P.finish(); return nc

        def dplr_phase(name, units, S_, post_setup, post, scalar_decay=False):
            with contextlib.ExitStack() as st:
                sb = lambda nm, shape, dt: st.enter_context(nc.sbuf_tensor(un(nm), shape, dt))
                psm = lambda nm, shape, dt: st.enter_context(nc.psum_tensor(un(nm), shape, dt))
                hd = units[0][0][2]
                ngs = len(set(cx[0] for cx in units[0]))
                SC = 4
                NG2 = 2 * ngs
                inb = {}
                for gs in range(NG2):
                    for k in ("r", "k", "a", "b", "v"):
                        inb[(gs, k)] = sb("in_%s%d" % (k, gs), [128, SC * 128], BF16)
                    inb[(gs, "lw")] = sb("in_lw%d" % gs, [128, SC * 128], F32)
                pobj = post_setup(st, NG2, SC)
                Lc = [sb("Lc%d" % g, [128, 129], F32) for g in range(NG2)]
                Lx = [sb("nLm%d" % g, [128, 1], F32) for g in range(NG2)]
                e1 = [sb("e1_%d" % g, [128, 129], F32) for g in range(NG2)]
                e2 = [sb("e2_%d" % g, [128, 129], F32) for g in range(NG2)]
                e5 = [sb("e5_%d" % g, [128, 128], F32) for g in range(NG2)]
                e6 = [sb("e6_%d" % g, [128, 128], F32) for g in range(NG2)]
                if scalar_decay:
                    Dm = {(g, k): sb("D%s_%d" % (k, g), [128, 128], F32) for g in range(NG2) for k in ("m_su", "m_sl", "m_ui")}
                    lcc = [sb("lcc%d" % g, [128, 2], F32) for g in range(NG2)]
                    dtmp = [sb("dtmp%d" % g, [128, 128], F32) for g in range(NG2)]
                opn = ("rh", "rt", "ah", "at", "bh", "kh", "btT", "ktT")
                ops_ = {(g, k): sb("%s%d" % (k, g), [128, 128], BF16) for g in range(NG2) for k in opn}
                tmj = {(g, k): sb("tm_%s%d" % (k, g), [128, 128], BF16) for g in range(NG2) for k in ("bt", "kt", "vt")}
                pstr = psm("pstr", [128, 4, 128], BF16)
                yT = [sb("yT%d" % g, [128, 128], F32) for g in range(ngs)]
                Sf = [sb("Sf%d" % g, [128, 128], F32) for g in range(ngs)]
                Sb = [sb("Sb%d" % g, [128, 128], BF16) for g in range(ngs)]
                hx = []
                for h in range(2):
                    o = TokCtx()
                    o.psx = [psm("psx%d_%d" % (h, i), [128, 512], F32) for i in range(2)]
                    o.psy = psm("psy%d" % h, [128, 512], F32)
                    o.xc = 0
                    for k in ("M0", "MT0", "Mb0", "Mb1", "MTb0", "MTb1", "P0", "P1", "Q0", "Q1", "Noff", "NoffT", "Z", "Z2", "AakT", "ArbT", "ArkT", "RHS", "U"):
                        setattr(o, k, sb("%s_%d" % (k, h), [128, 128], BF16))
                    hx.append(o)
                rn = lambda s: res(name + "_" + s)
                trc = [0]

                for unit in units:
                    gslots = []
                    for cx in unit:
                        if cx[0] not in gslots:
                            gslots.append(cx[0])
                    for g in range(ngs):
                        P.V(lambda e, g=g: e.memset(Sf[g][:], 0.0), writes=[rn("Sf%d" % g)])
                        P.V(lambda e, g=g: e.memset(Sb[g][:], 0.0), writes=[rn("Sb%d" % g)])
                        P.V(lambda e, g=g: e.memset(Lc[g][:, 0:1], 0.0), writes=[rn("Lc%d" % g)])
                        P.V(lambda e, g=g: e.memset(Lc[ngs + g][:, 0:1], 0.0), writes=[rn("Lc%d" % (ngs + g))])
                    def prologue(c):
                      sc, cl = c // SC, c % SC
                      ipar = (sc % 2) * ngs
                      cpar = (c % 2) * ngs
                      if cl == 0:
                            nsc = min(SC, NCH - c) * 128
                            for g0, grow in enumerate(gslots):
                                g = ipar + g0
                                for k in ("r", "k", "a", "b", "v", "lw"):
                                    P.dma("sync", inb[(g, k)][:, :nsc], S_[k][grow:grow + 128, c * 128:c * 128 + nsc],
                                          reads=[res("scr" + name)], writes=[rn("in_%s%d" % (k, g))])
                                post("load", g, grow, c, nsc, pobj)
                      cols = slice(cl * 128, (cl + 1) * 128)
                      for g0 in range(ngs):
                        g = cpar + g0
                        gi_ = ipar + g0
                        if True:
                            rl = [rn("Lc%d" % g)]
                            P.V(lambda e, g=g, gi_=gi_: e.tensor_tensor_scan(out=Lc[g][:, 1:129], data0=pvs("ones"), data1=inb[(gi_, "lw")][:, cols],
                                                                     initial=0.0, op0=ALU.mult, op1=ALU.add),
                                reads=[r_pv, rn("in_lw%d" % gi_)], writes=rl)
                            P.S(lambda e, g=g: e.activation(out=e2[g][:], in_=Lc[g][:], func=AF.Exp), reads=rl, writes=[rn("e2_%d" % g)])
                            P.S(lambda e, g=g: e.activation(out=e6[g][:], in_=Lc[g][:, 1:129], func=AF.Exp, bias=Lc[g][:, 128:129], scale=-1.0),
                                reads=rl, writes=[rn("e6_%d" % g)])
                            if not scalar_decay:
                                P.V(lambda e, g=g: e.tensor_scalar(out=Lx[g][:], in0=Lc[g][:, 64:65], scalar1=-1.0, scalar2=None, op0=ALU.mult),
                                    reads=rl, writes=[rn("nLm%d" % g)])
                                P.S(lambda e, g=g: e.activation(out=e1[g][:], in_=Lc[g][:], func=AF.Exp, bias=Lx[g][:, 0:1]),
                                    reads=rl + [rn("nLm%d" % g)], writes=[rn("e1_%d" % g)])
                                P.S(lambda e, g=g: e.activation(out=e5[g][:], in_=Lc[g][:, 1:129], func=AF.Exp, bias=Lc[g][:, 64:65], scale=-1.0),
                                    reads=rl, writes=[rn("e5_%d" % g)])
                                specs = [("rh", "r", e1[g][:, 1:129], "e1_"), ("rt", "r", e2[g][:, 1:129], "e2_"),
                                         ("ah", "a", e1[g][:, 0:128], "e1_"), ("at", "a", e2[g][:, 0:128], "e2_"),
                                         ("bh", "b", e5[g][:], "e5_"), ("kh", "k", e5[g][:], "e5_"),
                                         ("btT", "b", e6[g][:], "e6_"), ("ktT", "k", e6[g][:], "e6_")]
                            else:
                                rd = [rn("dtmp%d" % g)]
                                P.V(lambda e, g=g: e.tensor_tensor(out=dtmp[g][:], in0=Lc[g][:, 1:129], in1=pvs("ident"), op=ALU.mult), reads=rl + [r_pv], writes=rd)
                                P.V(lambda e, g=g: e.reduce_sum(out=lcc[g][:, 0:1], in_=dtmp[g][:], axis=mybir.AxisListType.X), reads=rd, writes=[rn("lcc%d" % g)])
                                P.V(lambda e, g=g: e.tensor_tensor(out=dtmp[g][:], in0=Lc[g][:, 0:128], in1=pvs("ident"), op=ALU.mult), reads=rl + [r_pv], writes=rd)
                                P.V(lambda e, g=g: e.reduce_sum(out=lcc[g][:, 1:2], in_=dtmp[g][:], axis=mybir.AxisListType.X), reads=rd, writes=[rn("lcc%d" % g)])
                                rlc = [rn("lcc%d" % g)]
                                for (mk, src, colj, neg) in (("m_ui", Lc[g][:, 1:129], 0, False), ("m_su", Lc[g][:, 0:128], 0, False), ("m_sl", Lc[g][:, 1:129], 1, True)):
                                    D_ = Dm[(g, mk)]; rD = rn("D%s_%d" % (mk, g))
                                    if not neg:
                                        P.V(lambda e, g=g, src=src, colj=colj: e.tensor_scalar(out=dtmp[g][:], in0=src, scalar1=lcc[g][:, colj:colj + 1], scalar2=0.0,
                                                                                                op0=ALU.subtract, op1=ALU.min), reads=rl + rlc, writes=rd)
                                    else:
                                        P.V(lambda e, g=g, src=src, colj=colj: e.tensor_scalar(out=dtmp[g][:], in0=src, scalar1=-1.0, scalar2=lcc[g][:, colj:colj + 1],
                                                                                                op0=ALU.mult, op1=ALU.add), reads=rl + rlc, writes=rd)
                                        P.V(lambda e, g=g: e.tensor_scalar(out=dtmp[g][:], in0=dtmp[g][:], scalar1=0.0, scalar2=None, op0=ALU.min), reads=rd, writes=rd)
                                    P.S(lambda e, g=g, D_=D_: e.activation(out=D_[:], in_=dtmp[g][:], func=AF.Exp), reads=rd, writes=[rD])
                                    P.G(lambda e, D_=D_, mk=mk: e.tensor_tensor(out=D_[:], in0=D_[:], in1=pvs(mk), op=ALU.mult), reads=[rD, r_pv], writes=[rD])
                                specs = [("rt", "r", e2[g][:, 1:129], "e2_"), ("at", "a", e2[g][:, 0:128], "e2_"),
                                         ("btT", "b", e6[g][:], "e6_"), ("ktT", "k", e6[g][:], "e6_")]
                            for qi, (on, ik, eap, en) in enumerate(specs):
                                eng = "vector" if qi % 2 == 0 else "gpsimd"
                                P.op(eng, lambda e, g=g, gi_=gi_, on=on, ik=ik, eap=eap: e.tensor_tensor(out=ops_[(g, on)][:], in0=inb[(gi_, ik)][:, cols], in1=eap, op=ALU.mult),
                                     reads=[rn("in_%s%d" % (ik, gi_)), rn(en + "%d" % g)], writes=[rn("%s%d" % (on, g))])
                            for (tn, src, rsrc) in (("bt", ops_[(g, "btT")][:], rn("btT%d" % g)), ("kt", ops_[(g, "ktT")][:], rn("ktT%d" % g)),
                                                    ("vt", inb[(gi_, "v")][:, cols], rn("in_v%d" % gi_))):
                                ti_ = trc[0] % 4; trc[0] += 1
                                P.T(lambda e, ti_=ti_, src=src: e.transpose(out=pstr[:, ti_, :], in_=src, identity=cb["ident"][:]),
                                    reads=[rsrc, r_cb], writes=[rn("pstr")])
                                P.S(lambda e, ti_=ti_, g=g, tn=tn: e.activation(out=tmj[(g, tn)][:], in_=pstr[:, ti_, :], func=AF.Copy),
                                    reads=[rn("pstr")], writes=[rn("tm_%s%d" % (tn, g))])
                    prologue(0)
                    for c in range(NCH):
                        sc, cl = c // SC, c % SC
                        ipar = (sc % 2) * ngs
                        cpar = (c % 2) * ngs
                        cols = slice(cl * 128, (cl + 1) * 128)
                        H = []
                        for hi, cx in enumerate(unit):
                            g = cpar + gslots.index(cx[0])
                            H.append((hx[hi], g, slice(cx[1], cx[1] + hd), hi))
                        def xslot(o, hi):
                            i = o.xc % 2; o.xc += 1
                            return o.psx[i][:, 0:128], rn("psx%d_%d" % (hi, i))
                        def opr(g, k, ps_):
                            if scalar_decay and k in ("rh", "ah", "bh", "kh"):
                                gi_ = ipar + (g - cpar)
                                return inb[(gi_, k[0])][ps_, cols], rn("in_%s%d" % (k[0], gi_))
                            return ops_[(g, k)][ps_, :], rn("%s%d" % (k, g))
                        for (o, g, ps_, hi) in H:
                            for (dst, l, r_, mask) in (("MT0", "bh", "ah", "m_su"), ("M0", "ah", "bh", "m_sl"), ("AakT", "kh", "ah", "m_su"),
                                                       ("ArbT", "bh", "rh", "m_ui"), ("ArkT", "kh", "rh", "m_ui")):
                                xa, rx_ = xslot(o, hi)
                                la, rl_ = opr(g, l, ps_); ra, rr_ = opr(g, r_, ps_)
                                P.T(lambda e, xa=xa, la=la, ra=ra: e.matmul(xa, lhsT=la, rhs=ra, start=True, stop=True), reads=[rl_, rr_], writes=[rx_])
                                mk_ap = Dm[(g, mask)][:] if scalar_decay else pvs(mask)
                                mk_r = rn("D%s_%d" % (mask, g)) if scalar_decay else r_pv
                                P.V(lambda e, o=o, dst=dst, xa=xa, mk_ap=mk_ap: e.tensor_tensor(out=getattr(o, dst)[:], in0=xa, in1=mk_ap, op=ALU.mult),
                                    reads=[rx_, mk_r], writes=[rn("%s_%d" % (dst, hi))])
                        def mm_evac(o, hi, lhs, rlhs, rhs, rrhs, dst, rdst, add=None, radd=None):
                            xa, rx_ = xslot(o, hi)
                            P.T(lambda e: e.matmul(xa, lhsT=lhs[:], rhs=rhs[:], start=True, stop=True), reads=[rlhs, rrhs], writes=[rx_])
                            if add is None:
                                P.S(lambda e: e.activation(out=dst[:], in_=xa, func=AF.Copy), reads=[rx_], writes=[rdst])
                            else:
                                P.V(lambda e: e.tensor_tensor(out=dst[:], in0=xa, in1=add[:], op=ALU.add), reads=[rx_, radd], writes=[rdst])
                        for (o, g, ps_, hi) in H:
                            R_ = lambda k: rn("%s_%d" % (k, hi))
                            P.G(lambda e, o=o: e.tensor_tensor(out=o.Mb0[:], in0=o.M0[:], in1=cb["bd32"][:], op=ALU.mult), reads=[R_("M0"), r_cb], writes=[R_("Mb0")])
                            P.G(lambda e, o=o: e.tensor_tensor(out=o.MTb0[:], in0=o.MT0[:], in1=cb["bd32"][:], op=ALU.mult), reads=[R_("MT0"), r_cb], writes=[R_("MTb0")])
                            P.G(lambda e, o=o: e.tensor_tensor(out=o.P0[:], in0=o.MTb0[:], in1=cb["ident"][:], op=ALU.add), reads=[R_("MTb0"), r_cb], writes=[R_("P0")])
                            P.G(lambda e, o=o: e.tensor_tensor(out=o.Q0[:], in0=o.Mb0[:], in1=cb["ident"][:], op=ALU.add), reads=[R_("Mb0"), r_cb], writes=[R_("Q0")])
                        cur = 0
                        cn = lambda k, i: "%s%d" % (k, i)
                        for m in range(1, 5):
                            nxt = 1 - cur
                            for (o, g, ps_, hi) in H:
                                R_ = lambda k: rn("%s_%d" % (k, hi))
                                G_ = lambda k: getattr(o, k)
                                mm_evac(o, hi, G_(cn("MTb", cur)), R_(cn("MTb", cur)), G_(cn("Mb", cur)), R_(cn("Mb", cur)), G_(cn("Mb", nxt)), R_(cn("Mb", nxt)))
                                mm_evac(o, hi, G_(cn("Mb", cur)), R_(cn("Mb", cur)), G_(cn("MTb", cur)), R_(cn("MTb", cur)), G_(cn("MTb", nxt)), R_(cn("MTb", nxt)))
                            for (o, g, ps_, hi) in H:
                                R_ = lambda k: rn("%s_%d" % (k, hi))
                                G_ = lambda k: getattr(o, k)
                                mm_evac(o, hi, G_(cn("Mb", nxt)), R_(cn("Mb", nxt)), G_(cn("P", cur)), R_(cn("P", cur)), G_(cn("P", nxt)), R_(cn("P", nxt)),
                                        add=G_(cn("P", cur)), radd=R_(cn("P", cur)))
                                mm_evac(o, hi, G_(cn("MTb", nxt)), R_(cn("MTb", nxt)), G_(cn("Q", cur)), R_(cn("Q", cur)), G_(cn("Q", nxt)), R_(cn("Q", nxt)),
                                        add=G_(cn("Q", cur)), radd=R_(cn("Q", cur)))
                            cur = nxt
                        for (lev, offm) in ((64, "off64"), (128, "off128")):
                            nxt = 1 - cur
                            for (o, g, ps_, hi) in H:
                                R_ = lambda k: rn("%s_%d" % (k, hi))
                                G_ = lambda k: getattr(o, k)
                                P.G(lambda e, o=o: e.tensor_tensor(out=o.Noff[:], in0=o.M0[:], in1=cb[offm][:], op=ALU.mult), reads=[R_("M0"), r_cb], writes=[R_("Noff")])
                                P.G(lambda e, o=o: e.tensor_tensor(out=o.NoffT[:], in0=o.MT0[:], in1=cb[offm][:], op=ALU.mult), reads=[R_("MT0"), r_cb], writes=[R_("NoffT")])
                            for (o, g, ps_, hi) in H:
                                R_ = lambda k: rn("%s_%d" % (k, hi))
                                G_ = lambda k: getattr(o, k)
                                X_, rX = G_(cn("Q", cur)), R_(cn("Q", cur))
                                XT_, rXT = G_(cn("P", cur)), R_(cn("P", cur))
                                if lev != 128:
                                    mm_evac(o, hi, o.NoffT, R_("NoffT"), X_, rX, o.Z, R_("Z"))
                                mm_evac(o, hi, o.Noff, R_("Noff"), XT_, rXT, o.Z2, R_("Z2"))
                            for (o, g, ps_, hi) in H:
                                R_ = lambda k: rn("%s_%d" % (k, hi))
                                G_ = lambda k: getattr(o, k)
                                X_, rX = G_(cn("Q", cur)), R_(cn("Q", cur))
                                XT_, rXT = G_(cn("P", cur)), R_(cn("P", cur))
                                if lev != 128:
                                    mm_evac(o, hi, XT_, rXT, o.Z, R_("Z"), G_(cn("Q", nxt)), R_(cn("Q", nxt)), add=X_, radd=rX)
                                mm_evac(o, hi, X_, rX, o.Z2, R_("Z2"), G_(cn("P", nxt)), R_(cn("P", nxt)), add=XT_, radd=rXT)
                            cur = nxt
                        if c + 1 < NCH:
                            prologue(c + 1)
                        for (o, g, ps_, hi) in H:
                            Pf, rPf = getattr(o, "P%d" % cur), rn("P%d_%d" % (cur, hi))
                            ry = rn("psy%d" % hi)
                            hs = slice(ps_.start, ps_.start + hd)
                            at_, rat = opr(g, "at", ps_)
                            rt_, rrt = opr(g, "rt", ps_)
                            P.T(lambda e, o=o, at_=at_, g=g, ps_=ps_: e.matmul(o.psy[:, 256:256 + hd], lhsT=at_, rhs=Sb[g - cpar][ps_, 0:hd], start=True, stop=False),
                                reads=[rat, rn("Sb%d" % (g - cpar))], writes=[ry])
                            P.T(lambda e, o=o, g=g, hs=hs: e.matmul(o.psy[:, 256:256 + hd], lhsT=o.AakT[:], rhs=tmj[(g, "vt")][:, hs], start=False, stop=True),
                                reads=[rn("AakT_%d" % hi), rn("tm_vt%d" % g)], writes=[ry])
                            P.S(lambda e, o=o: e.activation(out=o.RHS[:, :hd], in_=o.psy[:, 256:256 + hd], func=AF.Copy), reads=[ry], writes=[rn("RHS_%d" % hi)])
                        for (o, g, ps_, hi) in H:
                            Pf, rPf = getattr(o, "P%d" % cur), rn("P%d_%d" % (cur, hi))
                            ry = rn("psy%d" % hi)
                            hs = slice(ps_.start, ps_.start + hd)
                            at_, rat = opr(g, "at", ps_)
                            rt_, rrt = opr(g, "rt", ps_)
                            P.T(lambda e, o=o, Pf=Pf: e.matmul(o.psy[:, 384:384 + hd], lhsT=Pf[:], rhs=o.RHS[:, :hd], start=True, stop=True),
                                reads=[rPf, rn("RHS_%d" % hi)], writes=[ry])
                            P.S(lambda e, o=o: e.activation(out=o.U[:, :hd], in_=o.psy[:, 384:384 + hd], func=AF.Copy), reads=[ry], writes=[rn("U_%d" % hi)])
                        for (o, g, ps_, hi) in H:
                            Pf, rPf = getattr(o, "P%d" % cur), rn("P%d_%d" % (cur, hi))
                            ry = rn("psy%d" % hi)
                            hs = slice(ps_.start, ps_.start + hd)
                            at_, rat = opr(g, "at", ps_)
                            rt_, rrt = opr(g, "rt", ps_)
                            P.T(lambda e, o=o, g=g, ps_=ps_, rt_=rt_: e.matmul(o.psy[ps_, 0:128], lhsT=Sb[g - cpar][ps_, 0:hd], rhs=rt_, start=True, stop=False),
                                reads=[rn("Sb%d" % (g - cpar)), rrt], writes=[ry])
                            P.T(lambda e, o=o, ps_=ps_: e.matmul(o.psy[ps_, 0:128], lhsT=o.U[:, :hd], rhs=o.ArbT[:], start=False, stop=False),
                                reads=[rn("U_%d" % hi), rn("ArbT_%d" % hi)], writes=[ry])
                            P.T(lambda e, o=o, g=g, ps_=ps_, hs=hs: e.matmul(o.psy[ps_, 0:128], lhsT=tmj[(g, "vt")][:, hs], rhs=o.ArkT[:], start=False, stop=True),
                                reads=[rn("tm_vt%d" % g), rn("ArkT_%d" % hi)], writes=[ry])
                            P.V(lambda e, o=o, g=g, ps_=ps_: e.tensor_copy(out=yT[g - cpar][ps_, :], in_=o.psy[ps_, 0:128]), reads=[ry], writes=[rn("yT%d" % (g - cpar))])
                        for (o, g, ps_, hi) in H:
                            Pf, rPf = getattr(o, "P%d" % cur), rn("P%d_%d" % (cur, hi))
                            ry = rn("psy%d" % hi)
                            hs = slice(ps_.start, ps_.start + hd)
                            at_, rat = opr(g, "at", ps_)
                            rt_, rrt = opr(g, "rt", ps_)
                            P.T(lambda e, o=o, g=g, ps_=ps_, hs=hs: e.matmul(o.psy[ps_, 128:128 + hd], lhsT=tmj[(g, "bt")][:, hs], rhs=o.U[:, :hd], start=True, stop=False),
                                reads=[rn("tm_bt%d" % g), rn("U_%d" % hi)], writes=[ry])
                            P.T(lambda e, o=o, g=g, ps_=ps_, hs=hs: e.matmul(o.psy[ps_, 128:128 + hd], lhsT=tmj[(g, "kt")][:, hs], rhs=tmj[(g, "vt")][:, hs], start=False, stop=True),
                                reads=[rn("tm_kt%d" % g), rn("tm_vt%d" % g)], writes=[ry])
                            P.V(lambda e, o=o, g=g, ps_=ps_: e.scalar_tensor_tensor(out=Sf[g - cpar][ps_, 0:hd], in0=Sf[g - cpar][ps_, 0:hd], scalar=e2[g][ps_, 128:129],
                                                                                   in1=o.psy[ps_, 128:128 + hd], op0=ALU.mult, op1=ALU.add),
                                reads=[rn("Sf%d" % (g - cpar)), rn("e2_%d" % g), ry], writes=[rn("Sf%d" % (g - cpar))])
                            P.G(lambda e, g=g, ps_=ps_: e.tensor_copy(out=Sb[g - cpar][ps_, 0:hd], in_=Sf[g - cpar][ps_, 0:hd]), reads=[rn("Sf%d" % (g - cpar))], writes=[rn("Sb%d" % (g - cpar))])
                        for g0, grow in enumerate(gslots):
                            post("chunk", ipar + g0, grow, c, (yT[g0], rn("yT%d" % g0), cl), pobj)
            P.barrier()

        def gdn_post_setup(st, ngs, SC):
            sb = lambda nm, shape, dt: st.enter_context(nc.sbuf_tensor(un(nm), shape, dt))
            o = TokCtx()
            o.z = [sb("gz%d" % g, [128, SC * 128], BF16) for g in range(ngs)]
            o.sq = sb("gpsq", [128, 128], BF16)
            o.rs = sb("gprs", [128, 128], F32)
            o.ob = [sb("gpob%d" % i, [128, 128], BF16) for i in range(2)]
            o.ps = st.enter_context(nc.psum_tensor(un("gpps"), [128, 128], F32))
            o.k = 0
            return o

        def gdn_post(kind, g, grow, c, arg, o):
            if kind == "load":
                P.dma("sync", o.z[g][:, :arg], gS["z"][grow:grow + 128, c * 128:c * 128 + arg], reads=[res("scrA")], writes=[res("gz%d" % g)])
                return
            yT_, ry, cl = arg
            P.G(lambda e: e.tensor_tensor(out=o.sq[:], in0=yT_[:], in1=yT_[:], op=ALU.mult), reads=[ry], writes=[res("gpsq")])
            P.T(lambda e: e.matmul(o.ps[:], lhsT=cb["ones"][:], rhs=o.sq[:], start=True, stop=True), reads=[r_cb, res("gpsq")], writes=[res("gpps")])
            P.S(lambda e: e.activation(out=o.rs[:], in_=o.ps[:], func=AF.Sqrt, bias=cvec[:, 0:1], scale=1.0 / 128), reads=[res("gpps"), r_cb], writes=[res("gprs")])
            P.V(lambda e: e.reciprocal(out=o.rs[:], in_=o.rs[:]), reads=[res("gprs")], writes=[res("gprs")])
            P.V(lambda e: e.scalar_tensor_tensor(out=o.rs[:], in0=yT_[:], scalar=pvs("o_gain"), in1=o.rs[:], op0=ALU.mult, op1=ALU.mult),
                reads=[ry, r_pv, res("gprs")], writes=[res("gprs")])
            i = o.k % 2; o.k += 1
            P.V(lambda e: e.tensor_tensor(out=o.ob[i][:], in0=o.rs[:], in1=o.z[g][:, cl * 128:(cl + 1) * 128], op=ALU.mult),
                reads=[res("gprs"), res("gz%d" % g)], writes=[res("gpob%d" % i)])
            P.dma("sync", oT[512 + grow:512 + grow + 128, c * 128:(c + 1) * 128], o.ob[i][:], reads=[res("gpob%d" % i)], writes=[res("oT_dram")])

        R["scrgdn"] = res("scrA")
        if only is None or "C" in only:
          dplr_phase("gdn", [[(0, 0, 128), (128, 0, 128)], [(256, 0, 128), (384, 0, 128)]], gS, gdn_post_setup, gdn_post, scalar_decay=True)
        if stop_after == "C":
            P.finish(); return nc

        with contextlib.ExitStack() as st:
          if only is None or "D" in only:
            sb = lambda name, shape, dt: st.enter_context(nc.sbuf_tensor(un(name), shape, dt))
            c = make_tok_ctx(st, nwi=3, nwo=3)
            oin = sb("oin", [128, 8, 512], BF16)
            xx = sb("xx", [128, 8, 512], BF16)
            xm = sb("xm", [128, 8, 512], BF16)
            w1s = sb("w1s", [128, 8, 64], BF16); a1s = sb("a1s", [128, 8, 64], BF16); g1s = sb("g1s", [128, 8, 160], BF16)
            w2s = sb("w2s", [64, D], BF16); a2s = sb("a2s", [64, D], BF16); g2s = sb("g2s", [128, 2, D], BF16)
            P.dma("sync", w1s[:], wview(Wb["rwkv_w1"]), reads=[res("W_rwkv_w1")], writes=[res("lora")])
            P.dma("sync", a1s[:], wview(Wb["rwkv_a1"]), reads=[res("W_rwkv_a1")], writes=[res("lora")])
            P.dma("sync", g1s[:], wview(Wb["rwkv_g1"]), reads=[res("W_rwkv_g1")], writes=[res("lora")])
            P.dma("sync", w2s[:], Wb["rwkv_w2"], reads=[res("W_rwkv_w2")], writes=[res("lora")])
            P.dma("sync", a2s[:], Wb["rwkv_a2"], reads=[res("W_rwkv_a2")], writes=[res("lora")])
            P.dma("sync", g2s[:, 0, :], Wb["rwkv_g2"][0:128, :], reads=[res("W_rwkv_g2")], writes=[res("lora")])
            P.dma("sync", g2s[0:32, 1, :], Wb["rwkv_g2"][128:160, :], reads=[res("W_rwkv_g2")], writes=[res("lora")])
            r_lora = res("lora")
            lmid = [sb("lmid%d" % i, [128, 512], BF16) for i in range(2)]
            rr = sb("rr", [128, 8, 512], BF16)
            kk_ = sb("kkk", [128, 8, 512], F32)
            aa = sb("aa", [128, 512], F32)
            t1 = sb("t1", [128, 512], F32)
            t2 = sb("t2", [128, 512], F32)
            ob = [sb("obD%d" % i, [128, 512], BF16) for i in range(4)]
            obf = [sb("obF%d" % i, [128, 512], F32) for i in range(2)]
            ocnt = [0, 0]
            P.V(lambda e: e.memset(c.hn[:, :, 0:1], 0.0), writes=[res("hn")])
            omk = sb("omk", [128, 8], F32)
            print("phase D sbuf remaining", nc.sbuf_bytes_remaining)
            P.V(lambda e: e.tensor_scalar(out=omk[:], in0=pvs("k_a"), scalar1=-1.0, scalar2=1.0, op0=ALU.mult, op1=ALU.add), reads=[r_pv], writes=[res("omk")])

            def out_bf(dst_ap, make):
                i = ocnt[0] % 4; ocnt[0] += 1
                make(ob[i], res("obD%d" % i))
                P.dma("sync", dst_ap, ob[i][:, :dst_ap.shape[-1]], reads=[res("obD%d" % i)], writes=[res("scrrwkv")])

            for ti, (t0, N) in enumerate(tiles):
                h_t, r_h = c.hT[ti % 2], res("hT%d" % (ti % 2))
                P.dma("sync", h_t[:, :, :N], hview(hT)[:, :, t0:t0 + N], reads=[res("hT_dram")], writes=[r_h])
                P.dma("sync", oin[:, :, :N], hview(oT)[:, :, t0:t0 + N], reads=[res("oT_dram")], writes=[res("oin")])
                proj_residual(c, h_t, r_h, N, oin, res("oin"), "hyb_w_out")
                ffn(c, h_t, r_h, N, 1)
                ffn(c, h_t, r_h, N, 2)
                P.dma("sync", hview(hT)[:, :, t0:t0 + N], h_t[:, :, :N], reads=[r_h], writes=[res("hT_dram")])
                rmsnorm(c, h_t, r_h, N, "mix_norm1")
                import os as _os
                DCUT = float(_os.environ.get("DCUT", "99"))
                if DCUT <= 0:
                    continue
                for kc in range(8):
                    P.op("vector", lambda e, kc=kc: e.tensor_tensor(out=xx[:, kc, :N], in0=c.hn[:, kc, 0:N], in1=c.hn[:, kc, 1:1 + N], op=ALU.subtract),
                         reads=[res("hn")], writes=[res("xx")])
                o_mu, _ = PV_SLOTS["mu"]
                def mix(i):
                    for kc in range(8):
                        eng = "vector" if kc % 2 == 0 else "gpsimd"
                        P.op("vector", lambda e, kc=kc: e.scalar_tensor_tensor(out=xm[:, kc, :N], in0=xx[:, kc, :N], scalar=pv[:, o_mu + i * 8 + kc:o_mu + i * 8 + kc + 1],
                                                                          in1=c.hn[:, kc, 1:1 + N], op0=ALU.mult, op1=ALU.add),
                             reads=[res("xx"), res("hn"), r_pv], writes=[res("xm")])
                xfn = lambda kc: xm[:, kc, :N]
                rxm = [res("xm")]
                if DCUT <= 0.3:
                    continue
                mix(0)
                if DCUT <= 0.6:
                    continue
                def cons_r(fc, pi, m):
                    if _os.environ.get("RVAR", "") != "a":
                        P.S(lambda e: e.activation(out=rr[:, fc, :N], in_=c.pt[pi][:, :N], func=AF.Copy), reads=[res("pt%d" % pi)], writes=[res("rr")])
                    if _os.environ.get("RVAR", "") == "b":
                        return
                    out_bf(rS["r"][fc * 128:(fc + 1) * 128, t0:t0 + N],
                           lambda o, ro: P.V(lambda e: e.tensor_copy(out=o[:, :N], in_=c.pt[pi][:, :N]), reads=[res("pt%d" % pi)], writes=[ro]))
                lin_fm(c, xfn, rxm, N, "rwkv_w_r", 0, D, cons_r)
                if DCUT <= 1:
                    continue
                mix(1)
                pi = next_pt(c)
                for kc in range(8):
                    P.T(lambda e, pi=pi, kc=kc: e.matmul(c.pt[pi][:64, :N], lhsT=w1s[:, kc, :], rhs=xm[:, kc, :N], start=(kc == 0), stop=(kc == 7)),
                        reads=[r_lora, res("xm")], writes=[res("pt%d" % pi)])
                P.S(lambda e, pi=pi: e.activation(out=lmid[0][:64, :N], in_=c.pt[pi][:64, :N], func=AF.Tanh), reads=[res("pt%d" % pi)], writes=[res("lmid0")])
                for fc in range(8):
                    pj = next_pt(c)
                    P.T(lambda e, pj=pj, fc=fc: e.matmul(c.pt[pj][:, :N], lhsT=w2s[:, fc * 128:(fc + 1) * 128], rhs=lmid[0][:64, :N], start=True, stop=True),
                        reads=[r_lora, res("lmid0")], writes=[res("pt%d" % pj)])
                    fi = ocnt[1] % 2; ocnt[1] += 1
                    P.S(lambda e, pj=pj, fc=fc, fi=fi: e.activation(out=obf[fi][:, :N], in_=c.pt[pj][:, :N], func=AF.Sigmoid, bias=pvs("w0", fc, 1)),
                        reads=[res("pt%d" % pj), r_pv], writes=[res("obF%d" % fi)])
                    P.V(lambda e, fi=fi: e.tensor_scalar(out=obf[fi][:, :N], in0=obf[fi][:, :N], scalar1=-float(np.exp(-0.5)), scalar2=None, op0=ALU.mult),
                        reads=[res("obF%d" % fi)], writes=[res("obF%d" % fi)])
                    P.dma("sync", rS["lw"][fc * 128:(fc + 1) * 128, t0:t0 + N], obf[fi][:, :N], reads=[res("obF%d" % fi)], writes=[res("scrrwkv")])
                if DCUT <= 2:
                    continue
                mix(2)
                def cons_k(fc, pi, m):
                    P.S(lambda e: e.activation(out=kk_[:, fc, :N], in_=c.pt[pi][:, :N], func=AF.Copy), reads=[res("pt%d" % pi)], writes=[res("kkk")])
                lin_fm(c, xfn, rxm, N, "rwkv_w_k", 0, D, cons_k)
                mix(3)
                def cons_v(fc, pi, m):
                    out_bf(rS["v"][fc * 128:(fc + 1) * 128, t0:t0 + N],
                           lambda o, ro: P.V(lambda e: e.tensor_copy(out=o[:, :N], in_=c.pt[pi][:, :N]), reads=[res("pt%d" % pi)], writes=[ro]))
                    P.S(lambda e: e.activation(out=c.act[:, fc, :N], in_=c.pt[pi][:, :N], func=AF.Copy), reads=[res("pt%d" % pi)], writes=[res("act")])
                lin_fm(c, xfn, rxm, N, "rwkv_w_v", 0, D, cons_v)
                if DCUT <= 3:
                    continue
                mix(4)
                pi = next_pt(c)
                for kc in range(8):
                    P.T(lambda e, pi=pi, kc=kc: e.matmul(c.pt[pi][:64, :N], lhsT=a1s[:, kc, :], rhs=xm[:, kc, :N], start=(kc == 0), stop=(kc == 7)),
                        reads=[r_lora, res("xm")], writes=[res("pt%d" % pi)])
                P.S(lambda e, pi=pi: e.activation(out=lmid[1][:64, :N], in_=c.pt[pi][:64, :N], func=AF.Copy), reads=[res("pt%d" % pi)], writes=[res("lmid1")])
                for fc in range(8):
                    pj = next_pt(c)
                    P.T(lambda e, pj=pj, fc=fc: e.matmul(c.pt[pj][:, :N], lhsT=a2s[:, fc * 128:(fc + 1) * 128], rhs=lmid[1][:64, :N], start=True, stop=True),
                        reads=[r_lora, res("lmid1")], writes=[res("pt%d" % pj)])
                    P.S(lambda e, pj=pj, fc=fc: e.activation(out=aa[:, :N], in_=c.pt[pj][:, :N], func=AF.Sigmoid, bias=pvs("a0", fc, 1)),
                        reads=[res("pt%d" % pj), r_pv], writes=[res("aa")])
                    P.V(lambda e, fc=fc: e.tensor_scalar(out=t1[:, :N], in0=kk_[:, fc, :N], scalar1=pvs("k_k", fc, 1), scalar2=None, op0=ALU.mult),
                        reads=[res("kkk"), r_pv], writes=[res("t1")])
                    P.G(lambda e: e.tensor_tensor(out=c.sq[:, 0, :N], in0=t1[:, :N], in1=t1[:, :N], op=ALU.mult), reads=[res("t1")], writes=[res("sq")])
                    pq = sumsq_bc(c, lambda k: c.sq[:, 0, :N], 1, N, [res("sq")], lhs_name="blk64")
                    rsqrt_from_psum(c, pq, N, c.rstd[:, :N], res("rstd"))
                    P.V(lambda e: e.tensor_tensor(out=t1[:, :N], in0=t1[:, :N], in1=c.rstd[:, :N], op=ALU.mult), reads=[res("t1"), res("rstd")], writes=[res("t1")])
                    out_bf(rS["a"][fc * 128:(fc + 1) * 128, t0:t0 + N],
                           lambda o, ro: P.G(lambda e: e.tensor_scalar(out=o[:, :N], in0=t1[:, :N], scalar1=-1.0, scalar2=None, op0=ALU.mult), reads=[res("t1")], writes=[ro]))
                    out_bf(rS["b"][fc * 128:(fc + 1) * 128, t0:t0 + N],
                           lambda o, ro: P.V(lambda e: e.tensor_tensor(out=o[:, :N], in0=t1[:, :N], in1=aa[:, :N], op=ALU.mult), reads=[res("t1"), res("aa")], writes=[ro]))
                    P.V(lambda e, fc=fc: e.tensor_scalar(out=t2[:, :N], in0=aa[:, :N], scalar1=pvs("k_a", fc, 1), scalar2=omk[:, fc:fc + 1], op0=ALU.mult, op1=ALU.add),
                        reads=[res("aa"), r_pv, res("omk")], writes=[res("t2")])
                    P.V(lambda e, fc=fc: e.tensor_tensor(out=t2[:, :N], in0=t2[:, :N], in1=kk_[:, fc, :N], op=ALU.mult), reads=[res("t2"), res("kkk")], writes=[res("t2")])
                    out_bf(rS["k"][fc * 128:(fc + 1) * 128, t0:t0 + N],
                           lambda o, ro: P.G(lambda e: e.tensor_copy(out=o[:, :N], in_=t2[:, :N]), reads=[res("t2")], writes=[ro]))
                    P.V(lambda e, fc=fc: e.scalar_tensor_tensor(out=c.sq[:, 1, :N], in0=t2[:, :N], scalar=pvs("r_k", fc, 1), in1=rr[:, fc, :N], op0=ALU.mult, op1=ALU.mult),
                        reads=[res("t2"), r_pv, res("rr")], writes=[res("sq")])
                    pb = sumsq_bc(c, lambda k: c.sq[:, 1, :N], 1, N, [res("sq")], lhs_name="blk64")
                    out_bf(rS["bonus"][fc * 128:(fc + 1) * 128, t0:t0 + N],
                           lambda o, ro: P.V(lambda e, fc=fc: e.tensor_tensor(out=o[:, :N], in0=c.pt[pb][:, :N], in1=c.act[:, fc, :N], op=ALU.mult),
                                             reads=[res("pt%d" % pb), res("act")], writes=[ro]))
                if DCUT <= 4:
                    continue
                mix(5)
                for (m0, mm, li) in ((0, 128, 0), (128, 32, 1)):
                    pi = next_pt(c)
                    for kc in range(8):
                        P.T(lambda e, pi=pi, kc=kc, m0=m0, mm=mm: e.matmul(c.pt[pi][:mm, :N], lhsT=g1s[:, kc, m0:m0 + mm], rhs=xm[:, kc, :N], start=(kc == 0), stop=(kc == 7)),
                            reads=[r_lora, res("xm")], writes=[res("pt%d" % pi)])
                    P.S(lambda e, pi=pi, mm=mm, li=li: e.activation(out=lmid[li][:mm, :N], in_=c.pt[pi][:mm, :N], func=AF.Sigmoid),
                        reads=[res("pt%d" % pi)], writes=[res("lmid%d" % li)])
                for fc in range(8):
                    pj = next_pt(c)
                    P.T(lambda e, pj=pj, fc=fc: e.matmul(c.pt[pj][:, :N], lhsT=g2s[:, 0, fc * 128:(fc + 1) * 128], rhs=lmid[0][:, :N], start=True, stop=False),
                        reads=[r_lora, res("lmid0")], writes=[res("pt%d" % pj)])
                    P.T(lambda e, pj=pj, fc=fc: e.matmul(c.pt[pj][:, :N], lhsT=g2s[0:32, 1, fc * 128:(fc + 1) * 128], rhs=lmid[1][0:32, :N], start=False, stop=True),
                        reads=[r_lora, res("lmid1")], writes=[res("pt%d" % pj)])
                    out_bf(rS["g"][fc * 128:(fc + 1) * 128, t0:t0 + N],
                           lambda o, ro: P.V(lambda e, pj=pj: e.tensor_copy(out=o[:, :N], in_=c.pt[pj][:, :N]), reads=[res("pt%d" % pj)], writes=[ro]))
                P.G(lambda e: e.tensor_copy(out=c.hn[:, :, 0:1], in_=c.hn[:, :, N:N + 1]), reads=[res("hn")], writes=[res("hn")])
        P.barrier()
        if stop_after == "D":
            P.finish(); return nc

        def rw_post_setup(st, ngs, SC):
            sb = lambda nm, shape, dt: st.enter_context(nc.sbuf_tensor(un(nm), shape, dt))
            o = TokCtx()
            o.bon = [sb("rbon%d" % i, [128, SC * 128], BF16) for i in range(ngs)]
            o.g = [sb("rgate%d" % i, [128, SC * 128], BF16) for i in range(ngs)]
            o.sq = sb("rpsq", [128, 128], BF16)
            o.yb = sb("rpyb", [128, 128], BF16)
            o.mean = sb("rpmean", [128, 128], F32)
            o.var = sb("rpvar", [128, 128], F32)
            o.yc = sb("rpyc", [128, 128], F32)
            o.ob = [sb("rpob%d" % i, [128, 128], BF16) for i in range(2)]
            _ps = st.enter_context(nc.psum_tensor(un("rpps"), [128, 128], F32))
            o.ps = [_ps, _ps]
            o.k = 0
            return o

        def rw_post(kind, g, grow, c, arg, o):
            if kind == "load":
                P.dma("sync", o.bon[g][:, :arg], rS["bonus"][grow:grow + 128, c * 128:c * 128 + arg], reads=[res("scrrwkv")], writes=[res("rbon%d" % g)])
                P.dma("sync", o.g[g][:, :arg], rS["g"][grow:grow + 128, c * 128:c * 128 + arg], reads=[res("scrrwkv")], writes=[res("rgate%d" % g)])
                return
            yT_, ry, cl = arg
            fc = grow // 128
            cols = slice(cl * 128, (cl + 1) * 128)
            P.G(lambda e: e.tensor_copy(out=o.yb[:], in_=yT_[:]), reads=[ry], writes=[res("rpyb")])
            P.T(lambda e: e.matmul(o.ps[0][:], lhsT=cb["blk64"][:], rhs=o.yb[:], start=True, stop=True), reads=[r_cb, res("rpyb")], writes=[res("rpps")])
            P.V(lambda e: e.scalar_tensor_tensor(out=o.yc[:], in0=o.ps[0][:], scalar=-1.0 / 64, in1=yT_[:], op0=ALU.mult, op1=ALU.add),
                reads=[res("rpps"), ry], writes=[res("rpyc")])
            P.G(lambda e: e.tensor_tensor(out=o.sq[:], in0=o.yc[:], in1=o.yc[:], op=ALU.mult), reads=[res("rpyc")], writes=[res("rpsq")])
            P.T(lambda e: e.matmul(o.ps[1][:], lhsT=cb["blk64"][:], rhs=o.sq[:], start=True, stop=True), reads=[r_cb, res("rpsq")], writes=[res("rpps")])
            P.S(lambda e: e.activation(out=o.var[:], in_=o.ps[1][:], func=AF.Sqrt, bias=cvec[:, 2:3], scale=1.0 / 64), reads=[res("rpps"), r_cb], writes=[res("rpvar")])
            P.V(lambda e: e.reciprocal(out=o.var[:], in_=o.var[:]), reads=[res("rpvar")], writes=[res("rpvar")])
            P.V(lambda e: e.scalar_tensor_tensor(out=o.yc[:], in0=o.yc[:], scalar=pvs("ln_w", fc, 1), in1=o.var[:], op0=ALU.mult, op1=ALU.mult),
                reads=[res("rpyc"), r_pv, res("rpvar")], writes=[res("rpyc")])
            P.V(lambda e: e.scalar_tensor_tensor(out=o.yc[:], in0=o.yc[:], scalar=pvs("ln_b", fc, 1), in1=o.bon[g][:, cols], op0=ALU.add, op1=ALU.add),
                reads=[res("rpyc"), r_pv, res("rbon%d" % g)], writes=[res("rpyc")])
            i = o.k % 2; o.k += 1
            P.V(lambda e: e.tensor_tensor(out=o.ob[i][:], in0=o.yc[:], in1=o.g[g][:, cols], op=ALU.mult), reads=[res("rpyc"), res("rgate%d" % g)], writes=[res("rpob%d" % i)])
            P.dma("sync", zT[grow:grow + 128, c * 128:(c + 1) * 128], o.ob[i][:], reads=[res("rpob%d" % i)], writes=[res("zT_dram")])

        if only is None or "E" in only:
          dplr_phase("rwkv", [[(gq * 128, 0, 64), (gq * 128, 64, 64)] for gq in range(8)], rS, rw_post_setup, rw_post)
        if stop_after == "E":
            P.finish(); return nc

        with contextlib.ExitStack() as st:
            sb = lambda name, shape, dt: st.enter_context(nc.sbuf_tensor(un(name), shape, dt))
            c = make_tok_ctx(st, nwi=4, nwo=4)
            oin = sb("oinF", [128, 8, 512], BF16)
            for ti, (t0, N) in enumerate(tiles):
                h_t, r_h = c.hT[ti % 2], res("hT%d" % (ti % 2))
                P.dma("sync", h_t[:, :, :N], hview(hT)[:, :, t0:t0 + N], reads=[res("hT_dram")], writes=[r_h])
                P.dma("sync", oin[:, :, :N], hview(zT)[:, :, t0:t0 + N], reads=[res("zT_dram")], writes=[res("oinF")])
                proj_residual(c, h_t, r_h, N, oin, res("oinF"), "rwkv_w_o")
                ffn(c, h_t, r_h, N, 3)
                P.S(lambda e: e.activation(out=c.sq[:, :, :N], in_=h_t[:, :, :N], func=AF.Square), reads=[r_h], writes=[res("sq")])
                pi = sumsq_bc(c, lambda k: c.sq[:, k, :N], 8, N, [res("sq")])
                rsqrt_from_psum(c, pi, N, c.rstd[:, :N], res("rstd"))
                for kc in range(8):
                    eng = "vector" if kc % 2 == 0 else "gpsimd"
                    P.op("vector", lambda e, kc=kc: e.scalar_tensor_tensor(out=h_t[:, kc, :N], in0=h_t[:, kc, :N], scalar=pvs("final_norm", kc, 1), in1=c.rstd[:, :N],
                                                                       op0=ALU.mult, op1=ALU.mult), reads=[r_h, r_pv, res("rstd")], writes=[r_h])
                P.dma("sync", hview(outT)[:, :, t0:t0 + N], h_t[:, :, :N], reads=[r_h], writes=[res("out_dram")])
        P.finish()
    return nc


_NC_CACHE = {}


def make_in_maps(inp, T):
    x = np.asarray(inp["x"], np.float32)
    B, S, _ = x.shape
    meta = np.asarray(inp["meta"], np.float32)
    pv = pack_params(inp)
    base = {"pvec": pv}
    k = 0
    for l in range(2):
        for h in range(2):
            base["ffn_w_in%d" % k] = np.ascontiguousarray(inp["ffn_w_in"][l, h], np.float32)
            base["ffn_w_out%d" % k] = np.ascontiguousarray(inp["ffn_w_out"][l, h], np.float32)
            k += 1
    base["hyb_w_in"] = np.ascontiguousarray(inp["hyb_w_in"][0], np.float32)
    base["hyb_w_out"] = np.ascontiguousarray(inp["hyb_w_out"][0], np.float32)
    for nm in ("w_r", "w_k", "w_v", "w_o", "w1", "w2", "a1", "a2", "g1", "g2"):
        base["rwkv_" + nm] = np.ascontiguousarray(inp["rwkv_" + nm][0], np.float32)
    maps = []
    for core in range(8):
        b = core % B
        hT0 = np.zeros((D, T), np.float32)
        hT0[:, :NMETA] = meta.T
        hT0[:, NMETA:NMETA + S] = x[b].T
        m = dict(base)
        m["hT0"] = hT0
        maps.append(m)
    return maps


def kernel(**inp):
    x = np.asarray(inp["x"])
    B, S, _ = x.shape
    T = ((NMETA + S + 127) // 128) * 128
    if T not in _NC_CACHE:
        _NC_CACHE[T] = build(T)
    nc = _NC_CACHE[T]
    maps = make_in_maps(inp, T)
    res = run_bass_kernel_spmd(nc, maps, core_ids=list(range(8)))
    out = np.empty((B, S, D), np.float32)
    for b in range(B):
        out[b] = res.results[b]["outT"][:, NMETA:NMETA + S].T
    return out
```

```python
import contextlib
import numpy as np
import concourse.bass as bass
import concourse.mybir as mybir
from concourse.bass_utils import run_bass_kernel_spmd

F32 = mybir.dt.float32
BF16 = mybir.dt.bfloat16
ALU = mybir.AluOpType
AF = mybir.ActivationFunctionType

D = 1024
DFF = 2816
EPS = 1e-6
NMETA = 16
HYB_IN = 3600
GN_EPS = 64e-5


_NUN = [0]


NAMES = {}


def un(name):
    _NUN[0] += 1
    NAMES[name] = "t%d_%s" % (_NUN[0], name)
    return NAMES[name]


SAME_ENGINE_SYNC = True
PSUM_PREFIXES = ("pt", "py", "ps_", "psx", "psy", "pstr", "gpps", "rpps")


class Res:
    __slots__ = ("name", "w", "r", "excl")

    def __init__(self, name):
        self.name = name
        self.w = None
        self.r = []
        base = name.split("_", 1)[1] if name.startswith(("gdn_", "rwkv_")) else name
        self.excl = base.startswith(PSUM_PREFIXES)


class _Rec:
    def __init__(self):
        self.calls = []

    def __getattr__(self, name):
        def f(*a, **k):
            self.calls.append((name, a, k))
            return self
        return f


class Prog:
    ENGS = ("tensor", "vector", "scalar", "gpsimd", "sync")

    def __init__(self, nc, stack, n_dma_sems=12):
        self.nc = nc
        self.lists = {e: [] for e in self.ENGS}
        self.stack = stack
        self.epoch = 0
        self.esem = {e: stack.enter_context(nc.semaphore("s_" + e)) for e in self.ENGS}
        self.ecount = {e: 0 for e in self.ENGS}
        self.LIMIT = 12000
        self.seen = {e: {} for e in self.ENGS}
        self.dsems, self.dcount, self.dnext = {}, {}, {}
        for q in ("sync", "gpsimd", "scalar"):
            self.dsems[q] = [stack.enter_context(nc.semaphore("d_%s%d" % (q, i)))
                             for i in range(n_dma_sems if q != "scalar" else 2)]
            self.dcount[q] = [0] * len(self.dsems[q])
            self.dnext[q] = 0
        self.semobj = {}
        for e in self.ENGS:
            self.semobj[("e", e, 0)] = self.esem[e]
        for q in self.dsems:
            for i, s in enumerate(self.dsems[q]):
                self.semobj[("d", q, i)] = s
        self.n_ins = 0

    def _need(self, eng, deps, key, val):
        if key[0] == "e":
            if key[2] < self.epoch:
                return
            if key[1] == eng and (eng == "tensor" or not SAME_ENGINE_SYNC):
                return
        if self.seen[eng].get(key, 0) >= val:
            return
        if deps.get(key, 0) < val:
            deps[key] = val

    def _collect(self, eng, reads, writes):
        deps = {}
        for r in reads:
            if r.w is not None:
                self._need(eng, deps, r.w[0], r.w[1])
        for w in writes:
            if w.w is not None:
                self._need(eng, deps, w.w[0], w.w[1])
            for (k, v) in w.r:
                self._need(eng, deps, k, v)
        for k, v in deps.items():
            self.seen[eng][k] = v
        return list(deps.items())

    def _mark(self, key, val, reads, writes):
        for r in reads:
            r.r = [(k, v) for (k, v) in r.r if k != key]
            r.r.append((key, val))
        for w in writes:
            w.w = (key, val)
            w.r = []

    def op(self, eng, fn, reads=(), writes=()):
        rec = _Rec()
        fn(rec)
        assert len(rec.calls) == 1, rec.calls
        name, a, k = rec.calls[0]
        fn = (lambda e, name=name, a=a, k=k: getattr(e, name)(*a, **k))
        ex = [r for r in reads if r.excl]
        if ex:
            writes = list(writes) + ex
        if self.ecount[eng] >= self.LIMIT:
            self.barrier(rotate=True)
        waits = self._collect(eng, reads, writes)
        self.ecount[eng] += 1
        key = ("e", eng, self.epoch)
        self.lists[eng].append((waits, fn, key, 1))
        self._mark(key, self.ecount[eng], reads, writes)
        self.n_ins += 1

    def V(self, fn, reads=(), writes=()):
        self.op("vector", fn, reads, writes)

    def S(self, fn, reads=(), writes=()):
        self.op("scalar", fn, reads, writes)

    def G(self, fn, reads=(), writes=()):
        self.op("gpsimd", fn, reads, writes)

    def T(self, fn, reads=(), writes=()):
        self.op("tensor", fn, reads, writes)

    def dma(self, q, out, in_, reads=(), writes=(), **kw):
        i = self.dnext[q]
        self.dnext[q] = (i + 1) % len(self.dsems[q])
        key = ("d", q, i)
        waits = self._collect(q, reads, writes)
        prev = self.dcount[q][i]
        if prev > 0 and self.seen[q].get(key, 0) < prev:
            waits.append((key, prev))
            self.seen[q][key] = prev
        self.dcount[q][i] += 16
        self.lists[q].append((waits, (lambda e: e.dma_start(out=out, in_=in_, **kw)), key, 16))
        self._mark(key, self.dcount[q][i], reads, writes)
        self.n_ins += 1

    def barrier(self, rotate=False):
        allw = {}
        for e in self.ENGS:
            if self.ecount[e] > 0:
                allw[("e", e, self.epoch)] = self.ecount[e]
        for q in self.dsems:
            for i, c in enumerate(self.dcount[q]):
                if c > 0:
                    allw[("d", q, i)] = c
        for e in self.ENGS:
            ws = []
            for k, v in allw.items():
                if k[0] == "e" and k[1] == e:
                    continue
                if self.seen[e].get(k, 0) < v:
                    ws.append((k, v))
                    self.seen[e][k] = v
            if ws:
                self.lists[e].append((ws, None, None, 0))
        if rotate:
            self.epoch += 1
            for e in self.ENGS:
                if e == "sync":
                    continue
                self.esem[e] = self.stack.enter_context(self.nc.semaphore("s_%s_%d" % (e, self.epoch)))
                self.semobj[("e", e, self.epoch)] = self.esem[e]
                self.ecount[e] = 0
                self.seen[e] = {k: v for k, v in self.seen[e].items() if k[0] != "e"}
            self.seen["sync"] = {k: v for k, v in self.seen["sync"].items() if k[0] != "e"}

    def finish(self):
        self.barrier()
        semobj, lists = self.semobj, self.lists

        def run(e, name):
            for (ws, fn, key, inc) in lists[name]:
                for (k, v) in ws:
                    e.wait_ge(semobj[k], v)
                if fn is not None:
                    fn(e).then_inc(semobj[key], inc)

        with self.nc.Block() as block:
            @block.tensor
            def _(e):
                run(e, "tensor")

            @block.vector
            def _(e):
                run(e, "vector")

            @block.scalar
            def _(e):
                run(e, "scalar")

            @block.gpsimd
            def _(e):
                run(e, "gpsimd")

            @block.sync
            def _(e):
                run(e, "sync")


PV_SLOTS = {}


def _pv_layout():
    off = 0
    def add(name, n):
        nonlocal off
        PV_SLOTS[name] = (off, n)
        off += n
    for i in range(4):
        add("ffn_norm%d" % i, 8)
    add("mix_norm0", 8); add("mix_norm1", 8); add("final_norm", 8)
    add("fox_bf", 8); add("conv", 48); add("a_log", 4); add("dt_bias", 4); add("o_gain", 1)
    add("mu", 48); add("w0", 8); add("a0", 8); add("k_k", 8); add("k_a", 8); add("ln_w", 8); add("ln_b", 8); add("r_k", 8)
    for nm in ("cmean", "ident", "m_su", "m_ui", "m_sl", "ones", "blk64", "triu", "bd32", "off64", "off128"):
        add(nm, 128)
    add("sel65", 64)
    return off


NPV = _pv_layout()


def fm(vec):
    return np.ascontiguousarray(np.asarray(vec, np.float32).reshape(8, 128).T)


def pack_params(inp):
    pv = np.zeros((128, NPV), np.float32)
    def put(name, arr):
        o, n = PV_SLOTS[name]
        pv[:, o:o + n] = np.asarray(arr, np.float32).reshape(128, n)
    k = 0
    for l in range(2):
        for h in range(2):
            put("ffn_norm%d" % k, fm(inp["ffn_norm"][l, h])); k += 1
    put("mix_norm0", fm(inp["mix_norm"][0])); put("mix_norm1", fm(inp["mix_norm"][1]))
    put("final_norm", fm(inp["final_norm"]))
    put("fox_bf", np.broadcast_to(inp["hyb_fox_bf"][0][None, :], (128, 8)))
    cw = inp["hyb_conv"][0]
    put("conv", cw.reshape(4, 12, 128).transpose(2, 1, 0).reshape(128, 48))
    put("a_log", np.broadcast_to(inp["hyb_a_log"][0][None, :], (128, 4)))
    put("dt_bias", np.broadcast_to(inp["hyb_dt_bias"][0][None, :], (128, 4)))
    put("o_gain", inp["hyb_o_gain"][0].reshape(128, 1))
    put("mu", inp["rwkv_mu"][0].reshape(6, 8, 128).transpose(2, 0, 1).reshape(128, 48))
    for nm in ("w0", "a0", "k_k", "k_a", "ln_w", "ln_b"):
        put(nm, fm(inp["rwkv_" + nm][0]))
    put("r_k", fm(inp["rwkv_r_k"][0].reshape(-1)))
    idx = np.arange(128)
    put("cmean", np.full((128, 128), 1.0 / 1024))
    put("ident", np.eye(128))
    put("m_su", (idx[:, None] < idx[None, :]).astype(np.float32))
    put("m_ui", (idx[:, None] <= idx[None, :]).astype(np.float32))
    put("m_sl", (idx[:, None] > idx[None, :]).astype(np.float32))
    put("ones", np.ones((128, 128)))
    put("blk64", ((idx[:, None] // 64) == (idx[None, :] // 64)).astype(np.float32))
    put("triu", (idx[:, None] <= idx[None, :]).astype(np.float32))
    bdm = lambda b: ((idx[:, None] // b) == (idx[None, :] // b)).astype(np.float32)
    put("bd32", bdm(32)); put("off64", bdm(64) - bdm(32)); put("off128", bdm(128) - bdm(64))
    s = np.zeros((128, 64), np.float32); s[64, :] = 1.0
    put("sel65", s)
    return pv


WNAMES = [("ffn_w_in0", D, 2 * DFF), ("ffn_w_in1", D, 2 * DFF), ("ffn_w_in2", D, 2 * DFF), ("ffn_w_in3", D, 2 * DFF),
          ("ffn_w_out0", DFF, D), ("ffn_w_out1", DFF, D), ("ffn_w_out2", DFF, D), ("ffn_w_out3", DFF, D),
          ("hyb_w_in", D, HYB_IN), ("hyb_w_out", D, D),
          ("rwkv_w_r", D, D), ("rwkv_w_k", D, D), ("rwkv_w_v", D, D), ("rwkv_w_o", D, D),
          ("rwkv_w1", D, 64), ("rwkv_w2", 64, D), ("rwkv_a1", D, 64), ("rwkv_a2", 64, D),
          ("rwkv_g1", D, 160), ("rwkv_g2", 160, D)]


DEBUG_SCR = False


def build(T, stop_after=None, dbg=(), only=None):
    NCH = T // 128
    tiles = []
    t0 = 0
    while t0 < T:
        n = min(512, T - t0)
        tiles.append((t0, n)); t0 += n
    nc = bass.Bass("TRN2", target_bir_lowering=False)
    hT0 = nc.dram_tensor("hT0", [D, T], F32, kind="ExternalInput").ap()
    pvec = nc.dram_tensor("pvec", [128, NPV], F32, kind="ExternalInput").ap()
    Wf = {nm: nc.dram_tensor(nm, [r, c], F32, kind="ExternalInput").ap() for (nm, r, c) in WNAMES}
    outT = nc.dram_tensor("outT", [D, T], F32, kind="ExternalOutput").ap()
    Wb = {nm: nc.dram_tensor(nm + "_b", [r, c], BF16, kind="Internal").ap() for (nm, r, c) in WNAMES}
    scr = {}
    def dscr(name, shape, dt):
        scr[name] = nc.dram_tensor("scr_" + name, shape, dt, kind=("ExternalOutput" if DEBUG_SCR else "Internal")).ap()
        return scr[name]
    hT = dscr("hT", [D, T], F32)
    fqT = dscr("fqT", [512, T], BF16); fkT = dscr("fkT", [512, T], BF16)
    fV = dscr("fV", [T, 520], BF16); flogf = dscr("flogf", [T, 8], F32)
    gS = {k: dscr("g_" + k, [512, T], BF16) for k in ("r", "k", "a", "b", "v", "z")}
    gS["lw"] = dscr("g_lw", [512, T], F32)
    oT = dscr("oT", [D, T], BF16)
    rS = {k: dscr("r_" + k, [D, T], BF16) for k in ("r", "k", "a", "b", "v", "bonus", "g")}
    rS["lw"] = dscr("r_lw", [D, T], F32)
    zT = dscr("zT", [D, T], BF16)
    dbg_out = {}
    for nm, shape in dbg:
        dbg_out[nm] = nc.dram_tensor("dbg_" + nm, shape, F32, kind="ExternalOutput").ap()

    with contextlib.ExitStack() as gst:
        P = Prog(nc, gst)
        R = {}
        def res(name):
            if name not in R:
                R[name] = Res(name)
            return R[name]

        for (nm, r, c) in WNAMES:
            c0 = 0
            while c0 < c:
                cw = min(2048, c - c0)
                P.dma("gpsimd", Wb[nm][:, c0:c0 + cw], Wf[nm][:, c0:c0 + cw], writes=[res("W_" + nm)])
                c0 += cw

        pv = gst.enter_context(nc.sbuf_tensor("pv", [128, NPV], F32)); r_pv = res("pv")
        P.dma("sync", pv[:], pvec, writes=[r_pv])
        def pvs(name, a=0, n=None):
            o, nn = PV_SLOTS[name]
            n = nn - a if n is None else n
            return pv[:, o + a:o + a + n]
        cb = {}
        for nm in ("cmean", "ident", "ones", "blk64", "bd32", "off64", "off128"):
            cb[nm] = gst.enter_context(nc.sbuf_tensor("cb_" + nm, [128, 128], BF16))
            P.V(lambda e, nm=nm: e.tensor_copy(out=cb[nm][:], in_=pvs(nm)), reads=[r_pv], writes=[res("cb")])
        m_ui_b = gst.enter_context(nc.sbuf_tensor("m_ui_b", [128, 128], BF16))
        P.V(lambda e: e.tensor_copy(out=m_ui_b[:], in_=pvs("m_ui")), reads=[r_pv], writes=[res("cb")])
        cvec = gst.enter_context(nc.sbuf_tensor("cvec", [128, 4], F32))
        P.V(lambda e: e.memset(cvec[:, 0:1], EPS), writes=[res("cb")])
        P.V(lambda e: e.memset(cvec[:, 1:2], 1.0), writes=[res("cb")])
        P.V(lambda e: e.memset(cvec[:, 2:3], GN_EPS), writes=[res("cb")])
        P.V(lambda e: e.memset(cvec[:, 3:4], 0.0), writes=[res("cb")])
        r_cb = res("cb")
        P.barrier()

        hview = lambda ap: ap.rearrange("(kc p) t -> p kc t", p=128)
        wview = lambda ap: ap.rearrange("(kc p) f -> p kc f", p=128)

        class TokCtx:
            pass

        def make_tok_ctx(st, nwi=4, nwo=4):
            c = TokCtx()
            sb = lambda name, shape, dt: st.enter_context(nc.sbuf_tensor(un(name), shape, dt))
            psm = lambda name, shape, dt: st.enter_context(nc.psum_tensor(un(name), shape, dt))
            c.hT = [sb("hT%d" % i, [128, 8, 512], F32) for i in range(2)]
            c.sq = sb("sq", [128, 8, 512], BF16)
            c.rstd = sb("rstd", [128, 512], F32)
            c.hn = sb("hn", [128, 8, 513], BF16)
            c.act = sb("act", [128, 22, 512], BF16)
            c.sg = [sb("sg%d" % i, [128, 512], F32) for i in range(2)]
            c.wi = [sb("wi%d" % i, [128, 8, 512], BF16) for i in range(nwi)]
            c.wo = [sb("wo%d" % i, [128, 4, 512], BF16) for i in range(nwo)]
            c.pt = [psm("pt%d" % i, [128, 512], F32) for i in range(4)]
            c.py = [psm("py%d" % i, [128, 512], F32) for i in range(4)]
            c.cnt = {"wi": 0, "wo": 0, "pt": 0, "sg": 0}
            return c

        def next_pt(c):
            i = c.cnt["pt"] % 4; c.cnt["pt"] += 1
            return i

        def sumsq_bc(c, src_ap_fn, nk, N, r_src, lhs_name="cmean"):
            pi = next_pt(c)
            for k in range(nk):
                P.T(lambda e, pi=pi, k=k: e.matmul(c.pt[pi][:, :N], lhsT=cb[lhs_name][:], rhs=src_ap_fn(k),
                                                    start=(k == 0), stop=(k == nk - 1)),
                    reads=[r_cb] + r_src, writes=[res("pt%d" % pi)])
            return pi

        def rsqrt_from_psum(c, pi, N, dst, r_dst, eps_col=0):
            P.S(lambda e: e.activation(out=dst, in_=c.pt[pi][:, :N], func=AF.Sqrt, bias=cvec[:, eps_col:eps_col + 1]),
                reads=[res("pt%d" % pi), r_cb], writes=[r_dst])
            P.V(lambda e: e.reciprocal(out=dst, in_=dst), reads=[r_dst], writes=[r_dst])

        def rmsnorm(c, h_t, r_h, N, gain_name):
            P.S(lambda e: e.activation(out=c.sq[:, :, :N], in_=h_t[:, :, :N], func=AF.Square), reads=[r_h], writes=[res("sq")])
            pi = sumsq_bc(c, lambda k: c.sq[:, k, :N], 8, N, [res("sq")])
            rsqrt_from_psum(c, pi, N, c.rstd[:, :N], res("rstd"))
            for kc in range(8):
                eng = "vector" if kc % 2 == 0 else "gpsimd"
                P.op("vector", lambda e, kc=kc: e.scalar_tensor_tensor(
                    out=c.hn[:, kc, 1:1 + N], in0=h_t[:, kc, :N], scalar=pvs(gain_name, kc, 1), in1=c.rstd[:, :N],
                    op0=ALU.mult, op1=ALU.mult), reads=[r_h, r_pv, res("rstd")], writes=[res("hn")])

        def lin_fm(c, x_fn, r_x, N, wname, col0, ncols, consume, kchunks=8):
            wv = wview(Wb[wname])
            c0 = 0
            while c0 < ncols:
                cw = min(512, ncols - c0)
                bi = c.cnt["wi"] % len(c.wi); c.cnt["wi"] += 1
                P.dma("sync", c.wi[bi][:, :kchunks, :cw], wv[:, :, col0 + c0:col0 + c0 + cw],
                      reads=[res("W_" + wname)], writes=[res("wi%d" % bi)])
                f0 = 0
                while f0 < cw:
                    m = min(128, cw - f0)
                    pi = next_pt(c)
                    for kc in range(kchunks):
                        P.T(lambda e, pi=pi, bi=bi, kc=kc, f0=f0, m=m: e.matmul(
                            c.pt[pi][:m, :N], lhsT=c.wi[bi][:, kc, f0:f0 + m], rhs=x_fn(kc),
                            start=(kc == 0), stop=(kc == kchunks - 1)),
                            reads=[res("wi%d" % bi)] + r_x, writes=[res("pt%d" % pi)])
                    consume((c0 + f0) // 128, pi, m)
                    f0 += m
                c0 += cw

        def ffn(c, h_t, r_h, N, idx):
            rmsnorm(c, h_t, r_h, N, "ffn_norm%d" % idx)
            wn_in, wn_out = "ffn_w_in%d" % idx, "ffn_w_out%d" % idx
            wv = wview(Wb[wn_in])
            for pc in range(6):
                cw = 512 if pc < 5 else 256
                bufs = []
                for which in range(2):
                    bi = c.cnt["wi"] % len(c.wi); c.cnt["wi"] += 1
                    cc0 = which * DFF + pc * 512
                    P.dma("sync", c.wi[bi][:, :, :cw], wv[:, :, cc0:cc0 + cw], reads=[res("W_" + wn_in)], writes=[res("wi%d" % bi)])
                    bufs.append(bi)
                for fl in range(cw // 128):
                    fc = pc * 4 + fl
                    pis = []
                    for which in range(2):
                        pi = next_pt(c); pis.append(pi)
                        bi = bufs[which]
                        for kc in range(8):
                            P.T(lambda e, pi=pi, bi=bi, kc=kc, fl=fl: e.matmul(
                                c.pt[pi][:, :N], lhsT=c.wi[bi][:, kc, fl * 128:(fl + 1) * 128], rhs=c.hn[:, kc, 1:1 + N],
                                start=(kc == 0), stop=(kc == 7)), reads=[res("wi%d" % bi), res("hn")], writes=[res("pt%d" % pi)])
                    si = c.cnt["sg"] % 2; c.cnt["sg"] += 1
                    P.S(lambda e, si=si, pi=pis[0]: e.activation(out=c.sg[si][:, :N], in_=c.pt[pi][:, :N], func=AF.Silu),
                        reads=[res("pt%d" % pis[0])], writes=[res("sg%d" % si)])
                    P.V(lambda e, si=si, pi=pis[1], fc=fc: e.tensor_tensor(
                        out=c.act[:, fc, :N], in0=c.sg[si][:, :N], in1=c.pt[pi][:, :N], op=ALU.mult),
                        reads=[res("sg%d" % si), res("pt%d" % pis[1])], writes=[res("act")])
            wvo = Wb[wn_out].rearrange("(fc p) d -> p fc d", p=128)
            for half in range(2):
                for g in range(6):
                    nf = 4 if g < 5 else 2
                    bi = c.cnt["wo"] % len(c.wo); c.cnt["wo"] += 1
                    P.dma("sync", c.wo[bi][:, :nf, :], wvo[:, g * 4:g * 4 + nf, half * 512:(half + 1) * 512],
                          reads=[res("W_" + wn_out)], writes=[res("wo%d" % bi)])
                    for fl in range(nf):
                        fc = g * 4 + fl
                        for dq in range(4):
                            P.T(lambda e, bi=bi, fl=fl, fc=fc, dq=dq: e.matmul(
                                c.py[dq][:, :N], lhsT=c.wo[bi][:, fl, dq * 128:(dq + 1) * 128], rhs=c.act[:, fc, :N],
                                start=(fc == 0), stop=(fc == 21)), reads=[res("wo%d" % bi), res("act")], writes=[res("py%d" % dq)])
                for dq in range(4):
                    kc = half * 4 + dq
                    P.V(lambda e, dq=dq, kc=kc: e.scalar_tensor_tensor(
                        out=h_t[:, kc, :N], in0=c.py[dq][:, :N], scalar=0.5, in1=h_t[:, kc, :N],
                        op0=ALU.mult, op1=ALU.add), reads=[res("py%d" % dq), r_h], writes=[r_h])

        def proj_residual(c, h_t, r_h, N, x_t, r_x, wname):
            def consume(fc, pi, m):
                P.V(lambda e: e.tensor_tensor(out=h_t[:, fc, :N], in0=c.pt[pi][:, :N], in1=h_t[:, fc, :N], op=ALU.add),
                    reads=[res("pt%d" % pi), r_h], writes=[r_h])
            lin_fm(c, lambda kc: x_t[:, kc, :N], [r_x], N, wname, 0, D, consume)

        with contextlib.ExitStack() as st:
          if only is None or "A" in only:
            sb = lambda name, shape, dt: st.enter_context(nc.sbuf_tensor(un(name), shape, dt))
            c = make_tok_ctx(st)
            wtm = sb("wtm", [128, 8, 520], BF16)
            wrep = sb("wrep", [128, 8, 8, 128], BF16)
            wsm = sb("wsm", [128, 8, 8], F32)
            P.dma("sync", wtm[:], wview(Wb["hyb_w_in"])[:, :, 1024:1544], reads=[res("W_hyb_w_in")], writes=[res("wtm")])
            P.dma("sync", wsm[:], wview(Wf["hyb_w_in"])[:, :, 3080:3088], writes=[res("wsm")])
            for kc in range(8):
                for j in range(8):
                    eng = "vector" if (kc + j) % 2 == 0 else "gpsimd"
                    P.op(eng, lambda e, kc=kc, j=j: e.tensor_scalar(out=wrep[:, kc, j, :], in0=cb["ones"][:], scalar1=wsm[:, kc, j:j + 1],
                                                                     scalar2=None, op0=ALU.mult), reads=[res("wsm"), r_cb], writes=[res("wrep")])
            negA = sb("negA", [128, 4], F32)
            P.S(lambda e: e.activation(out=negA[:], in_=pvs("a_log"), func=AF.Exp), reads=[r_pv], writes=[res("negA")])
            P.V(lambda e: e.tensor_scalar(out=negA[:], in0=negA[:], scalar1=-1.0, scalar2=None, op0=ALU.mult), reads=[res("negA")], writes=[res("negA")])
            xgs = [sb("xg%d" % i, [128, 515], F32) for i in range(2)]
            halo = sb("halo", [128, 12, 3], F32)
            P.V(lambda e: e.memset(halo[:], 0.0), writes=[res("halo")])
            xgc = [0]
            cacc = sb("cacc", [128, 512], F32)
            qk = [sb("qk%d" % i, [128, 512], F32) for i in range(2)]
            kn = sb("kn", [128, 512], F32)
            gb_ = sb("gbeta", [128, 4, 512], F32)
            gg_ = sb("gg", [128, 4, 512], F32)
            glw = sb("glw", [128, 4, 512], F32)
            ob = [sb("ob%d" % i, [128, 512], BF16) for i in range(4)]
            vtm = [sb("vtm%d" % i, [128, 8, 65], BF16) for i in range(2)]
            for i in range(2):
                P.V(lambda e, i=i: e.memset(vtm[i][:], 1.0), writes=[res("vtm%d" % i)])
            lft = [sb("lft%d" % i, [128, 8], F32) for i in range(2)]
            ocnt = [0]
            print("phase A sbuf remaining", nc.sbuf_bytes_remaining)

            def out_bf(dst_ap, make, reads):
                i = ocnt[0] % 4; ocnt[0] += 1
                make(ob[i], res("ob%d" % i))
                P.dma("sync", dst_ap, ob[i][:, :dst_ap.shape[-1]], reads=[res("ob%d" % i)], writes=[res("scrA")])

            for ti, (t0, N) in enumerate(tiles):
                h_t, r_h = c.hT[ti % 2], res("hT%d" % (ti % 2))
                P.dma("sync", h_t[:, :, :N], hview(hT0)[:, :, t0:t0 + N], writes=[r_h])
                ffn(c, h_t, r_h, N, 0)
                P.dma("sync", hview(hT)[:, :, t0:t0 + N], h_t[:, :, :N], reads=[r_h], writes=[res("hT_dram")])
                rmsnorm(c, h_t, r_h, N, "mix_norm0")
                xfn = lambda kc: c.hn[:, kc, 1:1 + N]
                rx = [res("hn")]
                def cons_q(fc, pi, m):
                    out_bf(fqT[fc * 128:(fc + 1) * 128, t0:t0 + N],
                           lambda o, ro: P.S(lambda e: e.activation(out=o[:, :N], in_=c.pt[pi][:, :N], func=AF.Copy, scale=0.125),
                                             reads=[res("pt%d" % pi)], writes=[ro]), None)
                lin_fm(c, xfn, rx, N, "hyb_w_in", 0, 512, cons_q)
                def cons_k(fc, pi, m):
                    out_bf(fkT[fc * 128:(fc + 1) * 128, t0:t0 + N],
                           lambda o, ro: P.V(lambda e: e.tensor_copy(out=o[:, :N], in_=c.pt[pi][:, :N]),
                                             reads=[res("pt%d" % pi)], writes=[ro]), None)
                lin_fm(c, xfn, rx, N, "hyb_w_in", 512, 512, cons_k)
                for tb in range(N // 128):
                    vi = (ti * 4 + tb) % 2
                    pi = next_pt(c)
                    for kc in range(8):
                        P.T(lambda e, pi=pi, kc=kc, tb=tb: e.matmul(c.pt[pi][:, :512], lhsT=c.hn[:, kc, 1 + tb * 128:1 + (tb + 1) * 128],
                                                                  rhs=wtm[:, kc, 0:512], start=(kc == 0), stop=(kc == 7)),
                            reads=[res("hn"), res("wtm")], writes=[res("pt%d" % pi)])
                    P.S(lambda e, pi=pi, vi=vi: e.activation(out=vtm[vi][:, :, 0:64], in_=c.pt[pi][:, :512].rearrange("p (h d) -> p h d", h=8), func=AF.Copy),
                        reads=[res("pt%d" % pi)], writes=[res("vtm%d" % vi)])
                    P.dma("sync", fV[t0 + tb * 128:t0 + (tb + 1) * 128, :], vtm[vi][:].rearrange("p h d -> p (h d)"),
                          reads=[res("vtm%d" % vi)], writes=[res("scrA")])
                    pi2 = next_pt(c)
                    for kc in range(8):
                        P.T(lambda e, pi2=pi2, kc=kc, tb=tb: e.matmul(c.pt[pi2][:, :8], lhsT=c.hn[:, kc, 1 + tb * 128:1 + (tb + 1) * 128],
                                                                    rhs=wtm[:, kc, 512:520], start=(kc == 0), stop=(kc == 7)),
                            reads=[res("hn"), res("wtm")], writes=[res("pt%d" % pi2)])
                    P.V(lambda e, pi2=pi2, vi=vi: e.tensor_tensor(out=lft[vi][:], in0=c.pt[pi2][:, :8], in1=pvs("fox_bf"), op=ALU.add),
                        reads=[res("pt%d" % pi2), r_pv], writes=[res("lft%d" % vi)])
                    P.S(lambda e, vi=vi: e.activation(out=lft[vi][:], in_=lft[vi][:], func=AF.Sigmoid), reads=[res("lft%d" % vi)], writes=[res("lft%d" % vi)])
                    P.S(lambda e, vi=vi: e.activation(out=lft[vi][:], in_=lft[vi][:], func=AF.Ln), reads=[res("lft%d" % vi)], writes=[res("lft%d" % vi)])
                    P.dma("sync", flogf[t0 + tb * 128:t0 + (tb + 1) * 128, :], lft[vi][:], reads=[res("lft%d" % vi)], writes=[res("scrA")])
                for j in range(8):
                    pi = next_pt(c)
                    for kc in range(8):
                        P.T(lambda e, pi=pi, kc=kc, j=j: e.matmul(c.pt[pi][:, :N], lhsT=wrep[:, kc, j, :], rhs=c.hn[:, kc, 1:1 + N],
                                                               start=(kc == 0), stop=(kc == 7)),
                            reads=[res("wrep"), res("hn")], writes=[res("pt%d" % pi)])
                    if j < 4:
                        P.S(lambda e, pi=pi, j=j: e.activation(out=glw[:, j, :N], in_=c.pt[pi][:, :N], func=AF.Exp, bias=pvs("dt_bias", j, 1)),
                            reads=[res("pt%d" % pi), r_pv], writes=[res("glw")])
                        P.S(lambda e, j=j: e.activation(out=glw[:, j, :N], in_=glw[:, j, :N], func=AF.Ln, bias=cvec[:, 1:2]),
                            reads=[res("glw"), r_cb], writes=[res("glw")])
                        P.V(lambda e, j=j: e.tensor_scalar(out=glw[:, j, :N], in0=glw[:, j, :N], scalar1=negA[:, j:j + 1], scalar2=None, op0=ALU.mult),
                            reads=[res("glw"), res("negA")], writes=[res("glw")])
                        P.S(lambda e, j=j: e.activation(out=gg_[:, j, :N], in_=glw[:, j, :N], func=AF.Exp), reads=[res("glw")], writes=[res("gg")])
                        P.dma("sync", gS["lw"][j * 128:(j + 1) * 128, t0:t0 + N], glw[:, j, :N], reads=[res("glw")], writes=[res("scrA")])
                    else:
                        P.S(lambda e, pi=pi, j=j: e.activation(out=gb_[:, j - 4, :N], in_=c.pt[pi][:, :N], func=AF.Sigmoid),
                            reads=[res("pt%d" % pi)], writes=[res("gbeta")])
                def cons_g(fc, pi, m):
                    xi = xgc[0] % 2; xgc[0] += 1
                    xg, rxg = xgs[xi], res("xg%d" % xi)
                    P.G(lambda e: e.tensor_copy(out=xg[:, 0:3], in_=halo[:, fc, :]), reads=[res("halo")], writes=[rxg])
                    P.S(lambda e: e.activation(out=xg[:, 3:3 + N], in_=c.pt[pi][:, :N], func=AF.Copy), reads=[res("pt%d" % pi)], writes=[rxg])
                    o_, _ = PV_SLOTS["conv"]
                    cwp = lambda j: pv[:, o_ + fc * 4 + j:o_ + fc * 4 + j + 1]
                    P.V(lambda e: e.tensor_scalar(out=cacc[:, :N], in0=xg[:, 0:N], scalar1=cwp(0), scalar2=None, op0=ALU.mult),
                        reads=[rxg, r_pv], writes=[res("cacc")])
                    for j in range(1, 4):
                        P.V(lambda e, j=j: e.scalar_tensor_tensor(out=cacc[:, :N], in0=xg[:, j:j + N], scalar=cwp(j), in1=cacc[:, :N],
                                                               op0=ALU.mult, op1=ALU.add), reads=[rxg, r_pv, res("cacc")], writes=[res("cacc")])
                    P.G(lambda e: e.tensor_copy(out=halo[:, fc, :], in_=xg[:, N:N + 3]), reads=[rxg], writes=[res("halo")])
                    kind, hh = fc // 4, fc % 4
                    if kind == 2:
                        out_bf(gS["v"][hh * 128:(hh + 1) * 128, t0:t0 + N],
                               lambda o, ro: P.S(lambda e: e.activation(out=o[:, :N], in_=cacc[:, :N], func=AF.Silu), reads=[res("cacc")], writes=[ro]), None)
                        return
                    qq = qk[kind]; rq = res("qk%d" % kind)
                    P.S(lambda e: e.activation(out=qq[:, :N], in_=cacc[:, :N], func=AF.Silu), reads=[res("cacc")], writes=[rq])
                    P.G(lambda e: e.tensor_tensor(out=c.sq[:, 0, :N], in0=qq[:, :N], in1=qq[:, :N], op=ALU.mult), reads=[rq], writes=[res("sq")])
                    pj = sumsq_bc(c, lambda k: c.sq[:, 0, :N], 1, N, [res("sq")], lhs_name="ones")
                    rsqrt_from_psum(c, pj, N, c.rstd[:, :N], res("rstd"))
                    if kind == 0:
                        out_bf(gS["r"][hh * 128:(hh + 1) * 128, t0:t0 + N],
                               lambda o, ro: P.V(lambda e: e.scalar_tensor_tensor(out=o[:, :N], in0=qq[:, :N], scalar=128.0 ** -0.5, in1=c.rstd[:, :N],
                                                                                 op0=ALU.mult, op1=ALU.mult), reads=[rq, res("rstd")], writes=[ro]), None)
                    else:
                        P.V(lambda e: e.tensor_tensor(out=kn[:, :N], in0=qq[:, :N], in1=c.rstd[:, :N], op=ALU.mult), reads=[rq, res("rstd")], writes=[res("kn")])
                        out_bf(gS["a"][hh * 128:(hh + 1) * 128, t0:t0 + N],
                               lambda o, ro: P.G(lambda e: e.tensor_copy(out=o[:, :N], in_=kn[:, :N]), reads=[res("kn")], writes=[ro]), None)
                        P.V(lambda e: e.tensor_tensor(out=kn[:, :N], in0=kn[:, :N], in1=gb_[:, hh, :N], op=ALU.mult), reads=[res("kn"), res("gbeta")], writes=[res("kn")])
                        out_bf(gS["k"][hh * 128:(hh + 1) * 128, t0:t0 + N],
                               lambda o, ro: P.G(lambda e: e.tensor_copy(out=o[:, :N], in_=kn[:, :N]), reads=[res("kn")], writes=[ro]), None)
                        out_bf(gS["b"][hh * 128:(hh + 1) * 128, t0:t0 + N],
                               lambda o, ro: P.V(lambda e: e.scalar_tensor_tensor(out=o[:, :N], in0=kn[:, :N], scalar=-1.0, in1=gg_[:, hh, :N],
                                                                                 op0=ALU.mult, op1=ALU.mult), reads=[res("kn"), res("gg")], writes=[ro]), None)
                lin_fm(c, xfn, rx, N, "hyb_w_in", 1544, 1536, cons_g)
                def cons_z(fc, pi, m):
                    out_bf(gS["z"][fc * 128:(fc + 1) * 128, t0:t0 + N],
                           lambda o, ro: P.S(lambda e: e.activation(out=o[:, :N], in_=c.pt[pi][:, :N], func=AF.Silu), reads=[res("pt%d" % pi)], writes=[ro]), None)
                lin_fm(c, xfn, rx, N, "hyb_w_in", 3088, 512, cons_z)
        P.barrier()
        if stop_after == "A":
            P.finish(); return nc

        with contextlib.ExitStack() as st:
          if only is None or "B" in only:
            sb = lambda name, shape, dt: st.enter_context(nc.sbuf_tensor(un(name), shape, dt))
            psm = lambda name, shape, dt: st.enter_context(nc.psum_tensor(un(name), shape, dt))
            Vall = sb("Vall", [128, NCH, 584], BF16)
            P.V(lambda e: e.memset(Vall[:, :, 520:584], 0.0), writes=[res("Vall")])
            P.dma("sync", Vall[:, :, 0:520], fV.rearrange("(c p) f -> p c f", p=128), reads=[res("scrA")], writes=[res("Vall")])
            lf = sb("lf", [128, NCH, 8], F32)
            P.dma("sync", lf[:], flogf.rearrange("(c p) h -> p c h", p=128), reads=[res("scrA")], writes=[res("lf")])
            negc = sb("negc", [128, 8, NCH], F32)
            pe = sb("pe", [128, 8, NCH + 1], F32)
            lff = lf[:].rearrange("p c h -> p (c h)")
            ps_s = [psm("ps_s%d" % i, [128, 512], F32) for i in range(2)]
            ps_o = [psm("ps_o%d" % i, [128, 512], F32) for i in range(2)]
            ps_b = psm("ps_b", [128, 512], F32)
            ncol = NCH * 8
            ctmp = sb("ctmp", [128, NCH, 8], F32)
            ctot = sb("ctot", [128, NCH, 8], F32)
            cf = ctmp[:].rearrange("p c h -> p (c h)")
            tf = ctot[:].rearrange("p c h -> p (c h)")
            c0 = 0
            while c0 < ncol:
                cw = min(512, ncol - c0)
                P.T(lambda e, c0=c0, cw=cw: e.matmul(ps_s[0][:, :cw], lhsT=pvs("triu"), rhs=lff[:, c0:c0 + cw], start=True, stop=True),
                    reads=[r_pv, res("lf")], writes=[res("ps_s0")])
                P.V(lambda e, c0=c0, cw=cw: e.tensor_copy(out=cf[:, c0:c0 + cw], in_=ps_s[0][:, :cw]), reads=[res("ps_s0")], writes=[res("ctmp")])
                P.T(lambda e, c0=c0, cw=cw: e.matmul(ps_s[1][:, :cw], lhsT=pvs("ones"), rhs=lff[:, c0:c0 + cw], start=True, stop=True),
                    reads=[r_pv, res("lf")], writes=[res("ps_s1")])
                P.V(lambda e, c0=c0, cw=cw: e.tensor_copy(out=tf[:, c0:c0 + cw], in_=ps_s[1][:, :cw]), reads=[res("ps_s1")], writes=[res("ctot")])
                c0 += cw
            P.V(lambda e: e.memset(pe[:, :, 0:1], 0.0), writes=[res("pe")])
            for h in range(8):
                P.V(lambda e, h=h: e.tensor_tensor_scan(out=pe[:, h, 1:NCH + 1], data0=pvs("ones")[:, 0:NCH], data1=ctot[:, :, h],
                                                         initial=0.0, op0=ALU.mult, op1=ALU.add),
                    reads=[r_pv, res("ctot")], writes=[res("pe")])
                P.V(lambda e, h=h: e.tensor_tensor(out=negc[:, h, :], in0=ctmp[:, :, h], in1=pe[:, h, 0:NCH], op=ALU.add),
                    reads=[res("ctmp"), res("pe")], writes=[res("negc")])
                P.V(lambda e, h=h: e.tensor_scalar(out=negc[:, h, :], in0=negc[:, h, :], scalar1=-1.0, scalar2=None, op0=ALU.mult),
                    reads=[res("negc")], writes=[res("negc")])
            kTs = [sb("kTs%d" % i, [64, T], BF16) for i in range(2)]
            qTs = [sb("qTs%d" % i, [64, T], BF16) for i in range(2)]
            biasg = [sb("biasg%d" % i, [128, NCH], F32) for i in range(2)]
            pT = [sb("pT%d" % i, [128, 512], BF16) for i in range(3)]
            osb = [sb("osb%d" % i, [128, 512], F32) for i in range(2)]
            for i in range(2):
                P.V(lambda e, i=i: e.memset(osb[i][:], 0.0), writes=[res("osb%d" % i)])
            rinv = sb("rinv", [64, 512], F32)
            oout = [sb("oout%d" % i, [64, 512], BF16) for i in range(2)]
            cntB = {"s": 0, "p": 0, "g": 0}
            for h in range(8):
                hb = h % 2
                P.dma("sync", kTs[hb][:], fkT[h * 64:(h + 1) * 64, :], reads=[res("scrA")], writes=[res("kTs%d" % hb)])
                P.dma("sync", qTs[hb][:], fqT[h * 64:(h + 1) * 64, :], reads=[res("scrA")], writes=[res("qTs%d" % hb)])
                for (t0, N) in tiles:
                    gi = cntB["g"] % 2; cntB["g"] += 1
                    i0 = t0 // 128; nb = N // 128
                    anc = min(i0 + 2, NCH)
                    J = i0 + nb
                    P.V(lambda e, gi=gi, anc=anc, J=J, h=h: e.tensor_scalar(out=biasg[gi][:, :J], in0=negc[:, h, :J], scalar1=pe[:, h, anc:anc + 1],
                                                                             scalar2=None, op0=ALU.add),
                        reads=[res("negc"), res("pe")], writes=[res("biasg%d" % gi)])
                    pend = None
                    for j in range(J + 1):
                        if j < J:
                            cs = 0 if j < i0 else (j - i0) * 128
                            ncols = N - cs
                            si = cntB["s"] % 2; cntB["s"] += 1
                            pi = cntB["p"] % 3; cntB["p"] += 1
                            P.T(lambda e: e.matmul(ps_s[si][:, :ncols], lhsT=kTs[hb][:, j * 128:(j + 1) * 128], rhs=qTs[hb][:, t0 + cs:t0 + cs + ncols], start=True, stop=True),
                                reads=[res("kTs%d" % hb), res("qTs%d" % hb)], writes=[res("ps_s%d" % si)])
                            P.S(lambda e: e.activation(out=pT[pi][:, :ncols], in_=ps_s[si][:, :ncols], func=AF.Exp, bias=biasg[gi][:, j:j + 1]),
                                reads=[res("ps_s%d" % si), res("biasg%d" % gi)], writes=[res("pT%d" % pi)])
                            if j >= i0:
                                P.G(lambda e: e.tensor_tensor(out=pT[pi][:, 0:128], in0=pT[pi][:, 0:128], in1=m_ui_b[:], op=ALU.mult),
                                    reads=[res("pT%d" % pi), r_cb], writes=[res("pT%d" % pi)])
                        if pend is not None:
                            (pj, pcs, pncols, ppi) = pend
                            P.T(lambda e: e.matmul(ps_o[gi][:, pcs:pcs + pncols], lhsT=Vall[:, pj, h * 65:h * 65 + 128], rhs=pT[ppi][:, :pncols],
                                                   start=(pj == 0), stop=(pj == J - 1)), reads=[res("Vall"), res("pT%d" % ppi)], writes=[res("ps_o%d" % gi)])
                        pend = (j, cs, ncols, pi) if j < J else None
                    P.S(lambda e, gi=gi, N=N: e.activation(out=osb[gi][0:65, :N], in_=ps_o[gi][0:65, :N], func=AF.Copy),
                        reads=[res("ps_o%d" % gi)], writes=[res("osb%d" % gi)])
                    o_, _ = PV_SLOTS["sel65"]
                    P.T(lambda e, gi=gi, N=N: e.matmul(ps_b[:64, :N], lhsT=pv[:, o_:o_ + 64], rhs=osb[gi][:, :N], start=True, stop=True),
                        reads=[r_pv, res("osb%d" % gi)], writes=[res("ps_b")])
                    P.V(lambda e, N=N: e.reciprocal(out=rinv[:, :N], in_=ps_b[:64, :N]), reads=[res("ps_b")], writes=[res("rinv")])
                    P.V(lambda e, gi=gi, N=N: e.tensor_tensor(out=oout[gi][:, :N], in0=osb[gi][0:64, :N], in1=rinv[:, :N], op=ALU.mult),
                        reads=[res("osb%d" % gi), res("rinv")], writes=[res("oout%d" % gi)])
                    P.dma("sync", oT[h * 64:(h + 1) * 64, t0:t0 + N], oout[gi][:, :N], reads=[res("oout%d" % gi)], writes=[res("oT_dram")])
        P.barrier()
        if stop_after == "B":
            P.finish(); return nc

        def dplr_phase(name, units, S_, post_setup, post, scalar_decay=False):
            with contextlib.ExitStack() as st:
                sb = lambda nm, shape, dt: st.enter_context(nc.sbuf_tensor(un(nm), shape, dt))
                psm = lambda nm, shape, dt: st.enter_context(nc.psum_tensor(un(nm), shape, dt))
                hd = units[0][0][2]
                ngs = len(set(cx[0] for cx in units[0]))
                SC = 4
                NG2 = 2 * ngs
                inb = {}
                for gs in range(NG2):
                    for k in ("r", "k", "a", "b", "v"):
                        inb[(gs, k)] = sb("in_%s%d" % (k, gs), [128, SC * 128], BF16)
                    inb[(gs, "lw")] = sb("in_lw%d" % gs, [128, SC * 128], F32)
                pobj = post_setup(st, NG2, SC)
                Lc = [sb("Lc%d" % g, [128, 129], F32) for g in range(NG2)]
                Lx = [sb("nLm%d" % g, [128, 1], F32) for g in range(NG2)]
                e1 = [sb("e1_%d" % g, [128, 129], F32) for g in range(NG2)]
                e2 = [sb("e2_%d" % g, [128, 129], F32) for g in range(NG2)]
                e5 = [sb("e5_%d" % g, [128, 128], F32) for g in range(NG2)]
                e6 = [sb("e6_%d" % g, [128, 128], F32) for g in range(NG2)]
                if scalar_decay:
                    Dm = {(g, k): sb("D%s_%d" % (k, g), [128, 128], F32) for g in range(NG2) for k in ("m_su", "m_sl", "m_ui")}
                    lcc = [sb("lcc%d" % g, [128, 2], F32) for g in range(NG2)]
                    dtmp = [sb("dtmp%d" % g, [128, 128], F32) for g in range(NG2)]
                opn = ("rh", "rt", "ah", "at", "bh", "kh", "btT", "ktT")
                ops_ = {(g, k): sb("%s%d" % (k, g), [128, 128], BF16) for g in range(NG2) for k in opn}
                tmj = {(g, k): sb("tm_%s%d" % (k, g), [128, 128], BF16) for g in range(NG2) for k in ("bt", "kt", "vt")}
                pstr = psm("pstr", [128, 4, 128], BF16)
                yT = [sb("yT%d" % g, [128, 128], F32) for g in range(ngs)]
                Sf = [sb("Sf%d" % g, [128, 128], F32) for g in range(ngs)]
                Sb = [sb("Sb%d" % g, [128, 128], BF16) for g in range(ngs)]
                hx = []
                for h in range(2):
                    o = TokCtx()
                    o.psx = [psm("psx%d_%d" % (h, i), [128, 512], F32) for i in range(2)]
                    o.psy = psm("psy%d" % h, [128, 512], F32)
                    o.xc = 0
                    for k in ("M0", "MT0", "Mb0", "Mb1", "MTb0", "MTb1", "P0", "P1", "Q0", "Q1", "Noff", "NoffT", "Z", "Z2", "AakT", "ArbT", "ArkT", "RHS", "U"):
                        setattr(o, k, sb("%s_%d" % (k, h), [128, 128], BF16))
                    hx.append(o)
                rn = lambda s: res(name + "_" + s)
                trc = [0]

                for unit in units:
                    gslots = []
                    for cx in unit:
                        if cx[0] not in gslots:
                            gslots.append(cx[0])
                    for g in range(ngs):
                        P.V(lambda e, g=g: e.memset(Sf[g][:], 0.0), writes=[rn("Sf%d" % g)])
                        P.V(lambda e, g=g: e.memset(Sb[g][:], 0.0), writes=[rn("Sb%d" % g)])
                        P.V(lambda e, g=g: e.memset(Lc[g][:, 0:1], 0.0), writes=[rn("Lc%d" % g)])
                        P.V(lambda e, g=g: e.memset(Lc[ngs + g][:, 0:1], 0.0), writes=[rn("Lc%d" % (ngs + g))])
                    def prologue(c):
                      sc, cl = c // SC, c % SC
                      ipar = (sc % 2) * ngs
                      cpar = (c % 2) * ngs
                      if cl == 0:
                            nsc = min(SC, NCH - c) * 128
                            for g0, grow in enumerate(gslots):
                                g = ipar + g0
                                for k in ("r", "k", "a", "b", "v", "lw"):
                                    P.dma("sync", inb[(g, k)][:, :nsc], S_[k][grow:grow + 128, c * 128:c * 128 + nsc],
                                          reads=[res("scr" + name)], writes=[rn("in_%s%d" % (k, g))])
                                post("load", g, grow, c, nsc, pobj)
                      cols = slice(cl * 128, (cl + 1) * 128)
                      for g0 in range(ngs):
                        g = cpar + g0
                        gi_ = ipar + g0
                        if True:
                            rl = [rn("Lc%d" % g)]
                            P.V(lambda e, g=g, gi_=gi_: e.tensor_tensor_scan(out=Lc[g][:, 1:129], data0=pvs("ones"), data1=inb[(gi_, "lw")][:, cols],
                                                                     initial=0.0, op0=ALU.mult, op1=ALU.add),
                                reads=[r_pv, rn("in_lw%d" % gi_)], writes=rl)
                            P.S(lambda e, g=g: e.activation(out=e2[g][:], in_=Lc[g][:], func=AF.Exp), reads=rl, writes=[rn("e2_%d" % g)])
                            P.S(lambda e, g=g: e.activation(out=e6[g][:], in_=Lc[g][:, 1:129], func=AF.Exp, bias=Lc[g][:, 128:129], scale=-1.0),
                                reads=rl, writes=[rn("e6_%d" % g)])
                            if not scalar_decay:
                                P.V(lambda e, g=g: e.tensor_scalar(out=Lx[g][:], in0=Lc[g][:, 64:65], scalar1=-1.0, scalar2=None, op0=ALU.mult),
                                    reads=rl, writes=[rn("nLm%d" % g)])
                                P.S(lambda e, g=g: e.activation(out=e1[g][:], in_=Lc[g][:], func=AF.Exp, bias=Lx[g][:, 0:1]),
                                    reads=rl + [rn("nLm%d" % g)], writes=[rn("e1_%d" % g)])
                                P.S(lambda e, g=g: e.activation(out=e5[g][:], in_=Lc[g][:, 1:129], func=AF.Exp, bias=Lc[g][:, 64:65], scale=-1.0),
                                    reads=rl, writes=[rn("e5_%d" % g)])
                                specs = [("rh", "r", e1[g][:, 1:129], "e1_"), ("rt", "r", e2[g][:, 1:129], "e2_"),
                                         ("ah", "a", e1[g][:, 0:128], "e1_"), ("at", "a", e2[g][:, 0:128], "e2_"),
                                         ("bh", "b", e5[g][:], "e5_"), ("kh", "k", e5[g][:], "e5_"),
                                         ("btT", "b", e6[g][:], "e6_"), ("ktT", "k", e6[g][:], "e6_")]
                            else:
                                rd = [rn("dtmp%d" % g)]
                                P.V(lambda e, g=g: e.tensor_tensor(out=dtmp[g][:], in0=Lc[g][:, 1:129], in1=pvs("ident"), op=ALU.mult), reads=rl + [r_pv], writes=rd)
                                P.V(lambda e, g=g: e.reduce_sum(out=lcc[g][:, 0:1], in_=dtmp[g][:], axis=mybir.AxisListType.X), reads=rd, writes=[rn("lcc%d" % g)])
                                P.V(lambda e, g=g: e.tensor_tensor(out=dtmp[g][:], in0=Lc[g][:, 0:128], in1=pvs("ident"), op=ALU.mult), reads=rl + [r_pv], writes=rd)
                                P.V(lambda e, g=g: e.reduce_sum(out=lcc[g][:, 1:2], in_=dtmp[g][:], axis=mybir.AxisListType.X), reads=rd, writes=[rn("lcc%d" % g)])
                                rlc = [rn("lcc%d" % g)]
                                for (mk, src, colj, neg) in (("m_ui", Lc[g][:, 1:129], 0, False), ("m_su", Lc[g][:, 0:128], 0, False), ("m_sl", Lc[g][:, 1:129], 1, True)):
                                    D_ = Dm[(g, mk)]; rD = rn("D%s_%d" % (mk, g))
                                    if not neg:
                                        P.V(lambda e, g=g, src=src, colj=colj: e.tensor_scalar(out=dtmp[g][:], in0=src, scalar1=lcc[g][:, colj:colj + 1], scalar2=0.0,
                                                                                                op0=ALU.subtract, op1=ALU.min), reads=rl + rlc, writes=rd)
                                    else:
                                        P.V(lambda e, g=g, src=src, colj=colj: e.tensor_scalar(out=dtmp[g][:], in0=src, scalar1=-1.0, scalar2=lcc[g][:, colj:colj + 1],
                                                                                                op0=ALU.mult, op1=ALU.add), reads=rl + rlc, writes=rd)
                                        P.V(lambda e, g=g: e.tensor_scalar(out=dtmp[g][:], in0=dtmp[g][:], scalar1=0.0, scalar2=None, op0=ALU.min), reads=rd, writes=rd)
                                    P.S(lambda e, g=g, D_=D_: e.activation(out=D_[:], in_=dtmp[g][:], func=AF.Exp), reads=rd, writes=[rD])
                                    P.G(lambda e, D_=D_, mk=mk: e.tensor_tensor(out=D_[:], in0=D_[:], in1=pvs(mk), op=ALU.mult), reads=[rD, r_pv], writes=[rD])
                                specs = [("rt", "r", e2[g][:, 1:129], "e2_"), ("at", "a", e2[g][:, 0:128], "e2_"),
                                         ("btT", "b", e6[g][:], "e6_"), ("ktT", "k", e6[g][:], "e6_")]
                            for qi, (on, ik, eap, en) in enumerate(specs):
                                eng = "vector" if qi % 2 == 0 else "gpsimd"
                                P.op(eng, lambda e, g=g, gi_=gi_, on=on, ik=ik, eap=eap: e.tensor_tensor(out=ops_[(g, on)][:], in0=inb[(gi_, ik)][:, cols], in1=eap, op=ALU.mult),
                                     reads=[rn("in_%s%d" % (ik, gi_)), rn(en + "%d" % g)], writes=[rn("%s%d" % (on, g))])
                            for (tn, src, rsrc) in (("bt", ops_[(g, "btT")][:], rn("btT%d" % g)), ("kt", ops_[(g, "ktT")][:], rn("ktT%d" % g)),
                                                    ("vt", inb[(gi_, "v")][:, cols], rn("in_v%d" % gi_))):
                                ti_ = trc[0] % 4; trc[0] += 1
                                P.T(lambda e, ti_=ti_, src=src: e.transpose(out=pstr[:, ti_, :], in_=src, identity=cb["ident"][:]),
                                    reads=[rsrc, r_cb], writes=[rn("pstr")])
                                P.S(lambda e, ti_=ti_, g=g, tn=tn: e.activation(out=tmj[(g, tn)][:], in_=pstr[:, ti_, :], func=AF.Copy),
                                    reads=[rn("pstr")], writes=[rn("tm_%s%d" % (tn, g))])
                    prologue(0)
                    for c in range(NCH):
                        sc, cl = c // SC, c % SC
                        ipar = (sc % 2) * ngs
                        cpar = (c % 2) * ngs
                        cols = slice(cl * 128, (cl + 1) * 128)
                        H = []
                        for hi, cx in enumerate(unit):
                            g = cpar + gslots.index(cx[0])
                            H.append((hx[hi], g, slice(cx[1], cx[1] + hd), hi))
                        def xslot(o, hi):
                            i = o.xc % 2; o.xc += 1
                            return o.psx[i][:, 0:128], rn("psx%d_%d" % (hi, i))
                        def opr(g, k, ps_):
                            if scalar_decay and k in ("rh", "ah", "bh", "kh"):
                                gi_ = ipar + (g - cpar)
                                return inb[(gi_, k[0])][ps_, cols], rn("in_%s%d" % (k[0], gi_))
                            return ops_[(g, k)][ps_, :], rn("%s%d" % (k, g))
                        for (dst, l, r_, mask) in (("MT0", "bh", "ah", "m_su"), ("M0", "ah", "bh", "m_sl"), ("AakT", "kh", "ah", "m_su"),
                                                   ("ArbT", "bh", "rh", "m_ui"), ("ArkT", "kh", "rh", "m_ui")):
                            for (o, g, ps_, hi) in H:
                                xa, rx_ = xslot(o, hi)
                                la, rl_ = opr(g, l, ps_); ra, rr_ = opr(g, r_, ps_)
                                P.T(lambda e, xa=xa, la=la, ra=ra: e.matmul(xa, lhsT=la, rhs=ra, start=True, stop=True), reads=[rl_, rr_], writes=[rx_])
                                mk_ap = Dm[(g, mask)][:] if scalar_decay else pvs(mask)
                                mk_r = rn("D%s_%d" % (mask, g)) if scalar_decay else r_pv
                                P.V(lambda e, o=o, dst=dst, xa=xa, mk_ap=mk_ap: e.tensor_tensor(out=getattr(o, dst)[:], in0=xa, in1=mk_ap, op=ALU.mult),
                                    reads=[rx_, mk_r], writes=[rn("%s_%d" % (dst, hi))])
                        def mm_evac(o, hi, lhs, rlhs, rhs, rrhs, dst, rdst, add=None, radd=None):
                            xa, rx_ = xslot(o, hi)
                            P.T(lambda e: e.matmul(xa, lhsT=lhs[:], rhs=rhs[:], start=True, stop=True), reads=[rlhs, rrhs], writes=[rx_])
                            if add is None:
                                P.S(lambda e: e.activation(out=dst[:], in_=xa, func=AF.Copy), reads=[rx_], writes=[rdst])
                            else:
                                P.V(lambda e: e.tensor_tensor(out=dst[:], in0=xa, in1=add[:], op=ALU.add), reads=[rx_, radd], writes=[rdst])
                        for (dst_, src_, cst, op_) in (("MTb0", "MT0", "bd32", ALU.mult), ("Mb0", "M0", "bd32", ALU.mult), ("P0", "MTb0", "ident", ALU.add), ("Q0", "Mb0", "ident", ALU.add)):
                            for (o, g, ps_, hi) in H:
                                R_ = lambda k: rn("%s_%d" % (k, hi))
                                P.G(lambda e, o=o: e.tensor_tensor(out=getattr(o, dst_)[:], in0=getattr(o, src_)[:], in1=cb[cst][:], op=op_), reads=[R_(src_), r_cb], writes=[R_(dst_)])
                        cur = 0
                        cn = lambda k, i: "%s%d" % (k, i)
                        for m in range(1, 5):
                            nxt = 1 - cur
                            for (o, g, ps_, hi) in H:
                                R_ = lambda k: rn("%s_%d" % (k, hi))
                                G_ = lambda k: getattr(o, k)
                                mm_evac(o, hi, G_(cn("MTb", cur)), R_(cn("MTb", cur)), G_(cn("Mb", cur)), R_(cn("Mb", cur)), G_(cn("Mb", nxt)), R_(cn("Mb", nxt)))
                                mm_evac(o, hi, G_(cn("Mb", cur)), R_(cn("Mb", cur)), G_(cn("MTb", cur)), R_(cn("MTb", cur)), G_(cn("MTb", nxt)), R_(cn("MTb", nxt)))
                            for (o, g, ps_, hi) in H:
                                R_ = lambda k: rn("%s_%d" % (k, hi))
                                G_ = lambda k: getattr(o, k)
                                mm_evac(o, hi, G_(cn("Mb", nxt)), R_(cn("Mb", nxt)), G_(cn("P", cur)), R_(cn("P", cur)), G_(cn("P", nxt)), R_(cn("P", nxt)),
                                        add=G_(cn("P", cur)), radd=R_(cn("P", cur)))
                                mm_evac(o, hi, G_(cn("MTb", nxt)), R_(cn("MTb", nxt)), G_(cn("Q", cur)), R_(cn("Q", cur)), G_(cn("Q", nxt)), R_(cn("Q", nxt)),
                                        add=G_(cn("Q", cur)), radd=R_(cn("Q", cur)))
                            cur = nxt
                        for (lev, offm) in ((64, "off64"), (128, "off128")):
                            nxt = 1 - cur
                            for (o, g, ps_, hi) in H:
                                R_ = lambda k: rn("%s_%d" % (k, hi))
                                G_ = lambda k: getattr(o, k)
                                P.G(lambda e, o=o: e.tensor_tensor(out=o.Noff[:], in0=o.M0[:], in1=cb[offm][:], op=ALU.mult), reads=[R_("M0"), r_cb], writes=[R_("Noff")])
                                P.G(lambda e, o=o: e.tensor_tensor(out=o.NoffT[:], in0=o.MT0[:], in1=cb[offm][:], op=ALU.mult), reads=[R_("MT0"), r_cb], writes=[R_("NoffT")])
                            for (o, g, ps_, hi) in H:
                                R_ = lambda k: rn("%s_%d" % (k, hi))
                                G_ = lambda k: getattr(o, k)
                                X_, rX = G_(cn("Q", cur)), R_(cn("Q", cur))
                                XT_, rXT = G_(cn("P", cur)), R_(cn("P", cur))
                                if lev != 128:
                                    mm_evac(o, hi, o.NoffT, R_("NoffT"), X_, rX, o.Z, R_("Z"))
                                mm_evac(o, hi, o.Noff, R_("Noff"), XT_, rXT, o.Z2, R_("Z2"))
                            for (o, g, ps_, hi) in H:
                                R_ = lambda k: rn("%s_%d" % (k, hi))
                                G_ = lambda k: getattr(o, k)
                                X_, rX = G_(cn("Q", cur)), R_(cn("Q", cur))
                                XT_, rXT = G_(cn("P", cur)), R_(cn("P", cur))
                                if lev != 128:
                                    mm_evac(o, hi, XT_, rXT, o.Z, R_("Z"), G_(cn("Q", nxt)), R_(cn("Q", nxt)), add=X_, radd=rX)
                                mm_evac(o, hi, X_, rX, o.Z2, R_("Z2"), G_(cn("P", nxt)), R_(cn("P", nxt)), add=XT_, radd=rXT)
                            cur = nxt
                        if c + 1 < NCH:
                            prologue(c + 1)
                        for (o, g, ps_, hi) in H:
                            Pf, rPf = getattr(o, "P%d" % cur), rn("P%d_%d" % (cur, hi))
                            ry = rn("psy%d" % hi)
                            hs = slice(ps_.start, ps_.start + hd)
                            at_, rat = opr(g, "at", ps_)
                            rt_, rrt = opr(g, "rt", ps_)
                            P.T(lambda e, o=o, at_=at_, g=g, ps_=ps_: e.matmul(o.psy[:, 256:256 + hd], lhsT=at_, rhs=Sb[g - cpar][ps_, 0:hd], start=True, stop=False),
                                reads=[rat, rn("Sb%d" % (g - cpar))], writes=[ry])
                            P.T(lambda e, o=o, g=g, hs=hs: e.matmul(o.psy[:, 256:256 + hd], lhsT=o.AakT[:], rhs=tmj[(g, "vt")][:, hs], start=False, stop=True),
                                reads=[rn("AakT_%d" % hi), rn("tm_vt%d" % g)], writes=[ry])
                            P.S(lambda e, o=o: e.activation(out=o.RHS[:, :hd], in_=o.psy[:, 256:256 + hd], func=AF.Copy), reads=[ry], writes=[rn("RHS_%d" % hi)])
                        for (o, g, ps_, hi) in H:
                            Pf, rPf = getattr(o, "P%d" % cur), rn("P%d_%d" % (cur, hi))
                            ry = rn("psy%d" % hi)
                            hs = slice(ps_.start, ps_.start + hd)
                            at_, rat = opr(g, "at", ps_)
                            rt_, rrt = opr(g, "rt", ps_)
                            P.T(lambda e, o=o, Pf=Pf: e.matmul(o.psy[:, 384:384 + hd], lhsT=Pf[:], rhs=o.RHS[:, :hd], start=True, stop=True),
                                reads=[rPf, rn("RHS_%d" % hi)], writes=[ry])
                            P.S(lambda e, o=o: e.activation(out=o.U[:, :hd], in_=o.psy[:, 384:384 + hd], func=AF.Copy), reads=[ry], writes=[rn("U_%d" % hi)])
                        for (o, g, ps_, hi) in H:
                            Pf, rPf = getattr(o, "P%d" % cur), rn("P%d_%d" % (cur, hi))
                            ry = rn("psy%d" % hi)
                            hs = slice(ps_.start, ps_.start + hd)
                            at_, rat = opr(g, "at", ps_)
                            rt_, rrt = opr(g, "rt", ps_)
                            P.T(lambda e, o=o, g=g, ps_=ps_, rt_=rt_: e.matmul(o.psy[ps_, 0:128], lhsT=Sb[g - cpar][ps_, 0:hd], rhs=rt_, start=True, stop=False),
                                reads=[rn("Sb%d" % (g - cpar)), rrt], writes=[ry])
                            P.T(lambda e, o=o, ps_=ps_: e.matmul(o.psy[ps_, 0:128], lhsT=o.U[:, :hd], rhs=o.ArbT[:], start=False, stop=False),
                                reads=[rn("U_%d" % hi), rn("ArbT_%d" % hi)], writes=[ry])
                            P.T(lambda e, o=o, g=g, ps_=ps_, hs=hs: e.matmul(o.psy[ps_, 0:128], lhsT=tmj[(g, "vt")][:, hs], rhs=o.ArkT[:], start=False, stop=True),
                                reads=[rn("tm_vt%d" % g), rn("ArkT_%d" % hi)], writes=[ry])
                            P.V(lambda e, o=o, g=g, ps_=ps_: e.tensor_copy(out=yT[g - cpar][ps_, :], in_=o.psy[ps_, 0:128]), reads=[ry], writes=[rn("yT%d" % (g - cpar))])
                        for (o, g, ps_, hi) in H:
                            Pf, rPf = getattr(o, "P%d" % cur), rn("P%d_%d" % (cur, hi))
                            ry = rn("psy%d" % hi)
                            hs = slice(ps_.start, ps_.start + hd)
                            at_, rat = opr(g, "at", ps_)
                            rt_, rrt = opr(g, "rt", ps_)
                            P.T(lambda e, o=o, g=g, ps_=ps_, hs=hs: e.matmul(o.psy[ps_, 128:128 + hd], lhsT=tmj[(g, "bt")][:, hs], rhs=o.U[:, :hd], start=True, stop=False),
                                reads=[rn("tm_bt%d" % g), rn("U_%d" % hi)], writes=[ry])
                            P.T(lambda e, o=o, g=g, ps_=ps_, hs=hs: e.matmul(o.psy[ps_, 128:128 + hd], lhsT=tmj[(g, "kt")][:, hs], rhs=tmj[(g, "vt")][:, hs], start=False, stop=True),
                                reads=[rn("tm_kt%d" % g), rn("tm_vt%d" % g)], writes=[ry])
                            P.V(lambda e, o=o, g=g, ps_=ps_: e.scalar_tensor_tensor(out=Sf[g - cpar][ps_, 0:hd], in0=Sf[g - cpar][ps_, 0:hd], scalar=e2[g][ps_, 128:129],
                                                                                   in1=o.psy[ps_, 128:128 + hd], op0=ALU.mult, op1=ALU.add),
                                reads=[rn("Sf%d" % (g - cpar)), rn("e2_%d" % g), ry], writes=[rn("Sf%d" % (g - cpar))])
                            P.G(lambda e, g=g, ps_=ps_: e.tensor_copy(out=Sb[g - cpar][ps_, 0:hd], in_=Sf[g - cpar][ps_, 0:hd]), reads=[rn("Sf%d" % (g - cpar))], writes=[rn("Sb%d" % (g - cpar))])
                        for g0, grow in enumerate(gslots):
                            post("chunk", ipar + g0, grow, c, (yT[g0], rn("yT%d" % g0), cl), pobj)
            P.barrier()

        def gdn_post_setup(st, ngs, SC):
            sb = lambda nm, shape, dt: st.enter_context(nc.sbuf_tensor(un(nm), shape, dt))
            o = TokCtx()
            o.z = [sb("gz%d" % g, [128, SC * 128], BF16) for g in range(ngs)]
            o.sq = sb("gpsq", [128, 128], BF16)
            o.rs = sb("gprs", [128, 128], F32)
            o.ob = [sb("gpob%d" % i, [128, 128], BF16) for i in range(2)]
            o.ps = st.enter_context(nc.psum_tensor(un("gpps"), [128, 128], F32))
            o.k = 0
            return o

        def gdn_post(kind, g, grow, c, arg, o):
            if kind == "load":
                P.dma("sync", o.z[g][:, :arg], gS["z"][grow:grow + 128, c * 128:c * 128 + arg], reads=[res("scrA")], writes=[res("gz%d" % g)])
                return
            yT_, ry, cl = arg
            P.G(lambda e: e.tensor_tensor(out=o.sq[:], in0=yT_[:], in1=yT_[:], op=ALU.mult), reads=[ry], writes=[res("gpsq")])
            P.T(lambda e: e.matmul(o.ps[:], lhsT=cb["ones"][:], rhs=o.sq[:], start=True, stop=True), reads=[r_cb, res("gpsq")], writes=[res("gpps")])
            P.S(lambda e: e.activation(out=o.rs[:], in_=o.ps[:], func=AF.Sqrt, bias=cvec[:, 0:1], scale=1.0 / 128), reads=[res("gpps"), r_cb], writes=[res("gprs")])
            P.V(lambda e: e.reciprocal(out=o.rs[:], in_=o.rs[:]), reads=[res("gprs")], writes=[res("gprs")])
            P.V(lambda e: e.scalar_tensor_tensor(out=o.rs[:], in0=yT_[:], scalar=pvs("o_gain"), in1=o.rs[:], op0=ALU.mult, op1=ALU.mult),
                reads=[ry, r_pv, res("gprs")], writes=[res("gprs")])
            i = o.k % 2; o.k += 1
            P.V(lambda e: e.tensor_tensor(out=o.ob[i][:], in0=o.rs[:], in1=o.z[g][:, cl * 128:(cl + 1) * 128], op=ALU.mult),
                reads=[res("gprs"), res("gz%d" % g)], writes=[res("gpob%d" % i)])
            P.dma("sync", oT[512 + grow:512 + grow + 128, c * 128:(c + 1) * 128], o.ob[i][:], reads=[res("gpob%d" % i)], writes=[res("oT_dram")])

        R["scrgdn"] = res("scrA")
        if only is None or "C" in only:
          dplr_phase("gdn", [[(0, 0, 128), (128, 0, 128)], [(256, 0, 128), (384, 0, 128)]], gS, gdn_post_setup, gdn_post, scalar_decay=True)
        if stop_after == "C":
            P.finish(); return nc

        with contextlib.ExitStack() as st:
          if only is None or "D" in only:
            sb = lambda name, shape, dt: st.enter_context(nc.sbuf_tensor(un(name), shape, dt))
            c = make_tok_ctx(st, nwi=4, nwo=3)
            oin = sb("oin", [128, 8, 512], BF16)
            xx = sb("xx", [128, 8, 512], BF16)
            xm = sb("xm", [128, 8, 512], BF16)
            w1s = sb("w1s", [128, 8, 64], BF16); a1s = sb("a1s", [128, 8, 64], BF16); g1s = sb("g1s", [128, 8, 160], BF16)
            w2s = sb("w2s", [64, D], BF16); a2s = sb("a2s", [64, D], BF16); g2s = sb("g2s", [128, 2, D], BF16)
            P.dma("sync", w1s[:], wview(Wb["rwkv_w1"]), reads=[res("W_rwkv_w1")], writes=[res("lora")])
            P.dma("sync", a1s[:], wview(Wb["rwkv_a1"]), reads=[res("W_rwkv_a1")], writes=[res("lora")])
            P.dma("sync", g1s[:], wview(Wb["rwkv_g1"]), reads=[res("W_rwkv_g1")], writes=[res("lora")])
            P.dma("sync", w2s[:], Wb["rwkv_w2"], reads=[res("W_rwkv_w2")], writes=[res("lora")])
            P.dma("sync", a2s[:], Wb["rwkv_a2"], reads=[res("W_rwkv_a2")], writes=[res("lora")])
            P.dma("sync", g2s[:, 0, :], Wb["rwkv_g2"][0:128, :], reads=[res("W_rwkv_g2")], writes=[res("lora")])
            P.dma("sync", g2s[0:32, 1, :], Wb["rwkv_g2"][128:160, :], reads=[res("W_rwkv_g2")], writes=[res("lora")])
            r_lora = res("lora")
            lmid = [sb("lmid%d" % i, [128, 512], BF16) for i in range(2)]
            rr = sb("rr", [128, 8, 512], BF16)
            kk_ = sb("kkk", [128, 8, 512], F32)
            aa = sb("aa", [128, 512], F32)
            t1 = sb("t1", [128, 512], F32)
            t2 = sb("t2", [128, 512], F32)
            ob = [sb("obD%d" % i, [128, 512], BF16) for i in range(4)]
            obf = [sb("obF%d" % i, [128, 512], F32) for i in range(2)]
            ocnt = [0, 0]
            P.V(lambda e: e.memset(c.hn[:, :, 0:1], 0.0), writes=[res("hn")])
            omk = sb("omk", [128, 8], F32)
            print("phase D sbuf remaining", nc.sbuf_bytes_remaining)
            P.V(lambda e: e.tensor_scalar(out=omk[:], in0=pvs("k_a"), scalar1=-1.0, scalar2=1.0, op0=ALU.mult, op1=ALU.add), reads=[r_pv], writes=[res("omk")])

            def out_bf(dst_ap, make):
                i = ocnt[0] % 4; ocnt[0] += 1
                make(ob[i], res("obD%d" % i))
                P.dma("sync", dst_ap, ob[i][:, :dst_ap.shape[-1]], reads=[res("obD%d" % i)], writes=[res("scrrwkv")])

            for ti, (t0, N) in enumerate(tiles):
                h_t, r_h = c.hT[ti % 2], res("hT%d" % (ti % 2))
                P.dma("sync", h_t[:, :, :N], hview(hT)[:, :, t0:t0 + N], reads=[res("hT_dram")], writes=[r_h])
                P.dma("sync", oin[:, :, :N], hview(oT)[:, :, t0:t0 + N], reads=[res("oT_dram")], writes=[res("oin")])
                proj_residual(c, h_t, r_h, N, oin, res("oin"), "hyb_w_out")
                ffn(c, h_t, r_h, N, 1)
                ffn(c, h_t, r_h, N, 2)
                P.dma("sync", hview(hT)[:, :, t0:t0 + N], h_t[:, :, :N], reads=[r_h], writes=[res("hT_dram")])
                rmsnorm(c, h_t, r_h, N, "mix_norm1")
                import os as _os
                DCUT = float(_os.environ.get("DCUT", "99"))
                if DCUT <= 0:
                    continue
                for kc in range(8):
                    P.op("vector", lambda e, kc=kc: e.tensor_tensor(out=xx[:, kc, :N], in0=c.hn[:, kc, 0:N], in1=c.hn[:, kc, 1:1 + N], op=ALU.subtract),
                         reads=[res("hn")], writes=[res("xx")])
                o_mu, _ = PV_SLOTS["mu"]
                def mix(i):
                    for kc in range(8):
                        eng = "vector" if kc % 2 == 0 else "gpsimd"
                        P.op("vector", lambda e, kc=kc: e.scalar_tensor_tensor(out=xm[:, kc, :N], in0=xx[:, kc, :N], scalar=pv[:, o_mu + i * 8 + kc:o_mu + i * 8 + kc + 1],
                                                                          in1=c.hn[:, kc, 1:1 + N], op0=ALU.mult, op1=ALU.add),
                             reads=[res("xx"), res("hn"), r_pv], writes=[res("xm")])
                xfn = lambda kc: xm[:, kc, :N]
                rxm = [res("xm")]
                if DCUT <= 0.3:
                    continue
                mix(0)
                if DCUT <= 0.6:
                    continue
                def cons_r(fc, pi, m):
                    if _os.environ.get("RVAR", "") != "a":
                        P.S(lambda e: e.activation(out=rr[:, fc, :N], in_=c.pt[pi][:, :N], func=AF.Copy), reads=[res("pt%d" % pi)], writes=[res("rr")])
                    if _os.environ.get("RVAR", "") == "b":
                        return
                    out_bf(rS["r"][fc * 128:(fc + 1) * 128, t0:t0 + N],
                           lambda o, ro: P.V(lambda e: e.tensor_copy(out=o[:, :N], in_=c.pt[pi][:, :N]), reads=[res("pt%d" % pi)], writes=[ro]))
                lin_fm(c, xfn, rxm, N, "rwkv_w_r", 0, D, cons_r)
                if DCUT <= 1:
                    continue
                mix(1)
                pi = next_pt(c)
                for kc in range(8):
                    P.T(lambda e, pi=pi, kc=kc: e.matmul(c.pt[pi][:64, :N], lhsT=w1s[:, kc, :], rhs=xm[:, kc, :N], start=(kc == 0), stop=(kc == 7)),
                        reads=[r_lora, res("xm")], writes=[res("pt%d" % pi)])
                P.S(lambda e, pi=pi: e.activation(out=lmid[0][:64, :N], in_=c.pt[pi][:64, :N], func=AF.Tanh), reads=[res("pt%d" % pi)], writes=[res("lmid0")])
                for fc in range(8):
                    pj = next_pt(c)
                    P.T(lambda e, pj=pj, fc=fc: e.matmul(c.pt[pj][:, :N], lhsT=w2s[:, fc * 128:(fc + 1) * 128], rhs=lmid[0][:64, :N], start=True, stop=True),
                        reads=[r_lora, res("lmid0")], writes=[res("pt%d" % pj)])
                    fi = ocnt[1] % 2; ocnt[1] += 1
                    P.S(lambda e, pj=pj, fc=fc, fi=fi: e.activation(out=obf[fi][:, :N], in_=c.pt[pj][:, :N], func=AF.Sigmoid, bias=pvs("w0", fc, 1)),
                        reads=[res("pt%d" % pj), r_pv], writes=[res("obF%d" % fi)])
                    P.V(lambda e, fi=fi: e.tensor_scalar(out=obf[fi][:, :N], in0=obf[fi][:, :N], scalar1=-float(np.exp(-0.5)), scalar2=None, op0=ALU.mult),
                        reads=[res("obF%d" % fi)], writes=[res("obF%d" % fi)])
                    P.dma("sync", rS["lw"][fc * 128:(fc + 1) * 128, t0:t0 + N], obf[fi][:, :N], reads=[res("obF%d" % fi)], writes=[res("scrrwkv")])
                if DCUT <= 2:
                    continue
                mix(2)
                def cons_k(fc, pi, m):
                    P.S(lambda e: e.activation(out=kk_[:, fc, :N], in_=c.pt[pi][:, :N], func=AF.Copy), reads=[res("pt%d" % pi)], writes=[res("kkk")])
                lin_fm(c, xfn, rxm, N, "rwkv_w_k", 0, D, cons_k)
                mix(3)
                def cons_v(fc, pi, m):
                    out_bf(rS["v"][fc * 128:(fc + 1) * 128, t0:t0 + N],
                           lambda o, ro: P.V(lambda e: e.tensor_copy(out=o[:, :N], in_=c.pt[pi][:, :N]), reads=[res("pt%d" % pi)], writes=[ro]))
                    P.S(lambda e: e.activation(out=c.act[:, fc, :N], in_=c.pt[pi][:, :N], func=AF.Copy), reads=[res("pt%d" % pi)], writes=[res("act")])
                lin_fm(c, xfn, rxm, N, "rwkv_w_v", 0, D, cons_v)
                if DCUT <= 3:
                    continue
                mix(4)
                pi = next_pt(c)
                for kc in range(8):
                    P.T(lambda e, pi=pi, kc=kc: e.matmul(c.pt[pi][:64, :N], lhsT=a1s[:, kc, :], rhs=xm[:, kc, :N], start=(kc == 0), stop=(kc == 7)),
                        reads=[r_lora, res("xm")], writes=[res("pt%d" % pi)])
                P.S(lambda e, pi=pi: e.activation(out=lmid[1][:64, :N], in_=c.pt[pi][:64, :N], func=AF.Copy), reads=[res("pt%d" % pi)], writes=[res("lmid1")])
                for fc in range(8):
                    pj = next_pt(c)
                    P.T(lambda e, pj=pj, fc=fc: e.matmul(c.pt[pj][:, :N], lhsT=a2s[:, fc * 128:(fc + 1) * 128], rhs=lmid[1][:64, :N], start=True, stop=True),
                        reads=[r_lora, res("lmid1")], writes=[res("pt%d" % pj)])
                    P.S(lambda e, pj=pj, fc=fc: e.activation(out=aa[:, :N], in_=c.pt[pj][:, :N], func=AF.Sigmoid, bias=pvs("a0", fc, 1)),
                        reads=[res("pt%d" % pj), r_pv], writes=[res("aa")])
                    P.V(lambda e, fc=fc: e.tensor_scalar(out=t1[:, :N], in0=kk_[:, fc, :N], scalar1=pvs("k_k", fc, 1), scalar2=None, op0=ALU.mult),
                        reads=[res("kkk"), r_pv], writes=[res("t1")])
                    P.G(lambda e: e.tensor_tensor(out=c.sq[:, 0, :N], in0=t1[:, :N], in1=t1[:, :N], op=ALU.mult), reads=[res("t1")], writes=[res("sq")])
                    pq = sumsq_bc(c, lambda k: c.sq[:, 0, :N], 1, N, [res("sq")], lhs_name="blk64")
                    rsqrt_from_psum(c, pq, N, c.rstd[:, :N], res("rstd"))
                    P.V(lambda e: e.tensor_tensor(out=t1[:, :N], in0=t1[:, :N], in1=c.rstd[:, :N], op=ALU.mult), reads=[res("t1"), res("rstd")], writes=[res("t1")])
                    out_bf(rS["a"][fc * 128:(fc + 1) * 128, t0:t0 + N],
                           lambda o, ro: P.G(lambda e: e.tensor_scalar(out=o[:, :N], in0=t1[:, :N], scalar1=-1.0, scalar2=None, op0=ALU.mult), reads=[res("t1")], writes=[ro]))
                    out_bf(rS["b"][fc * 128:(fc + 1) * 128, t0:t0 + N],
                           lambda o, ro: P.V(lambda e: e.tensor_tensor(out=o[:, :N], in0=t1[:, :N], in1=aa[:, :N], op=ALU.mult), reads=[res("t1"), res("aa")], writes=[ro]))
                    P.V(lambda e, fc=fc: e.tensor_scalar(out=t2[:, :N], in0=aa[:, :N], scalar1=pvs("k_a", fc, 1), scalar2=omk[:, fc:fc + 1], op0=ALU.mult, op1=ALU.add),
                        reads=[res("aa"), r_pv, res("omk")], writes=[res("t2")])
                    P.V(lambda e, fc=fc: e.tensor_tensor(out=t2[:, :N], in0=t2[:, :N], in1=kk_[:, fc, :N], op=ALU.mult), reads=[res("t2"), res("kkk")], writes=[res("t2")])
                    out_bf(rS["k"][fc * 128:(fc + 1) * 128, t0:t0 + N],
                           lambda o, ro: P.G(lambda e: e.tensor_copy(out=o[:, :N], in_=t2[:, :N]), reads=[res("t2")], writes=[ro]))
                    P.V(lambda e, fc=fc: e.scalar_tensor_tensor(out=c.sq[:, 1, :N], in0=t2[:, :N], scalar=pvs("r_k", fc, 1), in1=rr[:, fc, :N], op0=ALU.mult, op1=ALU.mult),
                        reads=[res("t2"), r_pv, res("rr")], writes=[res("sq")])
                    pb = sumsq_bc(c, lambda k: c.sq[:, 1, :N], 1, N, [res("sq")], lhs_name="blk64")
                    out_bf(rS["bonus"][fc * 128:(fc + 1) * 128, t0:t0 + N],
                           lambda o, ro: P.V(lambda e, fc=fc: e.tensor_tensor(out=o[:, :N], in0=c.pt[pb][:, :N], in1=c.act[:, fc, :N], op=ALU.mult),
                                             reads=[res("pt%d" % pb), res("act")], writes=[ro]))
                if DCUT <= 4:
                    continue
                mix(5)
                for (m0, mm, li) in ((0, 128, 0), (128, 32, 1)):
                    pi = next_pt(c)
                    for kc in range(8):
                        P.T(lambda e, pi=pi, kc=kc, m0=m0, mm=mm: e.matmul(c.pt[pi][:mm, :N], lhsT=g1s[:, kc, m0:m0 + mm], rhs=xm[:, kc, :N], start=(kc == 0), stop=(kc == 7)),
                            reads=[r_lora, res("xm")], writes=[res("pt%d" % pi)])
                    P.S(lambda e, pi=pi, mm=mm, li=li: e.activation(out=lmid[li][:mm, :N], in_=c.pt[pi][:mm, :N], func=AF.Sigmoid),
                        reads=[res("pt%d" % pi)], writes=[res("lmid%d" % li)])
                for fc in range(8):
                    pj = next_pt(c)
                    P.T(lambda e, pj=pj, fc=fc: e.matmul(c.pt[pj][:, :N], lhsT=g2s[:, 0, fc * 128:(fc + 1) * 128], rhs=lmid[0][:, :N], start=True, stop=False),
                        reads=[r_lora, res("lmid0")], writes=[res("pt%d" % pj)])
                    P.T(lambda e, pj=pj, fc=fc: e.matmul(c.pt[pj][:, :N], lhsT=g2s[0:32, 1, fc * 128:(fc + 1) * 128], rhs=lmid[1][0:32, :N], start=False, stop=True),
                        reads=[r_lora, res("lmid1")], writes=[res("pt%d" % pj)])
                    out_bf(rS["g"][fc * 128:(fc + 1) * 128, t0:t0 + N],
                           lambda o, ro: P.V(lambda e, pj=pj: e.tensor_copy(out=o[:, :N], in_=c.pt[pj][:, :N]), reads=[res("pt%d" % pj)], writes=[ro]))
                P.G(lambda e: e.tensor_copy(out=c.hn[:, :, 0:1], in_=c.hn[:, :, N:N + 1]), reads=[res("hn")], writes=[res("hn")])
        P.barrier()
        if stop_after == "D":
            P.finish(); return nc

        def rw_post_setup(st, ngs, SC):
            sb = lambda nm, shape, dt: st.enter_context(nc.sbuf_tensor(un(nm), shape, dt))
            o = TokCtx()
            o.bon = [sb("rbon%d" % i, [128, SC * 128], BF16) for i in range(ngs)]
            o.g = [sb("rgate%d" % i, [128, SC * 128], BF16) for i in range(ngs)]
            o.sq = sb("rpsq", [128, 128], BF16)
            o.yb = sb("rpyb", [128, 128], BF16)
            o.mean = sb("rpmean", [128, 128], F32)
            o.var = sb("rpvar", [128, 128], F32)
            o.yc = sb("rpyc", [128, 128], F32)
            o.ob = [sb("rpob%d" % i, [128, 128], BF16) for i in range(2)]
            _ps = st.enter_context(nc.psum_tensor(un("rpps"), [128, 128], F32))
            o.ps = [_ps, _ps]
            o.k = 0
            return o

        def rw_post(kind, g, grow, c, arg, o):
            if kind == "load":
                P.dma("sync", o.bon[g][:, :arg], rS["bonus"][grow:grow + 128, c * 128:c * 128 + arg], reads=[res("scrrwkv")], writes=[res("rbon%d" % g)])
                P.dma("sync", o.g[g][:, :arg], rS["g"][grow:grow + 128, c * 128:c * 128 + arg], reads=[res("scrrwkv")], writes=[res("rgate%d" % g)])
                return
            yT_, ry, cl = arg
            fc = grow // 128
            cols = slice(cl * 128, (cl + 1) * 128)
            P.G(lambda e: e.tensor_copy(out=o.yb[:], in_=yT_[:]), reads=[ry], writes=[res("rpyb")])
            P.T(lambda e: e.matmul(o.ps[0][:], lhsT=cb["blk64"][:], rhs=o.yb[:], start=True, stop=True), reads=[r_cb, res("rpyb")], writes=[res("rpps")])
            P.V(lambda e: e.scalar_tensor_tensor(out=o.yc[:], in0=o.ps[0][:], scalar=-1.0 / 64, in1=yT_[:], op0=ALU.mult, op1=ALU.add),
                reads=[res("rpps"), ry], writes=[res("rpyc")])
            P.G(lambda e: e.tensor_tensor(out=o.sq[:], in0=o.yc[:], in1=o.yc[:], op=ALU.mult), reads=[res("rpyc")], writes=[res("rpsq")])
            P.T(lambda e: e.matmul(o.ps[1][:], lhsT=cb["blk64"][:], rhs=o.sq[:], start=True, stop=True), reads=[r_cb, res("rpsq")], writes=[res("rpps")])
            P.S(lambda e: e.activation(out=o.var[:], in_=o.ps[1][:], func=AF.Sqrt, bias=cvec[:, 2:3], scale=1.0 / 64), reads=[res("rpps"), r_cb], writes=[res("rpvar")])
            P.V(lambda e: e.reciprocal(out=o.var[:], in_=o.var[:]), reads=[res("rpvar")], writes=[res("rpvar")])
            P.V(lambda e: e.scalar_tensor_tensor(out=o.yc[:], in0=o.yc[:], scalar=pvs("ln_w", fc, 1), in1=o.var[:], op0=ALU.mult, op1=ALU.mult),
                reads=[res("rpyc"), r_pv, res("rpvar")], writes=[res("rpyc")])
            P.V(lambda e: e.scalar_tensor_tensor(out=o.yc[:], in0=o.yc[:], scalar=pvs("ln_b", fc, 1), in1=o.bon[g][:, cols], op0=ALU.add, op1=ALU.add),
                reads=[res("rpyc"), r_pv, res("rbon%d" % g)], writes=[res("rpyc")])
            i = o.k % 2; o.k += 1
            P.V(lambda e: e.tensor_tensor(out=o.ob[i][:], in0=o.yc[:], in1=o.g[g][:, cols], op=ALU.mult), reads=[res("rpyc"), res("rgate%d" % g)], writes=[res("rpob%d" % i)])
            P.dma("sync", zT[grow:grow + 128, c * 128:(c + 1) * 128], o.ob[i][:], reads=[res("rpob%d" % i)], writes=[res("zT_dram")])

        if only is None or "E" in only:
          dplr_phase("rwkv", [[(gq * 128, 0, 64), (gq * 128, 64, 64)] for gq in range(8)], rS, rw_post_setup, rw_post)
        if stop_after == "E":
            P.finish(); return nc

        with contextlib.ExitStack() as st:
            sb = lambda name, shape, dt: st.enter_context(nc.sbuf_tensor(un(name), shape, dt))
            c = make_tok_ctx(st, nwi=4, nwo=4)
            oin = sb("oinF", [128, 8, 512], BF16)
            for ti, (t0, N) in enumerate(tiles):
                h_t, r_h = c.hT[ti % 2], res("hT%d" % (ti % 2))
                P.dma("sync", h_t[:, :, :N], hview(hT)[:, :, t0:t0 + N], reads=[res("hT_dram")], writes=[r_h])
                P.dma("sync", oin[:, :, :N], hview(zT)[:, :, t0:t0 + N], reads=[res("zT_dram")], writes=[res("oinF")])
                proj_residual(c, h_t, r_h, N, oin, res("oinF"), "rwkv_w_o")
                ffn(c, h_t, r_h, N, 3)
                P.S(lambda e: e.activation(out=c.sq[:, :, :N], in_=h_t[:, :, :N], func=AF.Square), reads=[r_h], writes=[res("sq")])
                pi = sumsq_bc(c, lambda k: c.sq[:, k, :N], 8, N, [res("sq")])
                rsqrt_from_psum(c, pi, N, c.rstd[:, :N], res("rstd"))
                for kc in range(8):
                    eng = "vector" if kc % 2 == 0 else "gpsimd"
                    P.op("vector", lambda e, kc=kc: e.scalar_tensor_tensor(out=h_t[:, kc, :N], in0=h_t[:, kc, :N], scalar=pvs("final_norm", kc, 1), in1=c.rstd[:, :N],
                                                                       op0=ALU.mult, op1=ALU.mult), reads=[r_h, r_pv, res("rstd")], writes=[r_h])
                P.dma("sync", hview(outT)[:, :, t0:t0 + N], h_t[:, :, :N], reads=[r_h], writes=[res("out_dram")])
        P.finish()
    return nc


_NC_CACHE = {}


def make_in_maps(inp, T):
    x = np.asarray(inp["x"], np.float32)
    B, S, _ = x.shape
    meta = np.asarray(inp["meta"], np.float32)
    pv = pack_params(inp)
    base = {"pvec": pv}
    k = 0
    for l in range(2):
        for h in range(2):
            base["ffn_w_in%d" % k] = np.ascontiguousarray(inp["ffn_w_in"][l, h], np.float32)
            base["ffn_w_out%d" % k] = np.ascontiguousarray(inp["ffn_w_out"][l, h], np.float32)
            k += 1
    base["hyb_w_in"] = np.ascontiguousarray(inp["hyb_w_in"][0], np.float32)
    base["hyb_w_out"] = np.ascontiguousarray(inp["hyb_w_out"][0], np.float32)
    for nm in ("w_r", "w_k", "w_v", "w_o", "w1", "w2", "a1", "a2", "g1", "g2"):
        base["rwkv_" + nm] = np.ascontiguousarray(inp["rwkv_" + nm][0], np.float32)
    maps = []
    for core in range(8):
        b = core % B
        hT0 = np.zeros((D, T), np.float32)
        hT0[:, :NMETA] = meta.T
        hT0[:, NMETA:NMETA + S] = x[b].T
        m = dict(base)
        m["hT0"] = hT0
        maps.append(m)
    return maps


def kernel(**inp):
    x = np.asarray(inp["x"])
    B, S, _ = x.shape
    T = ((NMETA + S + 127) // 128) * 128
    if T not in _NC_CACHE:
        _NC_CACHE[T] = build(T)
    nc = _NC_CACHE[T]
    maps = make_in_maps(inp, T)
    res = run_bass_kernel_spmd(nc, maps, core_ids=list(range(8)))
    out = np.empty((B, S, D), np.float32)
    for b in range(B):
        out[b] = res.results[b]["outT"][:, NMETA:NMETA + S].T
    return out
```
